# Optimizing a Trainium2 kernel written in Bass

```python
import math
import jax, jax.numpy as jnp
from jax import lax
import numpy as np


D_MODEL = 1024
BATCH = 8
SEQ = 8192
DEPTH = 2

SSM_WIDTH = D_MODEL // 2
SSM_GROUP = 16
SSM_GROUPS = SSM_WIDTH // SSM_GROUP
SSM_STATE = 64
DT_MIN = 1e-3
DT_MAX = 1e-1
N_Q_HEADS = 8
N_KV_HEADS = 2
HEAD_DIM = 64
GQA_GROUP = N_Q_HEADS // N_KV_HEADS
CMP_BLOCK = 32
CMP_STRIDE = 16
CMP_HIDDEN = 256
SEL_BLOCK = 64
N_SEL = 16
WINDOW = 512
Q_BLOCK = 128
ROPE_THETA = 10000.0
N_NSA_BRANCH = 3
D_FF = 2816
CONV_WIDTH = 3
RMS_EPS = 1e-6
NEG_INF = -1e30
FORCE_BONUS = 1e4
Q_WIDTH = N_Q_HEADS * HEAD_DIM
KV_WIDTH = N_KV_HEADS * HEAD_DIM
IN_SPLITS = (SSM_WIDTH, Q_WIDTH, KV_WIDTH, KV_WIDTH, KV_WIDTH, KV_WIDTH, KV_WIDTH, KV_WIDTH, N_Q_HEADS * N_NSA_BRANCH, D_MODEL, D_MODEL)
IN_WIDTH = sum(IN_SPLITS)

kernel_name = 'hybrid_s5_nsa_convffn'


def rmsnorm(x, g):
    xf = x.astype(jnp.float32)
    y = xf * lax.rsqrt(jnp.mean(xf * xf, axis=-1, keepdims=True) + RMS_EPS)
    return (y * g.astype(jnp.float32)).astype(x.dtype)


def rope_tables(seq):
    inv_freq = ROPE_THETA ** (-jnp.arange(0, HEAD_DIM, 2, dtype=jnp.float32) / HEAD_DIM)
    ang = jnp.arange(seq, dtype=jnp.float32)[:, None] * inv_freq[None, :]
    return jnp.cos(ang), jnp.sin(ang)


def apply_rope(x, cos, sin):
    x1, x2 = jnp.split(x, 2, axis=-1)
    c = cos[None, :, None, :]
    s = sin[None, :, None, :]
    return jnp.concatenate([x1 * c - x2 * s, x1 * s + x2 * c], axis=-1).astype(x.dtype)


def masked_softmax(s, mask):
    p = jax.nn.softmax(jnp.where(mask, s, NEG_INF), axis=-1)
    return jnp.where(mask, p, 0.0)


def complex_affine_combine(e1, e2):
    a1r, a1i, b1r, b1i = e1
    a2r, a2i, b2r, b2i = e2
    return (a1r * a2r - a1i * a2i,
            a1r * a2i + a1i * a2r,
            a2r * b1r - a2i * b1i + b2r,
            a2r * b1i + a2i * b1r + b2i)


def s5_mixer(u, a_re, a_im, log_dt, b_re, b_im, c_re, c_im, d, w_glu):
    bsz, seq, _ = u.shape
    f32 = jnp.float32
    uf = u.astype(f32).reshape(bsz, seq, SSM_GROUPS, SSM_GROUP)
    lr = a_re.astype(f32)
    li = a_im.astype(f32)
    dt = jnp.exp(log_dt.astype(f32))[:, None]
    mag = jnp.exp(lr * dt)
    ab_re = mag * jnp.cos(li * dt)
    ab_im = mag * jnp.sin(li * dt)
    den = lr * lr + li * li
    f_re = ((ab_re - 1.0) * lr + ab_im * li) / den
    f_im = (ab_im * lr - (ab_re - 1.0) * li) / den
    bu_re = jnp.einsum('bsgh,gph->bsgp', uf, b_re.astype(f32))
    bu_im = jnp.einsum('bsgh,gph->bsgp', uf, b_im.astype(f32))
    bb_re = f_re * bu_re - f_im * bu_im
    bb_im = f_re * bu_im + f_im * bu_re
    a_seq_re = jnp.broadcast_to(ab_re, (1, seq, SSM_GROUPS, SSM_STATE))
    a_seq_im = jnp.broadcast_to(ab_im, (1, seq, SSM_GROUPS, SSM_STATE))
    _, _, x_re, x_im = lax.associative_scan(complex_affine_combine, (a_seq_re, a_seq_im, bb_re, bb_im), axis=1)
    y = (jnp.einsum('bsgp,ghp->bsgh', x_re, c_re.astype(f32))
         - jnp.einsum('bsgp,ghp->bsgh', x_im, c_im.astype(f32))
         + d.astype(f32).reshape(SSM_GROUPS, SSM_GROUP) * uf)
    y = jax.nn.gelu(y.reshape(bsz, seq, SSM_WIDTH)).astype(u.dtype)
    z_val, z_gate = jnp.split(y @ w_glu, 2, axis=-1)
    return z_val * jax.nn.sigmoid(z_gate)


def compress_blocks(k, pe, w1, b1, w2):
    bsz, seq = k.shape[:2]
    n_cmp = (seq - CMP_BLOCK) // CMP_STRIDE + 1
    idx = jnp.arange(n_cmp)[:, None] * CMP_STRIDE + jnp.arange(CMP_BLOCK)[None, :]
    blocks = k[:, idx] + pe[None, None, :, None, :]
    blocks = blocks.transpose(0, 1, 3, 2, 4).reshape(bsz, n_cmp, N_KV_HEADS, CMP_BLOCK * HEAD_DIM)
    return jax.nn.gelu(blocks @ w1 + b1) @ w2


def nsa_mixer(q, k_cmp, v_cmp, k_sel, v_sel, k_win, v_win, gate_logits,
              pe_k, w1_k, b1_k, w2_k, pe_v, w1_v, b1_v, w2_v):
    bsz, seq = q.shape[:2]
    f32 = jnp.float32
    scale = HEAD_DIM ** -0.5
    n_blk = seq // SEL_BLOCK
    n_top = min(N_SEL, n_blk)
    ratio = SEL_BLOCK // CMP_STRIDE
    span = ratio + CMP_BLOCK // CMP_STRIDE - 1
    q = q.reshape(bsz, seq, N_KV_HEADS, GQA_GROUP, HEAD_DIM)
    gates = jax.nn.sigmoid(gate_logits).reshape(bsz, seq, N_KV_HEADS, GQA_GROUP, N_NSA_BRANCH)
    kc = compress_blocks(k_cmp, pe_k, w1_k, b1_k, w2_k)
    vc = compress_blocks(v_cmp, pe_v, w1_v, b1_v, w2_v)
    n_cmp = kc.shape[1]
    cmp_end = jnp.arange(n_cmp) * CMP_STRIDE + CMP_BLOCK - 1
    pad_l = CMP_BLOCK // CMP_STRIDE - 1
    pad_r = ratio * (n_blk - 1) + span - n_cmp - pad_l
    kb = k_sel.reshape(bsz, n_blk, SEL_BLOCK, N_KV_HEADS, HEAD_DIM).transpose(0, 3, 1, 2, 4)
    vb = v_sel.reshape(bsz, n_blk, SEL_BLOCK, N_KV_HEADS, HEAD_DIM).transpose(0, 3, 1, 2, 4)
    kw_pad = jnp.pad(k_win, ((0, 0), (WINDOW, 0), (0, 0), (0, 0)))
    vw_pad = jnp.pad(v_win, ((0, 0), (WINDOW, 0), (0, 0), (0, 0)))
    blk_ids = jnp.arange(n_blk)
    gather_blocks = jax.vmap(jax.vmap(lambda blocks, ids: blocks[ids]))

    def query_block(qi):
        s0 = qi * Q_BLOCK
        qb = lax.dynamic_slice_in_dim(q, s0, Q_BLOCK, axis=1)
        gb = lax.dynamic_slice_in_dim(gates, s0, Q_BLOCK, axis=1)
        t = s0 + jnp.arange(Q_BLOCK)
        s_c = jnp.einsum('bqhgd,bnhd->bhgqn', qb, kc).astype(f32) * scale
        p_c = masked_softmax(s_c, cmp_end[None, :] <= t[:, None])
        o_cmp = jnp.einsum('bhgqn,bnhd->bqhgd', p_c.astype(vc.dtype), vc)
        p_grp = jnp.pad(p_c.sum(axis=2), ((0, 0), (0, 0), (0, 0), (pad_l, pad_r)))
        blk_score = sum(p_grp[..., o: o + ratio * (n_blk - 1) + 1: ratio] for o in range(span))
        cur = t // SEL_BLOCK
        forced = (blk_ids[None, :] == 0) | (blk_ids[None, :] == cur[:, None]) | (blk_ids[None, :] == cur[:, None] - 1)
        causal_b = blk_ids[None, :] * SEL_BLOCK <= t[:, None]
        blk_score = jnp.where(causal_b, blk_score + jnp.where(forced, FORCE_BONUS, 0.0), NEG_INF)
        _, top_idx = lax.top_k(blk_score, n_top)
        k_g = gather_blocks(kb, top_idx)
        v_g = gather_blocks(vb, top_idx)
        k_pos = top_idx[..., None] * SEL_BLOCK + jnp.arange(SEL_BLOCK)
        sel_mask = (k_pos <= t[None, None, :, None, None]).reshape(bsz, N_KV_HEADS, 1, Q_BLOCK, n_top * SEL_BLOCK)
        s_s = jnp.einsum('bqhgd,bhqnkd->bhgqnk', qb, k_g).astype(f32) * scale
        p_s = masked_softmax(s_s.reshape(bsz, N_KV_HEADS, GQA_GROUP, Q_BLOCK, n_top * SEL_BLOCK), sel_mask)
        o_sel = jnp.einsum('bhgqnk,bhqnkd->bqhgd', p_s.reshape(s_s.shape).astype(v_g.dtype), v_g)
        k_w = lax.dynamic_slice_in_dim(kw_pad, s0, WINDOW + Q_BLOCK, axis=1)
        v_w = lax.dynamic_slice_in_dim(vw_pad, s0, WINDOW + Q_BLOCK, axis=1)
        w_pos = s0 - WINDOW + jnp.arange(WINDOW + Q_BLOCK)
        win_mask = (w_pos[None, :] <= t[:, None]) & (w_pos[None, :] > t[:, None] - WINDOW) & (w_pos[None, :] >= 0)
        s_w = jnp.einsum('bqhgd,bkhd->bhgqk', qb, k_w).astype(f32) * scale
        p_w = masked_softmax(s_w, win_mask)
        o_win = jnp.einsum('bhgqk,bkhd->bqhgd', p_w.astype(v_w.dtype), v_w)
        return gb[..., 0:1] * o_cmp + gb[..., 1:2] * o_sel + gb[..., 2:3] * o_win

    out = lax.map(query_block, jnp.arange(seq // Q_BLOCK))
    return jnp.moveaxis(out, 0, 1).reshape(bsz, seq, Q_WIDTH)


def hybrid_mixer(h, cos, sin, w_in, a_re, a_im, log_dt, b_re, b_im, c_re, c_im, d, w_glu,
                 pe_k, w1_k, b1_k, w2_k, pe_v, w1_v, b1_v, w2_v, w_branch_ssm, w_branch_nsa, w_out):
    bsz, seq, _ = h.shape
    split_at = np.cumsum(IN_SPLITS)[:-1].tolist()
    (u, q, k_cmp, v_cmp, k_sel, v_sel, k_win, v_win,
     nsa_gate, g_ssm, g_nsa) = jnp.split(h @ w_in, split_at, axis=-1)
    y_ssm = s5_mixer(u, a_re, a_im, log_dt, b_re, b_im, c_re, c_im, d, w_glu) @ w_branch_ssm
    heads = lambda z, n: z.reshape(bsz, seq, n, HEAD_DIM)
    q = apply_rope(heads(q, N_Q_HEADS), cos, sin)
    k_cmp = apply_rope(heads(k_cmp, N_KV_HEADS), cos, sin)
    k_sel = apply_rope(heads(k_sel, N_KV_HEADS), cos, sin)
    k_win = apply_rope(heads(k_win, N_KV_HEADS), cos, sin)
    y_nsa = nsa_mixer(q, k_cmp, heads(v_cmp, N_KV_HEADS), k_sel, heads(v_sel, N_KV_HEADS),
                      k_win, heads(v_win, N_KV_HEADS), nsa_gate,
                      pe_k, w1_k, b1_k, w2_k, pe_v, w1_v, b1_v, w2_v) @ w_branch_nsa
    merged = jax.nn.sigmoid(g_ssm) * y_ssm + jax.nn.sigmoid(g_nsa) * y_nsa
    return merged @ w_out


def conv_ffn(h, w_in, conv_w, conv_b, w_out):
    a, b = jnp.split(h @ w_in, 2, axis=-1)
    a = lax.conv_general_dilated(a, conv_w[:, None, :], window_strides=(1,),
                                 padding=[(CONV_WIDTH - 1, 0)],
                                 dimension_numbers=('NWC', 'WIO', 'NWC'),
                                 feature_group_count=D_FF) + conv_b
    return (jax.nn.gelu(a) * b) @ w_out


def setup_inputs(seed: int = 0) -> dict:
    key = jax.random.key(seed)
    ks = jax.random.split(key, 29)
    f32 = jnp.float32

    def nrm(k, shape, scale):
        return jax.random.normal(k, shape, f32) * scale

    L, G, P, H = DEPTH, SSM_GROUPS, SSM_STATE, SSM_GROUP
    a_im0 = math.pi * jnp.arange(P, dtype=f32)
    cmp_in = CMP_BLOCK * HEAD_DIM
    return {
        'x': nrm(ks[0], (BATCH, SEQ, D_MODEL), 1.0),
        'norm_mix': 1.0 + nrm(ks[1], (L, D_MODEL), 0.01),
        'w_in': nrm(ks[2], (L, D_MODEL, IN_WIDTH), D_MODEL ** -0.5),
        'ssm_a_re': -0.5 + nrm(ks[3], (L, G, P), 0.01),
        'ssm_a_im': a_im0 + nrm(ks[4], (L, G, P), 0.01),
        'ssm_log_dt': jax.random.uniform(ks[5], (L, G), f32, math.log(DT_MIN), math.log(DT_MAX)),
        'ssm_b_re': nrm(ks[6], (L, G, P, H), (2 * H) ** -0.5),
        'ssm_b_im': nrm(ks[7], (L, G, P, H), (2 * H) ** -0.5),
        'ssm_c_re': nrm(ks[8], (L, G, H, P), P ** -0.5),
        'ssm_c_im': nrm(ks[9], (L, G, H, P), P ** -0.5),
        'ssm_d': nrm(ks[10], (L, SSM_WIDTH), 1.0),
        'ssm_w_glu': nrm(ks[11], (L, SSM_WIDTH, 2 * SSM_WIDTH), SSM_WIDTH ** -0.5),
        'cmp_pe_k': nrm(ks[12], (L, CMP_BLOCK, HEAD_DIM), 0.02),
        'cmp_w1_k': nrm(ks[13], (L, cmp_in, CMP_HIDDEN), cmp_in ** -0.5),
        'cmp_b1_k': nrm(ks[14], (L, CMP_HIDDEN), 0.01),
        'cmp_w2_k': nrm(ks[15], (L, CMP_HIDDEN, HEAD_DIM), CMP_HIDDEN ** -0.5),
        'cmp_pe_v': nrm(ks[16], (L, CMP_BLOCK, HEAD_DIM), 0.02),
        'cmp_w1_v': nrm(ks[17], (L, cmp_in, CMP_HIDDEN), cmp_in ** -0.5),
        'cmp_b1_v': nrm(ks[18], (L, CMP_HIDDEN), 0.01),
        'cmp_w2_v': nrm(ks[19], (L, CMP_HIDDEN, HEAD_DIM), CMP_HIDDEN ** -0.5),
        'w_branch_ssm': nrm(ks[20], (L, SSM_WIDTH, D_MODEL), SSM_WIDTH ** -0.5),
        'w_branch_nsa': nrm(ks[21], (L, Q_WIDTH, D_MODEL), Q_WIDTH ** -0.5),
        'w_out': nrm(ks[22], (L, D_MODEL, D_MODEL), D_MODEL ** -0.5),
        'norm_ffn': 1.0 + nrm(ks[23], (L, D_MODEL), 0.01),
        'w_ffn_in': nrm(ks[24], (L, D_MODEL, 2 * D_FF), D_MODEL ** -0.5),
        'ffn_conv_w': nrm(ks[25], (L, CONV_WIDTH, D_FF), CONV_WIDTH ** -0.5),
        'ffn_conv_b': nrm(ks[26], (L, D_FF), 0.01),
        'w_ffn_out': nrm(ks[27], (L, D_FF, D_MODEL), D_FF ** -0.5),
        'norm_final': 1.0 + nrm(ks[28], (D_MODEL,), 0.01),
    }


def reference(x, norm_mix, w_in, ssm_a_re, ssm_a_im, ssm_log_dt, ssm_b_re, ssm_b_im, ssm_c_re, ssm_c_im,
              ssm_d, ssm_w_glu, cmp_pe_k, cmp_w1_k, cmp_b1_k, cmp_w2_k, cmp_pe_v, cmp_w1_v, cmp_b1_v,
              cmp_w2_v, w_branch_ssm, w_branch_nsa, w_out, norm_ffn, w_ffn_in, ffn_conv_w, ffn_conv_b,
              w_ffn_out, norm_final):
    cos, sin = rope_tables(x.shape[1])
    for l in range(DEPTH):
        h = rmsnorm(x, norm_mix[l])
        x = x + hybrid_mixer(h, cos, sin, w_in[l], ssm_a_re[l], ssm_a_im[l], ssm_log_dt[l],
                             ssm_b_re[l], ssm_b_im[l], ssm_c_re[l], ssm_c_im[l], ssm_d[l], ssm_w_glu[l],
                             cmp_pe_k[l], cmp_w1_k[l], cmp_b1_k[l], cmp_w2_k[l],
                             cmp_pe_v[l], cmp_w1_v[l], cmp_b1_v[l], cmp_w2_v[l],
                             w_branch_ssm[l], w_branch_nsa[l], w_out[l])
        h = rmsnorm(x, norm_ffn[l])
        x = x + conv_ffn(h, w_ffn_in[l], ffn_conv_w[l], ffn_conv_b[l], w_ffn_out[l])
    return rmsnorm(x, norm_final)
```

```python
from contextlib import ExitStack
import numpy as np
import ml_dtypes
import concourse.bass as bass
import concourse.mybir as mybir
from concourse.bass_utils import run_bass_kernel_spmd

F32 = mybir.dt.float32
BF16 = mybir.dt.bfloat16
I32 = mybir.dt.int32
ALU = mybir.AluOpType
AF = mybir.ActivationFunctionType
AX = mybir.AxisListType

D = 1024
DFF = 2816
NFC = DFF // 128
INW = 3864
EPS = 1e-6
NEG = -30000.0
TWO_PI = float(2 * np.pi)
SIN_SCALE = TWO_PI * 0.999999


class Buf:
    __slots__ = ("name", "w", "rs", "sem", "cnt", "slot", "base", "uid")

    def __init__(self, name="b"):
        self.name = name
        self.w = None
        self.rs = {}
        self.sem = None
        self.cnt = 0
        self.slot = None
        self.base = 0
        self.uid = None


class Op:
    __slots__ = ("eng", "fn", "deps", "key", "val", "signal", "sigval", "dma", "slot", "semval")


ENGS = ("pe", "act", "dve", "pool", "sp")


class Prog:
    def __init__(self, nc):
        self.nc = nc
        self.ops = {e: [] for e in ENGS}
        self.seen = {e: {} for e in ENGS}
        self.dma_bufs = []
        self.last = {}
        self.slot_base = []
        self.free_slots = []
        self.live = []
        self.uid = 0

    def _get_slot(self, buf):
        if self.free_slots:
            sl = self.free_slots.pop()
        else:
            sl = len(self.slot_base)
            self.slot_base.append(0)
        self.uid += 1
        buf.sem = True
        buf.slot = sl
        buf.base = self.slot_base[sl]
        buf.cnt = 0
        buf.uid = self.uid
        self.live.append(buf)

    def barrier(self):
        lasts = list(self.last.values())
        self._barrier_ops(lasts)
        for b in self.live:
            self.slot_base[b.slot] = b.base + b.cnt
            self.free_slots.append(b.slot)
            b.sem = None
        self.live = []
        self.last = {kk: v for kk, v in self.last.items() if not isinstance(kk, tuple)}

    def _barrier_ops(self, lasts):
        for e in ENGS:
            o = Op()
            o.eng = e
            o.fn = None
            o.deps = []
            o.signal = False
            o.sigval = None
            o.dma = None
            o.key = e
            o.val = len(self.ops[e])
            for d in lasts:
                if d.key == e:
                    continue
                if self.seen[e].get(d.key, -1) >= d.val:
                    continue
                self.seen[e][d.key] = d.val
                o.deps.append(d)
            self.ops[e].append(o)

    def _dep(self, eng, d, deps, same_ok):
        if d is None:
            return
        key = d.key
        if key == eng:
            if eng == "pe" or same_ok:
                return
        if self.seen[eng].get(key, -1) >= d.val:
            return
        self.seen[eng][key] = d.val
        deps.append(d)

    def op(self, eng, fn, reads=(), writes=(), dma=None):
        o = Op()
        o.eng = eng
        o.fn = fn
        o.deps = []
        o.signal = False
        o.sigval = None
        o.dma = dma
        writes = [b for b in writes if b is not None]
        reads = [b for b in reads if b is not None]
        o.slot = None
        o.semval = None
        if dma is not None:
            if dma.sem is None:
                self._get_slot(dma)
            dma.cnt += 1
            o.key = ("dma", dma.uid)
            o.val = dma.cnt
            o.slot = dma.slot
            o.semval = 16 * (dma.base + dma.cnt)
            if dma not in writes:
                writes.append(dma)
            reads = [b for b in reads if b is not dma]
        else:
            o.key = eng
            o.val = len(self.ops[eng])
        for b in reads:
            self._dep(eng, b.w, o.deps, False)
        for b in writes:
            self._dep(eng, b.w, o.deps, True)
            for r in b.rs.values():
                self._dep(eng, r, o.deps, True)
        for b in reads:
            b.rs[o.key] = o
        for b in writes:
            b.w = o
            b.rs = {}
        self.ops[eng].append(o)
        self.last[o.key] = o
        return o

    def emit(self, stack):
        nc = self.nc
        for e in ENGS:
            for o in self.ops[e]:
                for d in o.deps:
                    if d.dma is None:
                        d.signal = True
        for e in ENGS:
            c = 0
            for o in self.ops[e]:
                if o.dma is None and o.signal:
                    c += 1
                    o.sigval = c
        esem = {}
        for e in ("pe", "act", "dve", "pool"):
            esem[e] = stack.enter_context(nc.semaphore("s_" + e))
        dsem = [stack.enter_context(nc.semaphore("d%d" % i)) for i in range(len(self.slot_base))]
        block = stack.enter_context(nc.Block())
        prog = self

        def run(name, eng):
            for o in prog.ops[name]:
                for d in o.deps:
                    if d.dma is not None:
                        eng.wait_ge(dsem[d.slot], d.semval)
                    else:
                        eng.wait_ge(esem[d.key], d.sigval)
                if o.fn is None:
                    continue
                ins = o.fn(eng)
                if o.dma is not None:
                    ins.then_inc(dsem[o.slot], 16)
                elif o.signal:
                    ins.then_inc(esem[name], 1)

        @block.sync
        def _(eng):
            run("sp", eng)

        @block.scalar
        def _(eng):
            run("act", eng)

        @block.vector
        def _(eng):
            run("dve", eng)

        @block.gpsimd
        def _(eng):
            run("pool", eng)

        @block.tensor
        def _(eng):
            run("pe", eng)


class K:
    def __init__(self, nc, P):
        self.nc = nc
        self.P = P
        self.n = 0

    def name(self, s):
        self.n += 1
        return "%s_%d" % (s, self.n)

    def sb(self, st, shape, dt=F32, name="t"):
        t = st.enter_context(self.nc.sbuf_tensor(self.name(name), list(shape), dt))
        return t, Buf(name)

    def dma(self, eng, out, in_, buf, reads=(), writes=()):
        self.P.op(eng, lambda e: e.dma_start(out=out, in_=in_), reads=reads, writes=writes, dma=buf)

    def mm(self, out, lhsT, rhs, start, stop, r, w):
        self.P.op("pe", lambda e: e.matmul(out, lhsT=lhsT, rhs=rhs, start=start, stop=stop), reads=r, writes=w)

    def tr(self, out, in_, ident, r, w):
        self.P.op("pe", lambda e: e.transpose(out=out, in_=in_, identity=ident), reads=r, writes=w)

    def act(self, out, in_, func, r, w, bias=None, scale=None, accum=None):
        kw = {}
        if bias is not None:
            kw["bias"] = bias
        if scale is not None:
            kw["scale"] = scale
        if accum is not None:
            kw["accum_out"] = accum
        self.P.op("act", lambda e: e.activation(out=out, in_=in_, func=func, **kw), reads=r, writes=w)

    def tt(self, eng, out, in0, in1, op, r, w):
        self.P.op(eng, lambda e: e.tensor_tensor(out=out, in0=in0, in1=in1, op=op), reads=r, writes=w)

    def ts(self, eng, out, in0, s1, op0, r, w, s2=None, op1=None):
        if op1 is None:
            self.P.op(eng, lambda e: e.tensor_scalar(out=out, in0=in0, scalar1=s1, scalar2=None, op0=op0), reads=r, writes=w)
        else:
            self.P.op(eng, lambda e: e.tensor_scalar(out=out, in0=in0, scalar1=s1, scalar2=s2, op0=op0, op1=op1), reads=r, writes=w)

    def stt(self, out, in0, scalar, in1, op0, op1, r, w):
        self.P.op("dve", lambda e: e.scalar_tensor_tensor(out=out, in0=in0, scalar=scalar, in1=in1, op0=op0, op1=op1), reads=r, writes=w)

    def copy(self, eng, out, in_, r, w):
        if eng == "act":
            self.P.op("act", lambda e: e.activation(out=out, in_=in_, func=AF.Copy), reads=r, writes=w)
        else:
            self.P.op(eng, lambda e: e.tensor_copy(out=out, in_=in_), reads=r, writes=w)

    def memset(self, eng, ap, val, w):
        self.P.op(eng, lambda e: e.memset(ap, val), writes=w)

    def scan(self, out, d0, d1, init, r, w):
        self.P.op("dve", lambda e: e.tensor_tensor_scan(out=out, data0=d0, data1=d1, initial=init, op0=ALU.mult, op1=ALU.add), reads=r, writes=w)

    def recip(self, out, in_, r, w):
        self.P.op("dve", lambda e: e.reciprocal(out=out, in_=in_), reads=r, writes=w)


class PsumRing:
    def __init__(self, k, st, n=8):
        self.banks = []
        for i in range(n):
            t = st.enter_context(k.nc.psum_tensor(k.name("ps"), [128, 512], F32))
            self.banks.append((t, Buf("ps%d" % i)))
        self.i = 0

    def get(self):
        t, b = self.banks[self.i % len(self.banks)]
        self.i += 1
        return t, b


def _swap_halves(w):
    sh = w.shape
    w4 = w.reshape(sh[:-1] + (sh[-1] // 64, 2, 32))
    return np.ascontiguousarray(w4[..., ::-1, :]).reshape(sh)


def host_consts(S, TS):
    c = {}
    inv = (10000.0 ** (-np.arange(0, 64, 2, dtype=np.float32) / np.float32(64))).astype(np.float32)
    ang = (np.arange(S, dtype=np.float32)[:, None] * inv[None, :]).astype(np.float32)
    cs, sn = np.cos(ang).astype(np.float32), np.sin(ang).astype(np.float32)
    cosT = np.concatenate([cs.T, cs.T], 0)
    sinT = np.concatenate([-sn.T, sn.T], 0)
    c["c_cos"] = np.ascontiguousarray(np.concatenate([cosT, cosT], 0))
    c["c_sin"] = np.ascontiguousarray(np.concatenate([sinT, sinT], 0))
    c["c_ident"] = np.eye(128, dtype=np.float32)
    c["c_tau"] = np.ascontiguousarray(np.broadcast_to(np.arange(TS, dtype=np.float32)[None, :], (128, TS)))
    k = np.arange(128)[:, None]
    q = np.arange(128)[None, :]
    c["c_caus"] = np.where(k <= q, 0.0, NEG).astype(np.float32)
    c["c_low"] = np.where(k > q, 0.0, NEG).astype(np.float32)
    cc = np.arange(1024)[None, :]
    qi = np.arange(128)[:, None]
    c["c_mgen"] = np.where(16 * (cc - 512) + 31 <= qi, 0.0, NEG).astype(np.float32)
    m = np.arange(16)[None, :, None]
    ni = np.arange(128)[:, None, None]
    qq = np.arange(128)[None, None, :]
    c["c_mt"] = np.where(16 * ni + 31 <= 128 * m + qq, 0.0, NEG).astype(np.float32)
    rel = np.arange(256)[None, :] - 126
    cur = (np.arange(128)[:, None] >= 64).astype(np.int64)
    g = np.where(rel > cur, -1e30, 0.0) + np.where((rel == cur) | (rel == cur - 1), 1e4, 0.0)
    c["c_g"] = g.astype(np.float32)
    NT = S // 128
    j = np.arange(128)[:, None, None]
    i = np.arange(NT)[None, :, None]
    kk = np.arange(128)[None, None, :]
    c["c_e"] = (j == 2 * i + (kk >= 64)).astype(np.float32)
    return c


def host_layout(inp, L):
    o = {}
    f = np.float32
    o["g_mix"] = np.ascontiguousarray(np.broadcast_to(inp["norm_mix"][:, None, :], (L, 128, D))).astype(f)
    o["g_ffn"] = np.ascontiguousarray(np.broadcast_to(inp["norm_ffn"][:, None, :], (L, 128, D))).astype(f)
    o["g_fin"] = np.ascontiguousarray(np.broadcast_to(inp["norm_final"][None, :], (128, D))).astype(f)
    w_in = inp["w_in"]
    o["w_in"] = w_in
    sw = np.concatenate([_swap_halves(w_in[:, :, 512:1024]), _swap_halves(w_in[:, :, 1024:1152]),
                         _swap_halves(w_in[:, :, 1280:1408]), _swap_halves(w_in[:, :, 1536:1664])], axis=-1)
    o["w_sw"] = np.ascontiguousarray(sw)

    def pair(a):
        return np.ascontiguousarray(a.reshape(L, 16, 2, 64).transpose(0, 2, 3, 1).reshape(L, 128, 16))
    o["s_are"] = pair(inp["ssm_a_re"])
    o["s_aim"] = pair(inp["ssm_a_im"])
    o["s_ldt"] = pair(np.broadcast_to(inp["ssm_log_dt"][:, :, None], (L, 32, 64)))
    for nm, src in (("s_bre", "ssm_b_re"), ("s_bim", "ssm_b_im")):
        b = inp[src].reshape(L, 16, 2, 64, 16)
        pad = np.zeros((L, 8, 16, 16, 2, 64), f)
        for j in range(16):
            for gl in range(2):
                pad[:, 2 * (j % 4) + gl, :, j, gl, :] = b[:, j, gl].transpose(0, 2, 1)
        o[nm] = pad.reshape(L, 128, 16, 128)
    for nm, src in (("s_cre", "ssm_c_re"), ("s_cim", "ssm_c_im")):
        cmat = inp[src].reshape(L, 16, 2, 16, 64)
        pad = np.zeros((L, 2, 64, 16, 8, 16), f)
        for j in range(16):
            for gl in range(2):
                pad[:, gl, :, j, 2 * (j % 4) + gl, :] = cmat[:, j, gl].transpose(0, 2, 1)
        o[nm] = pad.reshape(L, 128, 16, 128)
    o["s_d"] = np.ascontiguousarray(inp["ssm_d"].reshape(L, 4, 128).transpose(0, 2, 1))
    o["w_glu"] = inp["ssm_w_glu"]
    o["w_bssm"] = inp["w_branch_ssm"]
    o["w_bnsa"] = inp["w_branch_nsa"]
    o["w_out"] = inp["w_out"]
    for t in ("k", "v"):
        o["c_pe" + t] = np.ascontiguousarray(inp["cmp_pe_" + t].transpose(0, 2, 1))
        o["c_w1" + t] = inp["cmp_w1_" + t]
        o["c_b1" + t] = np.ascontiguousarray(inp["cmp_b1_" + t].reshape(L, 2, 128).transpose(0, 2, 1))
        o["c_w2" + t] = inp["cmp_w2_" + t]
    o["w_fin"] = inp["w_ffn_in"]
    o["w_fout"] = inp["w_ffn_out"]
    o["f_cw"] = np.ascontiguousarray(inp["ffn_conv_w"].reshape(L, 3, NFC, 128).transpose(0, 3, 2, 1))
    o["f_cb"] = np.ascontiguousarray(inp["ffn_conv_b"].reshape(L, NFC, 128).transpose(0, 2, 1))
    return o


def load_w(k, st, src2d, nch, ncols, prow=128, eng="pool", name="w"):
    t, _ = k.sb(st, [prow, nch, ncols], BF16, name)
    bufs = []
    for c in range(nch):
        b = Buf(name)
        k.dma(eng, t[:, c, :], src2d[c * prow:(c + 1) * prow, :], b)
        bufs.append(b)
    return t, bufs


def sincos(k, st, arg, n, out_sin=None, out_cos=None, rb=(), wsin=None, wcos=None):
    for (dst, off, wb) in ((out_sin, 0.0, wsin), (out_cos, 0.25, wcos)):
        if dst is None:
            continue
        a2, ba2 = k.sb(st, [128, n], F32, "sc_a")
        ti, bti = k.sb(st, [128, n], I32, "sc_i")
        tf, btf = k.sb(st, [128, n], F32, "sc_f")
        k.ts("dve", a2[:], arg, off, ALU.add, r=list(rb), w=[ba2])
        k.copy("dve", ti[:], a2[:], r=[ba2], w=[bti])
        k.copy("dve", tf[:], ti[:], r=[bti], w=[btf])
        k.tt("dve", a2[:], a2[:], tf[:], ALU.subtract, r=[ba2, btf], w=[ba2])
        k.act(dst, a2[:], AF.Sin, r=[ba2], w=[wb], scale=SIN_SCALE)


def phase1(k, l, S, TS, x_src, di, sc, cst):
    P = k.P
    TT = TS
    NTT = S // TT
    with ExitStack() as st:
        ring = PsumRing(k, st)
        ident, bid = cst["ident"]
        gam, bgam = k.sb(st, [128, D], F32, "gam")
        k.dma("sp", gam[:], di["g_mix"][l], bgam)
        dvec, bdvec = k.sb(st, [128, 4], F32, "dvec")
        k.dma("sp", dvec[:], di["s_d"][l], bdvec)
        RFre, bRFre = k.sb(st, [128, 16, TS], F32, "RFre")
        RFim, bRFim = k.sb(st, [128, 16, TS], F32, "RFim")
        COSb, bCOSb = k.sb(st, [128, 16, TS], BF16, "COSb")
        SINb, bSINb = k.sb(st, [128, 16, TS], BF16, "SINb")
        NSINb, bNSINb = k.sb(st, [128, 16, TS], BF16, "NSINb")
        dec, bdec = k.sb(st, [128, 16], F32, "dec")
        cT, bcT = k.sb(st, [128, 16], F32, "cT")
        sT, bsT = k.sb(st, [128, 16], F32, "sT")
        nsT, bnsT = k.sb(st, [128, 16], F32, "nsT")
        with ExitStack() as s2:
            are, bare = k.sb(s2, [128, 16], F32, "are")
            aim, baim = k.sb(s2, [128, 16], F32, "aim")
            ldt, bldt = k.sb(s2, [128, 16], F32, "ldt")
            tau, btau = k.sb(s2, [128, TS], F32, "tau")
            k.dma("sp", are[:], di["s_are"][l], bare)
            k.dma("sp", aim[:], di["s_aim"][l], baim)
            k.dma("sp", ldt[:], di["s_ldt"][l], bldt)
            k.dma("sp", tau[:], di["c_tau"], btau)
            dt_, bdt = k.sb(s2, [128, 16], F32, "dt")
            k.act(dt_[:], ldt[:], AF.Exp, r=[bldt], w=[bdt])
            rho, brho = k.sb(s2, [128, 16], F32, "rho")
            thn, bthn = k.sb(s2, [128, 16], F32, "thn")
            k.tt("dve", rho[:], are[:], dt_[:], ALU.mult, r=[bare, bdt], w=[brho])
            k.tt("dve", thn[:], aim[:], dt_[:], ALU.mult, r=[baim, bdt], w=[bthn])
            k.ts("dve", thn[:], thn[:], 1.0 / TWO_PI, ALU.mult, r=[bthn], w=[bthn])
            k.act(dec[:], rho[:], AF.Exp, r=[brho], w=[bdec])
            s1, bs1 = k.sb(s2, [128, 16], F32, "s1")
            c1, bc1 = k.sb(s2, [128, 16], F32, "c1")
            sincos(k, s2, thn[:], 16, s1[:], c1[:], rb=[bthn], wsin=bs1, wcos=bc1)
            abre, babre = k.sb(s2, [128, 16], F32, "abre")
            abim, babim = k.sb(s2, [128, 16], F32, "abim")
            k.tt("dve", abre[:], dec[:], c1[:], ALU.mult, r=[bdec, bc1], w=[babre])
            k.ts("dve", abre[:], abre[:], -1.0, ALU.add, r=[babre], w=[babre])
            k.tt("dve", abim[:], dec[:], s1[:], ALU.mult, r=[bdec, bs1], w=[babim])
            den, bden = k.sb(s2, [128, 16], F32, "den")
            t0, bt0 = k.sb(s2, [128, 16], F32, "t0")
            k.tt("dve", den[:], are[:], are[:], ALU.mult, r=[bare], w=[bden])
            k.tt("dve", t0[:], aim[:], aim[:], ALU.mult, r=[baim], w=[bt0])
            k.tt("dve", den[:], den[:], t0[:], ALU.add, r=[bden, bt0], w=[bden])
            k.recip(den[:], den[:], r=[bden], w=[bden])
            fre, bfre = k.sb(s2, [128, 16], F32, "fre")
            fim, bfim = k.sb(s2, [128, 16], F32, "fim")
            t1, bt1 = k.sb(s2, [128, 16], F32, "t1")
            k.tt("dve", fre[:], abre[:], are[:], ALU.mult, r=[babre, bare], w=[bfre])
            k.tt("dve", t1[:], abim[:], aim[:], ALU.mult, r=[babim, baim], w=[bt1])
            k.tt("dve", fre[:], fre[:], t1[:], ALU.add, r=[bfre, bt1], w=[bfre])
            k.tt("dve", fre[:], fre[:], den[:], ALU.mult, r=[bfre, bden], w=[bfre])
            k.tt("dve", fim[:], abim[:], are[:], ALU.mult, r=[babim, bare], w=[bfim])
            k.tt("dve", t1[:], abre[:], aim[:], ALU.mult, r=[babre, baim], w=[bt1])
            k.tt("dve", fim[:], fim[:], t1[:], ALU.subtract, r=[bfim, bt1], w=[bfim])
            k.tt("dve", fim[:], fim[:], den[:], ALU.mult, r=[bfim, bden], w=[bfim])
            aT, baT = k.sb(s2, [128, 16], F32, "aT")
            k.ts("dve", aT[:], thn[:], float(TS), ALU.mult, r=[bthn], w=[baT])
            sincos(k, s2, aT[:], 16, sT[:], cT[:], rb=[baT], wsin=bsT, wcos=bcT)
            k.ts("dve", nsT[:], sT[:], -1.0, ALU.mult, r=[bsT], w=[bnsT])
            ANG, bANG = k.sb(s2, [128, 16, TS], F32, "ANG")
            SINf, bSINf = k.sb(s2, [128, 16 * TS], F32, "SINf")
            COSf, bCOSf = k.sb(s2, [128, 16 * TS], F32, "COSf")
            for j in range(16):
                k.ts("dve", ANG[:, j, :], tau[:], thn[:, j:j + 1], ALU.mult, r=[btau, bthn], w=[bANG])
            sincos(k, s2, ANG[:].rearrange("p j t -> p (j t)"), 16 * TS, SINf[:], COSf[:], rb=[bANG], wsin=bSINf, wcos=bCOSf)
            SIN3 = SINf[:].rearrange("p (j t) -> p j t", j=16)
            COS3 = COSf[:].rearrange("p (j t) -> p j t", j=16)
            tmp, btmp = k.sb(s2, [128, TS], F32, "tmp")
            for j in range(16):
                k.ts("dve", tmp[:], SIN3[:, j, :], fim[:, j:j + 1], ALU.mult, r=[bSINf, bfim], w=[btmp])
                k.stt(RFre[:, j, :], COS3[:, j, :], fre[:, j:j + 1], tmp[:], ALU.mult, ALU.add, r=[bCOSf, bfre, btmp], w=[bRFre])
                k.ts("dve", tmp[:], SIN3[:, j, :], fre[:, j:j + 1], ALU.mult, r=[bSINf, bfre], w=[btmp])
                k.stt(RFim[:, j, :], COS3[:, j, :], fim[:, j:j + 1], tmp[:], ALU.mult, ALU.subtract, r=[bCOSf, bfim, btmp], w=[bRFim])
            k.copy("dve", COSb[:].rearrange("p j t -> p (j t)"), COSf[:], r=[bCOSf], w=[bCOSb])
            k.copy("dve", SINb[:].rearrange("p j t -> p (j t)"), SINf[:], r=[bSINf], w=[bSINb])
            k.ts("dve", NSINb[:].rearrange("p j t -> p (j t)"), SINf[:], -1.0, ALU.mult, r=[bSINf], w=[bNSINb])
        P.barrier()
        w_in, bw_in = load_w(k, st, di["w_in"][l], 8, INW, name="w_in")
        w_sw, bw_sw = load_w(k, st, di["w_sw"][l], 8, 896, name="w_sw")
        w_glu, bw_glu = load_w(k, st, di["w_glu"][l], 4, 1024, name="w_glu")
        w_bs, bw_bs = load_w(k, st, di["w_bssm"][l], 4, 1024, name="w_bs")
        bre, bbre = load_w(k, st, di["s_bre"][l].rearrange("p j m -> p (j m)"), 1, 2048, name="bre")
        bim, bbim = load_w(k, st, di["s_bim"][l].rearrange("p j m -> p (j m)"), 1, 2048, name="bim")
        cre, bcre = load_w(k, st, di["s_cre"][l].rearrange("p j m -> p (j m)"), 1, 2048, name="cre")
        cim, bcim = load_w(k, st, di["s_cim"][l].rearrange("p j m -> p (j m)"), 1, 2048, name="cim")
        xt = [k.sb(st, [128, TT // 128, D], F32, "xt") for _ in range(2)]
        cosr = [k.sb(st, [128, TT], F32, "cosr") for _ in range(2)]
        sinr = [k.sb(st, [128, TT], F32, "sinr") for _ in range(2)]
        NSB = TT // 128
        junk, bjunk = k.sb(st, [128, D], BF16, "junk")
        ss, bss = k.sb(st, [128, NSB], F32, "ss")
        ms, bms = k.sb(st, [128, NSB], F32, "ms")
        sd, bsd = k.sb(st, [128, NSB], F32, "sd")
        rstd, brstd = k.sb(st, [128, NSB], F32, "rstd")
        hh = [k.sb(st, [128, D], BF16, "h") for _ in range(2)]
        hT, bhT = k.sb(st, [128, 8, TT], BF16, "hT")
        uT, buT = k.sb(st, [128, 4, TT], BF16, "uT")
        qTs, bqTs = k.sb(st, [128, 4, TT], BF16, "qTs")
        kvs = {nm: k.sb(st, [128, TT], BF16, nm) for nm in ("kcT", "vcT", "ksT", "kwT")}
        sgs, bsgs = k.sb(st, [128, 8, TT], BF16, "sgs")
        sgn, bsgn = k.sb(st, [128, 8, TT], BF16, "sgn")
        sg, bsg = k.sb(st, [24, TT], F32, "sg")
        vsel, bvsel = k.sb(st, [128, NSB, 2, 65], BF16, "vsel")
        vwin, bvwin = k.sb(st, [128, NSB, 2, 65], BF16, "vwin")
        k.memset("pool", vsel[:], 1.0, [bvsel])
        k.memset("pool", vwin[:], 1.0, [bvwin])
        tmps = [k.sb(st, [128, TT], F32, "tmp") for _ in range(6)]
        tmpi = [0]

        def gettmp():
            t = tmps[tmpi[0] % len(tmps)]
            tmpi[0] += 1
            return t
        bsc = [k.sb(st, [128, TT], F32, "bsc") for _ in range(4)]
        wsc = [k.sb(st, [128, TT], F32, "wsc") for _ in range(4)]
        xre, bxre = k.sb(st, [128, 16, TT], BF16, "xre")
        nxim, bnxim = k.sb(st, [128, 16, TT], BF16, "nxim")
        car, bcar = k.sb(st, [128, 2, 16], F32, "car")
        k.memset("dve", car[:], 0.0, [bcar])
        ctmp, bctmp = k.sb(st, [128, 2], F32, "ctmp")
        ypre, bypre = k.sb(st, [128, TT], F32, "ypre")
        yT, byT = k.sb(st, [128, 4, TT], BF16, "yT")
        sgz, bsgz = k.sb(st, [128, TT], F32, "sgz")
        zzT, bzzT = k.sb(st, [128, 4, TT], BF16, "zzT")
        gss, bgss = k.sb(st, [128, 8, TT], BF16, "gss")

        def load_tile(i):
            t, b = xt[i % 2]
            k.dma("sp", t[:], x_src[i * TT:(i + 1) * TT, :].rearrange("(s p) d -> p s d", p=128), b)
            k.dma("sp", cosr[i % 2][0][:], di["c_cos"][:, i * TT:(i + 1) * TT], cosr[i % 2][1])
            k.dma("sp", sinr[i % 2][0][:], di["c_sin"][:, i * TT:(i + 1) * TT], sinr[i % 2][1])

        def proj(wt, wb, col0, M=128):
            ps, bp = ring.get()
            for c in range(8):
                k.mm(ps[0:M, 0:TT], wt[:, c, col0:col0 + M], hT[:, c, :], c == 0, c == 7, r=[wb[c], bhT], w=[bp])
            return ps, bp

        load_tile(0)
        for i in range(NTT):
            if i + 1 < NTT:
                load_tile(i + 1)
            x_t, bx = xt[i % 2]
            cos_t, bcos = cosr[i % 2]
            sin_t, bsin = sinr[i % 2]
            tok = slice(i * TT, (i + 1) * TT)
            for s_ in range(NSB):
                h_t, bh = hh[s_ % 2]
                k.act(junk[:], x_t[:, s_, :], AF.Square, r=[bx], w=[bjunk, bss], accum=ss[:, s_:s_ + 1])
                k.ts("dve", ms[:, s_:s_ + 1], ss[:, s_:s_ + 1], 1.0 / D, ALU.mult, r=[bss], w=[bms], s2=EPS, op1=ALU.add)
                k.act(sd[:, s_:s_ + 1], ms[:, s_:s_ + 1], AF.Sqrt, r=[bms], w=[bsd])
                k.recip(rstd[:, s_:s_ + 1], sd[:, s_:s_ + 1], r=[bsd], w=[brstd])
                k.stt(h_t[:], x_t[:, s_, :], rstd[:, s_:s_ + 1], gam[:], ALU.mult, ALU.mult, r=[bx, brstd, bgam], w=[bh])
                ps, bp = ring.get()
                pbf = ps[:].bitcast(BF16)
                for c in range(8):
                    k.tr(pbf[:, c * 128:(c + 1) * 128], h_t[:, c * 128:(c + 1) * 128], ident[:], r=[bh, bid], w=[bp])
                k.copy("act", hT[:, :, s_ * 128:(s_ + 1) * 128], pbf.rearrange("p (c t) -> p c t", c=8), r=[bp], w=[bhT])
            for c4 in range(4):
                ps, bp = proj(w_in, bw_in, c4 * 128)
                k.copy("act", uT[:, c4, :], ps[:, 0:TT], r=[bp], w=[buT])
            def rope(col, swcol, dst, bdst):
                psA, bA = proj(w_in, bw_in, col)
                psB, bB = proj(w_sw, bw_sw, swcol)
                t1_, bt1_ = gettmp()
                t2_, bt2_ = gettmp()
                k.tt("dve", t1_[:], psA[:, 0:TT], cos_t[:], ALU.mult, r=[bA, bcos], w=[bt1_])
                k.tt("dve", t2_[:], psB[:, 0:TT], sin_t[:], ALU.mult, r=[bB, bsin], w=[bt2_])
                k.tt("pool", dst, t1_[:], t2_[:], ALU.add, r=[bt1_, bt2_], w=[bdst])
            for c in range(4):
                rope(512 + c * 128, c * 128, qTs[:, c, :], bqTs)
            rope(1024, 512, kvs["kcT"][0][:], kvs["kcT"][1])
            rope(1280, 640, kvs["ksT"][0][:], kvs["ksT"][1])
            rope(1536, 768, kvs["kwT"][0][:], kvs["kwT"][1])
            ps, bp = proj(w_in, bw_in, 1152)
            k.copy("act", kvs["vcT"][0][:], ps[:, 0:TT], r=[bp], w=[kvs["vcT"][1]])
            for c in range(8):
                ps, bp = proj(w_in, bw_in, 1816 + c * 128)
                k.act(sgs[:, c, :], ps[:, 0:TT], AF.Sigmoid, r=[bp], w=[bsgs])
            for c in range(8):
                ps, bp = proj(w_in, bw_in, 2840 + c * 128)
                k.act(sgn[:, c, :], ps[:, 0:TT], AF.Sigmoid, r=[bp], w=[bsgn])
            ps, bp = proj(w_in, bw_in, 1792, M=24)
            k.act(sg[:], ps[0:24, 0:TT], AF.Sigmoid, r=[bp], w=[bsg])
            for s_ in range(NSB):
                for (col, vt, bv) in ((1408, vsel, bvsel), (1664, vwin, bvwin)):
                    ps, bp = ring.get()
                    for c in range(8):
                        k.mm(ps[:, 0:128], hT[:, c, s_ * 128:(s_ + 1) * 128], w_in[:, c, col:col + 128], c == 0, c == 7, r=[bw_in[c], bhT], w=[bp])
                    k.copy("act", vt[:, s_, :, 0:64], ps[:, 0:128].rearrange("p (h d) -> p h d", h=2), r=[bp], w=[bv])
            k.dma("sp", sc["qT"].rearrange("(c p) s -> p c s", p=128)[:, :, tok], qTs[:], bqTs)
            for nm in ("kcT", "vcT", "ksT", "kwT"):
                k.dma("sp", sc[nm][:, tok], kvs[nm][0][:], kvs[nm][1])
            k.dma("sp", sc["sgnT"].rearrange("(c p) s -> p c s", p=128)[:, :, tok], sgn[:], bsgn)
            k.dma("sp", sc["sgT"][:, tok], sg[:], bsg)
            k.dma("sp", sc["vs"][tok].rearrange("(s p) h c -> p s h c", p=128), vsel[:], bvsel)
            k.dma("sp", sc["vw"][tok].rearrange("(s p) h c -> p s h c", p=128), vwin[:], bvwin)
            for j in range(16):
                c4 = j // 4
                psr, bpr = ring.get()
                psi, bpi = ring.get()
                k.mm(psr[:, 0:TT], bre[:, 0, j * 128:(j + 1) * 128], uT[:, c4, :], True, True, r=[bbre[0], buT], w=[bpr])
                k.mm(psi[:, 0:TT], bim[:, 0, j * 128:(j + 1) * 128], uT[:, c4, :], True, True, r=[bbim[0], buT], w=[bpi])
                b_re, bb_re = bsc[(2 * j) % 4]
                b_im, bb_im = bsc[(2 * j + 1) % 4]
                w_re, bw_re = wsc[(2 * j) % 4]
                w_im, bw_im = wsc[(2 * j + 1) % 4]
                t1_, bt1_ = gettmp()
                t2_, bt2_ = gettmp()
                k.tt("dve", t1_[:], psr[:, 0:TT], RFre[:, j, :], ALU.mult, r=[bpr, bRFre], w=[bt1_])
                k.tt("dve", t2_[:], psi[:, 0:TT], RFim[:, j, :], ALU.mult, r=[bpi, bRFim], w=[bt2_])
                k.tt("pool", b_re[:], t1_[:], t2_[:], ALU.subtract, r=[bt1_, bt2_], w=[bb_re])
                t3_, bt3_ = gettmp()
                t4_, bt4_ = gettmp()
                k.tt("dve", t3_[:], psi[:, 0:TT], RFre[:, j, :], ALU.mult, r=[bpi, bRFre], w=[bt3_])
                k.tt("dve", t4_[:], psr[:, 0:TT], RFim[:, j, :], ALU.mult, r=[bpr, bRFim], w=[bt4_])
                k.tt("pool", b_im[:], t3_[:], t4_[:], ALU.add, r=[bt3_, bt4_], w=[bb_im])
                dj = dec[:, j:j + 1].to_broadcast([128, TT])
                k.scan(w_re[:], dj, b_re[:], car[:, 0, j:j + 1], r=[bdec, bb_re, bcar], w=[bw_re])
                k.scan(w_im[:], dj, b_im[:], car[:, 1, j:j + 1], r=[bdec, bb_im, bcar], w=[bw_im])
                k.ts("dve", ctmp[:, 0:1], w_re[:, TT - 1:TT], cT[:, j:j + 1], ALU.mult, r=[bw_re, bcT], w=[bctmp])
                k.ts("dve", ctmp[:, 1:2], w_im[:, TT - 1:TT], cT[:, j:j + 1], ALU.mult, r=[bw_im, bcT], w=[bctmp])
                k.stt(car[:, 0, j:j + 1], w_im[:, TT - 1:TT], nsT[:, j:j + 1], ctmp[:, 0:1], ALU.mult, ALU.add, r=[bw_im, bnsT, bctmp], w=[bcar])
                k.stt(car[:, 1, j:j + 1], w_re[:, TT - 1:TT], sT[:, j:j + 1], ctmp[:, 1:2], ALU.mult, ALU.add, r=[bw_re, bsT, bctmp], w=[bcar])
                t5_, bt5_ = gettmp()
                t6_, bt6_ = gettmp()
                k.tt("dve", t5_[:], w_re[:], COSb[:, j, :], ALU.mult, r=[bw_re, bCOSb], w=[bt5_])
                k.tt("dve", t6_[:], w_im[:], SINb[:, j, :], ALU.mult, r=[bw_im, bSINb], w=[bt6_])
                k.tt("pool", xre[:, j, :], t5_[:], t6_[:], ALU.subtract, r=[bt5_, bt6_], w=[bxre])
                t7_, bt7_ = gettmp()
                t8_, bt8_ = gettmp()
                k.tt("dve", t7_[:], w_re[:], NSINb[:, j, :], ALU.mult, r=[bw_re, bNSINb], w=[bt7_])
                k.tt("dve", t8_[:], w_im[:], COSb[:, j, :], ALU.mult, r=[bw_im, bCOSb], w=[bt8_])
                k.tt("pool", nxim[:, j, :], t7_[:], t8_[:], ALU.subtract, r=[bt7_, bt8_], w=[bnxim])
            for c4 in range(4):
                ps, bp = ring.get()
                for jj in range(4):
                    j = 4 * c4 + jj
                    k.mm(ps[:, 0:TT], cre[:, 0, j * 128:(j + 1) * 128], xre[:, j, :], jj == 0, False, r=[bcre[0], bxre], w=[bp])
                    k.mm(ps[:, 0:TT], cim[:, 0, j * 128:(j + 1) * 128], nxim[:, j, :], False, jj == 3, r=[bcim[0], bnxim], w=[bp])
                k.stt(ypre[:], uT[:, c4, :], dvec[:, c4:c4 + 1], ps[:, 0:TT], ALU.mult, ALU.add, r=[buT, bdvec, bp], w=[bypre])
                k.act(yT[:, c4, :], ypre[:], AF.Gelu_apprx_tanh, r=[bypre], w=[byT])
            for kk in range(4):
                psg, bpg = ring.get()
                for c4 in range(4):
                    k.mm(psg[:, 0:TT], w_glu[:, c4, (4 + kk) * 128:(5 + kk) * 128], yT[:, c4, :], c4 == 0, c4 == 3, r=[bw_glu[c4], byT], w=[bpg])
                k.act(sgz[:], psg[:, 0:TT], AF.Sigmoid, r=[bpg], w=[bsgz])
                psv, bpv = ring.get()
                for c4 in range(4):
                    k.mm(psv[:, 0:TT], w_glu[:, c4, kk * 128:(kk + 1) * 128], yT[:, c4, :], c4 == 0, c4 == 3, r=[bw_glu[c4], byT], w=[bpv])
                k.tt("dve", zzT[:, kk, :], psv[:, 0:TT], sgz[:], ALU.mult, r=[bpv, bsgz], w=[bzzT])
            for fc in range(8):
                ps, bp = ring.get()
                for kk in range(4):
                    k.mm(ps[:, 0:TT], w_bs[:, kk, fc * 128:(fc + 1) * 128], zzT[:, kk, :], kk == 0, kk == 3, r=[bw_bs[kk], bzzT], w=[bp])
                k.tt("dve", gss[:, fc, :], ps[:, 0:TT], sgs[:, fc, :], ALU.mult, r=[bp, bsgs], w=[bgss])
            k.dma("sp", sc["gssT"].rearrange("(c p) s -> p c s", p=128)[:, :, tok], gss[:], bgss)
    P.barrier()


def phase2(k, pers, l, S, di, sc, cst):
    P = k.P
    NC = S // 16 - 1
    NCP = S // 16
    NCT = NCP // 128
    KcT = [k.sb(pers, [64, NCP], BF16, "KcT") for _ in range(2)]
    Vc, bVc = k.sb(pers, [128, NCT, 2, 65], BF16, "Vc")
    for t, b in KcT:
        k.memset("pool", t[:], 0.0, [b])
    k.memset("pool", Vc[:], 1.0, [bVc])
    with ExitStack() as st:
        ring = PsumRing(k, st)
        for typ in ("k", "v"):
            with ExitStack() as s2:
                xT, bxT = k.sb(s2, [128, S], BF16, "cxT")
                k.dma("sp", xT[:], sc["kcT" if typ == "k" else "vcT"], bxT)
                w1, bw1 = k.sb(s2, [128, 32, 256], BF16, "w1")
                bw1b = Buf("w1b")
                src = di["c_w1" + typ][l].rearrange("(l d) c -> d l c", d=64)
                k.dma("pool", w1[0:64], src, bw1)
                k.dma("pool", w1[64:128], src, bw1b)
                pe2, bpe2 = k.sb(s2, [64, 32, 2], BF16, "pe2")
                pe_f, bpe_f = k.sb(s2, [64, 32], F32, "pe_f")
                k.dma("sp", pe_f[:], di["c_pe" + typ][l], bpe_f)
                k.copy("dve", pe2[:, :, 0], pe_f[:], r=[bpe_f], w=[bpe2])
                k.copy("dve", pe2[:, :, 1], pe_f[:], r=[bpe_f], w=[bpe2])
                b1, bb1 = k.sb(s2, [128, 2], F32, "b1")
                k.dma("sp", b1[:], di["c_b1" + typ][l], bb1)
                w2, bw2 = k.sb(s2, [128, 2, 64], BF16, "w2")
                k.dma("pool", w2[:], di["c_w2" + typ][l].rearrange("(c p) d -> p c d", p=128), bw2)
                bias, bbias = k.sb(s2, [128, 2], F32, "bias")
                for cc in range(2):
                    ps, bp = ring.get()
                    for li in range(32):
                        k.mm(ps[:, 0:2], w1[0:64, li, cc * 128:(cc + 1) * 128], pe2[:, li, :], li == 0, li == 31, r=[bw1, bpe2], w=[bp])
                    k.tt("dve", bias[:, cc:cc + 1], ps[:, 0:1], b1[:, cc:cc + 1], ALU.add, r=[bp, bb1], w=[bbias])
                for hk in range(2):
                    hid, bhid = k.sb(s2, [128, 2, NCP], BF16, "hid")
                    k.memset("pool", hid[:], 0.0, [bhid])
                    bw = bw1 if hk == 0 else bw1b
                    for cc in range(2):
                        ps, bp = ring.get()
                        for li in range(32):
                            k.mm(ps[:, 0:NC], w1[hk * 64:(hk + 1) * 64, li, cc * 128:(cc + 1) * 128],
                                 xT[hk * 64:(hk + 1) * 64, li:li + 16 * (NC - 1) + 1:16], li == 0, li == 31, r=[bw, bxT], w=[bp])
                        k.act(hid[:, cc, 0:NC], ps[:, 0:NC], AF.Gelu_apprx_tanh, r=[bp, bbias], w=[bhid], bias=bias[:, cc:cc + 1])
                    if typ == "k":
                        ps, bp = ring.get()
                        for cc in range(2):
                            k.mm(ps[0:64, 0:NC], w2[:, cc, :], hid[:, cc, 0:NC], cc == 0, cc == 1, r=[bw2, bhid], w=[bp])
                        k.copy("act", KcT[hk][0][:, 0:NC], ps[0:64, 0:NC], r=[bp], w=[KcT[hk][1]])
                    else:
                        for nt in range(NCT):
                            ps, bp = ring.get()
                            for cc in range(2):
                                k.mm(ps[:, 0:64], hid[:, cc, nt * 128:(nt + 1) * 128], w2[:, cc, :], cc == 0, cc == 1, r=[bhid, bw2], w=[bp])
                            k.copy("act", Vc[:, nt, hk, 0:64], ps[:, 0:64], r=[bp], w=[bVc])
            P.barrier()
    if "dbg_kc" in sc:
        for hk in range(2):
            k.dma("sp", sc["dbg_kc"][hk], KcT[hk][0][:], KcT[hk][1])
        k.dma("sp", sc["dbg_vc"], Vc[:], bVc)
    return KcT, (Vc, bVc)


def phase3(k, l, S, x_src, di, sc, cst, cmp_t):
    P = k.P
    NT = S // 128
    NCP = S // 16
    NCT = NCP // 128
    NB = S // 64
    KcT, (Vc, bVc) = cmp_t
    ident, bid = cst["ident"]
    with ExitStack() as st:
        ring = PsumRing(k, st, 4)
        ringO = PsumRing(k, st, 2)
        ringM = PsumRing(k, st, 2)
        KsT = []
        for hk in range(2):
            t, b = k.sb(st, [64, S], BF16, "KsT")
            k.dma("sp", t[:], sc["ksT"][hk * 64:(hk + 1) * 64, :], b)
            KsT.append((t, b))
        Vs, bVs = k.sb(st, [128, NT, 2, 65], BF16, "Vs")
        k.dma("sp", Vs[:], sc["vs"].rearrange("(n p) h c -> p n h c", p=128), bVs)

        def cload(name, shape, src, dt=BF16):
            t, b = k.sb(st, shape, dt, name)
            k.dma("pool" if dt == BF16 else "sp", t[:], src, b)
            return t, b
        E, bE = cload("E", [128, NT, 128], di["c_e"])
        caus, bcaus = cload("caus", [128, 128], di["c_caus"])
        low, blow = cload("low", [128, 128], di["c_low"])
        mgen, bmgen = cload("mgen", [128, 1024], di["c_mgen"])
        mt, bmt = cload("mt", [128, 16, 128], di["c_mt"])
        G, bG = cload("G", [128, 256], di["c_g"], F32)
        wbn, bwbn = cload("wbn", [64, 8, 1024], di["w_bnsa"][l].rearrange("(h d) n -> d h n", d=64))
        w_out, bw_out = load_w(k, st, di["w_out"][l], 8, 1024, name="w_out")
        ones16, bones = k.sb(st, [128, 64], BF16, "ones16")
        k.memset("pool", ones16[:], 1.0, [bones])
        rhi, brhi = k.sb(st, [65, 512], BF16, "rhi")
        rlo, brlo = k.sb(st, [65, 512], BF16, "rlo")
        QT = [[k.sb(st, [64, 4, 128], BF16, "QT") for _ in range(2)] for _ in range(2)]
        KwT = [[k.sb(st, [64, 640], BF16, "KwT") for _ in range(2)] for _ in range(2)]
        Vw = [k.sb(st, [128, 5, 2, 65], BF16, "Vw") for _ in range(2)]
        grow = [k.sb(st, [65, 2, 3, 4, 128], F32, "grow") for _ in range(2)]
        sgn = [k.sb(st, [128, 8, 128], BF16, "sgn") for _ in range(2)]
        gss = [k.sb(st, [128, 8, 128], BF16, "gss") for _ in range(2)]
        xin = [k.sb(st, [128, D], F32, "xin") for _ in range(2)]
        eg = [k.sb(st, [128, NCP], F32, "eg") for _ in range(4)]
        den4, bden4 = k.sb(st, [128, 4], F32, "den4")
        rden4, brden4 = k.sb(st, [128, 4], F32, "rden4")
        pg, bpg = k.sb(st, [128, NCP + 8], F32, "pg")
        k.memset("pool", pg[:], 0.0, [bpg])
        blk, bblk = k.sb(st, [128, NB], F32, "blk")
        blk2, bblk2 = k.sb(st, [128, NB], F32, "blk2")
        m8, bm8 = k.sb(st, [128, 16], F32, "m8")
        negm, bnegm = k.sb(st, [128, 128], BF16, "negm")
        k.memset("pool", negm[:], 0.0, [bnegm])
        negT4, bnegT4 = k.sb(st, [128, 4, 128], BF16, "negT4")
        pTs = [k.sb(st, [128, 512], BF16, "pT") for _ in range(3)]
        pti = [0]
        rr, brr = k.sb(st, [65, 512], F32, "rr")
        osb, bosb = k.sb(st, [64, 512], F32, "osb")
        oacc, boacc = k.sb(st, [64, 512], F32, "oacc")
        otmp, botmp = k.sb(st, [64, 512], F32, "otmp")
        oTb = [k.sb(st, [64, 4, 128], BF16, "oTb") for _ in range(2)]
        mtmp, bmtmp = k.sb(st, [128, 128], F32, "mtmp")
        mrg, bmrg = k.sb(st, [128, 8, 128], BF16, "mrg")
        xm = [k.sb(st, [128, D], F32, "xm") for _ in range(2)]

        def loads(qb):
            s0 = qb * 128
            pb = qb % 2
            qv = sc["qT"].rearrange("(h d) s -> d h s", d=64)
            for hk in range(2):
                t, b = QT[pb][hk]
                k.dma("sp", t[:], qv[:, hk * 4:(hk + 1) * 4, s0:s0 + 128], b)
                lo = max(0, s0 - 512)
                t, b = KwT[pb][hk]
                k.dma("sp", t[:, 640 - (s0 + 128 - lo):640], sc["kwT"][hk * 64:(hk + 1) * 64, lo:s0 + 128], b)
            lo = max(0, s0 - 512)
            nw = (s0 + 128 - lo) // 128
            t, b = Vw[pb]
            k.dma("sp", t[:, 5 - nw:5], sc["vw"][lo:s0 + 128].rearrange("(n p) h c -> p n h c", p=128), b)
            t, b = grow[pb]
            gv = sc["sgT"].rearrange("(hk g br) s -> hk br g s", hk=2, g=4, br=3)
            for hk in range(2):
                for br in range(3):
                    k.dma("sp", t[64:65, hk, br], gv[hk, br:br + 1, :, s0:s0 + 128], b)
            t, b = sgn[pb]
            k.dma("sp", t[:], sc["sgnT"].rearrange("(c p) s -> p c s", p=128)[:, :, s0:s0 + 128], b)
            t, b = gss[pb]
            k.dma("sp", t[:], sc["gssT"].rearrange("(c p) s -> p c s", p=128)[:, :, s0:s0 + 128], b)
            t, b = xin[pb]
            k.dma("sp", t[:], x_src[s0:s0 + 128, :], b)

        def attn_tile(Ops, bO, first, last_, KT_ap, bKT, V_ap, bV, Q2, bQ, smask=None, emask=None):
            psS, bS = ring.get()
            nmask = (4 if smask is not None else 0) + (1 if emask is not None else 0)
            k.mm(psS[:, 0:512], KT_ap, Q2, True, nmask == 0, r=[bKT, bQ], w=[bS])
            done = 0
            if emask is not None:
                done += 1
                k.mm(psS[:, 0:512], emask, negT4[:].rearrange("p g q -> p (g q)"), False, done == nmask, r=[bE, bnegT4], w=[bS])
            if smask is not None:
                m_ap, bm = smask
                for g in range(4):
                    done += 1
                    k.mm(psS[:, g * 128:(g + 1) * 128], ident[:], m_ap, False, done == nmask, r=[bid, bm], w=[bS])
            pT, bpT = pTs[pti[0] % 3]
            pti[0] += 1
            k.act(pT[:], psS[:, 0:512], AF.Exp, r=[bS], w=[bpT], scale=0.125)
            k.mm(Ops[0:65, 0:512], V_ap, pT[:], first, last_, r=[bV, bpT], w=[bO])

        def finalize(Ops, bO, gate_ap, bgate, first_branch, out_final=None, bout=None):
            k.ts("dve", rr[64:65, :], Ops[64:65, 0:512], 1e-20, ALU.max, r=[bO], w=[brr])
            k.recip(rr[64:65, :], rr[64:65, :], r=[brr], w=[brr])
            k.tt("dve", rr[64:65, :], rr[64:65, :], gate_ap, ALU.mult, r=[brr, bgate], w=[brr])
            k.copy("dve", rhi[64:65, :], rr[64:65, :], r=[brr], w=[brhi])
            k.tt("dve", rlo[64:65, :], rr[64:65, :], rhi[64:65, :], ALU.subtract, r=[brr, brhi], w=[brlo])
            psb, bpb = ringM.get()
            k.mm(psb[0:64, 0:512], ones16[64:65, 0:64], rhi[64:65, :], True, False, r=[bones, brhi], w=[bpb])
            k.mm(psb[0:64, 0:512], ones16[64:65, 0:64], rlo[64:65, :], False, True, r=[bones, brlo], w=[bpb])
            k.copy("act", osb[:], Ops[0:64, 0:512], r=[bO], w=[bosb])
            if first_branch:
                k.tt("dve", oacc[:], osb[:], psb[0:64, 0:512], ALU.mult, r=[bosb, bpb], w=[boacc])
            else:
                k.tt("dve", otmp[:], osb[:], psb[0:64, 0:512], ALU.mult, r=[bosb, bpb], w=[botmp])
                if out_final is None:
                    k.tt("pool", oacc[:], oacc[:], otmp[:], ALU.add, r=[boacc, botmp], w=[boacc])
                else:
                    k.tt("pool", out_final, oacc[:], otmp[:], ALU.add, r=[boacc, botmp], w=[bout])

        loads(0)
        for qb in range(NT):
            if qb + 1 < NT:
                loads(qb + 1)
            s0 = qb * 128
            pb = qb % 2
            for hk in range(2):
                Qt, bQ = QT[pb][hk]
                Q2 = Qt[:].rearrange("d g q -> d (g q)")
                gr, bgr = grow[pb]
                for g in range(4):
                    ps, bp = ring.get()
                    k.mm(ps[:, 0:NCP], Qt[:, g, :], KcT[hk][0][:, 0:NCP], True, False, r=[bQ, KcT[hk][1]], w=[bp])
                    k.mm(ps[:, 0:NCP], ident[:], mgen[:, 512 - 8 * qb:512 - 8 * qb + NCP], False, True, r=[bid, bmgen], w=[bp])
                    k.act(eg[g][0][:], ps[:, 0:NCP], AF.Exp, r=[bp], w=[eg[g][1], bden4], scale=0.125, accum=den4[:, g:g + 1])
                k.ts("dve", rden4[:], den4[:], 1e-20, ALU.max, r=[bden4], w=[brden4])
                k.recip(rden4[:], rden4[:], r=[brden4], w=[brden4])
                k.ts("dve", pg[:, 1:1 + NCP], eg[0][0][:], rden4[:, 0:1], ALU.mult, r=[eg[0][1], brden4], w=[bpg])
                for g in range(1, 4):
                    k.stt(pg[:, 1:1 + NCP], eg[g][0][:], rden4[:, g:g + 1], pg[:, 1:1 + NCP], ALU.mult, ALU.add, r=[eg[g][1], brden4, bpg], w=[bpg])
                P.op("dve", lambda e: e.tensor_reduce(out=blk[:], in_=pg[:, 0:NCP].rearrange("p (j o) -> p j o", o=4), axis=AX.X, op=ALU.add), reads=[bpg], writes=[bblk])
                k.tt("dve", blk[:], blk[:], pg[:, 4:4 + 4 * NB:4], ALU.add, r=[bblk, bpg], w=[bblk])
                k.tt("dve", blk[:], blk[:], G[:, 126 - 2 * qb:126 - 2 * qb + NB], ALU.add, r=[bblk, bG], w=[bblk])
                if qb >= 1:
                    k.ts("dve", blk[:, 0:1], blk[:, 0:1], 1e4, ALU.add, r=[bblk], w=[bblk])
                P.op("dve", lambda e: e.max(out=m8[:, 0:8], in_=blk[:]), reads=[bblk], writes=[bm8])
                P.op("dve", lambda e: e.match_replace(out=blk2[:], in_to_replace=m8[:, 0:8], in_values=blk[:], imm_value=-3e38), reads=[bblk, bm8], writes=[bblk2])
                P.op("dve", lambda e: e.max(out=m8[:, 8:16], in_=blk2[:]), reads=[bblk2], writes=[bm8])
                k.ts("dve", negm[:, 0:NB], blk[:], m8[:, 15:16], ALU.is_lt, r=[bblk, bm8], w=[bnegm], s2=NEG, op1=ALU.mult)
                ps, bp = ringM.get()
                pbf = ps[:].bitcast(BF16)
                k.tr(pbf[:, 0:128], negm[:], ident[:], r=[bnegm, bid], w=[bp])
                for g in range(4):
                    k.copy("act" if g % 2 == 0 else "dve", negT4[:, g, :], pbf[:, 0:128], r=[bp], w=[bnegT4])
                Ops, bO = ringO.get()
                tiles = [nt for nt in range(NCT) if qb - 16 * nt >= 0]
                for idx, nt in enumerate(tiles):
                    m = qb - 16 * nt
                    sm = (mt[:, m, :], bmt) if m < 16 else None
                    attn_tile(Ops, bO, idx == 0, idx == len(tiles) - 1, KcT[hk][0][:, nt * 128:(nt + 1) * 128], KcT[hk][1],
                              Vc[:, nt, hk, :], bVc, Q2, bQ, smask=sm)
                finalize(Ops, bO, gr[64:65, hk, 0].rearrange("o g q -> o (g q)"), bgr, True)
                Ops, bO = ringO.get()
                tiles = [wt for wt in range(5) if s0 - 512 + 128 * wt >= 0]
                kw_t, bkw = KwT[pb][hk]
                vw_t, bvw = Vw[pb]
                for idx, wt in enumerate(tiles):
                    sm = (low[:], blow) if wt == 0 else ((caus[:], bcaus) if wt == 4 else None)
                    attn_tile(Ops, bO, idx == 0, idx == len(tiles) - 1, kw_t[:, wt * 128:(wt + 1) * 128], bkw,
                              vw_t[:, wt, hk, :], bvw, Q2, bQ, smask=sm)
                finalize(Ops, bO, gr[64:65, hk, 2].rearrange("o g q -> o (g q)"), bgr, False)
                Ops, bO = ringO.get()
                for i in range(qb + 1):
                    sm = (caus[:], bcaus) if i == qb else None
                    attn_tile(Ops, bO, i == 0, i == qb, KsT[hk][0][:, i * 128:(i + 1) * 128], KsT[hk][1],
                              Vs[:, i, hk, :], bVs, Q2, bQ, smask=sm, emask=E[:, i, :])
                finalize(Ops, bO, gr[64:65, hk, 1].rearrange("o g q -> o (g q)"), bgr, False,
                         out_final=oTb[hk][0][:].rearrange("d g q -> d (g q)"), bout=oTb[hk][1])
            if "dbg_o" in sc:
                for hk in range(2):
                    k.dma("sp", sc["dbg_o"][hk, :, qb], oTb[hk][0][:].rearrange("d g q -> d (g q)"), oTb[hk][1])
            sg_t, bsgn_ = sgn[pb]
            gs_t, bgs_ = gss[pb]
            for half in range(2):
                ps, bp = ring.get()
                for f4 in range(4):
                    fc = half * 4 + f4
                    for h in range(8):
                        k.mm(ps[:, f4 * 128:(f4 + 1) * 128], wbn[:, h, fc * 128:(fc + 1) * 128], oTb[h // 4][0][:, h % 4, :],
                             h == 0, h == 7, r=[bwbn, oTb[h // 4][1]], w=[bp])
                for f4 in range(4):
                    fc = half * 4 + f4
                    k.tt("dve", mtmp[:], ps[:, f4 * 128:(f4 + 1) * 128], sg_t[:, fc, :], ALU.mult, r=[bp, bsgn_], w=[bmtmp])
                    k.tt("pool", mrg[:, fc, :], mtmp[:], gs_t[:, fc, :], ALU.add, r=[bmtmp, bgs_], w=[bmrg])
            x_t, bx = xin[pb]
            xm_t, bxm = xm[pb]
            for half in range(2):
                ps, bp = ring.get()
                for fc in range(8):
                    k.mm(ps[:, 0:512], mrg[:, fc, :], w_out[:, fc, half * 512:(half + 1) * 512], fc == 0, fc == 7, r=[bmrg, bw_out[fc]], w=[bp])
                k.tt("dve", xm_t[:, half * 512:(half + 1) * 512], ps[:, 0:512], x_t[:, half * 512:(half + 1) * 512], ALU.add, r=[bp, bx], w=[bxm])
            k.dma("sp", sc["xmid"][s0:s0 + 128, :], xm_t[:], bxm)
    P.barrier()


def phase4(k, l, S, di, sc, cst, dst, last):
    P = k.P
    TT = 256
    NTT = S // TT
    NSB = TT // 128
    ident, bid = cst["ident"]
    with ExitStack() as st:
        ring = PsumRing(k, st)
        w_fin, bw_fin = load_w(k, st, di["w_fin"][l], 8, 2 * DFF, name="w_fin")
        w_fo, bw_fo = load_w(k, st, di["w_fout"][l], NFC, D, name="w_fo")
        gam, bgam = k.sb(st, [128, D], F32, "gam")
        k.dma("sp", gam[:], di["g_ffn"][l], bgam)
        cw, bcw = k.sb(st, [128, NFC, 3], F32, "cw")
        k.dma("sp", cw[:], di["f_cw"][l], bcw)
        cb, bcb = k.sb(st, [128, NFC], F32, "cb")
        k.dma("sp", cb[:], di["f_cb"][l], bcb)
        if last:
            gfin, bgfin = k.sb(st, [128, D], F32, "gfin")
            k.dma("sp", gfin[:], di["g_fin"], bgfin)
        halo, bhalo = k.sb(st, [128, NFC, 2], F32, "halo")
        k.memset("pool", halo[:], 0.0, [bhalo])
        xt = [k.sb(st, [128, NSB, D], F32, "xt") for _ in range(2)]
        junk, bjunk = k.sb(st, [128, D], BF16, "junk")
        ss, bss = k.sb(st, [128, 2 * NSB], F32, "ss")
        ms, bms = k.sb(st, [128, 2 * NSB], F32, "ms")
        sd, bsd = k.sb(st, [128, 2 * NSB], F32, "sd")
        rstd, brstd = k.sb(st, [128, 2 * NSB], F32, "rstd")
        hh = [k.sb(st, [128, D], BF16, "h") for _ in range(2)]
        hT, bhT = k.sb(st, [128, 8, TT], BF16, "hT")
        a_sb = [k.sb(st, [128, TT + 2], F32, "a_sb") for _ in range(2)]
        cv = [k.sb(st, [128, TT], F32, "cv") for _ in range(2)]
        gl = [k.sb(st, [128, TT], F32, "gl") for _ in range(2)]
        actT, bactT = k.sb(st, [128, NFC, TT], BF16, "actT")
        xo = [k.sb(st, [128, NSB, D], F32, "xo") for _ in range(2)]

        def load_tile(i):
            t, b = xt[i % 2]
            k.dma("sp", t[:], sc["xmid"][i * TT:(i + 1) * TT, :].rearrange("(s p) d -> p s d", p=128), b)

        def rms(x_ap, bx, col, g_t, bg, out_ap, bout):
            k.act(junk[:], x_ap, AF.Square, r=[bx], w=[bjunk, bss], accum=ss[:, col:col + 1])
            k.ts("dve", ms[:, col:col + 1], ss[:, col:col + 1], 1.0 / D, ALU.mult, r=[bss], w=[bms], s2=EPS, op1=ALU.add)
            k.act(sd[:, col:col + 1], ms[:, col:col + 1], AF.Sqrt, r=[bms], w=[bsd])
            k.recip(rstd[:, col:col + 1], sd[:, col:col + 1], r=[bsd], w=[brstd])
            k.stt(out_ap, x_ap, rstd[:, col:col + 1], g_t[:], ALU.mult, ALU.mult, r=[bx, brstd, bg], w=[bout])

        load_tile(0)
        for i in range(NTT):
            if i + 1 < NTT:
                load_tile(i + 1)
            x_t, bx = xt[i % 2]
            for s_ in range(NSB):
                h_t, bh = hh[s_ % 2]
                rms(x_t[:, s_, :], bx, s_, gam, bgam, h_t[:], bh)
                ps, bp = ring.get()
                pbf = ps[:].bitcast(BF16)
                for c in range(8):
                    k.tr(pbf[:, c * 128:(c + 1) * 128], h_t[:, c * 128:(c + 1) * 128], ident[:], r=[bh, bid], w=[bp])
                k.copy("act", hT[:, :, s_ * 128:(s_ + 1) * 128], pbf.rearrange("p (c t) -> p c t", c=8), r=[bp], w=[bhT])
            for fc in range(NFC):
                psa, bpa = ring.get()
                for c in range(8):
                    k.mm(psa[:, 0:TT], w_fin[:, c, fc * 128:(fc + 1) * 128], hT[:, c, :], c == 0, c == 7, r=[bw_fin[c], bhT], w=[bpa])
                psb, bpb = ring.get()
                for c in range(8):
                    k.mm(psb[:, 0:TT], w_fin[:, c, DFF + fc * 128:DFF + (fc + 1) * 128], hT[:, c, :], c == 0, c == 7, r=[bw_fin[c], bhT], w=[bpb])
                a_t, ba = a_sb[fc % 2]
                c_t, bc = cv[fc % 2]
                g_t, bg = gl[fc % 2]
                k.copy("pool", a_t[:, 0:2], halo[:, fc, :], r=[bhalo], w=[ba])
                k.copy("act", a_t[:, 2:2 + TT], psa[:, 0:TT], r=[bpa], w=[ba])
                k.copy("pool", halo[:, fc, :], a_t[:, TT:TT + 2], r=[ba], w=[bhalo])
                k.ts("dve", c_t[:], a_t[:, 2:2 + TT], cw[:, fc, 2:3], ALU.mult, r=[ba, bcw, bcb], w=[bc], s2=cb[:, fc:fc + 1], op1=ALU.add)
                k.stt(c_t[:], a_t[:, 1:1 + TT], cw[:, fc, 1:2], c_t[:], ALU.mult, ALU.add, r=[ba, bcw, bc], w=[bc])
                k.stt(c_t[:], a_t[:, 0:TT], cw[:, fc, 0:1], c_t[:], ALU.mult, ALU.add, r=[ba, bcw, bc], w=[bc])
                k.act(g_t[:], c_t[:], AF.Gelu_apprx_tanh, r=[bc], w=[bg])
                k.tt("dve", actT[:, fc, :], psb[:, 0:TT], g_t[:], ALU.mult, r=[bpb, bg], w=[bactT])
            xo_t, bxo = xo[i % 2]
            for s_ in range(NSB):
                for half in range(2):
                    ps, bp = ring.get()
                    for fc in range(NFC):
                        k.mm(ps[:, 0:512], actT[:, fc, s_ * 128:(s_ + 1) * 128], w_fo[:, fc, half * 512:(half + 1) * 512], fc == 0, fc == NFC - 1,
                             r=[bactT, bw_fo[fc]], w=[bp])
                    k.tt("dve", xo_t[:, s_, half * 512:(half + 1) * 512], ps[:, 0:512], x_t[:, s_, half * 512:(half + 1) * 512], ALU.add, r=[bp, bx], w=[bxo])
            if last:
                for s_ in range(NSB):
                    rms(xo_t[:, s_, :], bxo, NSB + s_, gfin, bgfin, xo_t[:, s_, :], bxo)
            k.dma("sp", dst[i * TT:(i + 1) * TT, :].rearrange("(s p) d -> p s d", p=128), xo_t[:], bxo)
    P.barrier()


INPUT_SHAPES = None


def build(S, L, TS=128, dbg=False, phases=("p1", "p2", "p3", "p4")):
    nc = bass.Bass("TRN2", target_bir_lowering=False)
    NT = S // 128
    di = {}

    def din(name, shape):
        di[name] = nc.dram_tensor(name, list(shape), F32, kind="ExternalInput").ap()
    din("x", [S, D])
    for nm, shp in (("g_mix", [L, 128, D]), ("g_ffn", [L, 128, D]), ("g_fin", [128, D]),
                    ("w_in", [L, D, INW]), ("w_sw", [L, D, 896]),
                    ("s_are", [L, 128, 16]), ("s_aim", [L, 128, 16]), ("s_ldt", [L, 128, 16]),
                    ("s_bre", [L, 128, 16, 128]), ("s_bim", [L, 128, 16, 128]),
                    ("s_cre", [L, 128, 16, 128]), ("s_cim", [L, 128, 16, 128]), ("s_d", [L, 128, 4]),
                    ("w_glu", [L, 512, 1024]), ("w_bssm", [L, 512, 1024]), ("w_bnsa", [L, 512, 1024]),
                    ("w_out", [L, D, D]),
                    ("c_pek", [L, 64, 32]), ("c_w1k", [L, 2048, 256]), ("c_b1k", [L, 128, 2]), ("c_w2k", [L, 256, 64]),
                    ("c_pev", [L, 64, 32]), ("c_w1v", [L, 2048, 256]), ("c_b1v", [L, 128, 2]), ("c_w2v", [L, 256, 64]),
                    ("w_fin", [L, D, 2 * DFF]), ("w_fout", [L, DFF, D]), ("f_cw", [L, 128, NFC, 3]), ("f_cb", [L, 128, NFC]),
                    ("c_cos", [128, S]), ("c_sin", [128, S]), ("c_ident", [128, 128]), ("c_tau", [128, TS]),
                    ("c_caus", [128, 128]), ("c_low", [128, 128]), ("c_mgen", [128, 1024]), ("c_mt", [128, 16, 128]),
                    ("c_g", [128, 256]), ("c_e", [128, NT, 128])):
        din(nm, shp)
    out = nc.dram_tensor("out", [S, D], F32, kind="ExternalOutput").ap()
    skind = "ExternalOutput" if dbg else "Internal"
    sc = {}

    def scr(name, shape, dt):
        sc[name] = nc.dram_tensor(name, list(shape), dt, kind=skind).ap()
    scr("qT", [512, S], BF16)
    for nm in ("kcT", "vcT", "ksT", "kwT"):
        scr(nm, [128, S], BF16)
    scr("vs", [S, 2, 65], BF16)
    scr("vw", [S, 2, 65], BF16)
    scr("sgT", [24, S], F32)
    scr("sgnT", [1024, S], BF16)
    scr("gssT", [1024, S], BF16)
    scr("xmid", [S, D], F32)
    if dbg:
        scr("dbg_o", [2, 64, NT, 512], BF16)
        scr("dbg_kc", [2, 64, S // 16], BF16)
        scr("dbg_vc", [128, S // 2048, 2, 65], BF16)
    scr("x1", [S, D], F32)
    with ExitStack() as st:
        P = Prog(nc)
        k = K(nc, P)
        cst = {}
        ident, bid = k.sb(st, [128, 128], BF16, "ident")
        k.dma("pool", ident[:], di["c_ident"], bid)
        cst["ident"] = (ident, bid)
        x_src = di["x"]
        for l in range(L):
            last = l == L - 1
            if "p1" in phases:
                phase1(k, l, S, TS, x_src, di, sc, cst)
            if "p2" in phases:
                pers = ExitStack()
                cmp_t = phase2(k, pers, l, S, di, sc, cst)
            if "p3" in phases:
                phase3(k, l, S, x_src, di, sc, cst, cmp_t)
            if "p2" in phases:
                pers.close()
                P.barrier()
            if "p4" in phases:
                phase4(k, l, S, di, sc, cst, out if last else sc["x1"], last)
            x_src = sc["x1"]
        P.barrier()
        P.emit(st)
    return nc


_NC_CACHE = {}


def kernel(**inputs):
    S, L, NCORES = 8192, 2, 8
    inp = {k_: np.asarray(v) for k_, v in inputs.items()}
    hl = host_layout(inp, L)
    hc = host_consts(S, 128)
    common = {}
    common.update(hl)
    common.update(hc)
    common = {k_: np.ascontiguousarray(v, dtype=np.float32) for k_, v in common.items()}
    if "nc" not in _NC_CACHE:
        _NC_CACHE["nc"] = build(S, L)
    nc = _NC_CACHE["nc"]
    x = np.asarray(inp["x"], dtype=np.float32)
    in_maps = []
    for b in range(NCORES):
        m = dict(common)
        m["x"] = np.ascontiguousarray(x[b])
        in_maps.append(m)
    res = run_bass_kernel_spmd(nc, in_maps, core_ids=list(range(NCORES)))
    return np.stack([np.asarray(r["out"], dtype=np.float32) for r in res.results], axis=0)
```

```python
from contextlib import ExitStack
import numpy as np
import ml_dtypes
import concourse.bass as bass
import concourse.mybir as mybir
from concourse.bass_utils import run_bass_kernel_spmd

F32 = mybir.dt.float32
BF16 = mybir.dt.bfloat16
I32 = mybir.dt.int32
ALU = mybir.AluOpType
AF = mybir.ActivationFunctionType
AX = mybir.AxisListType

D = 1024
DFF = 2816
NFC = DFF // 128
INW = 3864
EPS = 1e-6
NEG = -30000.0
TWO_PI = float(2 * np.pi)
SIN_SCALE = TWO_PI * 0.999999


class Buf:
    __slots__ = ("name", "w", "rs", "sem", "cnt", "slot", "base", "uid")

    def __init__(self, name="b"):
        self.name = name
        self.w = None
        self.rs = {}
        self.sem = None
        self.cnt = 0
        self.slot = None
        self.base = 0
        self.uid = None


class Op:
    __slots__ = ("eng", "fn", "deps", "key", "val", "signal", "sigval", "dma", "slot", "semval")


ENGS = ("pe", "act", "dve", "pool", "sp")


class Prog:
    def __init__(self, nc):
        self.nc = nc
        self.ops = {e: [] for e in ENGS}
        self.seen = {e: {} for e in ENGS}
        self.dma_bufs = []
        self.last = {}
        self.slot_base = []
        self.free_slots = []
        self.live = []
        self.uid = 0

    def _get_slot(self, buf):
        if self.free_slots:
            sl = self.free_slots.pop()
        else:
            sl = len(self.slot_base)
            self.slot_base.append(0)
        self.uid += 1
        buf.sem = True
        buf.slot = sl
        buf.base = self.slot_base[sl]
        buf.cnt = 0
        buf.uid = self.uid
        self.live.append(buf)

    def barrier(self):
        lasts = list(self.last.values())
        self._barrier_ops(lasts)
        for b in self.live:
            self.slot_base[b.slot] = b.base + b.cnt
            self.free_slots.append(b.slot)
            b.sem = None
        self.live = []
        self.last = {kk: v for kk, v in self.last.items() if not isinstance(kk, tuple)}

    def _barrier_ops(self, lasts):
        for e in ENGS:
            o = Op()
            o.eng = e
            o.fn = None
            o.deps = []
            o.signal = False
            o.sigval = None
            o.dma = None
            o.key = e
            o.val = len(self.ops[e])
            for d in lasts:
                if d.key == e:
                    continue
                if self.seen[e].get(d.key, -1) >= d.val:
                    continue
                self.seen[e][d.key] = d.val
                o.deps.append(d)
            self.ops[e].append(o)

    def _dep(self, eng, d, deps, same_ok):
        if d is None:
            return
        key = d.key
        if key == eng:
            if eng == "pe" or same_ok:
                return
        if self.seen[eng].get(key, -1) >= d.val:
            return
        self.seen[eng][key] = d.val
        deps.append(d)

    def op(self, eng, fn, reads=(), writes=(), dma=None):
        o = Op()
        o.eng = eng
        o.fn = fn
        o.deps = []
        o.signal = False
        o.sigval = None
        o.dma = dma
        writes = [b for b in writes if b is not None]
        reads = [b for b in reads if b is not None]
        o.slot = None
        o.semval = None
        if dma is not None:
            if dma.sem is None:
                self._get_slot(dma)
            dma.cnt += 1
            o.key = ("dma", dma.uid)
            o.val = dma.cnt
            o.slot = dma.slot
            o.semval = 16 * (dma.base + dma.cnt)
            if dma not in writes:
                writes.append(dma)
            reads = [b for b in reads if b is not dma]
        else:
            o.key = eng
            o.val = len(self.ops[eng])
        for b in reads:
            self._dep(eng, b.w, o.deps, False)
        for b in writes:
            self._dep(eng, b.w, o.deps, True)
            for r in b.rs.values():
                self._dep(eng, r, o.deps, True)
        for b in reads:
            b.rs[o.key] = o
        for b in writes:
            b.w = o
            b.rs = {}
        self.ops[eng].append(o)
        self.last[o.key] = o
        return o

    def emit(self, stack):
        nc = self.nc
        for e in ENGS:
            for o in self.ops[e]:
                for d in o.deps:
                    if d.dma is None:
                        d.signal = True
        for e in ENGS:
            c = 0
            for o in self.ops[e]:
                if o.dma is None and o.signal:
                    c += 1
                    o.sigval = c
        esem = {}
        for e in ("pe", "act", "dve", "pool"):
            esem[e] = stack.enter_context(nc.semaphore("s_" + e))
        dsem = [stack.enter_context(nc.semaphore("d%d" % i)) for i in range(len(self.slot_base))]
        block = stack.enter_context(nc.Block())
        prog = self

        def run(name, eng):
            for o in prog.ops[name]:
                for d in o.deps:
                    if d.dma is not None:
                        eng.wait_ge(dsem[d.slot], d.semval)
                    else:
                        eng.wait_ge(esem[d.key], d.sigval)
                if o.fn is None:
                    continue
                ins = o.fn(eng)
                if o.dma is not None:
                    ins.then_inc(dsem[o.slot], 16)
                elif o.signal:
                    ins.then_inc(esem[name], 1)

        @block.sync
        def _(eng):
            run("sp", eng)

        @block.scalar
        def _(eng):
            run("act", eng)

        @block.vector
        def _(eng):
            run("dve", eng)

        @block.gpsimd
        def _(eng):
            run("pool", eng)

        @block.tensor
        def _(eng):
            run("pe", eng)


class K:
    def __init__(self, nc, P):
        self.nc = nc
        self.P = P
        self.n = 0

    def name(self, s):
        self.n += 1
        return "%s_%d" % (s, self.n)

    def sb(self, st, shape, dt=F32, name="t"):
        t = st.enter_context(self.nc.sbuf_tensor(self.name(name), list(shape), dt))
        return t, Buf(name)

    def dma(self, eng, out, in_, buf, reads=(), writes=()):
        self.P.op(eng, lambda e: e.dma_start(out=out, in_=in_), reads=reads, writes=writes, dma=buf)

    def mm(self, out, lhsT, rhs, start, stop, r, w):
        self.P.op("pe", lambda e: e.matmul(out, lhsT=lhsT, rhs=rhs, start=start, stop=stop), reads=r, writes=w)

    def tr(self, out, in_, ident, r, w):
        self.P.op("pe", lambda e: e.transpose(out=out, in_=in_, identity=ident), reads=r, writes=w)

    def act(self, out, in_, func, r, w, bias=None, scale=None, accum=None):
        kw = {}
        if bias is not None:
            kw["bias"] = bias
        if scale is not None:
            kw["scale"] = scale
        if accum is not None:
            kw["accum_out"] = accum
        self.P.op("act", lambda e: e.activation(out=out, in_=in_, func=func, **kw), reads=r, writes=w)

    def tt(self, eng, out, in0, in1, op, r, w):
        self.P.op(eng, lambda e: e.tensor_tensor(out=out, in0=in0, in1=in1, op=op), reads=r, writes=w)

    def ts(self, eng, out, in0, s1, op0, r, w, s2=None, op1=None):
        if op1 is None:
            self.P.op(eng, lambda e: e.tensor_scalar(out=out, in0=in0, scalar1=s1, scalar2=None, op0=op0), reads=r, writes=w)
        else:
            self.P.op(eng, lambda e: e.tensor_scalar(out=out, in0=in0, scalar1=s1, scalar2=s2, op0=op0, op1=op1), reads=r, writes=w)

    def stt(self, out, in0, scalar, in1, op0, op1, r, w):
        self.P.op("dve", lambda e: e.scalar_tensor_tensor(out=out, in0=in0, scalar=scalar, in1=in1, op0=op0, op1=op1), reads=r, writes=w)

    def copy(self, eng, out, in_, r, w):
        if eng == "act":
            self.P.op("act", lambda e: e.activation(out=out, in_=in_, func=AF.Copy), reads=r, writes=w)
        else:
            self.P.op(eng, lambda e: e.tensor_copy(out=out, in_=in_), reads=r, writes=w)

    def memset(self, eng, ap, val, w):
        self.P.op(eng, lambda e: e.memset(ap, val), writes=w)

    def scan(self, out, d0, d1, init, r, w):
        self.P.op("dve", lambda e: e.tensor_tensor_scan(out=out, data0=d0, data1=d1, initial=init, op0=ALU.mult, op1=ALU.add), reads=r, writes=w)

    def recip(self, out, in_, r, w):
        self.P.op("dve", lambda e: e.reciprocal(out=out, in_=in_), reads=r, writes=w)


class PsumRing:
    def __init__(self, k, st, n=8):
        self.banks = []
        for i in range(n):
            t = st.enter_context(k.nc.psum_tensor(k.name("ps"), [128, 512], F32))
            self.banks.append((t, Buf("ps%d" % i)))
        self.i = 0

    def get(self):
        t, b = self.banks[self.i % len(self.banks)]
        self.i += 1
        return t, b


def _swap_halves(w):
    sh = w.shape
    w4 = w.reshape(sh[:-1] + (sh[-1] // 64, 2, 32))
    return np.ascontiguousarray(w4[..., ::-1, :]).reshape(sh)


def host_consts(S, TS):
    c = {}
    inv = (10000.0 ** (-np.arange(0, 64, 2, dtype=np.float32) / np.float32(64))).astype(np.float32)
    ang = (np.arange(S, dtype=np.float32)[:, None] * inv[None, :]).astype(np.float32)
    cs, sn = np.cos(ang).astype(np.float32), np.sin(ang).astype(np.float32)
    cosT = np.concatenate([cs.T, cs.T], 0)
    sinT = np.concatenate([-sn.T, sn.T], 0)
    c["c_cos"] = np.ascontiguousarray(np.concatenate([cosT, cosT], 0))
    c["c_sin"] = np.ascontiguousarray(np.concatenate([sinT, sinT], 0))
    c["c_ident"] = np.eye(128, dtype=np.float32)
    c["c_tau"] = np.ascontiguousarray(np.broadcast_to(np.arange(TS, dtype=np.float32)[None, :], (128, TS)))
    k = np.arange(128)[:, None]
    q = np.arange(128)[None, :]
    c["c_caus"] = np.where(k <= q, 0.0, NEG).astype(np.float32)
    c["c_low"] = np.where(k > q, 0.0, NEG).astype(np.float32)
    cc = np.arange(1024)[None, :]
    qi = np.arange(128)[:, None]
    c["c_mgen"] = np.where(16 * (cc - 512) + 31 <= qi, 0.0, NEG).astype(np.float32)
    m = np.arange(16)[None, :, None]
    ni = np.arange(128)[:, None, None]
    qq = np.arange(128)[None, None, :]
    c["c_mt"] = np.where(16 * ni + 31 <= 128 * m + qq, 0.0, NEG).astype(np.float32)
    rel = np.arange(256)[None, :] - 126
    cur = (np.arange(128)[:, None] >= 64).astype(np.int64)
    g = np.where(rel > cur, -1e30, 0.0) + np.where((rel == cur) | (rel == cur - 1), 1e4, 0.0)
    c["c_g"] = g.astype(np.float32)
    NT = S // 128
    j = np.arange(128)[:, None, None]
    i = np.arange(NT)[None, :, None]
    kk = np.arange(128)[None, None, :]
    c["c_e"] = (j == 2 * i + (kk >= 64)).astype(np.float32)
    return c


def host_layout(inp, L):
    o = {}
    f = np.float32
    o["g_mix"] = np.ascontiguousarray(np.broadcast_to(inp["norm_mix"][:, None, :], (L, 128, D))).astype(f)
    o["g_ffn"] = np.ascontiguousarray(np.broadcast_to(inp["norm_ffn"][:, None, :], (L, 128, D))).astype(f)
    o["g_fin"] = np.ascontiguousarray(np.broadcast_to(inp["norm_final"][None, :], (128, D))).astype(f)
    w_in = inp["w_in"]
    o["w_in"] = w_in
    sw = np.concatenate([_swap_halves(w_in[:, :, 512:1024]), _swap_halves(w_in[:, :, 1024:1152]),
                         _swap_halves(w_in[:, :, 1280:1408]), _swap_halves(w_in[:, :, 1536:1664])], axis=-1)
    o["w_sw"] = np.ascontiguousarray(sw)

    def pair(a):
        return np.ascontiguousarray(a.reshape(L, 16, 2, 64).transpose(0, 2, 3, 1).reshape(L, 128, 16))
    o["s_are"] = pair(inp["ssm_a_re"])
    o["s_aim"] = pair(inp["ssm_a_im"])
    o["s_ldt"] = pair(np.broadcast_to(inp["ssm_log_dt"][:, :, None], (L, 32, 64)))
    for nm, src in (("s_bre", "ssm_b_re"), ("s_bim", "ssm_b_im")):
        b = inp[src].reshape(L, 16, 2, 64, 16)
        pad = np.zeros((L, 8, 16, 16, 2, 64), f)
        for j in range(16):
            for gl in range(2):
                pad[:, 2 * (j % 4) + gl, :, j, gl, :] = b[:, j, gl].transpose(0, 2, 1)
        o[nm] = pad.reshape(L, 128, 16, 128)
    for nm, src in (("s_cre", "ssm_c_re"), ("s_cim", "ssm_c_im")):
        cmat = inp[src].reshape(L, 16, 2, 16, 64)
        pad = np.zeros((L, 2, 64, 16, 8, 16), f)
        for j in range(16):
            for gl in range(2):
                pad[:, gl, :, j, 2 * (j % 4) + gl, :] = cmat[:, j, gl].transpose(0, 2, 1)
        o[nm] = pad.reshape(L, 128, 16, 128)
    o["s_d"] = np.ascontiguousarray(inp["ssm_d"].reshape(L, 4, 128).transpose(0, 2, 1))
    o["w_glu"] = inp["ssm_w_glu"]
    o["w_bssm"] = inp["w_branch_ssm"]
    o["w_bnsa"] = inp["w_branch_nsa"]
    o["w_out"] = inp["w_out"]
    for t in ("k", "v"):
        o["c_pe" + t] = np.ascontiguousarray(inp["cmp_pe_" + t].transpose(0, 2, 1))
        o["c_w1" + t] = inp["cmp_w1_" + t]
        o["c_b1" + t] = np.ascontiguousarray(inp["cmp_b1_" + t].reshape(L, 2, 128).transpose(0, 2, 1))
        o["c_w2" + t] = inp["cmp_w2_" + t]
    o["w_fin"] = inp["w_ffn_in"]
    o["w_fout"] = inp["w_ffn_out"]
    o["f_cw"] = np.ascontiguousarray(inp["ffn_conv_w"].reshape(L, 3, NFC, 128).transpose(0, 3, 2, 1))
    o["f_cb"] = np.ascontiguousarray(inp["ffn_conv_b"].reshape(L, NFC, 128).transpose(0, 2, 1))
    return o


def load_w(k, st, src2d, nch, ncols, prow=128, eng="pool", name="w"):
    t, _ = k.sb(st, [prow, nch, ncols], BF16, name)
    bufs = []
    for c in range(nch):
        b = Buf(name)
        k.dma(eng, t[:, c, :], src2d[c * prow:(c + 1) * prow, :], b)
        bufs.append(b)
    return t, bufs


def sincos(k, st, arg, n, out_sin=None, out_cos=None, rb=(), wsin=None, wcos=None):
    for (dst, off, wb) in ((out_sin, 0.0, wsin), (out_cos, 0.25, wcos)):
        if dst is None:
            continue
        a2, ba2 = k.sb(st, [128, n], F32, "sc_a")
        ti, bti = k.sb(st, [128, n], I32, "sc_i")
        tf, btf = k.sb(st, [128, n], F32, "sc_f")
        k.ts("dve", a2[:], arg, off, ALU.add, r=list(rb), w=[ba2])
        k.copy("dve", ti[:], a2[:], r=[ba2], w=[bti])
        k.copy("dve", tf[:], ti[:], r=[bti], w=[btf])
        k.tt("dve", a2[:], a2[:], tf[:], ALU.subtract, r=[ba2, btf], w=[ba2])
        k.act(dst, a2[:], AF.Sin, r=[ba2], w=[wb], scale=SIN_SCALE)


def phase1(k, l, S, TS, x_src, di, sc, cst):
    P = k.P
    TT = TS
    NTT = S // TT
    with ExitStack() as st:
        ring = PsumRing(k, st)
        ident, bid = cst["ident"]
        gam, bgam = k.sb(st, [128, D], F32, "gam")
        k.dma("sp", gam[:], di["g_mix"][l], bgam)
        dvec, bdvec = k.sb(st, [128, 4], F32, "dvec")
        k.dma("sp", dvec[:], di["s_d"][l], bdvec)
        RFre, bRFre = k.sb(st, [128, 16, TS], F32, "RFre")
        RFim, bRFim = k.sb(st, [128, 16, TS], F32, "RFim")
        COSb, bCOSb = k.sb(st, [128, 16, TS], BF16, "COSb")
        SINb, bSINb = k.sb(st, [128, 16, TS], BF16, "SINb")
        NSINb, bNSINb = k.sb(st, [128, 16, TS], BF16, "NSINb")
        dec, bdec = k.sb(st, [128, 16], F32, "dec")
        cT, bcT = k.sb(st, [128, 16], F32, "cT")
        sT, bsT = k.sb(st, [128, 16], F32, "sT")
        nsT, bnsT = k.sb(st, [128, 16], F32, "nsT")
        with ExitStack() as s2:
            are, bare = k.sb(s2, [128, 16], F32, "are")
            aim, baim = k.sb(s2, [128, 16], F32, "aim")
            ldt, bldt = k.sb(s2, [128, 16], F32, "ldt")
            tau, btau = k.sb(s2, [128, TS], F32, "tau")
            k.dma("sp", are[:], di["s_are"][l], bare)
            k.dma("sp", aim[:], di["s_aim"][l], baim)
            k.dma("sp", ldt[:], di["s_ldt"][l], bldt)
            k.dma("sp", tau[:], di["c_tau"], btau)
            dt_, bdt = k.sb(s2, [128, 16], F32, "dt")
            k.act(dt_[:], ldt[:], AF.Exp, r=[bldt], w=[bdt])
            rho, brho = k.sb(s2, [128, 16], F32, "rho")
            thn, bthn = k.sb(s2, [128, 16], F32, "thn")
            k.tt("dve", rho[:], are[:], dt_[:], ALU.mult, r=[bare, bdt], w=[brho])
            k.tt("dve", thn[:], aim[:], dt_[:], ALU.mult, r=[baim, bdt], w=[bthn])
            k.ts("dve", thn[:], thn[:], 1.0 / TWO_PI, ALU.mult, r=[bthn], w=[bthn])
            k.act(dec[:], rho[:], AF.Exp, r=[brho], w=[bdec])
            s1, bs1 = k.sb(s2, [128, 16], F32, "s1")
            c1, bc1 = k.sb(s2, [128, 16], F32, "c1")
            sincos(k, s2, thn[:], 16, s1[:], c1[:], rb=[bthn], wsin=bs1, wcos=bc1)
            abre, babre = k.sb(s2, [128, 16], F32, "abre")
            abim, babim = k.sb(s2, [128, 16], F32, "abim")
            k.tt("dve", abre[:], dec[:], c1[:], ALU.mult, r=[bdec, bc1], w=[babre])
            k.ts("dve", abre[:], abre[:], -1.0, ALU.add, r=[babre], w=[babre])
            k.tt("dve", abim[:], dec[:], s1[:], ALU.mult, r=[bdec, bs1], w=[babim])
            den, bden = k.sb(s2, [128, 16], F32, "den")
            t0, bt0 = k.sb(s2, [128, 16], F32, "t0")
            k.tt("dve", den[:], are[:], are[:], ALU.mult, r=[bare], w=[bden])
            k.tt("dve", t0[:], aim[:], aim[:], ALU.mult, r=[baim], w=[bt0])
            k.tt("dve", den[:], den[:], t0[:], ALU.add, r=[bden, bt0], w=[bden])
            k.recip(den[:], den[:], r=[bden], w=[bden])
            fre, bfre = k.sb(s2, [128, 16], F32, "fre")
            fim, bfim = k.sb(s2, [128, 16], F32, "fim")
            t1, bt1 = k.sb(s2, [128, 16], F32, "t1")
            k.tt("dve", fre[:], abre[:], are[:], ALU.mult, r=[babre, bare], w=[bfre])
            k.tt("dve", t1[:], abim[:], aim[:], ALU.mult, r=[babim, baim], w=[bt1])
            k.tt("dve", fre[:], fre[:], t1[:], ALU.add, r=[bfre, bt1], w=[bfre])
            k.tt("dve", fre[:], fre[:], den[:], ALU.mult, r=[bfre, bden], w=[bfre])
            k.tt("dve", fim[:], abim[:], are[:], ALU.mult, r=[babim, bare], w=[bfim])
            k.tt("dve", t1[:], abre[:], aim[:], ALU.mult, r=[babre, baim], w=[bt1])
            k.tt("dve", fim[:], fim[:], t1[:], ALU.subtract, r=[bfim, bt1], w=[bfim])
            k.tt("dve", fim[:], fim[:], den[:], ALU.mult, r=[bfim, bden], w=[bfim])
            aT, baT = k.sb(s2, [128, 16], F32, "aT")
            k.ts("dve", aT[:], thn[:], float(TS), ALU.mult, r=[bthn], w=[baT])
            sincos(k, s2, aT[:], 16, sT[:], cT[:], rb=[baT], wsin=bsT, wcos=bcT)
            k.ts("dve", nsT[:], sT[:], -1.0, ALU.mult, r=[bsT], w=[bnsT])
            ANG, bANG = k.sb(s2, [128, 16, TS], F32, "ANG")
            SINf, bSINf = k.sb(s2, [128, 16 * TS], F32, "SINf")
            COSf, bCOSf = k.sb(s2, [128, 16 * TS], F32, "COSf")
            for j in range(16):
                k.ts("dve", ANG[:, j, :], tau[:], thn[:, j:j + 1], ALU.mult, r=[btau, bthn], w=[bANG])
            sincos(k, s2, ANG[:].rearrange("p j t -> p (j t)"), 16 * TS, SINf[:], COSf[:], rb=[bANG], wsin=bSINf, wcos=bCOSf)
            SIN3 = SINf[:].rearrange("p (j t) -> p j t", j=16)
            COS3 = COSf[:].rearrange("p (j t) -> p j t", j=16)
            tmp, btmp = k.sb(s2, [128, TS], F32, "tmp")
            for j in range(16):
                k.ts("dve", tmp[:], SIN3[:, j, :], fim[:, j:j + 1], ALU.mult, r=[bSINf, bfim], w=[btmp])
                k.stt(RFre[:, j, :], COS3[:, j, :], fre[:, j:j + 1], tmp[:], ALU.mult, ALU.add, r=[bCOSf, bfre, btmp], w=[bRFre])
                k.ts("dve", tmp[:], SIN3[:, j, :], fre[:, j:j + 1], ALU.mult, r=[bSINf, bfre], w=[btmp])
                k.stt(RFim[:, j, :], COS3[:, j, :], fim[:, j:j + 1], tmp[:], ALU.mult, ALU.subtract, r=[bCOSf, bfim, btmp], w=[bRFim])
            k.copy("dve", COSb[:].rearrange("p j t -> p (j t)"), COSf[:], r=[bCOSf], w=[bCOSb])
            k.copy("dve", SINb[:].rearrange("p j t -> p (j t)"), SINf[:], r=[bSINf], w=[bSINb])
            k.ts("dve", NSINb[:].rearrange("p j t -> p (j t)"), SINf[:], -1.0, ALU.mult, r=[bSINf], w=[bNSINb])
        P.barrier()
        w_in, bw_in = load_w(k, st, di["w_in"][l], 8, INW, name="w_in")
        w_sw, bw_sw = load_w(k, st, di["w_sw"][l], 8, 896, name="w_sw")
        w_glu, bw_glu = load_w(k, st, di["w_glu"][l], 4, 1024, name="w_glu")
        w_bs, bw_bs = load_w(k, st, di["w_bssm"][l], 4, 1024, name="w_bs")
        bre, bbre = load_w(k, st, di["s_bre"][l].rearrange("p j m -> p (j m)"), 1, 2048, name="bre")
        bim, bbim = load_w(k, st, di["s_bim"][l].rearrange("p j m -> p (j m)"), 1, 2048, name="bim")
        cre, bcre = load_w(k, st, di["s_cre"][l].rearrange("p j m -> p (j m)"), 1, 2048, name="cre")
        cim, bcim = load_w(k, st, di["s_cim"][l].rearrange("p j m -> p (j m)"), 1, 2048, name="cim")
        xt = [k.sb(st, [128, TT // 128, D], F32, "xt") for _ in range(2)]
        cosr = [k.sb(st, [128, TT], F32, "cosr") for _ in range(2)]
        sinr = [k.sb(st, [128, TT], F32, "sinr") for _ in range(2)]
        NSB = TT // 128
        junk, bjunk = k.sb(st, [128, D], BF16, "junk")
        ss, bss = k.sb(st, [128, NSB], F32, "ss")
        ms, bms = k.sb(st, [128, NSB], F32, "ms")
        sd, bsd = k.sb(st, [128, NSB], F32, "sd")
        rstd, brstd = k.sb(st, [128, NSB], F32, "rstd")
        hh = [k.sb(st, [128, D], BF16, "h") for _ in range(2)]
        hT, bhT = k.sb(st, [128, 8, TT], BF16, "hT")
        uT, buT = k.sb(st, [128, 4, TT], BF16, "uT")
        qTs, bqTs = k.sb(st, [128, 4, TT], BF16, "qTs")
        kvs = {nm: k.sb(st, [128, TT], BF16, nm) for nm in ("kcT", "vcT", "ksT", "kwT")}
        sgs, bsgs = k.sb(st, [128, 8, TT], BF16, "sgs")
        sgn, bsgn = k.sb(st, [128, 8, TT], BF16, "sgn")
        sg, bsg = k.sb(st, [24, TT], F32, "sg")
        vsel, bvsel = k.sb(st, [128, NSB, 2, 65], BF16, "vsel")
        vwin, bvwin = k.sb(st, [128, NSB, 2, 65], BF16, "vwin")
        k.memset("pool", vsel[:], 1.0, [bvsel])
        k.memset("pool", vwin[:], 1.0, [bvwin])
        tmps = [k.sb(st, [128, TT], F32, "tmp") for _ in range(6)]
        tmpi = [0]

        def gettmp():
            t = tmps[tmpi[0] % len(tmps)]
            tmpi[0] += 1
            return t
        bsc = [k.sb(st, [128, TT], F32, "bsc") for _ in range(4)]
        wsc = [k.sb(st, [128, TT], F32, "wsc") for _ in range(4)]
        xre, bxre = k.sb(st, [128, 16, TT], BF16, "xre")
        nxim, bnxim = k.sb(st, [128, 16, TT], BF16, "nxim")
        car, bcar = k.sb(st, [128, 2, 16], F32, "car")
        k.memset("dve", car[:], 0.0, [bcar])
        ctmp, bctmp = k.sb(st, [128, 2], F32, "ctmp")
        ypre, bypre = k.sb(st, [128, TT], F32, "ypre")
        yT, byT = k.sb(st, [128, 4, TT], BF16, "yT")
        sgz, bsgz = k.sb(st, [128, TT], F32, "sgz")
        zzT, bzzT = k.sb(st, [128, 4, TT], BF16, "zzT")
        gss, bgss = k.sb(st, [128, 8, TT], BF16, "gss")

        def load_tile(i):
            t, b = xt[i % 2]
            k.dma("sp", t[:], x_src[i * TT:(i + 1) * TT, :].rearrange("(s p) d -> p s d", p=128), b)
            k.dma("sp", cosr[i % 2][0][:], di["c_cos"][:, i * TT:(i + 1) * TT], cosr[i % 2][1])
            k.dma("sp", sinr[i % 2][0][:], di["c_sin"][:, i * TT:(i + 1) * TT], sinr[i % 2][1])

        def proj(wt, wb, col0, M=128):
            ps, bp = ring.get()
            for c in range(8):
                k.mm(ps[0:M, 0:TT], wt[:, c, col0:col0 + M], hT[:, c, :], c == 0, c == 7, r=[wb[c], bhT], w=[bp])
            return ps, bp

        load_tile(0)
        for i in range(NTT):
            if i + 1 < NTT:
                load_tile(i + 1)
            x_t, bx = xt[i % 2]
            cos_t, bcos = cosr[i % 2]
            sin_t, bsin = sinr[i % 2]
            tok = slice(i * TT, (i + 1) * TT)
            for s_ in range(NSB):
                h_t, bh = hh[s_ % 2]
                k.act(junk[:], x_t[:, s_, :], AF.Square, r=[bx], w=[bjunk, bss], accum=ss[:, s_:s_ + 1])
                k.ts("dve", ms[:, s_:s_ + 1], ss[:, s_:s_ + 1], 1.0 / D, ALU.mult, r=[bss], w=[bms], s2=EPS, op1=ALU.add)
                k.act(sd[:, s_:s_ + 1], ms[:, s_:s_ + 1], AF.Sqrt, r=[bms], w=[bsd])
                k.recip(rstd[:, s_:s_ + 1], sd[:, s_:s_ + 1], r=[bsd], w=[brstd])
                k.stt(h_t[:], x_t[:, s_, :], rstd[:, s_:s_ + 1], gam[:], ALU.mult, ALU.mult, r=[bx, brstd, bgam], w=[bh])
                ps, bp = ring.get()
                pbf = ps[:].bitcast(BF16)
                for c in range(8):
                    k.tr(pbf[:, c * 128:(c + 1) * 128], h_t[:, c * 128:(c + 1) * 128], ident[:], r=[bh, bid], w=[bp])
                k.copy("act", hT[:, :, s_ * 128:(s_ + 1) * 128], pbf.rearrange("p (c t) -> p c t", c=8), r=[bp], w=[bhT])
            for c4 in range(4):
                ps, bp = proj(w_in, bw_in, c4 * 128)
                k.copy("act", uT[:, c4, :], ps[:, 0:TT], r=[bp], w=[buT])
            def rope(col, swcol, dst, bdst):
                psA, bA = proj(w_in, bw_in, col)
                psB, bB = proj(w_sw, bw_sw, swcol)
                t1_, bt1_ = gettmp()
                t2_, bt2_ = gettmp()
                k.tt("dve", t1_[:], psA[:, 0:TT], cos_t[:], ALU.mult, r=[bA, bcos], w=[bt1_])
                k.tt("dve", t2_[:], psB[:, 0:TT], sin_t[:], ALU.mult, r=[bB, bsin], w=[bt2_])
                k.tt("pool", dst, t1_[:], t2_[:], ALU.add, r=[bt1_, bt2_], w=[bdst])
            for c in range(4):
                rope(512 + c * 128, c * 128, qTs[:, c, :], bqTs)
            rope(1024, 512, kvs["kcT"][0][:], kvs["kcT"][1])
            rope(1280, 640, kvs["ksT"][0][:], kvs["ksT"][1])
            rope(1536, 768, kvs["kwT"][0][:], kvs["kwT"][1])
            ps, bp = proj(w_in, bw_in, 1152)
            k.copy("act", kvs["vcT"][0][:], ps[:, 0:TT], r=[bp], w=[kvs["vcT"][1]])
            for c in range(8):
                ps, bp = proj(w_in, bw_in, 1816 + c * 128)
                k.act(sgs[:, c, :], ps[:, 0:TT], AF.Sigmoid, r=[bp], w=[bsgs])
            for c in range(8):
                ps, bp = proj(w_in, bw_in, 2840 + c * 128)
                k.act(sgn[:, c, :], ps[:, 0:TT], AF.Sigmoid, r=[bp], w=[bsgn])
            ps, bp = proj(w_in, bw_in, 1792, M=24)
            k.act(sg[:], ps[0:24, 0:TT], AF.Sigmoid, r=[bp], w=[bsg])
            for s_ in range(NSB):
                for (col, vt, bv) in ((1408, vsel, bvsel), (1664, vwin, bvwin)):
                    ps, bp = ring.get()
                    for c in range(8):
                        k.mm(ps[:, 0:128], hT[:, c, s_ * 128:(s_ + 1) * 128], w_in[:, c, col:col + 128], c == 0, c == 7, r=[bw_in[c], bhT], w=[bp])
                    k.copy("act", vt[:, s_, :, 0:64], ps[:, 0:128].rearrange("p (h d) -> p h d", h=2), r=[bp], w=[bv])
            k.dma("sp", sc["qT"].rearrange("(c p) s -> p c s", p=128)[:, :, tok], qTs[:], bqTs)
            for nm in ("kcT", "vcT", "ksT", "kwT"):
                k.dma("sp", sc[nm][:, tok], kvs[nm][0][:], kvs[nm][1])
            k.dma("sp", sc["sgnT"].rearrange("(c p) s -> p c s", p=128)[:, :, tok], sgn[:], bsgn)
            k.dma("sp", sc["sgT"][:, tok], sg[:], bsg)
            k.dma("sp", sc["vs"][tok].rearrange("(s p) h c -> p s h c", p=128), vsel[:], bvsel)
            k.dma("sp", sc["vw"][tok].rearrange("(s p) h c -> p s h c", p=128), vwin[:], bvwin)
            for j in range(16):
                c4 = j // 4
                psr, bpr = ring.get()
                psi, bpi = ring.get()
                k.mm(psr[:, 0:TT], bre[:, 0, j * 128:(j + 1) * 128], uT[:, c4, :], True, True, r=[bbre[0], buT], w=[bpr])
                k.mm(psi[:, 0:TT], bim[:, 0, j * 128:(j + 1) * 128], uT[:, c4, :], True, True, r=[bbim[0], buT], w=[bpi])
                b_re, bb_re = bsc[(2 * j) % 4]
                b_im, bb_im = bsc[(2 * j + 1) % 4]
                w_re, bw_re = wsc[(2 * j) % 4]
                w_im, bw_im = wsc[(2 * j + 1) % 4]
                t1_, bt1_ = gettmp()
                t2_, bt2_ = gettmp()
                k.tt("dve", t1_[:], psr[:, 0:TT], RFre[:, j, :], ALU.mult, r=[bpr, bRFre], w=[bt1_])
                k.tt("dve", t2_[:], psi[:, 0:TT], RFim[:, j, :], ALU.mult, r=[bpi, bRFim], w=[bt2_])
                k.tt("pool", b_re[:], t1_[:], t2_[:], ALU.subtract, r=[bt1_, bt2_], w=[bb_re])
                t3_, bt3_ = gettmp()
                t4_, bt4_ = gettmp()
                k.tt("dve", t3_[:], psi[:, 0:TT], RFre[:, j, :], ALU.mult, r=[bpi, bRFre], w=[bt3_])
                k.tt("dve", t4_[:], psr[:, 0:TT], RFim[:, j, :], ALU.mult, r=[bpr, bRFim], w=[bt4_])
                k.tt("pool", b_im[:], t3_[:], t4_[:], ALU.add, r=[bt3_, bt4_], w=[bb_im])
                dj = dec[:, j:j + 1].to_broadcast([128, TT])
                k.scan(w_re[:], dj, b_re[:], car[:, 0, j:j + 1], r=[bdec, bb_re, bcar], w=[bw_re])
                k.scan(w_im[:], dj, b_im[:], car[:, 1, j:j + 1], r=[bdec, bb_im, bcar], w=[bw_im])
                k.ts("dve", ctmp[:, 0:1], w_re[:, TT - 1:TT], cT[:, j:j + 1], ALU.mult, r=[bw_re, bcT], w=[bctmp])
                k.ts("dve", ctmp[:, 1:2], w_im[:, TT - 1:TT], cT[:, j:j + 1], ALU.mult, r=[bw_im, bcT], w=[bctmp])
                k.stt(car[:, 0, j:j + 1], w_im[:, TT - 1:TT], nsT[:, j:j + 1], ctmp[:, 0:1], ALU.mult, ALU.add, r=[bw_im, bnsT, bctmp], w=[bcar])
                k.stt(car[:, 1, j:j + 1], w_re[:, TT - 1:TT], sT[:, j:j + 1], ctmp[:, 1:2], ALU.mult, ALU.add, r=[bw_re, bsT, bctmp], w=[bcar])
                t5_, bt5_ = gettmp()
                t6_, bt6_ = gettmp()
                k.tt("dve", t5_[:], w_re[:], COSb[:, j, :], ALU.mult, r=[bw_re, bCOSb], w=[bt5_])
                k.tt("dve", t6_[:], w_im[:], SINb[:, j, :], ALU.mult, r=[bw_im, bSINb], w=[bt6_])
                k.tt("pool", xre[:, j, :], t5_[:], t6_[:], ALU.subtract, r=[bt5_, bt6_], w=[bxre])
                t7_, bt7_ = gettmp()
                t8_, bt8_ = gettmp()
                k.tt("dve", t7_[:], w_re[:], NSINb[:, j, :], ALU.mult, r=[bw_re, bNSINb], w=[bt7_])
                k.tt("dve", t8_[:], w_im[:], COSb[:, j, :], ALU.mult, r=[bw_im, bCOSb], w=[bt8_])
                k.tt("pool", nxim[:, j, :], t7_[:], t8_[:], ALU.subtract, r=[bt7_, bt8_], w=[bnxim])
            for c4 in range(4):
                ps, bp = ring.get()
                for jj in range(4):
                    j = 4 * c4 + jj
                    k.mm(ps[:, 0:TT], cre[:, 0, j * 128:(j + 1) * 128], xre[:, j, :], jj == 0, False, r=[bcre[0], bxre], w=[bp])
                    k.mm(ps[:, 0:TT], cim[:, 0, j * 128:(j + 1) * 128], nxim[:, j, :], False, jj == 3, r=[bcim[0], bnxim], w=[bp])
                k.stt(ypre[:], uT[:, c4, :], dvec[:, c4:c4 + 1], ps[:, 0:TT], ALU.mult, ALU.add, r=[buT, bdvec, bp], w=[bypre])
                k.act(yT[:, c4, :], ypre[:], AF.Gelu_apprx_tanh, r=[bypre], w=[byT])
            for kk in range(4):
                psg, bpg = ring.get()
                for c4 in range(4):
                    k.mm(psg[:, 0:TT], w_glu[:, c4, (4 + kk) * 128:(5 + kk) * 128], yT[:, c4, :], c4 == 0, c4 == 3, r=[bw_glu[c4], byT], w=[bpg])
                k.act(sgz[:], psg[:, 0:TT], AF.Sigmoid, r=[bpg], w=[bsgz])
                psv, bpv = ring.get()
                for c4 in range(4):
                    k.mm(psv[:, 0:TT], w_glu[:, c4, kk * 128:(kk + 1) * 128], yT[:, c4, :], c4 == 0, c4 == 3, r=[bw_glu[c4], byT], w=[bpv])
                k.tt("dve", zzT[:, kk, :], psv[:, 0:TT], sgz[:], ALU.mult, r=[bpv, bsgz], w=[bzzT])
            for fc in range(8):
                ps, bp = ring.get()
                for kk in range(4):
                    k.mm(ps[:, 0:TT], w_bs[:, kk, fc * 128:(fc + 1) * 128], zzT[:, kk, :], kk == 0, kk == 3, r=[bw_bs[kk], bzzT], w=[bp])
                k.tt("dve", gss[:, fc, :], ps[:, 0:TT], sgs[:, fc, :], ALU.mult, r=[bp, bsgs], w=[bgss])
            k.dma("sp", sc["gssT"].rearrange("(c p) s -> p c s", p=128)[:, :, tok], gss[:], bgss)
    P.barrier()


def phase2(k, pers, l, S, di, sc, cst):
    P = k.P
    NC = S // 16 - 1
    NCP = S // 16
    NCT = NCP // 128
    KcT, bKcT = k.sb(pers, [128, NCP], BF16, "KcT")
    Vc, bVc = k.sb(pers, [128, NCT, 2, 65], BF16, "Vc")
    k.memset("pool", KcT[:], 0.0, [bKcT])
    k.memset("pool", Vc[:], 1.0, [bVc])
    with ExitStack() as st:
        ring = PsumRing(k, st)
        for typ in ("k", "v"):
            with ExitStack() as s2:
                xT, bxT = k.sb(s2, [128, S], BF16, "cxT")
                k.dma("sp", xT[:], sc["kcT" if typ == "k" else "vcT"], bxT)
                w1, bw1 = k.sb(s2, [128, 32, 256], BF16, "w1")
                bw1b = Buf("w1b")
                src = di["c_w1" + typ][l].rearrange("(l d) c -> d l c", d=64)
                k.dma("pool", w1[0:64], src, bw1)
                k.dma("pool", w1[64:128], src, bw1b)
                pe2, bpe2 = k.sb(s2, [64, 32, 2], BF16, "pe2")
                pe_f, bpe_f = k.sb(s2, [64, 32], F32, "pe_f")
                k.dma("sp", pe_f[:], di["c_pe" + typ][l], bpe_f)
                k.copy("dve", pe2[:, :, 0], pe_f[:], r=[bpe_f], w=[bpe2])
                k.copy("dve", pe2[:, :, 1], pe_f[:], r=[bpe_f], w=[bpe2])
                b1, bb1 = k.sb(s2, [128, 2], F32, "b1")
                k.dma("sp", b1[:], di["c_b1" + typ][l], bb1)
                w2, bw2 = k.sb(s2, [128, 2, 64], BF16, "w2")
                k.dma("pool", w2[:], di["c_w2" + typ][l].rearrange("(c p) d -> p c d", p=128), bw2)
                bias, bbias = k.sb(s2, [128, 2], F32, "bias")
                for cc in range(2):
                    ps, bp = ring.get()
                    for li in range(32):
                        k.mm(ps[:, 0:2], w1[0:64, li, cc * 128:(cc + 1) * 128], pe2[:, li, :], li == 0, li == 31, r=[bw1, bpe2], w=[bp])
                    k.tt("dve", bias[:, cc:cc + 1], ps[:, 0:1], b1[:, cc:cc + 1], ALU.add, r=[bp, bb1], w=[bbias])
                for hk in range(2):
                    hid, bhid = k.sb(s2, [128, 2, NCP], BF16, "hid")
                    k.memset("pool", hid[:], 0.0, [bhid])
                    bw = bw1 if hk == 0 else bw1b
                    for cc in range(2):
                        ps, bp = ring.get()
                        for li in range(32):
                            k.mm(ps[:, 0:NC], w1[hk * 64:(hk + 1) * 64, li, cc * 128:(cc + 1) * 128],
                                 xT[hk * 64:(hk + 1) * 64, li:li + 16 * (NC - 1) + 1:16], li == 0, li == 31, r=[bw, bxT], w=[bp])
                        k.act(hid[:, cc, 0:NC], ps[:, 0:NC], AF.Gelu_apprx_tanh, r=[bp, bbias], w=[bhid], bias=bias[:, cc:cc + 1])
                    if typ == "k":
                        ps, bp = ring.get()
                        for cc in range(2):
                            k.mm(ps[hk * 64:(hk + 1) * 64, 0:NC], w2[:, cc, :], hid[:, cc, 0:NC], cc == 0, cc == 1, r=[bw2, bhid], w=[bp])
                        k.copy("act", KcT[hk * 64:(hk + 1) * 64, 0:NC], ps[hk * 64:(hk + 1) * 64, 0:NC], r=[bp], w=[bKcT])
                    else:
                        for nt in range(NCT):
                            ps, bp = ring.get()
                            for cc in range(2):
                                k.mm(ps[:, 0:64], hid[:, cc, nt * 128:(nt + 1) * 128], w2[:, cc, :], cc == 0, cc == 1, r=[bhid, bw2], w=[bp])
                            k.copy("act", Vc[:, nt, hk, 0:64], ps[:, 0:64], r=[bp], w=[bVc])
            P.barrier()
    if "dbg_kc" in sc:
        k.dma("sp", sc["dbg_kc"].rearrange("h d n -> (h d) n"), KcT[:], bKcT)
        k.dma("sp", sc["dbg_vc"], Vc[:], bVc)
    return (KcT, bKcT), (Vc, bVc)


def phase3(k, l, S, x_src, di, sc, cst, cmp_t):
    P = k.P
    NT = S // 128
    NCP = S // 16
    NCT = NCP // 128
    NB = S // 64
    (KcT, bKcT), (Vc, bVc) = cmp_t
    ident, bid = cst["ident"]
    with ExitStack() as st:
        ring = PsumRing(k, st, 3)
        ringO = PsumRing(k, st, 3)
        ringM = PsumRing(k, st, 2)
        KsT, bKsT = k.sb(st, [128, S], BF16, "KsT")
        k.dma("sp", KsT[:], sc["ksT"], bKsT)
        Vs, bVs = k.sb(st, [128, NT, 2, 65], BF16, "Vs")
        k.dma("sp", Vs[:], sc["vs"].rearrange("(n p) h c -> p n h c", p=128), bVs)

        def cload(name, shape, src, dt=BF16):
            t, b = k.sb(st, shape, dt, name)
            k.dma("pool" if dt == BF16 else "sp", t[:], src, b)
            return t, b
        E, bE = cload("E", [128, NT, 128], di["c_e"])
        caus, bcaus = cload("caus", [128, 128], di["c_caus"])
        low, blow = cload("low", [128, 128], di["c_low"])
        mgen, bmgen = cload("mgen", [128, 1024], di["c_mgen"])
        mt, bmt = cload("mt", [128, 16, 128], di["c_mt"])
        G, bG = cload("G", [128, 256], di["c_g"], F32)
        wbn, bwbn = cload("wbn", [64, 8, 1024], di["w_bnsa"][l].rearrange("(h d) n -> d h n", d=64))
        w_out, bw_out = load_w(k, st, di["w_out"][l], 8, 1024, name="w_out")
        ones16, bones = k.sb(st, [128, 64], BF16, "ones16")
        k.memset("pool", ones16[:], 1.0, [bones])
        rhis = [k.sb(st, [65, 512], BF16, "rhi") for _ in range(4)]
        rlos = [k.sb(st, [65, 512], BF16, "rlo") for _ in range(4)]
        QT = [k.sb(st, [128, 4, 128], BF16, "QT") for _ in range(2)]
        KwT = [k.sb(st, [128, 640], BF16, "KwT") for _ in range(2)]
        Vw = [k.sb(st, [128, 5, 2, 65], BF16, "Vw") for _ in range(2)]
        grow = [k.sb(st, [65, 2, 3, 4, 128], F32, "grow") for _ in range(2)]
        sgn = [k.sb(st, [128, 8, 128], BF16, "sgn") for _ in range(2)]
        gss = [k.sb(st, [128, 8, 128], BF16, "gss") for _ in range(2)]
        xin = [k.sb(st, [128, D], F32, "xin") for _ in range(2)]
        eg = [k.sb(st, [128, NCP], F32, "eg") for _ in range(4)]
        den4, bden4 = k.sb(st, [128, 4], F32, "den4")
        rden4, brden4 = k.sb(st, [128, 4], F32, "rden4")
        pg, bpg = k.sb(st, [128, NCP + 8], F32, "pg")
        k.memset("pool", pg[:], 0.0, [bpg])
        blk, bblk = k.sb(st, [128, NB], F32, "blk")
        blk2, bblk2 = k.sb(st, [128, NB], F32, "blk2")
        m8, bm8 = k.sb(st, [128, 16], F32, "m8")
        negm, bnegm = k.sb(st, [128, 128], BF16, "negm")
        k.memset("pool", negm[:], 0.0, [bnegm])
        negT4s = [k.sb(st, [128, 4, 128], BF16, "negT4") for _ in range(2)]
        pTs = [k.sb(st, [128, 512], BF16, "pT") for _ in range(5)]
        pti = [0]
        rrs = [k.sb(st, [65, 512], F32, "rr") for _ in range(4)]
        osbs = [k.sb(st, [64, 512], F32, "osb") for _ in range(4)]
        oacc, boacc = k.sb(st, [64, 512], F32, "oacc")
        otmp, botmp = k.sb(st, [64, 512], F32, "otmp")
        oTb = [k.sb(st, [64, 4, 128], BF16, "oTb") for _ in range(2)]
        mtmp, bmtmp = k.sb(st, [128, 128], F32, "mtmp")
        mrg, bmrg = k.sb(st, [128, 8, 128], BF16, "mrg")
        xm = [k.sb(st, [128, D], F32, "xm") for _ in range(2)]

        def loads(qb):
            s0 = qb * 128
            pb = qb % 2
            qv = sc["qT"].rearrange("(h d) s -> d h s", d=64)
            t, b = QT[pb]
            for hk in range(2):
                k.dma("sp", t[hk * 64:(hk + 1) * 64], qv[:, hk * 4:(hk + 1) * 4, s0:s0 + 128], b)
            lo = max(0, s0 - 512)
            t, b = KwT[pb]
            k.dma("sp", t[:, 640 - (s0 + 128 - lo):640], sc["kwT"][:, lo:s0 + 128], b)
            nw = (s0 + 128 - lo) // 128
            t, b = Vw[pb]
            k.dma("sp", t[:, 5 - nw:5], sc["vw"][lo:s0 + 128].rearrange("(n p) h c -> p n h c", p=128), b)
            t, b = grow[pb]
            gv = sc["sgT"].rearrange("(hk g br) s -> hk br g s", hk=2, g=4, br=3)
            for hk in range(2):
                for br in range(3):
                    k.dma("sp", t[64:65, hk, br], gv[hk, br:br + 1, :, s0:s0 + 128], b)
            t, b = sgn[pb]
            k.dma("sp", t[:], sc["sgnT"].rearrange("(c p) s -> p c s", p=128)[:, :, s0:s0 + 128], b)
            t, b = gss[pb]
            k.dma("sp", t[:], sc["gssT"].rearrange("(c p) s -> p c s", p=128)[:, :, s0:s0 + 128], b)
            t, b = xin[pb]
            k.dma("sp", t[:], x_src[s0:s0 + 128, :], b)

        DEPTH = 2
        pipe = []
        delayed = []

        def tick():
            for d in delayed:
                d[0] -= 1
            while delayed and delayed[0][0] <= 0:
                delayed.pop(0)[1]()

        def push(score_fn, pv_fn, after=None):
            tok_ = score_fn()
            pipe.append((pv_fn, tok_, after))
            if len(pipe) > DEPTH:
                pv, tk, af = pipe.pop(0)
                pv(tk)
                if af is not None:
                    af()
            tick()

        def flush():
            while pipe:
                pv, tk, af = pipe.pop(0)
                pv(tk)
                if af is not None:
                    af()
            while delayed:
                delayed.pop(0)[1]()

        def attn_tile(Ops, bO, first, last_, KT_ap, bKT, V_ap, bV, Q2, bQ, smask=None, emask=None, after=None):
            def score():
                psS, bS = ring.get()
                nmask = (4 if smask is not None else 0) + (1 if emask is not None else 0)
                k.mm(psS[:, 0:512], KT_ap, Q2, True, nmask == 0, r=[bKT, bQ], w=[bS])
                done = 0
                if emask is not None:
                    done += 1
                    k.mm(psS[:, 0:512], emask, negT4[:].rearrange("p g q -> p (g q)"), False, done == nmask, r=[bE, bnegT4], w=[bS])
                if smask is not None:
                    m_ap, bm = smask
                    for g in range(4):
                        done += 1
                        k.mm(psS[:, g * 128:(g + 1) * 128], ident[:], m_ap, False, done == nmask, r=[bid, bm], w=[bS])
                pT, bpT = pTs[pti[0] % len(pTs)]
                pti[0] += 1
                k.act(pT[:], psS[:, 0:512], AF.Exp, r=[bS], w=[bpT], scale=0.125)
                return (pT, bpT)

            def pv(tk):
                pT, bpT = tk
                k.mm(Ops[0:65, 0:512], V_ap, pT[:], first, last_, r=[bV, bpT], w=[bO])
            push(score, pv, after)

        fin_i = [0]

        def finalize(Ops, bO, gate_ap, bgate, first_branch, out_final=None, bout=None):
            def stage_a():
                rr, brr = rrs[fin_i[0] % 4]
                rhi, brhi = rhis[fin_i[0] % 4]
                rlo, brlo = rlos[fin_i[0] % 4]
                osb, bosb = osbs[fin_i[0] % 4]
                fin_i[0] += 1
                k.ts("dve", rr[64:65, :], Ops[64:65, 0:512], 1e-20, ALU.max, r=[bO], w=[brr])
                k.recip(rr[64:65, :], rr[64:65, :], r=[brr], w=[brr])
                k.tt("dve", rr[64:65, :], rr[64:65, :], gate_ap, ALU.mult, r=[brr, bgate], w=[brr])
                k.copy("dve", rhi[64:65, :], rr[64:65, :], r=[brr], w=[brhi])
                k.tt("dve", rlo[64:65, :], rr[64:65, :], rhi[64:65, :], ALU.subtract, r=[brr, brhi], w=[brlo])
                k.copy("act", osb[:], Ops[0:64, 0:512], r=[bO], w=[bosb])

                def stage_b():
                    psb, bpb = ringM.get()
                    k.mm(psb[0:64, 0:512], ones16[64:65, 0:64], rhi[64:65, :], True, False, r=[bones, brhi], w=[bpb])
                    k.mm(psb[0:64, 0:512], ones16[64:65, 0:64], rlo[64:65, :], False, True, r=[bones, brlo], w=[bpb])
                    if first_branch:
                        k.tt("dve", oacc[:], osb[:], psb[0:64, 0:512], ALU.mult, r=[bosb, bpb], w=[boacc])
                    else:
                        k.tt("dve", otmp[:], osb[:], psb[0:64, 0:512], ALU.mult, r=[bosb, bpb], w=[botmp])
                        if out_final is None:
                            k.tt("pool", oacc[:], oacc[:], otmp[:], ALU.add, r=[boacc, botmp], w=[boacc])
                        else:
                            k.tt("pool", out_final, oacc[:], otmp[:], ALU.add, r=[boacc, botmp], w=[bout])
                delayed.append([3, stage_b])
            return stage_a

        def epilogue(qb):
            s0 = qb * 128
            pb = qb % 2
            if "dbg_o" in sc:
                for hk in range(2):
                    k.dma("sp", sc["dbg_o"][hk, :, qb], oTb[hk][0][:].rearrange("d g q -> d (g q)"), oTb[hk][1])
            sg_t, bsgn_ = sgn[pb]
            gs_t, bgs_ = gss[pb]
            for half in range(2):
                ps, bp = ringM.get()
                for f4 in range(4):
                    fc = half * 4 + f4
                    for h in range(8):
                        k.mm(ps[:, f4 * 128:(f4 + 1) * 128], wbn[:, h, fc * 128:(fc + 1) * 128], oTb[h // 4][0][:, h % 4, :],
                             h == 0, h == 7, r=[bwbn, oTb[h // 4][1]], w=[bp])
                for f4 in range(4):
                    fc = half * 4 + f4
                    k.tt("dve", mtmp[:], ps[:, f4 * 128:(f4 + 1) * 128], sg_t[:, fc, :], ALU.mult, r=[bp, bsgn_], w=[bmtmp])
                    k.tt("pool", mrg[:, fc, :], mtmp[:], gs_t[:, fc, :], ALU.add, r=[bmtmp, bgs_], w=[bmrg])
            x_t, bx = xin[pb]
            xm_t, bxm = xm[qb % 2]
            for half in range(2):
                ps, bp = ringM.get()
                for fc in range(8):
                    k.mm(ps[:, 0:512], mrg[:, fc, :], w_out[:, fc, half * 512:(half + 1) * 512], fc == 0, fc == 7, r=[bmrg, bw_out[fc]], w=[bp])
                k.tt("dve", xm_t[:, half * 512:(half + 1) * 512], ps[:, 0:512], x_t[:, half * 512:(half + 1) * 512], ALU.add, r=[bp, bx], w=[bxm])
            k.dma("sp", sc["xmid"][s0:s0 + 128, :], xm_t[:], bxm)

        loads(0)
        for qb in range(NT):
            if qb <= 5:
                flush()
            s0 = qb * 128
            pb = qb % 2
            for hk in range(2):
                hs = slice(hk * 64, (hk + 1) * 64)
                Qt_full, bQ = QT[pb]
                Qt = Qt_full[hs]
                Q2 = Qt_full[hs].rearrange("d g q -> d (g q)")
                gr, bgr = grow[pb]
                negT4, bnegT4 = negT4s[(2 * qb + hk) % 2]
                for g in range(4):
                    ps, bp = ringM.get()
                    k.mm(ps[:, 0:NCP], Qt[:, g, :], KcT[hs, 0:NCP], True, False, r=[bQ, bKcT], w=[bp])
                    k.mm(ps[:, 0:NCP], ident[:], mgen[:, 512 - 8 * qb:512 - 8 * qb + NCP], False, True, r=[bid, bmgen], w=[bp])
                    k.act(eg[g][0][:], ps[:, 0:NCP], AF.Exp, r=[bp], w=[eg[g][1], bden4], scale=0.125, accum=den4[:, g:g + 1])
                k.ts("dve", rden4[:], den4[:], 1e-20, ALU.max, r=[bden4], w=[brden4])
                k.recip(rden4[:], rden4[:], r=[brden4], w=[brden4])
                k.ts("dve", pg[:, 1:1 + NCP], eg[0][0][:], rden4[:, 0:1], ALU.mult, r=[eg[0][1], brden4], w=[bpg])
                for g in range(1, 4):
                    k.stt(pg[:, 1:1 + NCP], eg[g][0][:], rden4[:, g:g + 1], pg[:, 1:1 + NCP], ALU.mult, ALU.add, r=[eg[g][1], brden4, bpg], w=[bpg])
                P.op("dve", lambda e: e.tensor_reduce(out=blk[:], in_=pg[:, 0:NCP].rearrange("p (j o) -> p j o", o=4), axis=AX.X, op=ALU.add), reads=[bpg], writes=[bblk])
                k.tt("dve", blk[:], blk[:], pg[:, 4:4 + 4 * NB:4], ALU.add, r=[bblk, bpg], w=[bblk])
                k.tt("dve", blk[:], blk[:], G[:, 126 - 2 * qb:126 - 2 * qb + NB], ALU.add, r=[bblk, bG], w=[bblk])
                if qb >= 1:
                    k.ts("dve", blk[:, 0:1], blk[:, 0:1], 1e4, ALU.add, r=[bblk], w=[bblk])
                P.op("dve", lambda e: e.max(out=m8[:, 0:8], in_=blk[:]), reads=[bblk], writes=[bm8])
                P.op("dve", lambda e: e.match_replace(out=blk2[:], in_to_replace=m8[:, 0:8], in_values=blk[:], imm_value=-3e38), reads=[bblk, bm8], writes=[bblk2])
                P.op("dve", lambda e: e.max(out=m8[:, 8:16], in_=blk2[:]), reads=[bblk2], writes=[bm8])
                k.ts("dve", negm[:, 0:NB], blk[:], m8[:, 15:16], ALU.is_lt, r=[bblk, bm8], w=[bnegm], s2=NEG, op1=ALU.mult)
                ps, bp = ringM.get()
                pbf = ps[:].bitcast(BF16)
                k.tr(pbf[:, 0:128], negm[:], ident[:], r=[bnegm, bid], w=[bp])
                for g in range(4):
                    k.copy("dve", negT4[:, g, :], pbf[:, 0:128], r=[bp], w=[bnegT4])
                Ops, bO = ringO.get()
                fa = finalize(Ops, bO, gr[64:65, hk, 0].rearrange("o g q -> o (g q)"), bgr, True)
                tiles = [nt for nt in range(NCT) if qb - 16 * nt >= 0]
                for idx, nt in enumerate(tiles):
                    m = qb - 16 * nt
                    sm = (mt[:, m, :], bmt) if m < 16 else None
                    attn_tile(Ops, bO, idx == 0, idx == len(tiles) - 1, KcT[hs, nt * 128:(nt + 1) * 128], bKcT,
                              Vc[:, nt, hk, :], bVc, Q2, bQ, smask=sm, after=fa if idx == len(tiles) - 1 else None)
                Ops, bO = ringO.get()
                fa = finalize(Ops, bO, gr[64:65, hk, 2].rearrange("o g q -> o (g q)"), bgr, False)
                tiles = [wt for wt in range(5) if s0 - 512 + 128 * wt >= 0]
                kw_full, bkw = KwT[pb]
                kw_t = kw_full[hs]
                vw_t, bvw = Vw[pb]
                for idx, wt in enumerate(tiles):
                    sm = (low[:], blow) if wt == 0 else ((caus[:], bcaus) if wt == 4 else None)
                    attn_tile(Ops, bO, idx == 0, idx == len(tiles) - 1, kw_t[:, wt * 128:(wt + 1) * 128], bkw,
                              vw_t[:, wt, hk, :], bvw, Q2, bQ, smask=sm, after=fa if idx == len(tiles) - 1 else None)
                if hk == 0 and qb + 1 < NT:
                    loads(qb + 1)
                Ops, bO = ringO.get()
                fa = finalize(Ops, bO, gr[64:65, hk, 1].rearrange("o g q -> o (g q)"), bgr, False,
                              out_final=oTb[hk][0][:].rearrange("d g q -> d (g q)"), bout=oTb[hk][1])
                for i in range(qb + 1):
                    sm = (caus[:], bcaus) if i == qb else None
                    attn_tile(Ops, bO, i == 0, i == qb, KsT[hs, i * 128:(i + 1) * 128], bKsT,
                              Vs[:, i, hk, :], bVs, Q2, bQ, smask=sm, emask=E[:, i, :], after=fa if i == qb else None)
                if hk == 1:
                    delayed_ep = (lambda q_=qb: (lambda: delayed.append([4, lambda: epilogue(q_)])))(qb)
                    pipe[-1] = (pipe[-1][0], pipe[-1][1], (lambda f1=pipe[-1][2], f2=delayed_ep: (f1(), f2())))
        flush()
    P.barrier()


def phase4(k, l, S, di, sc, cst, dst, last):
    P = k.P
    TT = 256
    NTT = S // TT
    NSB = TT // 128
    ident, bid = cst["ident"]
    with ExitStack() as st:
        ring = PsumRing(k, st)
        w_fin, bw_fin = load_w(k, st, di["w_fin"][l], 8, 2 * DFF, name="w_fin")
        w_fo, bw_fo = load_w(k, st, di["w_fout"][l], NFC, D, name="w_fo")
        gam, bgam = k.sb(st, [128, D], F32, "gam")
        k.dma("sp", gam[:], di["g_ffn"][l], bgam)
        cw, bcw = k.sb(st, [128, NFC, 3], F32, "cw")
        k.dma("sp", cw[:], di["f_cw"][l], bcw)
        cb, bcb = k.sb(st, [128, NFC], F32, "cb")
        k.dma("sp", cb[:], di["f_cb"][l], bcb)
        if last:
            gfin, bgfin = k.sb(st, [128, D], F32, "gfin")
            k.dma("sp", gfin[:], di["g_fin"], bgfin)
        halo, bhalo = k.sb(st, [128, NFC, 2], F32, "halo")
        k.memset("pool", halo[:], 0.0, [bhalo])
        xt = [k.sb(st, [128, NSB, D], F32, "xt") for _ in range(2)]
        junk, bjunk = k.sb(st, [128, D], BF16, "junk")
        ss, bss = k.sb(st, [128, 2 * NSB], F32, "ss")
        ms, bms = k.sb(st, [128, 2 * NSB], F32, "ms")
        sd, bsd = k.sb(st, [128, 2 * NSB], F32, "sd")
        rstd, brstd = k.sb(st, [128, 2 * NSB], F32, "rstd")
        hh = [k.sb(st, [128, D], BF16, "h") for _ in range(2)]
        hT, bhT = k.sb(st, [128, 8, TT], BF16, "hT")
        a_sb = [k.sb(st, [128, TT + 2], F32, "a_sb") for _ in range(2)]
        cv = [k.sb(st, [128, TT], F32, "cv") for _ in range(2)]
        gl = [k.sb(st, [128, TT], F32, "gl") for _ in range(2)]
        actT, bactT = k.sb(st, [128, NFC, TT], BF16, "actT")
        xo = [k.sb(st, [128, NSB, D], F32, "xo") for _ in range(2)]

        def load_tile(i):
            t, b = xt[i % 2]
            k.dma("sp", t[:], sc["xmid"][i * TT:(i + 1) * TT, :].rearrange("(s p) d -> p s d", p=128), b)

        def rms(x_ap, bx, col, g_t, bg, out_ap, bout):
            k.act(junk[:], x_ap, AF.Square, r=[bx], w=[bjunk, bss], accum=ss[:, col:col + 1])
            k.ts("dve", ms[:, col:col + 1], ss[:, col:col + 1], 1.0 / D, ALU.mult, r=[bss], w=[bms], s2=EPS, op1=ALU.add)
            k.act(sd[:, col:col + 1], ms[:, col:col + 1], AF.Sqrt, r=[bms], w=[bsd])
            k.recip(rstd[:, col:col + 1], sd[:, col:col + 1], r=[bsd], w=[brstd])
            k.stt(out_ap, x_ap, rstd[:, col:col + 1], g_t[:], ALU.mult, ALU.mult, r=[bx, brstd, bg], w=[bout])

        load_tile(0)
        for i in range(NTT):
            if i + 1 < NTT:
                load_tile(i + 1)
            x_t, bx = xt[i % 2]
            for s_ in range(NSB):
                h_t, bh = hh[s_ % 2]
                rms(x_t[:, s_, :], bx, s_, gam, bgam, h_t[:], bh)
                ps, bp = ring.get()
                pbf = ps[:].bitcast(BF16)
                for c in range(8):
                    k.tr(pbf[:, c * 128:(c + 1) * 128], h_t[:, c * 128:(c + 1) * 128], ident[:], r=[bh, bid], w=[bp])
                k.copy("act", hT[:, :, s_ * 128:(s_ + 1) * 128], pbf.rearrange("p (c t) -> p c t", c=8), r=[bp], w=[bhT])
            for fc in range(NFC):
                psa, bpa = ring.get()
                for c in range(8):
                    k.mm(psa[:, 0:TT], w_fin[:, c, fc * 128:(fc + 1) * 128], hT[:, c, :], c == 0, c == 7, r=[bw_fin[c], bhT], w=[bpa])
                psb, bpb = ring.get()
                for c in range(8):
                    k.mm(psb[:, 0:TT], w_fin[:, c, DFF + fc * 128:DFF + (fc + 1) * 128], hT[:, c, :], c == 0, c == 7, r=[bw_fin[c], bhT], w=[bpb])
                a_t, ba = a_sb[fc % 2]
                c_t, bc = cv[fc % 2]
                g_t, bg = gl[fc % 2]
                k.copy("pool", a_t[:, 0:2], halo[:, fc, :], r=[bhalo], w=[ba])
                k.copy("act", a_t[:, 2:2 + TT], psa[:, 0:TT], r=[bpa], w=[ba])
                k.copy("pool", halo[:, fc, :], a_t[:, TT:TT + 2], r=[ba], w=[bhalo])
                k.ts("dve", c_t[:], a_t[:, 2:2 + TT], cw[:, fc, 2:3], ALU.mult, r=[ba, bcw, bcb], w=[bc], s2=cb[:, fc:fc + 1], op1=ALU.add)
                k.stt(c_t[:], a_t[:, 1:1 + TT], cw[:, fc, 1:2], c_t[:], ALU.mult, ALU.add, r=[ba, bcw, bc], w=[bc])
                k.stt(c_t[:], a_t[:, 0:TT], cw[:, fc, 0:1], c_t[:], ALU.mult, ALU.add, r=[ba, bcw, bc], w=[bc])
                k.act(g_t[:], c_t[:], AF.Gelu_apprx_tanh, r=[bc], w=[bg])
                k.tt("dve", actT[:, fc, :], psb[:, 0:TT], g_t[:], ALU.mult, r=[bpb, bg], w=[bactT])
            xo_t, bxo = xo[i % 2]
            for s_ in range(NSB):
                for half in range(2):
                    ps, bp = ring.get()
                    for fc in range(NFC):
                        k.mm(ps[:, 0:512], actT[:, fc, s_ * 128:(s_ + 1) * 128], w_fo[:, fc, half * 512:(half + 1) * 512], fc == 0, fc == NFC - 1,
                             r=[bactT, bw_fo[fc]], w=[bp])
                    k.tt("dve", xo_t[:, s_, half * 512:(half + 1) * 512], ps[:, 0:512], x_t[:, s_, half * 512:(half + 1) * 512], ALU.add, r=[bp, bx], w=[bxo])
            if last:
                for s_ in range(NSB):
                    rms(xo_t[:, s_, :], bxo, NSB + s_, gfin, bgfin, xo_t[:, s_, :], bxo)
            k.dma("sp", dst[i * TT:(i + 1) * TT, :].rearrange("(s p) d -> p s d", p=128), xo_t[:], bxo)
    P.barrier()


INPUT_SHAPES = None


def build(S, L, TS=128, dbg=False, phases=("p1", "p2", "p3", "p4")):
    nc = bass.Bass("TRN2", target_bir_lowering=False)
    NT = S // 128
    di = {}

    def din(name, shape):
        di[name] = nc.dram_tensor(name, list(shape), F32, kind="ExternalInput").ap()
    din("x", [S, D])
    for nm, shp in (("g_mix", [L, 128, D]), ("g_ffn", [L, 128, D]), ("g_fin", [128, D]),
                    ("w_in", [L, D, INW]), ("w_sw", [L, D, 896]),
                    ("s_are", [L, 128, 16]), ("s_aim", [L, 128, 16]), ("s_ldt", [L, 128, 16]),
                    ("s_bre", [L, 128, 16, 128]), ("s_bim", [L, 128, 16, 128]),
                    ("s_cre", [L, 128, 16, 128]), ("s_cim", [L, 128, 16, 128]), ("s_d", [L, 128, 4]),
                    ("w_glu", [L, 512, 1024]), ("w_bssm", [L, 512, 1024]), ("w_bnsa", [L, 512, 1024]),
                    ("w_out", [L, D, D]),
                    ("c_pek", [L, 64, 32]), ("c_w1k", [L, 2048, 256]), ("c_b1k", [L, 128, 2]), ("c_w2k", [L, 256, 64]),
                    ("c_pev", [L, 64, 32]), ("c_w1v", [L, 2048, 256]), ("c_b1v", [L, 128, 2]), ("c_w2v", [L, 256, 64]),
                    ("w_fin", [L, D, 2 * DFF]), ("w_fout", [L, DFF, D]), ("f_cw", [L, 128, NFC, 3]), ("f_cb", [L, 128, NFC]),
                    ("c_cos", [128, S]), ("c_sin", [128, S]), ("c_ident", [128, 128]), ("c_tau", [128, TS]),
                    ("c_caus", [128, 128]), ("c_low", [128, 128]), ("c_mgen", [128, 1024]), ("c_mt", [128, 16, 128]),
                    ("c_g", [128, 256]), ("c_e", [128, NT, 128])):
        din(nm, shp)
    out = nc.dram_tensor("out", [S, D], F32, kind="ExternalOutput").ap()
    skind = "ExternalOutput" if dbg else "Internal"
    sc = {}

    def scr(name, shape, dt):
        sc[name] = nc.dram_tensor(name, list(shape), dt, kind=skind).ap()
    scr("qT", [512, S], BF16)
    for nm in ("kcT", "vcT", "ksT", "kwT"):
        scr(nm, [128, S], BF16)
    scr("vs", [S, 2, 65], BF16)
    scr("vw", [S, 2, 65], BF16)
    scr("sgT", [24, S], F32)
    scr("sgnT", [1024, S], BF16)
    scr("gssT", [1024, S], BF16)
    scr("xmid", [S, D], F32)
    if dbg:
        scr("dbg_o", [2, 64, NT, 512], BF16)
        scr("dbg_kc", [2, 64, S // 16], BF16)
        scr("dbg_vc", [128, S // 2048, 2, 65], BF16)
    scr("x1", [S, D], F32)
    with ExitStack() as st:
        P = Prog(nc)
        k = K(nc, P)
        cst = {}
        ident, bid = k.sb(st, [128, 128], BF16, "ident")
        k.dma("pool", ident[:], di["c_ident"], bid)
        cst["ident"] = (ident, bid)
        x_src = di["x"]
        for l in range(L):
            last = l == L - 1
            if "p1" in phases:
                phase1(k, l, S, TS, x_src, di, sc, cst)
            if "p2" in phases:
                pers = ExitStack()
                cmp_t = phase2(k, pers, l, S, di, sc, cst)
            if "p3" in phases:
                phase3(k, l, S, x_src, di, sc, cst, cmp_t)
            if "p2" in phases:
                pers.close()
                P.barrier()
            if "p4" in phases:
                phase4(k, l, S, di, sc, cst, out if last else sc["x1"], last)
            x_src = sc["x1"]
        P.barrier()
        P.emit(st)
    return nc


_NC_CACHE = {}


def kernel(**inputs):
    S, L, NCORES = 8192, 2, 8
    inp = {k_: np.asarray(v) for k_, v in inputs.items()}
    hl = host_layout(inp, L)
    hc = host_consts(S, 128)
    common = {}
    common.update(hl)
    common.update(hc)
    common = {k_: np.ascontiguousarray(v, dtype=np.float32) for k_, v in common.items()}
    if "nc" not in _NC_CACHE:
        _NC_CACHE["nc"] = build(S, L)
    nc = _NC_CACHE["nc"]
    x = np.asarray(inp["x"], dtype=np.float32)
    in_maps = []
    for b in range(NCORES):
        m = dict(common)
        m["x"] = np.ascontiguousarray(x[b])
        in_maps.append(m)
    res = run_bass_kernel_spmd(nc, in_maps, core_ids=list(range(NCORES)))
    return np.stack([np.asarray(r["out"], dtype=np.float32) for r in res.results], axis=0)
```

```python
from contextlib import ExitStack
import numpy as np
import ml_dtypes
import concourse.bass as bass
import concourse.mybir as mybir
from concourse.bass_utils import run_bass_kernel_spmd

F32 = mybir.dt.float32
BF16 = mybir.dt.bfloat16
I32 = mybir.dt.int32
ALU = mybir.AluOpType
AF = mybir.ActivationFunctionType
AX = mybir.AxisListType

D = 1024
DFF = 2816
NFC = DFF // 128
INW = 3864
EPS = 1e-6
NEG = -30000.0
TWO_PI = float(2 * np.pi)
SIN_SCALE = TWO_PI * 0.999999


class Buf:
    __slots__ = ("name", "w", "rs", "sem", "cnt", "slot", "base", "uid")

    def __init__(self, name="b"):
        self.name = name
        self.w = None
        self.rs = {}
        self.sem = None
        self.cnt = 0
        self.slot = None
        self.base = 0
        self.uid = None


class Op:
    __slots__ = ("eng", "fn", "deps", "key", "val", "signal", "sigval", "dma", "slot", "semval")


ENGS = ("pe", "act", "dve", "pool", "sp")


class Prog:
    def __init__(self, nc):
        self.nc = nc
        self.ops = {e: [] for e in ENGS}
        self.seen = {e: {} for e in ENGS}
        self.dma_bufs = []
        self.last = {}
        self.slot_base = []
        self.free_slots = []
        self.live = []
        self.uid = 0

    def _get_slot(self, buf):
        if self.free_slots:
            sl = self.free_slots.pop()
        else:
            sl = len(self.slot_base)
            self.slot_base.append(0)
        self.uid += 1
        buf.sem = True
        buf.slot = sl
        buf.base = self.slot_base[sl]
        buf.cnt = 0
        buf.uid = self.uid
        self.live.append(buf)

    def barrier(self):
        lasts = list(self.last.values())
        self._barrier_ops(lasts)
        for b in self.live:
            self.slot_base[b.slot] = b.base + b.cnt
            self.free_slots.append(b.slot)
            b.sem = None
        self.live = []
        self.last = {kk: v for kk, v in self.last.items() if not isinstance(kk, tuple)}

    def _barrier_ops(self, lasts):
        for e in ENGS:
            o = Op()
            o.eng = e
            o.fn = None
            o.deps = []
            o.signal = False
            o.sigval = None
            o.dma = None
            o.key = e
            o.val = len(self.ops[e])
            for d in lasts:
                if d.key == e:
                    continue
                if self.seen[e].get(d.key, -1) >= d.val:
                    continue
                self.seen[e][d.key] = d.val
                o.deps.append(d)
            self.ops[e].append(o)

    def _dep(self, eng, d, deps, same_ok):
        if d is None:
            return
        key = d.key
        if key == eng:
            if eng == "pe" or same_ok:
                return
        if self.seen[eng].get(key, -1) >= d.val:
            return
        self.seen[eng][key] = d.val
        deps.append(d)

    def op(self, eng, fn, reads=(), writes=(), dma=None):
        o = Op()
        o.eng = eng
        o.fn = fn
        o.deps = []
        o.signal = False
        o.sigval = None
        o.dma = dma
        writes = [b for b in writes if b is not None]
        reads = [b for b in reads if b is not None]
        o.slot = None
        o.semval = None
        if dma is not None:
            if dma.sem is None:
                self._get_slot(dma)
            dma.cnt += 1
            o.key = ("dma", dma.uid)
            o.val = dma.cnt
            o.slot = dma.slot
            o.semval = 16 * (dma.base + dma.cnt)
            if dma not in writes:
                writes.append(dma)
            reads = [b for b in reads if b is not dma]
        else:
            o.key = eng
            o.val = len(self.ops[eng])
        for b in reads:
            self._dep(eng, b.w, o.deps, False)
        for b in writes:
            self._dep(eng, b.w, o.deps, True)
            for r in b.rs.values():
                self._dep(eng, r, o.deps, True)
        for b in reads:
            b.rs[o.key] = o
        for b in writes:
            b.w = o
            b.rs = {}
        self.ops[eng].append(o)
        self.last[o.key] = o
        return o

    def emit(self, stack):
        nc = self.nc
        for e in ENGS:
            for o in self.ops[e]:
                for d in o.deps:
                    if d.dma is None:
                        d.signal = True
        for e in ENGS:
            c = 0
            for o in self.ops[e]:
                if o.dma is None and o.signal:
                    c += 1
                    o.sigval = c
        esem = {}
        for e in ("pe", "act", "dve", "pool"):
            esem[e] = stack.enter_context(nc.semaphore("s_" + e))
        dsem = [stack.enter_context(nc.semaphore("d%d" % i)) for i in range(len(self.slot_base))]
        block = stack.enter_context(nc.Block())
        prog = self

        def run(name, eng):
            for o in prog.ops[name]:
                for d in o.deps:
                    if d.dma is not None:
                        eng.wait_ge(dsem[d.slot], d.semval)
                    else:
                        eng.wait_ge(esem[d.key], d.sigval)
                if o.fn is None:
                    continue
                ins = o.fn(eng)
                if o.dma is not None:
                    ins.then_inc(dsem[o.slot], 16)
                elif o.signal:
                    ins.then_inc(esem[name], 1)

        @block.sync
        def _(eng):
            run("sp", eng)

        @block.scalar
        def _(eng):
            run("act", eng)

        @block.vector
        def _(eng):
            run("dve", eng)

        @block.gpsimd
        def _(eng):
            run("pool", eng)

        @block.tensor
        def _(eng):
            run("pe", eng)


class K:
    def __init__(self, nc, P):
        self.nc = nc
        self.P = P
        self.n = 0

    def name(self, s):
        self.n += 1
        return "%s_%d" % (s, self.n)

    def sb(self, st, shape, dt=F32, name="t"):
        t = st.enter_context(self.nc.sbuf_tensor(self.name(name), list(shape), dt))
        return t, Buf(name)

    def dma(self, eng, out, in_, buf, reads=(), writes=()):
        self.P.op(eng, lambda e: e.dma_start(out=out, in_=in_), reads=reads, writes=writes, dma=buf)

    def mm(self, out, lhsT, rhs, start, stop, r, w):
        self.P.op("pe", lambda e: e.matmul(out, lhsT=lhsT, rhs=rhs, start=start, stop=stop), reads=r, writes=w)

    def tr(self, out, in_, ident, r, w):
        self.P.op("pe", lambda e: e.transpose(out=out, in_=in_, identity=ident), reads=r, writes=w)

    def act(self, out, in_, func, r, w, bias=None, scale=None, accum=None):
        kw = {}
        if bias is not None:
            kw["bias"] = bias
        if scale is not None:
            kw["scale"] = scale
        if accum is not None:
            kw["accum_out"] = accum
        self.P.op("act", lambda e: e.activation(out=out, in_=in_, func=func, **kw), reads=r, writes=w)

    def tt(self, eng, out, in0, in1, op, r, w):
        self.P.op(eng, lambda e: e.tensor_tensor(out=out, in0=in0, in1=in1, op=op), reads=r, writes=w)

    def ts(self, eng, out, in0, s1, op0, r, w, s2=None, op1=None):
        if op1 is None:
            self.P.op(eng, lambda e: e.tensor_scalar(out=out, in0=in0, scalar1=s1, scalar2=None, op0=op0), reads=r, writes=w)
        else:
            self.P.op(eng, lambda e: e.tensor_scalar(out=out, in0=in0, scalar1=s1, scalar2=s2, op0=op0, op1=op1), reads=r, writes=w)

    def stt(self, out, in0, scalar, in1, op0, op1, r, w):
        self.P.op("dve", lambda e: e.scalar_tensor_tensor(out=out, in0=in0, scalar=scalar, in1=in1, op0=op0, op1=op1), reads=r, writes=w)

    def copy(self, eng, out, in_, r, w):
        if eng == "act":
            self.P.op("act", lambda e: e.activation(out=out, in_=in_, func=AF.Copy), reads=r, writes=w)
        else:
            self.P.op(eng, lambda e: e.tensor_copy(out=out, in_=in_), reads=r, writes=w)

    def memset(self, eng, ap, val, w):
        self.P.op(eng, lambda e: e.memset(ap, val), writes=w)

    def scan(self, out, d0, d1, init, r, w):
        self.P.op("dve", lambda e: e.tensor_tensor_scan(out=out, data0=d0, data1=d1, initial=init, op0=ALU.mult, op1=ALU.add), reads=r, writes=w)

    def recip(self, out, in_, r, w):
        self.P.op("dve", lambda e: e.reciprocal(out=out, in_=in_), reads=r, writes=w)


class PsumRing:
    def __init__(self, k, st, n=8):
        self.banks = []
        for i in range(n):
            t = st.enter_context(k.nc.psum_tensor(k.name("ps"), [128, 512], F32))
            self.banks.append((t, Buf("ps%d" % i)))
        self.i = 0

    def get(self):
        t, b = self.banks[self.i % len(self.banks)]
        self.i += 1
        return t, b


def _swap_halves(w):
    sh = w.shape
    w4 = w.reshape(sh[:-1] + (sh[-1] // 64, 2, 32))
    return np.ascontiguousarray(w4[..., ::-1, :]).reshape(sh)


def host_consts(S, TS):
    c = {}
    inv = (10000.0 ** (-np.arange(0, 64, 2, dtype=np.float32) / np.float32(64))).astype(np.float32)
    ang = (np.arange(S, dtype=np.float32)[:, None] * inv[None, :]).astype(np.float32)
    cs, sn = np.cos(ang).astype(np.float32), np.sin(ang).astype(np.float32)
    cosT = np.concatenate([cs.T, cs.T], 0)
    sinT = np.concatenate([-sn.T, sn.T], 0)
    c["c_cos"] = np.ascontiguousarray(np.concatenate([cosT, cosT], 0))
    c["c_sin"] = np.ascontiguousarray(np.concatenate([sinT, sinT], 0))
    c["c_ident"] = np.eye(128, dtype=np.float32)
    c["c_tau"] = np.ascontiguousarray(np.broadcast_to(np.arange(TS, dtype=np.float32)[None, :], (128, TS)))
    k = np.arange(128)[:, None]
    q = np.arange(128)[None, :]
    c["c_caus"] = np.where(k <= q, 0.0, NEG).astype(np.float32)
    c["c_low"] = np.where(k > q, 0.0, NEG).astype(np.float32)
    cc = np.arange(1024)[None, :]
    qi = np.arange(128)[:, None]
    c["c_mgen"] = np.where(16 * (cc - 512) + 31 <= qi, 0.0, NEG).astype(np.float32)
    m = np.arange(16)[None, :, None]
    ni = np.arange(128)[:, None, None]
    qq = np.arange(128)[None, None, :]
    c["c_mt"] = np.where(16 * ni + 31 <= 128 * m + qq, 0.0, NEG).astype(np.float32)
    rel = np.arange(256)[None, :] - 126
    cur = (np.arange(128)[:, None] >= 64).astype(np.int64)
    g = np.where(rel > cur, -1e30, 0.0) + np.where((rel == cur) | (rel == cur - 1), 1e4, 0.0)
    c["c_g"] = g.astype(np.float32)
    NT = S // 128
    j = np.arange(128)[:, None, None]
    i = np.arange(NT)[None, :, None]
    kk = np.arange(128)[None, None, :]
    c["c_e"] = (j == 2 * i + (kk >= 64)).astype(np.float32)
    r_ = np.arange(64)[:, None]
    cidx = np.arange(S)[None, :]
    c["c_ind"] = (((cidx // 64) % 64) == r_).astype(np.float32)
    return c


def host_layout(inp, L):
    o = {}
    f = np.float32
    o["g_mix"] = np.ascontiguousarray(np.broadcast_to(inp["norm_mix"][:, None, :], (L, 128, D))).astype(f)
    o["g_ffn"] = np.ascontiguousarray(np.broadcast_to(inp["norm_ffn"][:, None, :], (L, 128, D))).astype(f)
    o["g_fin"] = np.ascontiguousarray(np.broadcast_to(inp["norm_final"][None, :], (128, D))).astype(f)
    w_in = inp["w_in"]
    o["w_in"] = w_in
    sw = np.concatenate([_swap_halves(w_in[:, :, 512:1024]), _swap_halves(w_in[:, :, 1024:1152]),
                         _swap_halves(w_in[:, :, 1280:1408]), _swap_halves(w_in[:, :, 1536:1664])], axis=-1)
    o["w_sw"] = np.ascontiguousarray(sw)

    def pair(a):
        return np.ascontiguousarray(a.reshape(L, 16, 2, 64).transpose(0, 2, 3, 1).reshape(L, 128, 16))
    o["s_are"] = pair(inp["ssm_a_re"])
    o["s_aim"] = pair(inp["ssm_a_im"])
    o["s_ldt"] = pair(np.broadcast_to(inp["ssm_log_dt"][:, :, None], (L, 32, 64)))
    for nm, src in (("s_bre", "ssm_b_re"), ("s_bim", "ssm_b_im")):
        b = inp[src].reshape(L, 16, 2, 64, 16)
        pad = np.zeros((L, 8, 16, 16, 2, 64), f)
        for j in range(16):
            for gl in range(2):
                pad[:, 2 * (j % 4) + gl, :, j, gl, :] = b[:, j, gl].transpose(0, 2, 1)
        o[nm] = pad.reshape(L, 128, 16, 128)
    for nm, src in (("s_cre", "ssm_c_re"), ("s_cim", "ssm_c_im")):
        cmat = inp[src].reshape(L, 16, 2, 16, 64)
        pad = np.zeros((L, 2, 64, 16, 8, 16), f)
        for j in range(16):
            for gl in range(2):
                pad[:, gl, :, j, 2 * (j % 4) + gl, :] = cmat[:, j, gl].transpose(0, 2, 1)
        o[nm] = pad.reshape(L, 128, 16, 128)
    o["s_d"] = np.ascontiguousarray(inp["ssm_d"].reshape(L, 4, 128).transpose(0, 2, 1))
    o["w_glu"] = inp["ssm_w_glu"]
    o["w_bssm"] = inp["w_branch_ssm"]
    o["w_bnsa"] = inp["w_branch_nsa"]
    o["w_out"] = inp["w_out"]
    for t in ("k", "v"):
        o["c_pe" + t] = np.ascontiguousarray(inp["cmp_pe_" + t].transpose(0, 2, 1))
        o["c_w1" + t] = inp["cmp_w1_" + t]
        o["c_b1" + t] = np.ascontiguousarray(inp["cmp_b1_" + t].reshape(L, 2, 128).transpose(0, 2, 1))
        o["c_w2" + t] = inp["cmp_w2_" + t]
    o["w_fin"] = inp["w_ffn_in"]
    o["w_fout"] = inp["w_ffn_out"]
    o["f_cw"] = np.ascontiguousarray(inp["ffn_conv_w"].reshape(L, 3, NFC, 128).transpose(0, 3, 2, 1))
    o["f_cb"] = np.ascontiguousarray(inp["ffn_conv_b"].reshape(L, NFC, 128).transpose(0, 2, 1))
    return o


def load_w(k, st, src2d, nch, ncols, prow=128, eng="pool", name="w"):
    t, _ = k.sb(st, [prow, nch, ncols], BF16, name)
    bufs = []
    for c in range(nch):
        b = Buf(name)
        k.dma(eng, t[:, c, :], src2d[c * prow:(c + 1) * prow, :], b)
        bufs.append(b)
    return t, bufs


def sincos(k, st, arg, n, out_sin=None, out_cos=None, rb=(), wsin=None, wcos=None):
    for (dst, off, wb) in ((out_sin, 0.0, wsin), (out_cos, 0.25, wcos)):
        if dst is None:
            continue
        a2, ba2 = k.sb(st, [128, n], F32, "sc_a")
        ti, bti = k.sb(st, [128, n], I32, "sc_i")
        tf, btf = k.sb(st, [128, n], F32, "sc_f")
        k.ts("dve", a2[:], arg, off, ALU.add, r=list(rb), w=[ba2])
        k.copy("dve", ti[:], a2[:], r=[ba2], w=[bti])
        k.copy("dve", tf[:], ti[:], r=[bti], w=[btf])
        k.tt("dve", a2[:], a2[:], tf[:], ALU.subtract, r=[ba2, btf], w=[ba2])
        k.act(dst, a2[:], AF.Sin, r=[ba2], w=[wb], scale=SIN_SCALE)


def phase1(k, l, S, TS, x_src, di, sc, cst):
    P = k.P
    TT = TS
    NTT = S // TT
    with ExitStack() as st:
        ring = PsumRing(k, st)
        ident, bid = cst["ident"]
        gam, bgam = k.sb(st, [128, D], F32, "gam")
        k.dma("sp", gam[:], di["g_mix"][l], bgam)
        dvec, bdvec = k.sb(st, [128, 4], F32, "dvec")
        k.dma("sp", dvec[:], di["s_d"][l], bdvec)
        RFre, bRFre = k.sb(st, [128, 16, TS], F32, "RFre")
        RFim, bRFim = k.sb(st, [128, 16, TS], F32, "RFim")
        COSb, bCOSb = k.sb(st, [128, 16, TS], BF16, "COSb")
        SINb, bSINb = k.sb(st, [128, 16, TS], BF16, "SINb")
        NSINb, bNSINb = k.sb(st, [128, 16, TS], BF16, "NSINb")
        dec, bdec = k.sb(st, [128, 16], F32, "dec")
        cT, bcT = k.sb(st, [128, 16], F32, "cT")
        sT, bsT = k.sb(st, [128, 16], F32, "sT")
        nsT, bnsT = k.sb(st, [128, 16], F32, "nsT")
        with ExitStack() as s2:
            are, bare = k.sb(s2, [128, 16], F32, "are")
            aim, baim = k.sb(s2, [128, 16], F32, "aim")
            ldt, bldt = k.sb(s2, [128, 16], F32, "ldt")
            tau, btau = k.sb(s2, [128, TS], F32, "tau")
            k.dma("sp", are[:], di["s_are"][l], bare)
            k.dma("sp", aim[:], di["s_aim"][l], baim)
            k.dma("sp", ldt[:], di["s_ldt"][l], bldt)
            k.dma("sp", tau[:], di["c_tau"], btau)
            dt_, bdt = k.sb(s2, [128, 16], F32, "dt")
            k.act(dt_[:], ldt[:], AF.Exp, r=[bldt], w=[bdt])
            rho, brho = k.sb(s2, [128, 16], F32, "rho")
            thn, bthn = k.sb(s2, [128, 16], F32, "thn")
            k.tt("dve", rho[:], are[:], dt_[:], ALU.mult, r=[bare, bdt], w=[brho])
            k.tt("dve", thn[:], aim[:], dt_[:], ALU.mult, r=[baim, bdt], w=[bthn])
            k.ts("dve", thn[:], thn[:], 1.0 / TWO_PI, ALU.mult, r=[bthn], w=[bthn])
            k.act(dec[:], rho[:], AF.Exp, r=[brho], w=[bdec])
            s1, bs1 = k.sb(s2, [128, 16], F32, "s1")
            c1, bc1 = k.sb(s2, [128, 16], F32, "c1")
            sincos(k, s2, thn[:], 16, s1[:], c1[:], rb=[bthn], wsin=bs1, wcos=bc1)
            abre, babre = k.sb(s2, [128, 16], F32, "abre")
            abim, babim = k.sb(s2, [128, 16], F32, "abim")
            k.tt("dve", abre[:], dec[:], c1[:], ALU.mult, r=[bdec, bc1], w=[babre])
            k.ts("dve", abre[:], abre[:], -1.0, ALU.add, r=[babre], w=[babre])
            k.tt("dve", abim[:], dec[:], s1[:], ALU.mult, r=[bdec, bs1], w=[babim])
            den, bden = k.sb(s2, [128, 16], F32, "den")
            t0, bt0 = k.sb(s2, [128, 16], F32, "t0")
            k.tt("dve", den[:], are[:], are[:], ALU.mult, r=[bare], w=[bden])
            k.tt("dve", t0[:], aim[:], aim[:], ALU.mult, r=[baim], w=[bt0])
            k.tt("dve", den[:], den[:], t0[:], ALU.add, r=[bden, bt0], w=[bden])
            k.recip(den[:], den[:], r=[bden], w=[bden])
            fre, bfre = k.sb(s2, [128, 16], F32, "fre")
            fim, bfim = k.sb(s2, [128, 16], F32, "fim")
            t1, bt1 = k.sb(s2, [128, 16], F32, "t1")
            k.tt("dve", fre[:], abre[:], are[:], ALU.mult, r=[babre, bare], w=[bfre])
            k.tt("dve", t1[:], abim[:], aim[:], ALU.mult, r=[babim, baim], w=[bt1])
            k.tt("dve", fre[:], fre[:], t1[:], ALU.add, r=[bfre, bt1], w=[bfre])
            k.tt("dve", fre[:], fre[:], den[:], ALU.mult, r=[bfre, bden], w=[bfre])
            k.tt("dve", fim[:], abim[:], are[:], ALU.mult, r=[babim, bare], w=[bfim])
            k.tt("dve", t1[:], abre[:], aim[:], ALU.mult, r=[babre, baim], w=[bt1])
            k.tt("dve", fim[:], fim[:], t1[:], ALU.subtract, r=[bfim, bt1], w=[bfim])
            k.tt("dve", fim[:], fim[:], den[:], ALU.mult, r=[bfim, bden], w=[bfim])
            aT, baT = k.sb(s2, [128, 16], F32, "aT")
            k.ts("dve", aT[:], thn[:], float(TS), ALU.mult, r=[bthn], w=[baT])
            sincos(k, s2, aT[:], 16, sT[:], cT[:], rb=[baT], wsin=bsT, wcos=bcT)
            k.ts("dve", nsT[:], sT[:], -1.0, ALU.mult, r=[bsT], w=[bnsT])
            ANG, bANG = k.sb(s2, [128, 16, TS], F32, "ANG")
            SINf, bSINf = k.sb(s2, [128, 16 * TS], F32, "SINf")
            COSf, bCOSf = k.sb(s2, [128, 16 * TS], F32, "COSf")
            for j in range(16):
                k.ts("dve", ANG[:, j, :], tau[:], thn[:, j:j + 1], ALU.mult, r=[btau, bthn], w=[bANG])
            sincos(k, s2, ANG[:].rearrange("p j t -> p (j t)"), 16 * TS, SINf[:], COSf[:], rb=[bANG], wsin=bSINf, wcos=bCOSf)
            SIN3 = SINf[:].rearrange("p (j t) -> p j t", j=16)
            COS3 = COSf[:].rearrange("p (j t) -> p j t", j=16)
            tmp, btmp = k.sb(s2, [128, TS], F32, "tmp")
            for j in range(16):
                k.ts("dve", tmp[:], SIN3[:, j, :], fim[:, j:j + 1], ALU.mult, r=[bSINf, bfim], w=[btmp])
                k.stt(RFre[:, j, :], COS3[:, j, :], fre[:, j:j + 1], tmp[:], ALU.mult, ALU.add, r=[bCOSf, bfre, btmp], w=[bRFre])
                k.ts("dve", tmp[:], SIN3[:, j, :], fre[:, j:j + 1], ALU.mult, r=[bSINf, bfre], w=[btmp])
                k.stt(RFim[:, j, :], COS3[:, j, :], fim[:, j:j + 1], tmp[:], ALU.mult, ALU.subtract, r=[bCOSf, bfim, btmp], w=[bRFim])
            k.copy("dve", COSb[:].rearrange("p j t -> p (j t)"), COSf[:], r=[bCOSf], w=[bCOSb])
            k.copy("dve", SINb[:].rearrange("p j t -> p (j t)"), SINf[:], r=[bSINf], w=[bSINb])
            k.ts("dve", NSINb[:].rearrange("p j t -> p (j t)"), SINf[:], -1.0, ALU.mult, r=[bSINf], w=[bNSINb])
        P.barrier()
        w_in, bw_in = load_w(k, st, di["w_in"][l], 8, INW, name="w_in")
        w_sw, bw_sw = load_w(k, st, di["w_sw"][l], 8, 896, name="w_sw")
        w_glu, bw_glu = load_w(k, st, di["w_glu"][l], 4, 1024, name="w_glu")
        w_bs, bw_bs = load_w(k, st, di["w_bssm"][l], 4, 1024, name="w_bs")
        bre, bbre = load_w(k, st, di["s_bre"][l].rearrange("p j m -> p (j m)"), 1, 2048, name="bre")
        bim, bbim = load_w(k, st, di["s_bim"][l].rearrange("p j m -> p (j m)"), 1, 2048, name="bim")
        cre, bcre = load_w(k, st, di["s_cre"][l].rearrange("p j m -> p (j m)"), 1, 2048, name="cre")
        cim, bcim = load_w(k, st, di["s_cim"][l].rearrange("p j m -> p (j m)"), 1, 2048, name="cim")
        xt = [k.sb(st, [128, TT // 128, D], F32, "xt") for _ in range(2)]
        cosr = [k.sb(st, [128, TT], F32, "cosr") for _ in range(2)]
        sinr = [k.sb(st, [128, TT], F32, "sinr") for _ in range(2)]
        NSB = TT // 128
        junk, bjunk = k.sb(st, [128, D], BF16, "junk")
        ss, bss = k.sb(st, [128, NSB], F32, "ss")
        ms, bms = k.sb(st, [128, NSB], F32, "ms")
        sd, bsd = k.sb(st, [128, NSB], F32, "sd")
        rstd, brstd = k.sb(st, [128, NSB], F32, "rstd")
        hh = [k.sb(st, [128, D], BF16, "h") for _ in range(2)]
        hT, bhT = k.sb(st, [128, 8, TT], BF16, "hT")
        uT, buT = k.sb(st, [128, 4, TT], BF16, "uT")
        qTs, bqTs = k.sb(st, [128, 4, TT], BF16, "qTs")
        kvs = {nm: k.sb(st, [128, TT], BF16, nm) for nm in ("kcT", "vcT", "ksT", "kwT")}
        sgs, bsgs = k.sb(st, [128, 8, TT], BF16, "sgs")
        sgn, bsgn = k.sb(st, [128, 8, TT], BF16, "sgn")
        sg, bsg = k.sb(st, [24, TT], F32, "sg")
        vsel, bvsel = k.sb(st, [128, NSB, 2, 65], BF16, "vsel")
        vwin, bvwin = k.sb(st, [128, NSB, 2, 65], BF16, "vwin")
        k.memset("pool", vsel[:], 1.0, [bvsel])
        k.memset("pool", vwin[:], 1.0, [bvwin])
        tmps = [k.sb(st, [128, TT], F32, "tmp") for _ in range(6)]
        tmpi = [0]

        def gettmp():
            t = tmps[tmpi[0] % len(tmps)]
            tmpi[0] += 1
            return t
        bsc = [k.sb(st, [128, TT], F32, "bsc") for _ in range(4)]
        wsc = [k.sb(st, [128, TT], F32, "wsc") for _ in range(4)]
        xre, bxre = k.sb(st, [128, 16, TT], BF16, "xre")
        nxim, bnxim = k.sb(st, [128, 16, TT], BF16, "nxim")
        car, bcar = k.sb(st, [128, 2, 16], F32, "car")
        k.memset("dve", car[:], 0.0, [bcar])
        ctmp, bctmp = k.sb(st, [128, 2], F32, "ctmp")
        ypre, bypre = k.sb(st, [128, TT], F32, "ypre")
        yT, byT = k.sb(st, [128, 4, TT], BF16, "yT")
        sgz, bsgz = k.sb(st, [128, TT], F32, "sgz")
        zzT, bzzT = k.sb(st, [128, 4, TT], BF16, "zzT")
        gss, bgss = k.sb(st, [128, 8, TT], BF16, "gss")

        def load_tile(i):
            t, b = xt[i % 2]
            k.dma("sp", t[:], x_src[i * TT:(i + 1) * TT, :].rearrange("(s p) d -> p s d", p=128), b)
            k.dma("sp", cosr[i % 2][0][:], di["c_cos"][:, i * TT:(i + 1) * TT], cosr[i % 2][1])
            k.dma("sp", sinr[i % 2][0][:], di["c_sin"][:, i * TT:(i + 1) * TT], sinr[i % 2][1])

        def proj(wt, wb, col0, M=128):
            ps, bp = ring.get()
            for c in range(8):
                k.mm(ps[0:M, 0:TT], wt[:, c, col0:col0 + M], hT[:, c, :], c == 0, c == 7, r=[wb[c], bhT], w=[bp])
            return ps, bp

        load_tile(0)
        for i in range(NTT):
            if i + 1 < NTT:
                load_tile(i + 1)
            x_t, bx = xt[i % 2]
            cos_t, bcos = cosr[i % 2]
            sin_t, bsin = sinr[i % 2]
            tok = slice(i * TT, (i + 1) * TT)
            for s_ in range(NSB):
                h_t, bh = hh[s_ % 2]
                k.act(junk[:], x_t[:, s_, :], AF.Square, r=[bx], w=[bjunk, bss], accum=ss[:, s_:s_ + 1])
                k.ts("dve", ms[:, s_:s_ + 1], ss[:, s_:s_ + 1], 1.0 / D, ALU.mult, r=[bss], w=[bms], s2=EPS, op1=ALU.add)
                k.act(sd[:, s_:s_ + 1], ms[:, s_:s_ + 1], AF.Sqrt, r=[bms], w=[bsd])
                k.recip(rstd[:, s_:s_ + 1], sd[:, s_:s_ + 1], r=[bsd], w=[brstd])
                k.stt(h_t[:], x_t[:, s_, :], rstd[:, s_:s_ + 1], gam[:], ALU.mult, ALU.mult, r=[bx, brstd, bgam], w=[bh])
                ps, bp = ring.get()
                pbf = ps[:].bitcast(BF16)
                for c in range(8):
                    k.tr(pbf[:, c * 128:(c + 1) * 128], h_t[:, c * 128:(c + 1) * 128], ident[:], r=[bh, bid], w=[bp])
                k.copy("act", hT[:, :, s_ * 128:(s_ + 1) * 128], pbf.rearrange("p (c t) -> p c t", c=8), r=[bp], w=[bhT])
            for c4 in range(4):
                ps, bp = proj(w_in, bw_in, c4 * 128)
                k.copy("act", uT[:, c4, :], ps[:, 0:TT], r=[bp], w=[buT])
            def rope(col, swcol, dst, bdst):
                psA, bA = proj(w_in, bw_in, col)
                psB, bB = proj(w_sw, bw_sw, swcol)
                t1_, bt1_ = gettmp()
                t2_, bt2_ = gettmp()
                k.tt("dve", t1_[:], psA[:, 0:TT], cos_t[:], ALU.mult, r=[bA, bcos], w=[bt1_])
                k.tt("dve", t2_[:], psB[:, 0:TT], sin_t[:], ALU.mult, r=[bB, bsin], w=[bt2_])
                k.tt("pool", dst, t1_[:], t2_[:], ALU.add, r=[bt1_, bt2_], w=[bdst])
            for c in range(4):
                rope(512 + c * 128, c * 128, qTs[:, c, :], bqTs)
            rope(1024, 512, kvs["kcT"][0][:], kvs["kcT"][1])
            rope(1280, 640, kvs["ksT"][0][:], kvs["ksT"][1])
            rope(1536, 768, kvs["kwT"][0][:], kvs["kwT"][1])
            ps, bp = proj(w_in, bw_in, 1152)
            k.copy("act", kvs["vcT"][0][:], ps[:, 0:TT], r=[bp], w=[kvs["vcT"][1]])
            for c in range(8):
                ps, bp = proj(w_in, bw_in, 1816 + c * 128)
                k.act(sgs[:, c, :], ps[:, 0:TT], AF.Sigmoid, r=[bp], w=[bsgs])
            for c in range(8):
                ps, bp = proj(w_in, bw_in, 2840 + c * 128)
                k.act(sgn[:, c, :], ps[:, 0:TT], AF.Sigmoid, r=[bp], w=[bsgn])
            ps, bp = proj(w_in, bw_in, 1792, M=24)
            k.act(sg[:], ps[0:24, 0:TT], AF.Sigmoid, r=[bp], w=[bsg])
            for s_ in range(NSB):
                for (col, vt, bv) in ((1408, vsel, bvsel), (1664, vwin, bvwin)):
                    ps, bp = ring.get()
                    for c in range(8):
                        k.mm(ps[:, 0:128], hT[:, c, s_ * 128:(s_ + 1) * 128], w_in[:, c, col:col + 128], c == 0, c == 7, r=[bw_in[c], bhT], w=[bp])
                    k.copy("act", vt[:, s_, :, 0:64], ps[:, 0:128].rearrange("p (h d) -> p h d", h=2), r=[bp], w=[bv])
            k.dma("sp", sc["qT"].rearrange("(c p) s -> p c s", p=128)[:, :, tok], qTs[:], bqTs)
            for nm in ("kcT", "vcT", "ksT", "kwT"):
                k.dma("sp", sc[nm][:, tok], kvs[nm][0][:], kvs[nm][1])
            k.dma("sp", sc["sgnT"].rearrange("(c p) s -> p c s", p=128)[:, :, tok], sgn[:], bsgn)
            k.dma("sp", sc["sgT"][:, tok], sg[:], bsg)
            k.dma("sp", sc["vs"][tok].rearrange("(s p) h c -> p s h c", p=128), vsel[:], bvsel)
            k.dma("sp", sc["vw"][tok].rearrange("(s p) h c -> p s h c", p=128), vwin[:], bvwin)
            for j in range(16):
                c4 = j // 4
                psr, bpr = ring.get()
                psi, bpi = ring.get()
                k.mm(psr[:, 0:TT], bre[:, 0, j * 128:(j + 1) * 128], uT[:, c4, :], True, True, r=[bbre[0], buT], w=[bpr])
                k.mm(psi[:, 0:TT], bim[:, 0, j * 128:(j + 1) * 128], uT[:, c4, :], True, True, r=[bbim[0], buT], w=[bpi])
                b_re, bb_re = bsc[(2 * j) % 4]
                b_im, bb_im = bsc[(2 * j + 1) % 4]
                w_re, bw_re = wsc[(2 * j) % 4]
                w_im, bw_im = wsc[(2 * j + 1) % 4]
                t1_, bt1_ = gettmp()
                t2_, bt2_ = gettmp()
                k.tt("dve", t1_[:], psr[:, 0:TT], RFre[:, j, :], ALU.mult, r=[bpr, bRFre], w=[bt1_])
                k.tt("dve", t2_[:], psi[:, 0:TT], RFim[:, j, :], ALU.mult, r=[bpi, bRFim], w=[bt2_])
                k.tt("pool", b_re[:], t1_[:], t2_[:], ALU.subtract, r=[bt1_, bt2_], w=[bb_re])
                t3_, bt3_ = gettmp()
                t4_, bt4_ = gettmp()
                k.tt("dve", t3_[:], psi[:, 0:TT], RFre[:, j, :], ALU.mult, r=[bpi, bRFre], w=[bt3_])
                k.tt("dve", t4_[:], psr[:, 0:TT], RFim[:, j, :], ALU.mult, r=[bpr, bRFim], w=[bt4_])
                k.tt("pool", b_im[:], t3_[:], t4_[:], ALU.add, r=[bt3_, bt4_], w=[bb_im])
                dj = dec[:, j:j + 1].to_broadcast([128, TT])
                k.scan(w_re[:], dj, b_re[:], car[:, 0, j:j + 1], r=[bdec, bb_re, bcar], w=[bw_re])
                k.scan(w_im[:], dj, b_im[:], car[:, 1, j:j + 1], r=[bdec, bb_im, bcar], w=[bw_im])
                k.ts("dve", ctmp[:, 0:1], w_re[:, TT - 1:TT], cT[:, j:j + 1], ALU.mult, r=[bw_re, bcT], w=[bctmp])
                k.ts("dve", ctmp[:, 1:2], w_im[:, TT - 1:TT], cT[:, j:j + 1], ALU.mult, r=[bw_im, bcT], w=[bctmp])
                k.stt(car[:, 0, j:j + 1], w_im[:, TT - 1:TT], nsT[:, j:j + 1], ctmp[:, 0:1], ALU.mult, ALU.add, r=[bw_im, bnsT, bctmp], w=[bcar])
                k.stt(car[:, 1, j:j + 1], w_re[:, TT - 1:TT], sT[:, j:j + 1], ctmp[:, 1:2], ALU.mult, ALU.add, r=[bw_re, bsT, bctmp], w=[bcar])
                t5_, bt5_ = gettmp()
                t6_, bt6_ = gettmp()
                k.tt("dve", t5_[:], w_re[:], COSb[:, j, :], ALU.mult, r=[bw_re, bCOSb], w=[bt5_])
                k.tt("dve", t6_[:], w_im[:], SINb[:, j, :], ALU.mult, r=[bw_im, bSINb], w=[bt6_])
                k.tt("pool", xre[:, j, :], t5_[:], t6_[:], ALU.subtract, r=[bt5_, bt6_], w=[bxre])
                t7_, bt7_ = gettmp()
                t8_, bt8_ = gettmp()
                k.tt("dve", t7_[:], w_re[:], NSINb[:, j, :], ALU.mult, r=[bw_re, bNSINb], w=[bt7_])
                k.tt("dve", t8_[:], w_im[:], COSb[:, j, :], ALU.mult, r=[bw_im, bCOSb], w=[bt8_])
                k.tt("pool", nxim[:, j, :], t7_[:], t8_[:], ALU.subtract, r=[bt7_, bt8_], w=[bnxim])
            for c4 in range(4):
                ps, bp = ring.get()
                for jj in range(4):
                    j = 4 * c4 + jj
                    k.mm(ps[:, 0:TT], cre[:, 0, j * 128:(j + 1) * 128], xre[:, j, :], jj == 0, False, r=[bcre[0], bxre], w=[bp])
                    k.mm(ps[:, 0:TT], cim[:, 0, j * 128:(j + 1) * 128], nxim[:, j, :], False, jj == 3, r=[bcim[0], bnxim], w=[bp])
                k.stt(ypre[:], uT[:, c4, :], dvec[:, c4:c4 + 1], ps[:, 0:TT], ALU.mult, ALU.add, r=[buT, bdvec, bp], w=[bypre])
                k.act(yT[:, c4, :], ypre[:], AF.Gelu_apprx_tanh, r=[bypre], w=[byT])
            for kk in range(4):
                psg, bpg = ring.get()
                for c4 in range(4):
                    k.mm(psg[:, 0:TT], w_glu[:, c4, (4 + kk) * 128:(5 + kk) * 128], yT[:, c4, :], c4 == 0, c4 == 3, r=[bw_glu[c4], byT], w=[bpg])
                k.act(sgz[:], psg[:, 0:TT], AF.Sigmoid, r=[bpg], w=[bsgz])
                psv, bpv = ring.get()
                for c4 in range(4):
                    k.mm(psv[:, 0:TT], w_glu[:, c4, kk * 128:(kk + 1) * 128], yT[:, c4, :], c4 == 0, c4 == 3, r=[bw_glu[c4], byT], w=[bpv])
                k.tt("dve", zzT[:, kk, :], psv[:, 0:TT], sgz[:], ALU.mult, r=[bpv, bsgz], w=[bzzT])
            for fc in range(8):
                ps, bp = ring.get()
                for kk in range(4):
                    k.mm(ps[:, 0:TT], w_bs[:, kk, fc * 128:(fc + 1) * 128], zzT[:, kk, :], kk == 0, kk == 3, r=[bw_bs[kk], bzzT], w=[bp])
                k.tt("dve", gss[:, fc, :], ps[:, 0:TT], sgs[:, fc, :], ALU.mult, r=[bp, bsgs], w=[bgss])
            k.dma("sp", sc["gssT"].rearrange("(c p) s -> p c s", p=128)[:, :, tok], gss[:], bgss)
    P.barrier()


def phase2(k, pers, l, S, di, sc, cst):
    P = k.P
    NC = S // 16 - 1
    NCP = S // 16
    NCT = NCP // 128
    KcT, bKcT = k.sb(pers, [128, NCP], BF16, "KcT")
    Vc, bVc = k.sb(pers, [128, NCT, 2, 65], BF16, "Vc")
    k.memset("pool", KcT[:], 0.0, [bKcT])
    k.memset("pool", Vc[:], 1.0, [bVc])
    with ExitStack() as st:
        ring = PsumRing(k, st)
        for typ in ("k", "v"):
            with ExitStack() as s2:
                xT, bxT = k.sb(s2, [128, S], BF16, "cxT")
                k.dma("sp", xT[:], sc["kcT" if typ == "k" else "vcT"], bxT)
                w1, bw1 = k.sb(s2, [128, 32, 256], BF16, "w1")
                bw1b = Buf("w1b")
                src = di["c_w1" + typ][l].rearrange("(l d) c -> d l c", d=64)
                k.dma("pool", w1[0:64], src, bw1)
                k.dma("pool", w1[64:128], src, bw1b)
                pe2, bpe2 = k.sb(s2, [64, 32, 2], BF16, "pe2")
                pe_f, bpe_f = k.sb(s2, [64, 32], F32, "pe_f")
                k.dma("sp", pe_f[:], di["c_pe" + typ][l], bpe_f)
                k.copy("dve", pe2[:, :, 0], pe_f[:], r=[bpe_f], w=[bpe2])
                k.copy("dve", pe2[:, :, 1], pe_f[:], r=[bpe_f], w=[bpe2])
                b1, bb1 = k.sb(s2, [128, 2], F32, "b1")
                k.dma("sp", b1[:], di["c_b1" + typ][l], bb1)
                w2, bw2 = k.sb(s2, [128, 2, 64], BF16, "w2")
                k.dma("pool", w2[:], di["c_w2" + typ][l].rearrange("(c p) d -> p c d", p=128), bw2)
                bias, bbias = k.sb(s2, [128, 2], F32, "bias")
                for cc in range(2):
                    ps, bp = ring.get()
                    for li in range(32):
                        k.mm(ps[:, 0:2], w1[0:64, li, cc * 128:(cc + 1) * 128], pe2[:, li, :], li == 0, li == 31, r=[bw1, bpe2], w=[bp])
                    k.tt("dve", bias[:, cc:cc + 1], ps[:, 0:1], b1[:, cc:cc + 1], ALU.add, r=[bp, bb1], w=[bbias])
                for hk in range(2):
                    hid, bhid = k.sb(s2, [128, 2, NCP], BF16, "hid")
                    k.memset("pool", hid[:], 0.0, [bhid])
                    bw = bw1 if hk == 0 else bw1b
                    for cc in range(2):
                        ps, bp = ring.get()
                        for li in range(32):
                            k.mm(ps[:, 0:NC], w1[hk * 64:(hk + 1) * 64, li, cc * 128:(cc + 1) * 128],
                                 xT[hk * 64:(hk + 1) * 64, li:li + 16 * (NC - 1) + 1:16], li == 0, li == 31, r=[bw, bxT], w=[bp])
                        k.act(hid[:, cc, 0:NC], ps[:, 0:NC], AF.Gelu_apprx_tanh, r=[bp, bbias], w=[bhid], bias=bias[:, cc:cc + 1])
                    if typ == "k":
                        ps, bp = ring.get()
                        for cc in range(2):
                            k.mm(ps[hk * 64:(hk + 1) * 64, 0:NC], w2[:, cc, :], hid[:, cc, 0:NC], cc == 0, cc == 1, r=[bw2, bhid], w=[bp])
                        k.copy("act", KcT[hk * 64:(hk + 1) * 64, 0:NC], ps[hk * 64:(hk + 1) * 64, 0:NC], r=[bp], w=[bKcT])
                    else:
                        for nt in range(NCT):
                            ps, bp = ring.get()
                            for cc in range(2):
                                k.mm(ps[:, 0:64], hid[:, cc, nt * 128:(nt + 1) * 128], w2[:, cc, :], cc == 0, cc == 1, r=[bhid, bw2], w=[bp])
                            k.copy("act", Vc[:, nt, hk, 0:64], ps[:, 0:64], r=[bp], w=[bVc])
            P.barrier()
    if "dbg_kc" in sc:
        k.dma("sp", sc["dbg_kc"].rearrange("h d n -> (h d) n"), KcT[:], bKcT)
        k.dma("sp", sc["dbg_vc"], Vc[:], bVc)
    return (KcT, bKcT), (Vc, bVc)


def phase3(k, l, S, x_src, di, sc, cst, cmp_t):
    P = k.P
    NT = S // 128
    NCP = S // 16
    NCT = NCP // 128
    NB = S // 64
    (KcT, bKcT), (Vc, bVc) = cmp_t
    ident, bid = cst["ident"]
    with ExitStack() as st:
        ring = PsumRing(k, st, 3)
        ringO = PsumRing(k, st, 3)
        ringM = PsumRing(k, st, 2)
        KsM = []
        for hk in range(2):
            t, b = k.sb(st, [128, S], BF16, "KsM")
            b2 = Buf("KsMi")
            k.dma("sp", t[hk * 64:(hk + 1) * 64], sc["ksT"][hk * 64:(hk + 1) * 64, :], b)
            k.dma("pool", t[(1 - hk) * 64:(2 - hk) * 64], di["c_ind"], b2)
            KsM.append((t, b, b2))
        Vs, bVs = k.sb(st, [128, NT, 2, 65], BF16, "Vs")
        k.dma("sp", Vs[:], sc["vs"].rearrange("(n p) h c -> p n h c", p=128), bVs)

        def cload(name, shape, src, dt=BF16):
            t, b = k.sb(st, shape, dt, name)
            k.dma("pool" if dt == BF16 else "sp", t[:], src, b)
            return t, b
        caus, bcaus = cload("caus", [128, 128], di["c_caus"])
        low, blow = cload("low", [128, 128], di["c_low"])
        mgen, bmgen = cload("mgen", [128, 1024], di["c_mgen"])
        mt, bmt = cload("mt", [128, 16, 128], di["c_mt"])
        G, bG = cload("G", [128, 256], di["c_g"], F32)
        wbn, bwbn = cload("wbn", [64, 8, 1024], di["w_bnsa"][l].rearrange("(h d) n -> d h n", d=64))
        w_out, bw_out = load_w(k, st, di["w_out"][l], 8, 1024, name="w_out")
        ones16, bones = k.sb(st, [128, 64], BF16, "ones16")
        k.memset("pool", ones16[:], 1.0, [bones])
        rhis = [k.sb(st, [65, 512], BF16, "rhi") for _ in range(4)]
        rlos = [k.sb(st, [65, 512], BF16, "rlo") for _ in range(4)]
        QT = [[k.sb(st, [128, 4, 128], BF16, "QT") for _ in range(2)] for _ in range(2)]
        for pb_ in range(2):
            for hk_ in range(2):
                k.memset("pool", QT[pb_][hk_][0][:], 0.0, [QT[pb_][hk_][1]])
        NHALF = max(1, NB // 64)
        Qsel = [[[k.sb(st, [128, 4, 128], BF16, "Qsel") + (Buf("Qselm"),) for _ in range(NHALF)] for _ in range(2)] for _ in range(2)]
        negm_sw, bnegm_sw = k.sb(st, [128, 128], BF16, "negm_sw")
        k.memset("pool", negm_sw[:], 0.0, [bnegm_sw])
        KwT = [k.sb(st, [128, 640], BF16, "KwT") for _ in range(2)]
        Vw = [k.sb(st, [128, 5, 2, 65], BF16, "Vw") for _ in range(2)]
        grow = [k.sb(st, [65, 2, 3, 4, 128], F32, "grow") for _ in range(2)]
        sgn = [k.sb(st, [128, 8, 128], BF16, "sgn") for _ in range(2)]
        gss = [k.sb(st, [128, 8, 128], BF16, "gss") for _ in range(2)]
        xin = [k.sb(st, [128, D], F32, "xin") for _ in range(2)]
        eg = [k.sb(st, [128, NCP], F32, "eg") for _ in range(4)]
        den4, bden4 = k.sb(st, [128, 4], F32, "den4")
        rden4, brden4 = k.sb(st, [128, 4], F32, "rden4")
        pg, bpg = k.sb(st, [128, NCP + 8], F32, "pg")
        k.memset("pool", pg[:], 0.0, [bpg])
        blk, bblk = k.sb(st, [128, NB], F32, "blk")
        blk2, bblk2 = k.sb(st, [128, NB], F32, "blk2")
        m8, bm8 = k.sb(st, [128, 16], F32, "m8")
        negm, bnegm = k.sb(st, [128, 128], BF16, "negm")
        k.memset("pool", negm[:], 0.0, [bnegm])
        pTs = [k.sb(st, [128, 512], BF16, "pT") for _ in range(5)]
        pti = [0]
        rrs = [k.sb(st, [65, 512], F32, "rr") for _ in range(4)]
        osbs = [k.sb(st, [64, 512], F32, "osb") for _ in range(4)]
        oacc, boacc = k.sb(st, [64, 512], F32, "oacc")
        otmp, botmp = k.sb(st, [64, 512], F32, "otmp")
        oTb = [k.sb(st, [64, 4, 128], BF16, "oTb") for _ in range(2)]
        mtmp, bmtmp = k.sb(st, [128, 128], F32, "mtmp")
        mrg, bmrg = k.sb(st, [128, 8, 128], BF16, "mrg")
        xm = [k.sb(st, [128, D], F32, "xm") for _ in range(2)]

        def loads(qb):
            s0 = qb * 128
            pb = qb % 2
            qv = sc["qT"].rearrange("(h d) s -> d h s", d=64)
            for hk in range(2):
                t, b = QT[pb][hk]
                k.dma("sp", t[hk * 64:(hk + 1) * 64], qv[:, hk * 4:(hk + 1) * 4, s0:s0 + 128], b)
                for hf in range(NHALF):
                    if hf * 32 <= qb:
                        t, b, _ = Qsel[pb][hk][hf]
                        k.dma("sp", t[hk * 64:(hk + 1) * 64], qv[:, hk * 4:(hk + 1) * 4, s0:s0 + 128], b)
            lo = max(0, s0 - 512)
            t, b = KwT[pb]
            k.dma("sp", t[:, 640 - (s0 + 128 - lo):640], sc["kwT"][:, lo:s0 + 128], b)
            nw = (s0 + 128 - lo) // 128
            t, b = Vw[pb]
            k.dma("sp", t[:, 5 - nw:5], sc["vw"][lo:s0 + 128].rearrange("(n p) h c -> p n h c", p=128), b)
            t, b = grow[pb]
            gv = sc["sgT"].rearrange("(hk g br) s -> hk br g s", hk=2, g=4, br=3)
            for hk in range(2):
                for br in range(3):
                    k.dma("sp", t[64:65, hk, br], gv[hk, br:br + 1, :, s0:s0 + 128], b)
            t, b = sgn[pb]
            k.dma("sp", t[:], sc["sgnT"].rearrange("(c p) s -> p c s", p=128)[:, :, s0:s0 + 128], b)
            t, b = gss[pb]
            k.dma("sp", t[:], sc["gssT"].rearrange("(c p) s -> p c s", p=128)[:, :, s0:s0 + 128], b)
            t, b = xin[pb]
            k.dma("sp", t[:], x_src[s0:s0 + 128, :], b)

        DEPTH = 2
        pipe = []
        delayed = []

        def tick():
            for d in delayed:
                d[0] -= 1
            while delayed and delayed[0][0] <= 0:
                delayed.pop(0)[1]()

        def push(score_fn, pv_fn, after=None):
            tok_ = score_fn()
            pipe.append((pv_fn, tok_, after))
            if len(pipe) > DEPTH:
                pv, tk, af = pipe.pop(0)
                pv(tk)
                if af is not None:
                    af()
            tick()

        def flush():
            while pipe:
                pv, tk, af = pipe.pop(0)
                pv(tk)
                if af is not None:
                    af()
            while delayed:
                delayed.pop(0)[1]()

        def attn_tile(Ops, bO, first, last_, KT_ap, bKT, V_ap, bV, Q2, bQ, smask=None, emask=None, after=None):
            def score():
                psS, bS = ring.get()
                nmask = (4 if smask is not None else 0) + (1 if emask is not None else 0)
                rl = (bKT if isinstance(bKT, list) else [bKT]) + (bQ if isinstance(bQ, list) else [bQ])
                k.mm(psS[:, 0:512], KT_ap, Q2, True, nmask == 0, r=rl, w=[bS])
                done = 0
                assert emask is None
                if smask is not None:
                    m_ap, bm = smask
                    for g in range(4):
                        done += 1
                        k.mm(psS[:, g * 128:(g + 1) * 128], ident[:], m_ap, False, done == nmask, r=[bid, bm], w=[bS])
                pT, bpT = pTs[pti[0] % len(pTs)]
                pti[0] += 1
                k.act(pT[:], psS[:, 0:512], AF.Exp, r=[bS], w=[bpT], scale=0.125)
                return (pT, bpT)

            def pv(tk):
                pT, bpT = tk
                k.mm(Ops[0:65, 0:512], V_ap, pT[:], first, last_, r=[bV, bpT], w=[bO])
            push(score, pv, after)

        fin_i = [0]

        def finalize(Ops, bO, gate_ap, bgate, first_branch, out_final=None, bout=None):
            def stage_a():
                rr, brr = rrs[fin_i[0] % 4]
                rhi, brhi = rhis[fin_i[0] % 4]
                rlo, brlo = rlos[fin_i[0] % 4]
                osb, bosb = osbs[fin_i[0] % 4]
                fin_i[0] += 1
                k.ts("dve", rr[64:65, :], Ops[64:65, 0:512], 1e-20, ALU.max, r=[bO], w=[brr])
                k.recip(rr[64:65, :], rr[64:65, :], r=[brr], w=[brr])
                k.tt("dve", rr[64:65, :], rr[64:65, :], gate_ap, ALU.mult, r=[brr, bgate], w=[brr])
                k.copy("dve", rhi[64:65, :], rr[64:65, :], r=[brr], w=[brhi])
                k.tt("dve", rlo[64:65, :], rr[64:65, :], rhi[64:65, :], ALU.subtract, r=[brr, brhi], w=[brlo])
                k.copy("act", osb[:], Ops[0:64, 0:512], r=[bO], w=[bosb])

                def stage_b():
                    psb, bpb = ringM.get()
                    k.mm(psb[0:64, 0:512], ones16[64:65, 0:64], rhi[64:65, :], True, False, r=[bones, brhi], w=[bpb])
                    k.mm(psb[0:64, 0:512], ones16[64:65, 0:64], rlo[64:65, :], False, True, r=[bones, brlo], w=[bpb])
                    if first_branch:
                        k.tt("dve", oacc[:], osb[:], psb[0:64, 0:512], ALU.mult, r=[bosb, bpb], w=[boacc])
                    else:
                        k.tt("dve", otmp[:], osb[:], psb[0:64, 0:512], ALU.mult, r=[bosb, bpb], w=[botmp])
                        if out_final is None:
                            k.tt("pool", oacc[:], oacc[:], otmp[:], ALU.add, r=[boacc, botmp], w=[boacc])
                        else:
                            k.tt("pool", out_final, oacc[:], otmp[:], ALU.add, r=[boacc, botmp], w=[bout])
                delayed.append([3, stage_b])
            return stage_a

        def epilogue(qb):
            s0 = qb * 128
            pb = qb % 2
            if "dbg_o" in sc:
                for hk in range(2):
                    k.dma("sp", sc["dbg_o"][hk, :, qb], oTb[hk][0][:].rearrange("d g q -> d (g q)"), oTb[hk][1])
            sg_t, bsgn_ = sgn[pb]
            gs_t, bgs_ = gss[pb]
            for half in range(2):
                ps, bp = ringM.get()
                for f4 in range(4):
                    fc = half * 4 + f4
                    for h in range(8):
                        k.mm(ps[:, f4 * 128:(f4 + 1) * 128], wbn[:, h, fc * 128:(fc + 1) * 128], oTb[h // 4][0][:, h % 4, :],
                             h == 0, h == 7, r=[bwbn, oTb[h // 4][1]], w=[bp])
                for f4 in range(4):
                    fc = half * 4 + f4
                    k.tt("dve", mtmp[:], ps[:, f4 * 128:(f4 + 1) * 128], sg_t[:, fc, :], ALU.mult, r=[bp, bsgn_], w=[bmtmp])
                    k.tt("pool", mrg[:, fc, :], mtmp[:], gs_t[:, fc, :], ALU.add, r=[bmtmp, bgs_], w=[bmrg])
            x_t, bx = xin[pb]
            xm_t, bxm = xm[qb % 2]
            for half in range(2):
                ps, bp = ringM.get()
                for fc in range(8):
                    k.mm(ps[:, 0:512], mrg[:, fc, :], w_out[:, fc, half * 512:(half + 1) * 512], fc == 0, fc == 7, r=[bmrg, bw_out[fc]], w=[bp])
                k.tt("dve", xm_t[:, half * 512:(half + 1) * 512], ps[:, 0:512], x_t[:, half * 512:(half + 1) * 512], ALU.add, r=[bp, bx], w=[bxm])
            k.dma("sp", sc["xmid"][s0:s0 + 128, :], xm_t[:], bxm)

        loads(0)
        for qb in range(NT):
            if qb <= 5:
                flush()
            s0 = qb * 128
            pb = qb % 2
            for hk in range(2):
                hs = slice(hk * 64, (hk + 1) * 64)
                Qt, bQ = QT[pb][hk]
                Q2 = Qt[:].rearrange("d g q -> d (g q)")
                gr, bgr = grow[pb]
                for g in range(4):
                    ps, bp = ringM.get()
                    k.mm(ps[:, 0:NCP], Qt[:, g, :], KcT[:, 0:NCP], True, False, r=[bQ, bKcT], w=[bp])
                    k.mm(ps[:, 0:NCP], ident[:], mgen[:, 512 - 8 * qb:512 - 8 * qb + NCP], False, True, r=[bid, bmgen], w=[bp])
                    k.act(eg[g][0][:], ps[:, 0:NCP], AF.Exp, r=[bp], w=[eg[g][1], bden4], scale=0.125, accum=den4[:, g:g + 1])
                k.ts("dve", rden4[:], den4[:], 1e-20, ALU.max, r=[bden4], w=[brden4])
                k.recip(rden4[:], rden4[:], r=[brden4], w=[brden4])
                k.ts("dve", pg[:, 1:1 + NCP], eg[0][0][:], rden4[:, 0:1], ALU.mult, r=[eg[0][1], brden4], w=[bpg])
                for g in range(1, 4):
                    k.stt(pg[:, 1:1 + NCP], eg[g][0][:], rden4[:, g:g + 1], pg[:, 1:1 + NCP], ALU.mult, ALU.add, r=[eg[g][1], brden4, bpg], w=[bpg])
                P.op("dve", lambda e: e.tensor_reduce(out=blk[:], in_=pg[:, 0:NCP].rearrange("p (j o) -> p j o", o=4), axis=AX.X, op=ALU.add), reads=[bpg], writes=[bblk])
                k.tt("dve", blk[:], blk[:], pg[:, 4:4 + 4 * NB:4], ALU.add, r=[bblk, bpg], w=[bblk])
                k.tt("dve", blk[:], blk[:], G[:, 126 - 2 * qb:126 - 2 * qb + NB], ALU.add, r=[bblk, bG], w=[bblk])
                if qb >= 1:
                    k.ts("dve", blk[:, 0:1], blk[:, 0:1], 1e4, ALU.add, r=[bblk], w=[bblk])
                P.op("dve", lambda e: e.max(out=m8[:, 0:8], in_=blk[:]), reads=[bblk], writes=[bm8])
                P.op("dve", lambda e: e.match_replace(out=blk2[:], in_to_replace=m8[:, 0:8], in_values=blk[:], imm_value=-3e38), reads=[bblk, bm8], writes=[bblk2])
                P.op("dve", lambda e: e.max(out=m8[:, 8:16], in_=blk2[:]), reads=[bblk2], writes=[bm8])
                os_ = slice((1 - hk) * 64, (2 - hk) * 64)
                nhalf_used = 1 if qb < 32 else NHALF
                need_nat = (hk == 1) or nhalf_used > 1
                need_sw = (hk == 0) or nhalf_used > 1
                if need_nat:
                    k.ts("dve", negm[:, 0:NB], blk[:], m8[:, 15:16], ALU.is_lt, r=[bblk, bm8], w=[bnegm], s2=NEG, op1=ALU.mult)
                if need_sw:
                    n0 = min(NB, 64)
                    k.ts("dve", negm_sw[:, 64:64 + n0], blk[:, 0:n0], m8[:, 15:16], ALU.is_lt, r=[bblk, bm8], w=[bnegm_sw], s2=NEG, op1=ALU.mult)
                    if NB > 64:
                        k.ts("dve", negm_sw[:, 0:NB - 64], blk[:, 64:NB], m8[:, 15:16], ALU.is_lt, r=[bblk, bm8], w=[bnegm_sw], s2=NEG, op1=ALU.mult)
                for hf in range(nhalf_used):
                    use_sw = (hk == 0 and hf == 0) or (hk == 1 and hf == 1)
                    src_t, bsrc = (negm_sw, bnegm_sw) if use_sw else (negm, bnegm)
                    ps, bp = ringM.get()
                    pbf = ps[:].bitcast(BF16)
                    k.tr(pbf[:, 0:128], src_t[:], ident[:], r=[bsrc, bid], w=[bp])
                    qs_t, _, bqm = Qsel[pb][hk][hf]
                    for g in range(4):
                        k.copy("dve", qs_t[os_, g, :], pbf[os_, 0:128], r=[bp], w=[bqm])
                Ops, bO = ringO.get()
                fa = finalize(Ops, bO, gr[64:65, hk, 0].rearrange("o g q -> o (g q)"), bgr, True)
                tiles = [nt for nt in range(NCT) if qb - 16 * nt >= 0]
                for idx, nt in enumerate(tiles):
                    m = qb - 16 * nt
                    sm = (mt[:, m, :], bmt) if m < 16 else None
                    attn_tile(Ops, bO, idx == 0, idx == len(tiles) - 1, KcT[:, nt * 128:(nt + 1) * 128], bKcT,
                              Vc[:, nt, hk, :], bVc, Q2, bQ, smask=sm, after=fa if idx == len(tiles) - 1 else None)
                Ops, bO = ringO.get()
                fa = finalize(Ops, bO, gr[64:65, hk, 2].rearrange("o g q -> o (g q)"), bgr, False)
                tiles = [wt for wt in range(5) if s0 - 512 + 128 * wt >= 0]
                kw_full, bkw = KwT[pb]
                kw_t = kw_full
                vw_t, bvw = Vw[pb]
                for idx, wt in enumerate(tiles):
                    sm = (low[:], blow) if wt == 0 else ((caus[:], bcaus) if wt == 4 else None)
                    attn_tile(Ops, bO, idx == 0, idx == len(tiles) - 1, kw_t[:, wt * 128:(wt + 1) * 128], bkw,
                              vw_t[:, wt, hk, :], bvw, Q2, bQ, smask=sm, after=fa if idx == len(tiles) - 1 else None)
                if hk == 0 and qb + 1 < NT:
                    loads(qb + 1)
                Ops, bO = ringO.get()
                fa = finalize(Ops, bO, gr[64:65, hk, 1].rearrange("o g q -> o (g q)"), bgr, False,
                              out_final=oTb[hk][0][:].rearrange("d g q -> d (g q)"), bout=oTb[hk][1])
                for i in range(qb + 1):
                    sm = (caus[:], bcaus) if i == qb else None
                    qs_t, bqs, bqm = Qsel[pb][hk][i // 32]
                    attn_tile(Ops, bO, i == 0, i == qb, KsM[hk][0][:, i * 128:(i + 1) * 128], [KsM[hk][1], KsM[hk][2]],
                              Vs[:, i, hk, :], bVs, qs_t[:].rearrange("d g q -> d (g q)"), [bqs, bqm], smask=sm, after=fa if i == qb else None)
                if hk == 1:
                    delayed_ep = (lambda q_=qb: (lambda: delayed.append([4, lambda: epilogue(q_)])))(qb)
                    pipe[-1] = (pipe[-1][0], pipe[-1][1], (lambda f1=pipe[-1][2], f2=delayed_ep: (f1(), f2())))
        flush()
    P.barrier()


def phase4(k, l, S, di, sc, cst, dst, last):
    P = k.P
    TT = 256
    NTT = S // TT
    NSB = TT // 128
    ident, bid = cst["ident"]
    with ExitStack() as st:
        ring = PsumRing(k, st)
        w_fin, bw_fin = load_w(k, st, di["w_fin"][l], 8, 2 * DFF, name="w_fin")
        w_fo, bw_fo = load_w(k, st, di["w_fout"][l], NFC, D, name="w_fo")
        gam, bgam = k.sb(st, [128, D], F32, "gam")
        k.dma("sp", gam[:], di["g_ffn"][l], bgam)
        cw, bcw = k.sb(st, [128, NFC, 3], F32, "cw")
        k.dma("sp", cw[:], di["f_cw"][l], bcw)
        cb, bcb = k.sb(st, [128, NFC], F32, "cb")
        k.dma("sp", cb[:], di["f_cb"][l], bcb)
        if last:
            gfin, bgfin = k.sb(st, [128, D], F32, "gfin")
            k.dma("sp", gfin[:], di["g_fin"], bgfin)
        halo, bhalo = k.sb(st, [128, NFC, 2], F32, "halo")
        k.memset("pool", halo[:], 0.0, [bhalo])
        xt = [k.sb(st, [128, NSB, D], F32, "xt") for _ in range(2)]
        junk, bjunk = k.sb(st, [128, D], BF16, "junk")
        ss, bss = k.sb(st, [128, 2 * NSB], F32, "ss")
        ms, bms = k.sb(st, [128, 2 * NSB], F32, "ms")
        sd, bsd = k.sb(st, [128, 2 * NSB], F32, "sd")
        rstd, brstd = k.sb(st, [128, 2 * NSB], F32, "rstd")
        hh = [k.sb(st, [128, D], BF16, "h") for _ in range(2)]
        hT, bhT = k.sb(st, [128, 8, TT], BF16, "hT")
        a_sb = [k.sb(st, [128, TT + 2], F32, "a_sb") for _ in range(2)]
        cv = [k.sb(st, [128, TT], F32, "cv") for _ in range(2)]
        gl = [k.sb(st, [128, TT], F32, "gl") for _ in range(2)]
        actT, bactT = k.sb(st, [128, NFC, TT], BF16, "actT")
        xo = [k.sb(st, [128, NSB, D], F32, "xo") for _ in range(2)]

        def load_tile(i):
            t, b = xt[i % 2]
            k.dma("sp", t[:], sc["xmid"][i * TT:(i + 1) * TT, :].rearrange("(s p) d -> p s d", p=128), b)

        def rms(x_ap, bx, col, g_t, bg, out_ap, bout):
            k.act(junk[:], x_ap, AF.Square, r=[bx], w=[bjunk, bss], accum=ss[:, col:col + 1])
            k.ts("dve", ms[:, col:col + 1], ss[:, col:col + 1], 1.0 / D, ALU.mult, r=[bss], w=[bms], s2=EPS, op1=ALU.add)
            k.act(sd[:, col:col + 1], ms[:, col:col + 1], AF.Sqrt, r=[bms], w=[bsd])
            k.recip(rstd[:, col:col + 1], sd[:, col:col + 1], r=[bsd], w=[brstd])
            k.stt(out_ap, x_ap, rstd[:, col:col + 1], g_t[:], ALU.mult, ALU.mult, r=[bx, brstd, bg], w=[bout])

        load_tile(0)
        for i in range(NTT):
            if i + 1 < NTT:
                load_tile(i + 1)
            x_t, bx = xt[i % 2]
            for s_ in range(NSB):
                h_t, bh = hh[s_ % 2]
                rms(x_t[:, s_, :], bx, s_, gam, bgam, h_t[:], bh)
                ps, bp = ring.get()
                pbf = ps[:].bitcast(BF16)
                for c in range(8):
                    k.tr(pbf[:, c * 128:(c + 1) * 128], h_t[:, c * 128:(c + 1) * 128], ident[:], r=[bh, bid], w=[bp])
                k.copy("act", hT[:, :, s_ * 128:(s_ + 1) * 128], pbf.rearrange("p (c t) -> p c t", c=8), r=[bp], w=[bhT])
            for fc in range(NFC):
                psa, bpa = ring.get()
                for c in range(8):
                    k.mm(psa[:, 0:TT], w_fin[:, c, fc * 128:(fc + 1) * 128], hT[:, c, :], c == 0, c == 7, r=[bw_fin[c], bhT], w=[bpa])
                psb, bpb = ring.get()
                for c in range(8):
                    k.mm(psb[:, 0:TT], w_fin[:, c, DFF + fc * 128:DFF + (fc + 1) * 128], hT[:, c, :], c == 0, c == 7, r=[bw_fin[c], bhT], w=[bpb])
                a_t, ba = a_sb[fc % 2]
                c_t, bc = cv[fc % 2]
                g_t, bg = gl[fc % 2]
                k.copy("pool", a_t[:, 0:2], halo[:, fc, :], r=[bhalo], w=[ba])
                k.copy("act", a_t[:, 2:2 + TT], psa[:, 0:TT], r=[bpa], w=[ba])
                k.copy("pool", halo[:, fc, :], a_t[:, TT:TT + 2], r=[ba], w=[bhalo])
                k.ts("dve", c_t[:], a_t[:, 2:2 + TT], cw[:, fc, 2:3], ALU.mult, r=[ba, bcw, bcb], w=[bc], s2=cb[:, fc:fc + 1], op1=ALU.add)
                k.stt(c_t[:], a_t[:, 1:1 + TT], cw[:, fc, 1:2], c_t[:], ALU.mult, ALU.add, r=[ba, bcw, bc], w=[bc])
                k.stt(c_t[:], a_t[:, 0:TT], cw[:, fc, 0:1], c_t[:], ALU.mult, ALU.add, r=[ba, bcw, bc], w=[bc])
                k.act(g_t[:], c_t[:], AF.Gelu_apprx_tanh, r=[bc], w=[bg])
                k.tt("dve", actT[:, fc, :], psb[:, 0:TT], g_t[:], ALU.mult, r=[bpb, bg], w=[bactT])
            xo_t, bxo = xo[i % 2]
            for s_ in range(NSB):
                for half in range(2):
                    ps, bp = ring.get()
                    for fc in range(NFC):
                        k.mm(ps[:, 0:512], actT[:, fc, s_ * 128:(s_ + 1) * 128], w_fo[:, fc, half * 512:(half + 1) * 512], fc == 0, fc == NFC - 1,
                             r=[bactT, bw_fo[fc]], w=[bp])
                    k.tt("dve", xo_t[:, s_, half * 512:(half + 1) * 512], ps[:, 0:512], x_t[:, s_, half * 512:(half + 1) * 512], ALU.add, r=[bp, bx], w=[bxo])
            if last:
                for s_ in range(NSB):
                    rms(xo_t[:, s_, :], bxo, NSB + s_, gfin, bgfin, xo_t[:, s_, :], bxo)
            k.dma("sp", dst[i * TT:(i + 1) * TT, :].rearrange("(s p) d -> p s d", p=128), xo_t[:], bxo)
    P.barrier()


INPUT_SHAPES = None


def build(S, L, TS=128, dbg=False, phases=("p1", "p2", "p3", "p4")):
    nc = bass.Bass("TRN2", target_bir_lowering=False)
    NT = S // 128
    di = {}

    def din(name, shape):
        di[name] = nc.dram_tensor(name, list(shape), F32, kind="ExternalInput").ap()
    din("x", [S, D])
    for nm, shp in (("g_mix", [L, 128, D]), ("g_ffn", [L, 128, D]), ("g_fin", [128, D]),
                    ("w_in", [L, D, INW]), ("w_sw", [L, D, 896]),
                    ("s_are", [L, 128, 16]), ("s_aim", [L, 128, 16]), ("s_ldt", [L, 128, 16]),
                    ("s_bre", [L, 128, 16, 128]), ("s_bim", [L, 128, 16, 128]),
                    ("s_cre", [L, 128, 16, 128]), ("s_cim", [L, 128, 16, 128]), ("s_d", [L, 128, 4]),
                    ("w_glu", [L, 512, 1024]), ("w_bssm", [L, 512, 1024]), ("w_bnsa", [L, 512, 1024]),
                    ("w_out", [L, D, D]),
                    ("c_pek", [L, 64, 32]), ("c_w1k", [L, 2048, 256]), ("c_b1k", [L, 128, 2]), ("c_w2k", [L, 256, 64]),
                    ("c_pev", [L, 64, 32]), ("c_w1v", [L, 2048, 256]), ("c_b1v", [L, 128, 2]), ("c_w2v", [L, 256, 64]),
                    ("w_fin", [L, D, 2 * DFF]), ("w_fout", [L, DFF, D]), ("f_cw", [L, 128, NFC, 3]), ("f_cb", [L, 128, NFC]),
                    ("c_cos", [128, S]), ("c_sin", [128, S]), ("c_ident", [128, 128]), ("c_tau", [128, TS]),
                    ("c_caus", [128, 128]), ("c_low", [128, 128]), ("c_mgen", [128, 1024]), ("c_mt", [128, 16, 128]),
                    ("c_g", [128, 256]), ("c_ind", [64, S])):
        din(nm, shp)
    out = nc.dram_tensor("out", [S, D], F32, kind="ExternalOutput").ap()
    skind = "ExternalOutput" if dbg else "Internal"
    sc = {}

    def scr(name, shape, dt):
        sc[name] = nc.dram_tensor(name, list(shape), dt, kind=skind).ap()
    scr("qT", [512, S], BF16)
    for nm in ("kcT", "vcT", "ksT", "kwT"):
        scr(nm, [128, S], BF16)
    scr("vs", [S, 2, 65], BF16)
    scr("vw", [S, 2, 65], BF16)
    scr("sgT", [24, S], F32)
    scr("sgnT", [1024, S], BF16)
    scr("gssT", [1024, S], BF16)
    scr("xmid", [S, D], F32)
    if dbg:
        scr("dbg_o", [2, 64, NT, 512], BF16)
        scr("dbg_kc", [2, 64, S // 16], BF16)
        scr("dbg_vc", [128, S // 2048, 2, 65], BF16)
    scr("x1", [S, D], F32)
    with ExitStack() as st:
        P = Prog(nc)
        k = K(nc, P)
        cst = {}
        ident, bid = k.sb(st, [128, 128], BF16, "ident")
        k.dma("pool", ident[:], di["c_ident"], bid)
        cst["ident"] = (ident, bid)
        x_src = di["x"]
        for l in range(L):
            last = l == L - 1
            if "p1" in phases:
                phase1(k, l, S, TS, x_src, di, sc, cst)
            if "p2" in phases:
                pers = ExitStack()
                cmp_t = phase2(k, pers, l, S, di, sc, cst)
            if "p3" in phases:
                phase3(k, l, S, x_src, di, sc, cst, cmp_t)
            if "p2" in phases:
                pers.close()
                P.barrier()
            if "p4" in phases:
                phase4(k, l, S, di, sc, cst, out if last else sc["x1"], last)
            x_src = sc["x1"]
        P.barrier()
        P.emit(st)
    return nc


_NC_CACHE = {}


def kernel(**inputs):
    S, L, NCORES = 8192, 2, 8
    inp = {k_: np.asarray(v) for k_, v in inputs.items()}
    hl = host_layout(inp, L)
    hc = host_consts(S, 128)
    common = {}
    common.update(hl)
    common.update(hc)
    common = {k_: np.ascontiguousarray(v, dtype=np.float32) for k_, v in common.items()}
    if "nc" not in _NC_CACHE:
        _NC_CACHE["nc"] = build(S, L)
    nc = _NC_CACHE["nc"]
    x = np.asarray(inp["x"], dtype=np.float32)
    in_maps = []
    for b in range(NCORES):
        m = dict(common)
        m["x"] = np.ascontiguousarray(x[b])
        in_maps.append(m)
    res = run_bass_kernel_spmd(nc, in_maps, core_ids=list(range(NCORES)))
    return np.stack([np.asarray(r["out"], dtype=np.float32) for r in res.results], axis=0)
```

```python
from contextlib import ExitStack
import numpy as np
import ml_dtypes
import concourse.bass as bass
import concourse.mybir as mybir
from concourse.bass_utils import run_bass_kernel_spmd

F32 = mybir.dt.float32
BF16 = mybir.dt.bfloat16
I32 = mybir.dt.int32
ALU = mybir.AluOpType
AF = mybir.ActivationFunctionType
AX = mybir.AxisListType

D = 1024
DFF = 2816
NFC = DFF // 128
INW = 3864
EPS = 1e-6
NEG = -30000.0
TWO_PI = float(2 * np.pi)
SIN_SCALE = TWO_PI * 0.999999


class Buf:
    __slots__ = ("name", "w", "rs", "sem", "cnt", "slot", "base", "uid")

    def __init__(self, name="b"):
        self.name = name
        self.w = None
        self.rs = {}
        self.sem = None
        self.cnt = 0
        self.slot = None
        self.base = 0
        self.uid = None


class Op:
    __slots__ = ("eng", "fn", "deps", "key", "val", "signal", "sigval", "dma", "slot", "semval")


ENGS = ("pe", "act", "dve", "pool", "sp")


class Prog:
    def __init__(self, nc):
        self.nc = nc
        self.ops = {e: [] for e in ENGS}
        self.seen = {e: {} for e in ENGS}
        self.dma_bufs = []
        self.last = {}
        self.slot_base = []
        self.free_slots = []
        self.live = []
        self.uid = 0

    def _get_slot(self, buf):
        if self.free_slots:
            sl = self.free_slots.pop()
        else:
            sl = len(self.slot_base)
            self.slot_base.append(0)
        self.uid += 1
        buf.sem = True
        buf.slot = sl
        buf.base = self.slot_base[sl]
        buf.cnt = 0
        buf.uid = self.uid
        self.live.append(buf)

    def barrier(self):
        lasts = list(self.last.values())
        self._barrier_ops(lasts)
        for b in self.live:
            self.slot_base[b.slot] = b.base + b.cnt
            self.free_slots.append(b.slot)
            b.sem = None
        self.live = []
        self.last = {kk: v for kk, v in self.last.items() if not isinstance(kk, tuple)}

    def _barrier_ops(self, lasts):
        for e in ENGS:
            o = Op()
            o.eng = e
            o.fn = None
            o.deps = []
            o.signal = False
            o.sigval = None
            o.dma = None
            o.key = e
            o.val = len(self.ops[e])
            for d in lasts:
                if d.key == e:
                    continue
                if self.seen[e].get(d.key, -1) >= d.val:
                    continue
                self.seen[e][d.key] = d.val
                o.deps.append(d)
            self.ops[e].append(o)

    def _dep(self, eng, d, deps, same_ok):
        if d is None:
            return
        key = d.key
        if key == eng:
            if eng == "pe" or same_ok:
                return
        if self.seen[eng].get(key, -1) >= d.val:
            return
        self.seen[eng][key] = d.val
        deps.append(d)

    def op(self, eng, fn, reads=(), writes=(), dma=None):
        o = Op()
        o.eng = eng
        o.fn = fn
        o.deps = []
        o.signal = False
        o.sigval = None
        o.dma = dma
        writes = [b for b in writes if b is not None]
        reads = [b for b in reads if b is not None]
        o.slot = None
        o.semval = None
        if dma is not None:
            if dma.sem is None:
                self._get_slot(dma)
            dma.cnt += 1
            o.key = ("dma", dma.uid)
            o.val = dma.cnt
            o.slot = dma.slot
            o.semval = 16 * (dma.base + dma.cnt)
            if dma not in writes:
                writes.append(dma)
            reads = [b for b in reads if b is not dma]
        else:
            o.key = eng
            o.val = len(self.ops[eng])
        for b in reads:
            self._dep(eng, b.w, o.deps, False)
        for b in writes:
            self._dep(eng, b.w, o.deps, True)
            for r in b.rs.values():
                self._dep(eng, r, o.deps, True)
        for b in reads:
            b.rs[o.key] = o
        for b in writes:
            b.w = o
            b.rs = {}
        self.ops[eng].append(o)
        self.last[o.key] = o
        return o

    def emit(self, stack):
        nc = self.nc
        for e in ENGS:
            for o in self.ops[e]:
                for d in o.deps:
                    if d.dma is None:
                        d.signal = True
        for e in ENGS:
            c = 0
            for o in self.ops[e]:
                if o.dma is None and o.signal:
                    c += 1
                    o.sigval = c
        esem = {}
        for e in ("pe", "act", "dve", "pool"):
            esem[e] = stack.enter_context(nc.semaphore("s_" + e))
        dsem = [stack.enter_context(nc.semaphore("d%d" % i)) for i in range(len(self.slot_base))]
        block = stack.enter_context(nc.Block())
        prog = self

        def run(name, eng):
            for o in prog.ops[name]:
                for d in o.deps:
                    if d.dma is not None:
                        eng.wait_ge(dsem[d.slot], d.semval)
                    else:
                        eng.wait_ge(esem[d.key], d.sigval)
                if o.fn is None:
                    continue
                ins = o.fn(eng)
                if o.dma is not None:
                    ins.then_inc(dsem[o.slot], 16)
                elif o.signal:
                    ins.then_inc(esem[name], 1)

        @block.sync
        def _(eng):
            run("sp", eng)

        @block.scalar
        def _(eng):
            run("act", eng)

        @block.vector
        def _(eng):
            run("dve", eng)

        @block.gpsimd
        def _(eng):
            run("pool", eng)

        @block.tensor
        def _(eng):
            run("pe", eng)


class K:
    def __init__(self, nc, P):
        self.nc = nc
        self.P = P
        self.n = 0

    def name(self, s):
        self.n += 1
        return "%s_%d" % (s, self.n)

    def sb(self, st, shape, dt=F32, name="t"):
        t = st.enter_context(self.nc.sbuf_tensor(self.name(name), list(shape), dt))
        return t, Buf(name)

    def dma(self, eng, out, in_, buf, reads=(), writes=()):
        self.P.op(eng, lambda e: e.dma_start(out=out, in_=in_), reads=reads, writes=writes, dma=buf)

    def mm(self, out, lhsT, rhs, start, stop, r, w):
        self.P.op("pe", lambda e: e.matmul(out, lhsT=lhsT, rhs=rhs, start=start, stop=stop), reads=r, writes=w)

    def tr(self, out, in_, ident, r, w):
        self.P.op("pe", lambda e: e.transpose(out=out, in_=in_, identity=ident), reads=r, writes=w)

    def act(self, out, in_, func, r, w, bias=None, scale=None, accum=None):
        kw = {}
        if bias is not None:
            kw["bias"] = bias
        if scale is not None:
            kw["scale"] = scale
        if accum is not None:
            kw["accum_out"] = accum
        self.P.op("act", lambda e: e.activation(out=out, in_=in_, func=func, **kw), reads=r, writes=w)

    def tt(self, eng, out, in0, in1, op, r, w):
        self.P.op(eng, lambda e: e.tensor_tensor(out=out, in0=in0, in1=in1, op=op), reads=r, writes=w)

    def ts(self, eng, out, in0, s1, op0, r, w, s2=None, op1=None):
        if op1 is None:
            self.P.op(eng, lambda e: e.tensor_scalar(out=out, in0=in0, scalar1=s1, scalar2=None, op0=op0), reads=r, writes=w)
        else:
            self.P.op(eng, lambda e: e.tensor_scalar(out=out, in0=in0, scalar1=s1, scalar2=s2, op0=op0, op1=op1), reads=r, writes=w)

    def stt(self, out, in0, scalar, in1, op0, op1, r, w):
        self.P.op("dve", lambda e: e.scalar_tensor_tensor(out=out, in0=in0, scalar=scalar, in1=in1, op0=op0, op1=op1), reads=r, writes=w)

    def copy(self, eng, out, in_, r, w):
        if eng == "act":
            self.P.op("act", lambda e: e.activation(out=out, in_=in_, func=AF.Copy), reads=r, writes=w)
        else:
            self.P.op(eng, lambda e: e.tensor_copy(out=out, in_=in_), reads=r, writes=w)

    def memset(self, eng, ap, val, w):
        self.P.op(eng, lambda e: e.memset(ap, val), writes=w)

    def scan(self, out, d0, d1, init, r, w):
        self.P.op("dve", lambda e: e.tensor_tensor_scan(out=out, data0=d0, data1=d1, initial=init, op0=ALU.mult, op1=ALU.add), reads=r, writes=w)

    def recip(self, out, in_, r, w):
        self.P.op("dve", lambda e: e.reciprocal(out=out, in_=in_), reads=r, writes=w)


class PsumRing:
    def __init__(self, k, st, n=8):
        self.banks = []
        for i in range(n):
            t = st.enter_context(k.nc.psum_tensor(k.name("ps"), [128, 512], F32))
            self.banks.append((t, Buf("ps%d" % i)))
        self.i = 0

    def get(self):
        t, b = self.banks[self.i % len(self.banks)]
        self.i += 1
        return t, b


def _swap_halves(w):
    sh = w.shape
    w4 = w.reshape(sh[:-1] + (sh[-1] // 64, 2, 32))
    return np.ascontiguousarray(w4[..., ::-1, :]).reshape(sh)


def host_consts(S, TS):
    c = {}
    inv = (10000.0 ** (-np.arange(0, 64, 2, dtype=np.float32) / np.float32(64))).astype(np.float32)
    ang = (np.arange(S, dtype=np.float32)[:, None] * inv[None, :]).astype(np.float32)
    cs, sn = np.cos(ang).astype(np.float32), np.sin(ang).astype(np.float32)
    cosT = np.concatenate([cs.T, cs.T], 0)
    sinT = np.concatenate([-sn.T, sn.T], 0)
    c["c_cos"] = np.ascontiguousarray(np.concatenate([cosT, cosT], 0))
    c["c_sin"] = np.ascontiguousarray(np.concatenate([sinT, sinT], 0))
    c["c_ident"] = np.eye(128, dtype=np.float32)
    c["c_tau"] = np.ascontiguousarray(np.broadcast_to(np.arange(TS, dtype=np.float32)[None, :], (128, TS)))
    k = np.arange(128)[:, None]
    q = np.arange(128)[None, :]
    c["c_caus"] = np.where(k <= q, 0.0, NEG).astype(np.float32)
    c["c_low"] = np.where(k > q, 0.0, NEG).astype(np.float32)
    cc = np.arange(1024)[None, :]
    qi = np.arange(128)[:, None]
    c["c_mgen"] = np.where(16 * (cc - 512) + 31 <= qi, 0.0, NEG).astype(np.float32)
    m = np.arange(16)[None, :, None]
    ni = np.arange(128)[:, None, None]
    qq = np.arange(128)[None, None, :]
    c["c_mt"] = np.where(16 * ni + 31 <= 128 * m + qq, 0.0, NEG).astype(np.float32)
    rel = np.arange(256)[None, :] - 126
    cur = (np.arange(128)[:, None] >= 64).astype(np.int64)
    g = np.where(rel > cur, -1e30, 0.0) + np.where((rel == cur) | (rel == cur - 1), 1e4, 0.0)
    c["c_g"] = g.astype(np.float32)
    NT = S // 128
    j = np.arange(128)[:, None, None]
    i = np.arange(NT)[None, :, None]
    kk = np.arange(128)[None, None, :]
    c["c_e"] = (j == 2 * i + (kk >= 64)).astype(np.float32)
    r_ = np.arange(64)[:, None]
    cidx = np.arange(S)[None, :]
    c["c_ind"] = (((cidx // 64) % 64) == r_).astype(np.float32)
    return c


def host_layout(inp, L):
    o = {}
    f = np.float32
    o["g_mix"] = np.ascontiguousarray(np.broadcast_to(inp["norm_mix"][:, None, :], (L, 128, D))).astype(f)
    o["g_ffn"] = np.ascontiguousarray(np.broadcast_to(inp["norm_ffn"][:, None, :], (L, 128, D))).astype(f)
    o["g_fin"] = np.ascontiguousarray(np.broadcast_to(inp["norm_final"][None, :], (128, D))).astype(f)
    w_in = inp["w_in"]
    o["w_in"] = w_in
    sw = np.concatenate([_swap_halves(w_in[:, :, 512:1024]), _swap_halves(w_in[:, :, 1024:1152]),
                         _swap_halves(w_in[:, :, 1280:1408]), _swap_halves(w_in[:, :, 1536:1664])], axis=-1)
    o["w_sw"] = np.ascontiguousarray(sw)

    def pair(a):
        return np.ascontiguousarray(a.reshape(L, 16, 2, 64).transpose(0, 2, 3, 1).reshape(L, 128, 16))
    o["s_are"] = pair(inp["ssm_a_re"])
    o["s_aim"] = pair(inp["ssm_a_im"])
    o["s_ldt"] = pair(np.broadcast_to(inp["ssm_log_dt"][:, :, None], (L, 32, 64)))
    for nm, src in (("s_bre", "ssm_b_re"), ("s_bim", "ssm_b_im")):
        b = inp[src].reshape(L, 16, 2, 64, 16)
        pad = np.zeros((L, 8, 16, 16, 2, 64), f)
        for j in range(16):
            for gl in range(2):
                pad[:, 2 * (j % 4) + gl, :, j, gl, :] = b[:, j, gl].transpose(0, 2, 1)
        o[nm] = pad.reshape(L, 128, 16, 128)
    for nm, src in (("s_cre", "ssm_c_re"), ("s_cim", "ssm_c_im")):
        cmat = inp[src].reshape(L, 16, 2, 16, 64)
        pad = np.zeros((L, 2, 64, 16, 8, 16), f)
        for j in range(16):
            for gl in range(2):
                pad[:, gl, :, j, 2 * (j % 4) + gl, :] = cmat[:, j, gl].transpose(0, 2, 1)
        o[nm] = pad.reshape(L, 128, 16, 128)
    o["s_d"] = np.ascontiguousarray(inp["ssm_d"].reshape(L, 4, 128).transpose(0, 2, 1))
    o["w_glu"] = inp["ssm_w_glu"]
    o["w_bssm"] = inp["w_branch_ssm"]
    o["w_bnsa"] = inp["w_branch_nsa"]
    o["w_out"] = inp["w_out"]
    for t in ("k", "v"):
        o["c_pe" + t] = np.ascontiguousarray(inp["cmp_pe_" + t].transpose(0, 2, 1))
        o["c_w1" + t] = inp["cmp_w1_" + t]
        o["c_b1" + t] = np.ascontiguousarray(inp["cmp_b1_" + t].reshape(L, 2, 128).transpose(0, 2, 1))
        o["c_w2" + t] = inp["cmp_w2_" + t]
    o["w_fin"] = inp["w_ffn_in"]
    o["w_fout"] = inp["w_ffn_out"]
    o["f_cw"] = np.ascontiguousarray(inp["ffn_conv_w"].reshape(L, 3, NFC, 128).transpose(0, 3, 2, 1))
    o["f_cb"] = np.ascontiguousarray(inp["ffn_conv_b"].reshape(L, NFC, 128).transpose(0, 2, 1))
    return o


def load_w(k, st, src2d, nch, ncols, prow=128, eng="pool", name="w"):
    t, _ = k.sb(st, [prow, nch, ncols], BF16, name)
    bufs = []
    for c in range(nch):
        b = Buf(name)
        k.dma(eng, t[:, c, :], src2d[c * prow:(c + 1) * prow, :], b)
        bufs.append(b)
    return t, bufs


def sincos(k, st, arg, n, out_sin=None, out_cos=None, rb=(), wsin=None, wcos=None):
    for (dst, off, wb) in ((out_sin, 0.0, wsin), (out_cos, 0.25, wcos)):
        if dst is None:
            continue
        a2, ba2 = k.sb(st, [128, n], F32, "sc_a")
        ti, bti = k.sb(st, [128, n], I32, "sc_i")
        tf, btf = k.sb(st, [128, n], F32, "sc_f")
        k.ts("dve", a2[:], arg, off, ALU.add, r=list(rb), w=[ba2])
        k.copy("dve", ti[:], a2[:], r=[ba2], w=[bti])
        k.copy("dve", tf[:], ti[:], r=[bti], w=[btf])
        k.tt("dve", a2[:], a2[:], tf[:], ALU.subtract, r=[ba2, btf], w=[ba2])
        k.act(dst, a2[:], AF.Sin, r=[ba2], w=[wb], scale=SIN_SCALE)


def phase1(k, l, S, TS, x_src, di, sc, cst):
    P = k.P
    TT = TS
    NTT = S // TT
    with ExitStack() as st:
        ring = PsumRing(k, st)
        ident, bid = cst["ident"]
        gam, bgam = k.sb(st, [128, D], F32, "gam")
        k.dma("sp", gam[:], di["g_mix"][l], bgam)
        dvec, bdvec = k.sb(st, [128, 4], F32, "dvec")
        k.dma("sp", dvec[:], di["s_d"][l], bdvec)
        RFre, bRFre = k.sb(st, [128, 16, TS], F32, "RFre")
        RFim, bRFim = k.sb(st, [128, 16, TS], F32, "RFim")
        COSb, bCOSb = k.sb(st, [128, 16, TS], BF16, "COSb")
        SINb, bSINb = k.sb(st, [128, 16, TS], BF16, "SINb")
        NSINb, bNSINb = k.sb(st, [128, 16, TS], BF16, "NSINb")
        dec, bdec = k.sb(st, [128, 16], F32, "dec")
        cT, bcT = k.sb(st, [128, 16], F32, "cT")
        sT, bsT = k.sb(st, [128, 16], F32, "sT")
        nsT, bnsT = k.sb(st, [128, 16], F32, "nsT")
        with ExitStack() as s2:
            are, bare = k.sb(s2, [128, 16], F32, "are")
            aim, baim = k.sb(s2, [128, 16], F32, "aim")
            ldt, bldt = k.sb(s2, [128, 16], F32, "ldt")
            tau, btau = k.sb(s2, [128, TS], F32, "tau")
            k.dma("sp", are[:], di["s_are"][l], bare)
            k.dma("sp", aim[:], di["s_aim"][l], baim)
            k.dma("sp", ldt[:], di["s_ldt"][l], bldt)
            k.dma("sp", tau[:], di["c_tau"], btau)
            dt_, bdt = k.sb(s2, [128, 16], F32, "dt")
            k.act(dt_[:], ldt[:], AF.Exp, r=[bldt], w=[bdt])
            rho, brho = k.sb(s2, [128, 16], F32, "rho")
            thn, bthn = k.sb(s2, [128, 16], F32, "thn")
            k.tt("dve", rho[:], are[:], dt_[:], ALU.mult, r=[bare, bdt], w=[brho])
            k.tt("dve", thn[:], aim[:], dt_[:], ALU.mult, r=[baim, bdt], w=[bthn])
            k.ts("dve", thn[:], thn[:], 1.0 / TWO_PI, ALU.mult, r=[bthn], w=[bthn])
            k.act(dec[:], rho[:], AF.Exp, r=[brho], w=[bdec])
            s1, bs1 = k.sb(s2, [128, 16], F32, "s1")
            c1, bc1 = k.sb(s2, [128, 16], F32, "c1")
            sincos(k, s2, thn[:], 16, s1[:], c1[:], rb=[bthn], wsin=bs1, wcos=bc1)
            abre, babre = k.sb(s2, [128, 16], F32, "abre")
            abim, babim = k.sb(s2, [128, 16], F32, "abim")
            k.tt("dve", abre[:], dec[:], c1[:], ALU.mult, r=[bdec, bc1], w=[babre])
            k.ts("dve", abre[:], abre[:], -1.0, ALU.add, r=[babre], w=[babre])
            k.tt("dve", abim[:], dec[:], s1[:], ALU.mult, r=[bdec, bs1], w=[babim])
            den, bden = k.sb(s2, [128, 16], F32, "den")
            t0, bt0 = k.sb(s2, [128, 16], F32, "t0")
            k.tt("dve", den[:], are[:], are[:], ALU.mult, r=[bare], w=[bden])
            k.tt("dve", t0[:], aim[:], aim[:], ALU.mult, r=[baim], w=[bt0])
            k.tt("dve", den[:], den[:], t0[:], ALU.add, r=[bden, bt0], w=[bden])
            k.recip(den[:], den[:], r=[bden], w=[bden])
            fre, bfre = k.sb(s2, [128, 16], F32, "fre")
            fim, bfim = k.sb(s2, [128, 16], F32, "fim")
            t1, bt1 = k.sb(s2, [128, 16], F32, "t1")
            k.tt("dve", fre[:], abre[:], are[:], ALU.mult, r=[babre, bare], w=[bfre])
            k.tt("dve", t1[:], abim[:], aim[:], ALU.mult, r=[babim, baim], w=[bt1])
            k.tt("dve", fre[:], fre[:], t1[:], ALU.add, r=[bfre, bt1], w=[bfre])
            k.tt("dve", fre[:], fre[:], den[:], ALU.mult, r=[bfre, bden], w=[bfre])
            k.tt("dve", fim[:], abim[:], are[:], ALU.mult, r=[babim, bare], w=[bfim])
            k.tt("dve", t1[:], abre[:], aim[:], ALU.mult, r=[babre, baim], w=[bt1])
            k.tt("dve", fim[:], fim[:], t1[:], ALU.subtract, r=[bfim, bt1], w=[bfim])
            k.tt("dve", fim[:], fim[:], den[:], ALU.mult, r=[bfim, bden], w=[bfim])
            aT, baT = k.sb(s2, [128, 16], F32, "aT")
            k.ts("dve", aT[:], thn[:], float(TS), ALU.mult, r=[bthn], w=[baT])
            sincos(k, s2, aT[:], 16, sT[:], cT[:], rb=[baT], wsin=bsT, wcos=bcT)
            k.ts("dve", nsT[:], sT[:], -1.0, ALU.mult, r=[bsT], w=[bnsT])
            ANG, bANG = k.sb(s2, [128, 16, TS], F32, "ANG")
            SINf, bSINf = k.sb(s2, [128, 16 * TS], F32, "SINf")
            COSf, bCOSf = k.sb(s2, [128, 16 * TS], F32, "COSf")
            for j in range(16):
                k.ts("dve", ANG[:, j, :], tau[:], thn[:, j:j + 1], ALU.mult, r=[btau, bthn], w=[bANG])
            sincos(k, s2, ANG[:].rearrange("p j t -> p (j t)"), 16 * TS, SINf[:], COSf[:], rb=[bANG], wsin=bSINf, wcos=bCOSf)
            SIN3 = SINf[:].rearrange("p (j t) -> p j t", j=16)
            COS3 = COSf[:].rearrange("p (j t) -> p j t", j=16)
            tmp, btmp = k.sb(s2, [128, TS], F32, "tmp")
            for j in range(16):
                k.ts("dve", tmp[:], SIN3[:, j, :], fim[:, j:j + 1], ALU.mult, r=[bSINf, bfim], w=[btmp])
                k.stt(RFre[:, j, :], COS3[:, j, :], fre[:, j:j + 1], tmp[:], ALU.mult, ALU.add, r=[bCOSf, bfre, btmp], w=[bRFre])
                k.ts("dve", tmp[:], SIN3[:, j, :], fre[:, j:j + 1], ALU.mult, r=[bSINf, bfre], w=[btmp])
                k.stt(RFim[:, j, :], COS3[:, j, :], fim[:, j:j + 1], tmp[:], ALU.mult, ALU.subtract, r=[bCOSf, bfim, btmp], w=[bRFim])
            k.copy("dve", COSb[:].rearrange("p j t -> p (j t)"), COSf[:], r=[bCOSf], w=[bCOSb])
            k.copy("dve", SINb[:].rearrange("p j t -> p (j t)"), SINf[:], r=[bSINf], w=[bSINb])
            k.ts("dve", NSINb[:].rearrange("p j t -> p (j t)"), SINf[:], -1.0, ALU.mult, r=[bSINf], w=[bNSINb])
        P.barrier()
        w_in, bw_in = load_w(k, st, di["w_in"][l], 8, INW, name="w_in")
        w_sw, bw_sw = load_w(k, st, di["w_sw"][l], 8, 896, name="w_sw")
        w_glu, bw_glu = load_w(k, st, di["w_glu"][l], 4, 1024, name="w_glu")
        w_bs, bw_bs = load_w(k, st, di["w_bssm"][l], 4, 1024, name="w_bs")
        bre, bbre = load_w(k, st, di["s_bre"][l].rearrange("p j m -> p (j m)"), 1, 2048, name="bre")
        bim, bbim = load_w(k, st, di["s_bim"][l].rearrange("p j m -> p (j m)"), 1, 2048, name="bim")
        cre, bcre = load_w(k, st, di["s_cre"][l].rearrange("p j m -> p (j m)"), 1, 2048, name="cre")
        cim, bcim = load_w(k, st, di["s_cim"][l].rearrange("p j m -> p (j m)"), 1, 2048, name="cim")
        xt = [k.sb(st, [128, TT // 128, D], F32, "xt") for _ in range(2)]
        cosr = [k.sb(st, [128, TT], F32, "cosr") for _ in range(2)]
        sinr = [k.sb(st, [128, TT], F32, "sinr") for _ in range(2)]
        NSB = TT // 128
        junk, bjunk = k.sb(st, [128, D], BF16, "junk")
        ss, bss = k.sb(st, [128, NSB], F32, "ss")
        ms, bms = k.sb(st, [128, NSB], F32, "ms")
        sd, bsd = k.sb(st, [128, NSB], F32, "sd")
        rstd, brstd = k.sb(st, [128, NSB], F32, "rstd")
        hh = [k.sb(st, [128, D], BF16, "h") for _ in range(2)]
        hT, bhT = k.sb(st, [128, 8, TT], BF16, "hT")
        uT, buT = k.sb(st, [128, 4, TT], BF16, "uT")
        qTs, bqTs = k.sb(st, [128, 4, TT], BF16, "qTs")
        kvs = {nm: k.sb(st, [128, TT], BF16, nm) for nm in ("kcT", "vcT", "ksT", "kwT")}
        sgs, bsgs = k.sb(st, [128, 8, TT], BF16, "sgs")
        sgn, bsgn = k.sb(st, [128, 8, TT], BF16, "sgn")
        sg, bsg = k.sb(st, [24, TT], F32, "sg")
        vsel, bvsel = k.sb(st, [128, NSB, 2, 65], BF16, "vsel")
        vwin, bvwin = k.sb(st, [128, NSB, 2, 65], BF16, "vwin")
        k.memset("pool", vsel[:], 1.0, [bvsel])
        k.memset("pool", vwin[:], 1.0, [bvwin])
        tmps = [k.sb(st, [128, TT], F32, "tmp") for _ in range(6)]
        tmpi = [0]

        def gettmp():
            t = tmps[tmpi[0] % len(tmps)]
            tmpi[0] += 1
            return t
        bsc = [k.sb(st, [128, TT], F32, "bsc") for _ in range(4)]
        wsc = [k.sb(st, [128, TT], F32, "wsc") for _ in range(4)]
        xre, bxre = k.sb(st, [128, 16, TT], BF16, "xre")
        nxim, bnxim = k.sb(st, [128, 16, TT], BF16, "nxim")
        car, bcar = k.sb(st, [128, 2, 16], F32, "car")
        k.memset("dve", car[:], 0.0, [bcar])
        ctmp, bctmp = k.sb(st, [128, 2], F32, "ctmp")
        ypre, bypre = k.sb(st, [128, TT], F32, "ypre")
        yT, byT = k.sb(st, [128, 4, TT], BF16, "yT")
        sgz, bsgz = k.sb(st, [128, TT], F32, "sgz")
        zzT, bzzT = k.sb(st, [128, 4, TT], BF16, "zzT")
        gss, bgss = k.sb(st, [128, 8, TT], BF16, "gss")

        def load_tile(i):
            t, b = xt[i % 2]
            k.dma("sp", t[:], x_src[i * TT:(i + 1) * TT, :].rearrange("(s p) d -> p s d", p=128), b)
            k.dma("sp", cosr[i % 2][0][:], di["c_cos"][:, i * TT:(i + 1) * TT], cosr[i % 2][1])
            k.dma("sp", sinr[i % 2][0][:], di["c_sin"][:, i * TT:(i + 1) * TT], sinr[i % 2][1])

        def proj(wt, wb, col0, M=128):
            ps, bp = ring.get()
            for c in range(8):
                k.mm(ps[0:M, 0:TT], wt[:, c, col0:col0 + M], hT[:, c, :], c == 0, c == 7, r=[wb[c], bhT], w=[bp])
            return ps, bp

        load_tile(0)
        for i in range(NTT):
            if i + 1 < NTT:
                load_tile(i + 1)
            x_t, bx = xt[i % 2]
            cos_t, bcos = cosr[i % 2]
            sin_t, bsin = sinr[i % 2]
            tok = slice(i * TT, (i + 1) * TT)
            for s_ in range(NSB):
                h_t, bh = hh[s_ % 2]
                k.act(junk[:], x_t[:, s_, :], AF.Square, r=[bx], w=[bjunk, bss], accum=ss[:, s_:s_ + 1])
                k.ts("dve", ms[:, s_:s_ + 1], ss[:, s_:s_ + 1], 1.0 / D, ALU.mult, r=[bss], w=[bms], s2=EPS, op1=ALU.add)
                k.act(sd[:, s_:s_ + 1], ms[:, s_:s_ + 1], AF.Sqrt, r=[bms], w=[bsd])
                k.recip(rstd[:, s_:s_ + 1], sd[:, s_:s_ + 1], r=[bsd], w=[brstd])
                k.stt(h_t[:], x_t[:, s_, :], rstd[:, s_:s_ + 1], gam[:], ALU.mult, ALU.mult, r=[bx, brstd, bgam], w=[bh])
                ps, bp = ring.get()
                pbf = ps[:].bitcast(BF16)
                for c in range(8):
                    k.tr(pbf[:, c * 128:(c + 1) * 128], h_t[:, c * 128:(c + 1) * 128], ident[:], r=[bh, bid], w=[bp])
                k.copy("act", hT[:, :, s_ * 128:(s_ + 1) * 128], pbf.rearrange("p (c t) -> p c t", c=8), r=[bp], w=[bhT])
            for c4 in range(4):
                ps, bp = proj(w_in, bw_in, c4 * 128)
                k.copy("act", uT[:, c4, :], ps[:, 0:TT], r=[bp], w=[buT])
            def rope(col, swcol, dst, bdst):
                psA, bA = proj(w_in, bw_in, col)
                psB, bB = proj(w_sw, bw_sw, swcol)
                t1_, bt1_ = gettmp()
                t2_, bt2_ = gettmp()
                k.tt("dve", t1_[:], psA[:, 0:TT], cos_t[:], ALU.mult, r=[bA, bcos], w=[bt1_])
                k.tt("dve", t2_[:], psB[:, 0:TT], sin_t[:], ALU.mult, r=[bB, bsin], w=[bt2_])
                k.tt("pool", dst, t1_[:], t2_[:], ALU.add, r=[bt1_, bt2_], w=[bdst])
            for c in range(4):
                rope(512 + c * 128, c * 128, qTs[:, c, :], bqTs)
            rope(1024, 512, kvs["kcT"][0][:], kvs["kcT"][1])
            rope(1280, 640, kvs["ksT"][0][:], kvs["ksT"][1])
            rope(1536, 768, kvs["kwT"][0][:], kvs["kwT"][1])
            ps, bp = proj(w_in, bw_in, 1152)
            k.copy("act", kvs["vcT"][0][:], ps[:, 0:TT], r=[bp], w=[kvs["vcT"][1]])
            for c in range(8):
                ps, bp = proj(w_in, bw_in, 1816 + c * 128)
                k.act(sgs[:, c, :], ps[:, 0:TT], AF.Sigmoid, r=[bp], w=[bsgs])
            for c in range(8):
                ps, bp = proj(w_in, bw_in, 2840 + c * 128)
                k.act(sgn[:, c, :], ps[:, 0:TT], AF.Sigmoid, r=[bp], w=[bsgn])
            ps, bp = proj(w_in, bw_in, 1792, M=24)
            k.act(sg[:], ps[0:24, 0:TT], AF.Sigmoid, r=[bp], w=[bsg])
            for s_ in range(NSB):
                for (col, vt, bv) in ((1408, vsel, bvsel), (1664, vwin, bvwin)):
                    ps, bp = ring.get()
                    for c in range(8):
                        k.mm(ps[:, 0:128], hT[:, c, s_ * 128:(s_ + 1) * 128], w_in[:, c, col:col + 128], c == 0, c == 7, r=[bw_in[c], bhT], w=[bp])
                    k.copy("act", vt[:, s_, :, 0:64], ps[:, 0:128].rearrange("p (h d) -> p h d", h=2), r=[bp], w=[bv])
            k.dma("sp", sc["qT"].rearrange("(c p) s -> p c s", p=128)[:, :, tok], qTs[:], bqTs)
            for nm in ("kcT", "vcT", "ksT", "kwT"):
                k.dma("sp", sc[nm][:, tok], kvs[nm][0][:], kvs[nm][1])
            k.dma("sp", sc["sgnT"].rearrange("(c p) s -> p c s", p=128)[:, :, tok], sgn[:], bsgn)
            k.dma("sp", sc["sgT"][:, tok], sg[:], bsg)
            k.dma("sp", sc["vs"][tok].rearrange("(s p) h c -> p s h c", p=128), vsel[:], bvsel)
            k.dma("sp", sc["vw"][tok].rearrange("(s p) h c -> p s h c", p=128), vwin[:], bvwin)
            for j in range(16):
                c4 = j // 4
                psr, bpr = ring.get()
                psi, bpi = ring.get()
                k.mm(psr[:, 0:TT], bre[:, 0, j * 128:(j + 1) * 128], uT[:, c4, :], True, True, r=[bbre[0], buT], w=[bpr])
                k.mm(psi[:, 0:TT], bim[:, 0, j * 128:(j + 1) * 128], uT[:, c4, :], True, True, r=[bbim[0], buT], w=[bpi])
                b_re, bb_re = bsc[(2 * j) % 4]
                b_im, bb_im = bsc[(2 * j + 1) % 4]
                w_re, bw_re = wsc[(2 * j) % 4]
                w_im, bw_im = wsc[(2 * j + 1) % 4]
                t1_, bt1_ = gettmp()
                t2_, bt2_ = gettmp()
                k.tt("dve", t1_[:], psr[:, 0:TT], RFre[:, j, :], ALU.mult, r=[bpr, bRFre], w=[bt1_])
                k.tt("dve", t2_[:], psi[:, 0:TT], RFim[:, j, :], ALU.mult, r=[bpi, bRFim], w=[bt2_])
                k.tt("pool", b_re[:], t1_[:], t2_[:], ALU.subtract, r=[bt1_, bt2_], w=[bb_re])
                t3_, bt3_ = gettmp()
                t4_, bt4_ = gettmp()
                k.tt("dve", t3_[:], psi[:, 0:TT], RFre[:, j, :], ALU.mult, r=[bpi, bRFre], w=[bt3_])
                k.tt("dve", t4_[:], psr[:, 0:TT], RFim[:, j, :], ALU.mult, r=[bpr, bRFim], w=[bt4_])
                k.tt("pool", b_im[:], t3_[:], t4_[:], ALU.add, r=[bt3_, bt4_], w=[bb_im])
                dj = dec[:, j:j + 1].to_broadcast([128, TT])
                k.scan(w_re[:], dj, b_re[:], car[:, 0, j:j + 1], r=[bdec, bb_re, bcar], w=[bw_re])
                k.scan(w_im[:], dj, b_im[:], car[:, 1, j:j + 1], r=[bdec, bb_im, bcar], w=[bw_im])
                k.ts("dve", ctmp[:, 0:1], w_re[:, TT - 1:TT], cT[:, j:j + 1], ALU.mult, r=[bw_re, bcT], w=[bctmp])
                k.ts("dve", ctmp[:, 1:2], w_im[:, TT - 1:TT], cT[:, j:j + 1], ALU.mult, r=[bw_im, bcT], w=[bctmp])
                k.stt(car[:, 0, j:j + 1], w_im[:, TT - 1:TT], nsT[:, j:j + 1], ctmp[:, 0:1], ALU.mult, ALU.add, r=[bw_im, bnsT, bctmp], w=[bcar])
                k.stt(car[:, 1, j:j + 1], w_re[:, TT - 1:TT], sT[:, j:j + 1], ctmp[:, 1:2], ALU.mult, ALU.add, r=[bw_re, bsT, bctmp], w=[bcar])
                t5_, bt5_ = gettmp()
                t6_, bt6_ = gettmp()
                k.tt("dve", t5_[:], w_re[:], COSb[:, j, :], ALU.mult, r=[bw_re, bCOSb], w=[bt5_])
                k.tt("dve", t6_[:], w_im[:], SINb[:, j, :], ALU.mult, r=[bw_im, bSINb], w=[bt6_])
                k.tt("pool", xre[:, j, :], t5_[:], t6_[:], ALU.subtract, r=[bt5_, bt6_], w=[bxre])
                t7_, bt7_ = gettmp()
                t8_, bt8_ = gettmp()
                k.tt("dve", t7_[:], w_re[:], NSINb[:, j, :], ALU.mult, r=[bw_re, bNSINb], w=[bt7_])
                k.tt("dve", t8_[:], w_im[:], COSb[:, j, :], ALU.mult, r=[bw_im, bCOSb], w=[bt8_])
                k.tt("pool", nxim[:, j, :], t7_[:], t8_[:], ALU.subtract, r=[bt7_, bt8_], w=[bnxim])
            for c4 in range(4):
                ps, bp = ring.get()
                for jj in range(4):
                    j = 4 * c4 + jj
                    k.mm(ps[:, 0:TT], cre[:, 0, j * 128:(j + 1) * 128], xre[:, j, :], jj == 0, False, r=[bcre[0], bxre], w=[bp])
                    k.mm(ps[:, 0:TT], cim[:, 0, j * 128:(j + 1) * 128], nxim[:, j, :], False, jj == 3, r=[bcim[0], bnxim], w=[bp])
                k.stt(ypre[:], uT[:, c4, :], dvec[:, c4:c4 + 1], ps[:, 0:TT], ALU.mult, ALU.add, r=[buT, bdvec, bp], w=[bypre])
                k.act(yT[:, c4, :], ypre[:], AF.Gelu_apprx_tanh, r=[bypre], w=[byT])
            for kk in range(4):
                psg, bpg = ring.get()
                for c4 in range(4):
                    k.mm(psg[:, 0:TT], w_glu[:, c4, (4 + kk) * 128:(5 + kk) * 128], yT[:, c4, :], c4 == 0, c4 == 3, r=[bw_glu[c4], byT], w=[bpg])
                k.act(sgz[:], psg[:, 0:TT], AF.Sigmoid, r=[bpg], w=[bsgz])
                psv, bpv = ring.get()
                for c4 in range(4):
                    k.mm(psv[:, 0:TT], w_glu[:, c4, kk * 128:(kk + 1) * 128], yT[:, c4, :], c4 == 0, c4 == 3, r=[bw_glu[c4], byT], w=[bpv])
                k.tt("dve", zzT[:, kk, :], psv[:, 0:TT], sgz[:], ALU.mult, r=[bpv, bsgz], w=[bzzT])
            for fc in range(8):
                ps, bp = ring.get()
                for kk in range(4):
                    k.mm(ps[:, 0:TT], w_bs[:, kk, fc * 128:(fc + 1) * 128], zzT[:, kk, :], kk == 0, kk == 3, r=[bw_bs[kk], bzzT], w=[bp])
                k.tt("dve", gss[:, fc, :], ps[:, 0:TT], sgs[:, fc, :], ALU.mult, r=[bp, bsgs], w=[bgss])
            k.dma("sp", sc["gssT"].rearrange("(c p) s -> p c s", p=128)[:, :, tok], gss[:], bgss)
    P.barrier()


def phase2(k, pers, l, S, di, sc, cst):
    P = k.P
    NC = S // 16 - 1
    NCP = S // 16
    NCT = NCP // 128
    KcT, bKcT = k.sb(pers, [128, NCP], BF16, "KcT")
    Vc, bVc = k.sb(pers, [128, NCT, 2, 65], BF16, "Vc")
    k.memset("pool", KcT[:], 0.0, [bKcT])
    k.memset("pool", Vc[:], 1.0, [bVc])
    with ExitStack() as st:
        ring = PsumRing(k, st)
        for typ in ("k", "v"):
            with ExitStack() as s2:
                xT, bxT = k.sb(s2, [128, S], BF16, "cxT")
                k.dma("sp", xT[:], sc["kcT" if typ == "k" else "vcT"], bxT)
                w1, bw1 = k.sb(s2, [128, 32, 256], BF16, "w1")
                bw1b = Buf("w1b")
                src = di["c_w1" + typ][l].rearrange("(l d) c -> d l c", d=64)
                k.dma("pool", w1[0:64], src, bw1)
                k.dma("pool", w1[64:128], src, bw1b)
                pe2, bpe2 = k.sb(s2, [64, 32, 2], BF16, "pe2")
                pe_f, bpe_f = k.sb(s2, [64, 32], F32, "pe_f")
                k.dma("sp", pe_f[:], di["c_pe" + typ][l], bpe_f)
                k.copy("dve", pe2[:, :, 0], pe_f[:], r=[bpe_f], w=[bpe2])
                k.copy("dve", pe2[:, :, 1], pe_f[:], r=[bpe_f], w=[bpe2])
                b1, bb1 = k.sb(s2, [128, 2], F32, "b1")
                k.dma("sp", b1[:], di["c_b1" + typ][l], bb1)
                w2, bw2 = k.sb(s2, [128, 2, 64], BF16, "w2")
                k.dma("pool", w2[:], di["c_w2" + typ][l].rearrange("(c p) d -> p c d", p=128), bw2)
                bias, bbias = k.sb(s2, [128, 2], F32, "bias")
                for cc in range(2):
                    ps, bp = ring.get()
                    for li in range(32):
                        k.mm(ps[:, 0:2], w1[0:64, li, cc * 128:(cc + 1) * 128], pe2[:, li, :], li == 0, li == 31, r=[bw1, bpe2], w=[bp])
                    k.tt("dve", bias[:, cc:cc + 1], ps[:, 0:1], b1[:, cc:cc + 1], ALU.add, r=[bp, bb1], w=[bbias])
                for hk in range(2):
                    hid, bhid = k.sb(s2, [128, 2, NCP], BF16, "hid")
                    k.memset("pool", hid[:], 0.0, [bhid])
                    bw = bw1 if hk == 0 else bw1b
                    for cc in range(2):
                        ps, bp = ring.get()
                        for li in range(32):
                            k.mm(ps[:, 0:NC], w1[hk * 64:(hk + 1) * 64, li, cc * 128:(cc + 1) * 128],
                                 xT[hk * 64:(hk + 1) * 64, li:li + 16 * (NC - 1) + 1:16], li == 0, li == 31, r=[bw, bxT], w=[bp])
                        k.act(hid[:, cc, 0:NC], ps[:, 0:NC], AF.Gelu_apprx_tanh, r=[bp, bbias], w=[bhid], bias=bias[:, cc:cc + 1])
                    if typ == "k":
                        ps, bp = ring.get()
                        for cc in range(2):
                            k.mm(ps[hk * 64:(hk + 1) * 64, 0:NC], w2[:, cc, :], hid[:, cc, 0:NC], cc == 0, cc == 1, r=[bw2, bhid], w=[bp])
                        k.copy("act", KcT[hk * 64:(hk + 1) * 64, 0:NC], ps[hk * 64:(hk + 1) * 64, 0:NC], r=[bp], w=[bKcT])
                    else:
                        for nt in range(NCT):
                            ps, bp = ring.get()
                            for cc in range(2):
                                k.mm(ps[:, 0:64], hid[:, cc, nt * 128:(nt + 1) * 128], w2[:, cc, :], cc == 0, cc == 1, r=[bhid, bw2], w=[bp])
                            k.copy("act", Vc[:, nt, hk, 0:64], ps[:, 0:64], r=[bp], w=[bVc])
            P.barrier()
    if "dbg_kc" in sc:
        k.dma("sp", sc["dbg_kc"].rearrange("h d n -> (h d) n"), KcT[:], bKcT)
        k.dma("sp", sc["dbg_vc"], Vc[:], bVc)
    return (KcT, bKcT), (Vc, bVc)


def phase3(k, l, S, x_src, di, sc, cst, cmp_t):
    P = k.P
    NT = S // 128
    NCP = S // 16
    NCT = NCP // 128
    NB = S // 64
    (KcT, bKcT), (Vc, bVc) = cmp_t
    ident, bid = cst["ident"]
    with ExitStack() as st:
        ring = PsumRing(k, st, 3)
        ringO = PsumRing(k, st, 3)
        ringM = PsumRing(k, st, 2)
        KsM = []
        for hk in range(2):
            t, b = k.sb(st, [128, S], BF16, "KsM")
            b2 = Buf("KsMi")
            k.dma("sp", t[hk * 64:(hk + 1) * 64], sc["ksT"][hk * 64:(hk + 1) * 64, :], b)
            k.dma("pool", t[(1 - hk) * 64:(2 - hk) * 64], di["c_ind"], b2)
            KsM.append((t, b, b2))
        Vs, bVs = k.sb(st, [128, NT, 2, 65], BF16, "Vs")
        k.dma("sp", Vs[:], sc["vs"].rearrange("(n p) h c -> p n h c", p=128), bVs)

        def cload(name, shape, src, dt=BF16):
            t, b = k.sb(st, shape, dt, name)
            k.dma("pool" if dt == BF16 else "sp", t[:], src, b)
            return t, b
        caus, bcaus = cload("caus", [128, 128], di["c_caus"])
        low, blow = cload("low", [128, 128], di["c_low"])
        mgen, bmgen = cload("mgen", [128, 1024], di["c_mgen"])
        mt, bmt = cload("mt", [128, 16, 128], di["c_mt"])
        G, bG = cload("G", [128, 256], di["c_g"], F32)
        wbn, bwbn = cload("wbn", [64, 8, 1024], di["w_bnsa"][l].rearrange("(h d) n -> d h n", d=64))
        w_out, bw_out = load_w(k, st, di["w_out"][l], 8, 1024, name="w_out")
        ones16, bones = k.sb(st, [128, 64], BF16, "ones16")
        k.memset("pool", ones16[:], 1.0, [bones])
        rhis = [k.sb(st, [65, 512], BF16, "rhi") for _ in range(4)]
        rlos = [k.sb(st, [65, 512], BF16, "rlo") for _ in range(4)]
        QT = [[k.sb(st, [128, 4, 128], BF16, "QT") for _ in range(2)] for _ in range(2)]
        for pb_ in range(2):
            for hk_ in range(2):
                k.memset("pool", QT[pb_][hk_][0][:], 0.0, [QT[pb_][hk_][1]])
        NHALF = max(1, NB // 64)
        Qsel = [[[k.sb(st, [128, 4, 128], BF16, "Qsel") + (Buf("Qselm"),) for _ in range(NHALF)] for _ in range(2)] for _ in range(2)]
        negm_sw, bnegm_sw = k.sb(st, [128, 128], BF16, "negm_sw")
        k.memset("pool", negm_sw[:], 0.0, [bnegm_sw])
        KwT = [k.sb(st, [128, 640], BF16, "KwT") for _ in range(2)]
        Vw = [k.sb(st, [128, 5, 2, 65], BF16, "Vw") for _ in range(2)]
        grow = [k.sb(st, [65, 2, 3, 4, 128], F32, "grow") for _ in range(2)]
        sgn = [k.sb(st, [128, 8, 128], BF16, "sgn") for _ in range(2)]
        gss = [k.sb(st, [128, 8, 128], BF16, "gss") for _ in range(2)]
        xin = [k.sb(st, [128, D], F32, "xin") for _ in range(2)]
        eg = [k.sb(st, [128, NCP], F32, "eg") for _ in range(4)]
        den4, bden4 = k.sb(st, [128, 4], F32, "den4")
        rden4, brden4 = k.sb(st, [128, 4], F32, "rden4")
        pg, bpg = k.sb(st, [128, NCP + 8], F32, "pg")
        k.memset("pool", pg[:], 0.0, [bpg])
        blk, bblk = k.sb(st, [128, NB], F32, "blk")
        blk2, bblk2 = k.sb(st, [128, NB], F32, "blk2")
        m8, bm8 = k.sb(st, [128, 16], F32, "m8")
        negm, bnegm = k.sb(st, [128, 128], BF16, "negm")
        k.memset("pool", negm[:], 0.0, [bnegm])
        pTs = [k.sb(st, [128, 512], BF16, "pT") for _ in range(5)]
        pti = [0]
        rrs = [k.sb(st, [65, 512], F32, "rr") for _ in range(4)]
        osbs = [k.sb(st, [64, 512], F32, "osb") for _ in range(4)]
        oacc, boacc = k.sb(st, [64, 512], F32, "oacc")
        otmp, botmp = k.sb(st, [64, 512], F32, "otmp")
        oTb = [k.sb(st, [64, 4, 128], BF16, "oTb") for _ in range(2)]
        mrg, bmrg = k.sb(st, [128, 8, 128], BF16, "mrg")
        xm = [k.sb(st, [128, D], F32, "xm") for _ in range(2)]

        def loads(qb):
            s0 = qb * 128
            pb = qb % 2
            qv = sc["qT"].rearrange("(h d) s -> d h s", d=64)
            for hk in range(2):
                t, b = QT[pb][hk]
                k.dma("sp", t[hk * 64:(hk + 1) * 64], qv[:, hk * 4:(hk + 1) * 4, s0:s0 + 128], b)
                for hf in range(NHALF):
                    if hf * 32 <= qb:
                        t, b, _ = Qsel[pb][hk][hf]
                        k.dma("sp", t[hk * 64:(hk + 1) * 64], qv[:, hk * 4:(hk + 1) * 4, s0:s0 + 128], b)
            lo = max(0, s0 - 512)
            t, b = KwT[pb]
            k.dma("sp", t[:, 640 - (s0 + 128 - lo):640], sc["kwT"][:, lo:s0 + 128], b)
            nw = (s0 + 128 - lo) // 128
            t, b = Vw[pb]
            k.dma("sp", t[:, 5 - nw:5], sc["vw"][lo:s0 + 128].rearrange("(n p) h c -> p n h c", p=128), b)
            t, b = grow[pb]
            gv = sc["sgT"].rearrange("(hk g br) s -> hk br g s", hk=2, g=4, br=3)
            for hk in range(2):
                for br in range(3):
                    k.dma("sp", t[64:65, hk, br], gv[hk, br:br + 1, :, s0:s0 + 128], b)
            t, b = sgn[pb]
            k.dma("sp", t[:], sc["sgnT"].rearrange("(c p) s -> p c s", p=128)[:, :, s0:s0 + 128], b)
            t, b = gss[pb]
            k.dma("sp", t[:], sc["gssT"].rearrange("(c p) s -> p c s", p=128)[:, :, s0:s0 + 128], b)
            t, b = xin[pb]
            k.dma("sp", t[:], x_src[s0:s0 + 128, :], b)

        DEPTH = 2
        pipe = []
        delayed = []

        def tick():
            for d in delayed:
                d[0] -= 1
            while delayed and delayed[0][0] <= 0:
                delayed.pop(0)[1]()

        cur_tag = [0]

        def push(score_fn, pv_fn, after=None):
            tok_ = score_fn()
            pipe.append((pv_fn, tok_, after, cur_tag[0]))
            if len(pipe) > DEPTH:
                pv, tk, af, _ = pipe.pop(0)
                pv(tk)
                if af is not None:
                    af()
            tick()

        def flush():
            while pipe:
                pv, tk, af, _ = pipe.pop(0)
                pv(tk)
                if af is not None:
                    af()
            while delayed:
                delayed.pop(0)[1]()

        def attn_tile(Ops, bO, first, last_, KT_ap, bKT, V_ap, bV, Q2, bQ, smask=None, emask=None, after=None):
            def score():
                psS, bS = ring.get()
                nmask = (4 if smask is not None else 0) + (1 if emask is not None else 0)
                rl = (bKT if isinstance(bKT, list) else [bKT]) + (bQ if isinstance(bQ, list) else [bQ])
                k.mm(psS[:, 0:512], KT_ap, Q2, True, nmask == 0, r=rl, w=[bS])
                done = 0
                assert emask is None
                if smask is not None:
                    m_ap, bm = smask
                    for g in range(4):
                        done += 1
                        k.mm(psS[:, g * 128:(g + 1) * 128], ident[:], m_ap, False, done == nmask, r=[bid, bm], w=[bS])
                pT, bpT = pTs[pti[0] % len(pTs)]
                pti[0] += 1
                k.act(pT[:], psS[:, 0:512], AF.Exp, r=[bS], w=[bpT], scale=0.125)
                return (pT, bpT)

            def pv(tk):
                pT, bpT = tk
                k.mm(Ops[0:65, 0:512], V_ap, pT[:], first, last_, r=[bV, bpT], w=[bO])
            push(score, pv, after)

        fin_i = [0]

        def finalize(Ops, bO, gate_ap, bgate, first_branch, out_final=None, bout=None):
            tag_ = cur_tag[0]

            def stage_a():
                rr, brr = rrs[fin_i[0] % 4]
                rhi, brhi = rhis[fin_i[0] % 4]
                rlo, brlo = rlos[fin_i[0] % 4]
                osb, bosb = osbs[fin_i[0] % 4]
                fin_i[0] += 1
                k.ts("dve", rr[64:65, :], Ops[64:65, 0:512], 1e-20, ALU.max, r=[bO], w=[brr])
                k.recip(rr[64:65, :], rr[64:65, :], r=[brr], w=[brr])
                k.tt("dve", rr[64:65, :], rr[64:65, :], gate_ap, ALU.mult, r=[brr, bgate], w=[brr])
                k.copy("dve", rhi[64:65, :], rr[64:65, :], r=[brr], w=[brhi])
                k.tt("dve", rlo[64:65, :], rr[64:65, :], rhi[64:65, :], ALU.subtract, r=[brr, brhi], w=[brlo])
                k.copy("act", osb[:], Ops[0:64, 0:512], r=[bO], w=[bosb])

                def stage_b():
                    psb, bpb = ringM.get()
                    k.mm(psb[0:64, 0:512], ones16[64:65, 0:64], rhi[64:65, :], True, False, r=[bones, brhi], w=[bpb])
                    k.mm(psb[0:64, 0:512], ones16[64:65, 0:64], rlo[64:65, :], False, True, r=[bones, brlo], w=[bpb])
                    if first_branch:
                        k.tt("dve", oacc[:], osb[:], psb[0:64, 0:512], ALU.mult, r=[bosb, bpb], w=[boacc])
                    else:
                        k.tt("dve", otmp[:], osb[:], psb[0:64, 0:512], ALU.mult, r=[bosb, bpb], w=[botmp])
                        if out_final is None:
                            k.tt("pool", oacc[:], oacc[:], otmp[:], ALU.add, r=[boacc, botmp], w=[boacc])
                        else:
                            k.tt("pool", out_final, oacc[:], otmp[:], ALU.add, r=[boacc, botmp], w=[bout])
                delayed.append([3, stage_b, tag_])
            return stage_a

        mtmps = [k.sb(st, [128, 128], F32, "mtmp") for _ in range(4)]

        def epilogue1(qb):
            pb = qb % 2
            if "dbg_o" in sc:
                for hk in range(2):
                    k.dma("sp", sc["dbg_o"][hk, :, qb], oTb[hk][0][:].rearrange("d g q -> d (g q)"), oTb[hk][1])
            sg_t, bsgn_ = sgn[pb]
            gs_t, bgs_ = gss[pb]
            for half in range(2):
                ps, bp = ringM.get()
                for f4 in range(4):
                    fc = half * 4 + f4
                    for h in range(8):
                        k.mm(ps[:, f4 * 128:(f4 + 1) * 128], wbn[:, h, fc * 128:(fc + 1) * 128], oTb[h // 4][0][:, h % 4, :],
                             h == 0, h == 7, r=[bwbn, oTb[h // 4][1]], w=[bp])
                for f4 in range(4):
                    fc = half * 4 + f4
                    mt_, bmt_ = mtmps[fc % 4]
                    k.tt("dve", mt_[:], ps[:, f4 * 128:(f4 + 1) * 128], sg_t[:, fc, :], ALU.mult, r=[bp, bsgn_], w=[bmt_])
                    k.tt("pool", mrg[:, fc, :], mt_[:], gs_t[:, fc, :], ALU.add, r=[bmt_, bgs_], w=[bmrg])
            delayed.append([8, lambda: epilogue2(qb), qb])

        def epilogue2(qb):
            s0 = qb * 128
            pb = qb % 2
            x_t, bx = xin[pb]
            xm_t, bxm = xm[qb % 2]
            for half in range(2):
                ps, bp = ringM.get()
                for fc in range(8):
                    k.mm(ps[:, 0:512], mrg[:, fc, :], w_out[:, fc, half * 512:(half + 1) * 512], fc == 0, fc == 7, r=[bmrg, bw_out[fc]], w=[bp])
                k.tt("dve", xm_t[:, half * 512:(half + 1) * 512], ps[:, 0:512], x_t[:, half * 512:(half + 1) * 512], ALU.add, r=[bp, bx], w=[bxm])
            k.dma("sp", sc["xmid"][s0:s0 + 128, :], xm_t[:], bxm)

        def force(tag_max):
            while pipe and pipe[0][3] <= tag_max:
                pv, tk, af, _ = pipe.pop(0)
                pv(tk)
                if af is not None:
                    af()
            progressed = True
            while progressed:
                progressed = False
                for idx_, d_ in enumerate(delayed):
                    if d_[2] <= tag_max:
                        delayed.pop(idx_)
                        d_[1]()
                        progressed = True
                        break

        def chainA(qb, hk):
            pb = qb % 2
            Qt, bQ = QT[pb][hk]
            for g in range(4):
                ps, bp = ringM.get()
                k.mm(ps[:, 0:NCP], Qt[:, g, :], KcT[:, 0:NCP], True, False, r=[bQ, bKcT], w=[bp])
                k.mm(ps[:, 0:NCP], ident[:], mgen[:, 512 - 8 * qb:512 - 8 * qb + NCP], False, True, r=[bid, bmgen], w=[bp])
                k.act(eg[g][0][:], ps[:, 0:NCP], AF.Exp, r=[bp], w=[eg[g][1], bden4], scale=0.125, accum=den4[:, g:g + 1])
            k.ts("dve", rden4[:], den4[:], 1e-20, ALU.max, r=[bden4], w=[brden4])
            k.recip(rden4[:], rden4[:], r=[brden4], w=[brden4])
            k.ts("dve", pg[:, 1:1 + NCP], eg[0][0][:], rden4[:, 0:1], ALU.mult, r=[eg[0][1], brden4], w=[bpg])
            for g in range(1, 4):
                k.stt(pg[:, 1:1 + NCP], eg[g][0][:], rden4[:, g:g + 1], pg[:, 1:1 + NCP], ALU.mult, ALU.add, r=[eg[g][1], brden4, bpg], w=[bpg])
            P.op("dve", lambda e: e.tensor_reduce(out=blk[:], in_=pg[:, 0:NCP].rearrange("p (j o) -> p j o", o=4), axis=AX.X, op=ALU.add), reads=[bpg], writes=[bblk])
            k.tt("dve", blk[:], blk[:], pg[:, 4:4 + 4 * NB:4], ALU.add, r=[bblk, bpg], w=[bblk])
            k.tt("dve", blk[:], blk[:], G[:, 126 - 2 * qb:126 - 2 * qb + NB], ALU.add, r=[bblk, bG], w=[bblk])
            if qb >= 1:
                k.ts("dve", blk[:, 0:1], blk[:, 0:1], 1e4, ALU.add, r=[bblk], w=[bblk])
            P.op("dve", lambda e: e.max(out=m8[:, 0:8], in_=blk[:]), reads=[bblk], writes=[bm8])
            P.op("dve", lambda e: e.match_replace(out=blk2[:], in_to_replace=m8[:, 0:8], in_values=blk[:], imm_value=-3e38), reads=[bblk, bm8], writes=[bblk2])
            P.op("dve", lambda e: e.max(out=m8[:, 8:16], in_=blk2[:]), reads=[bblk2], writes=[bm8])
            nhalf_used = 1 if qb < 32 else NHALF
            need_nat = (hk == 1) or nhalf_used > 1
            need_sw = (hk == 0) or nhalf_used > 1
            if need_nat:
                k.ts("dve", negm[:, 0:NB], blk[:], m8[:, 15:16], ALU.is_lt, r=[bblk, bm8], w=[bnegm], s2=NEG, op1=ALU.mult)
            if need_sw:
                n0 = min(NB, 64)
                k.ts("dve", negm_sw[:, 64:64 + n0], blk[:, 0:n0], m8[:, 15:16], ALU.is_lt, r=[bblk, bm8], w=[bnegm_sw], s2=NEG, op1=ALU.mult)
                if NB > 64:
                    k.ts("dve", negm_sw[:, 0:NB - 64], blk[:, 64:NB], m8[:, 15:16], ALU.is_lt, r=[bblk, bm8], w=[bnegm_sw], s2=NEG, op1=ALU.mult)

        def chainB(qb, hk):
            pb = qb % 2
            os_ = slice((1 - hk) * 64, (2 - hk) * 64)
            nhalf_used = 1 if qb < 32 else NHALF
            for hf in range(nhalf_used):
                use_sw = (hk == 0 and hf == 0) or (hk == 1 and hf == 1)
                src_t, bsrc = (negm_sw, bnegm_sw) if use_sw else (negm, bnegm)
                ps, bp = ringM.get()
                pbf = ps[:].bitcast(BF16)
                k.tr(pbf[:, 0:128], src_t[:], ident[:], r=[bsrc, bid], w=[bp])
                qs_t, _, bqm = Qsel[pb][hk][hf]
                for g in range(4):
                    k.copy("dve", qs_t[os_, g, :], pbf[os_, 0:128], r=[bp], w=[bqm])

        loads(0)
        chainA(0, 0)
        for qb in range(NT):
            s0 = qb * 128
            pb = qb % 2
            for hk in range(2):
                cur_tag[0] = qb
                Qt, bQ = QT[pb][hk]
                Q2 = Qt[:].rearrange("d g q -> d (g q)")
                gr, bgr = grow[pb]
                chainB(qb, hk)
                Ops, bO = ringO.get()
                fa = finalize(Ops, bO, gr[64:65, hk, 0].rearrange("o g q -> o (g q)"), bgr, True)
                tiles = [nt for nt in range(NCT) if qb - 16 * nt >= 0]
                for idx, nt in enumerate(tiles):
                    m = qb - 16 * nt
                    sm = (mt[:, m, :], bmt) if m < 16 else None
                    attn_tile(Ops, bO, idx == 0, idx == len(tiles) - 1, KcT[:, nt * 128:(nt + 1) * 128], bKcT,
                              Vc[:, nt, hk, :], bVc, Q2, bQ, smask=sm, after=fa if idx == len(tiles) - 1 else None)
                Ops, bO = ringO.get()
                fa = finalize(Ops, bO, gr[64:65, hk, 2].rearrange("o g q -> o (g q)"), bgr, False)
                tiles = [wt for wt in range(5) if s0 - 512 + 128 * wt >= 0]
                kw_full, bkw = KwT[pb]
                kw_t = kw_full
                vw_t, bvw = Vw[pb]
                for idx, wt in enumerate(tiles):
                    sm = (low[:], blow) if wt == 0 else ((caus[:], bcaus) if wt == 4 else None)
                    attn_tile(Ops, bO, idx == 0, idx == len(tiles) - 1, kw_t[:, wt * 128:(wt + 1) * 128], bkw,
                              vw_t[:, wt, hk, :], bvw, Q2, bQ, smask=sm, after=fa if idx == len(tiles) - 1 else None)
                if hk == 1 and qb + 1 < NT:
                    force(qb - 1)
                    loads(qb + 1)
                if hk == 0:
                    chainA(qb, 1)
                elif qb + 1 < NT:
                    chainA(qb + 1, 0)
                Ops, bO = ringO.get()
                fa = finalize(Ops, bO, gr[64:65, hk, 1].rearrange("o g q -> o (g q)"), bgr, False,
                              out_final=oTb[hk][0][:].rearrange("d g q -> d (g q)"), bout=oTb[hk][1])
                for i in range(qb + 1):
                    sm = (caus[:], bcaus) if i == qb else None
                    qs_t, bqs, bqm = Qsel[pb][hk][i // 32]
                    attn_tile(Ops, bO, i == 0, i == qb, KsM[hk][0][:, i * 128:(i + 1) * 128], [KsM[hk][1], KsM[hk][2]],
                              Vs[:, i, hk, :], bVs, qs_t[:].rearrange("d g q -> d (g q)"), [bqs, bqm], smask=sm, after=fa if i == qb else None)
                if hk == 1:
                    delayed_ep = (lambda q_=qb: (lambda: delayed.append([4, lambda: epilogue1(q_), q_])))(qb)
                    pipe[-1] = (pipe[-1][0], pipe[-1][1], (lambda f1=pipe[-1][2], f2=delayed_ep: (f1(), f2())), pipe[-1][3])
        flush()
    P.barrier()


def phase4(k, l, S, di, sc, cst, dst, last):
    P = k.P
    TT = 256
    NTT = S // TT
    NSB = TT // 128
    ident, bid = cst["ident"]
    with ExitStack() as st:
        ring = PsumRing(k, st)
        w_fin, bw_fin = load_w(k, st, di["w_fin"][l], 8, 2 * DFF, name="w_fin")
        w_fo, bw_fo = load_w(k, st, di["w_fout"][l], NFC, D, name="w_fo")
        gam, bgam = k.sb(st, [128, D], F32, "gam")
        k.dma("sp", gam[:], di["g_ffn"][l], bgam)
        cw, bcw = k.sb(st, [128, NFC, 3], F32, "cw")
        k.dma("sp", cw[:], di["f_cw"][l], bcw)
        cb, bcb = k.sb(st, [128, NFC], F32, "cb")
        k.dma("sp", cb[:], di["f_cb"][l], bcb)
        if last:
            gfin, bgfin = k.sb(st, [128, D], F32, "gfin")
            k.dma("sp", gfin[:], di["g_fin"], bgfin)
        halo, bhalo = k.sb(st, [128, NFC, 2], F32, "halo")
        k.memset("pool", halo[:], 0.0, [bhalo])
        xt = [k.sb(st, [128, NSB, D], F32, "xt") for _ in range(2)]
        junk, bjunk = k.sb(st, [128, D], BF16, "junk")
        ss, bss = k.sb(st, [128, 2 * NSB], F32, "ss")
        ms, bms = k.sb(st, [128, 2 * NSB], F32, "ms")
        sd, bsd = k.sb(st, [128, 2 * NSB], F32, "sd")
        rstd, brstd = k.sb(st, [128, 2 * NSB], F32, "rstd")
        hh = [k.sb(st, [128, D], BF16, "h") for _ in range(2)]
        hT, bhT = k.sb(st, [128, 8, TT], BF16, "hT")
        a_sb = [k.sb(st, [128, TT + 2], F32, "a_sb") for _ in range(2)]
        cv = [k.sb(st, [128, TT], F32, "cv") for _ in range(2)]
        gl = [k.sb(st, [128, TT], F32, "gl") for _ in range(2)]
        actT, bactT = k.sb(st, [128, NFC, TT], BF16, "actT")
        xo = [k.sb(st, [128, NSB, D], F32, "xo") for _ in range(2)]

        def load_tile(i):
            t, b = xt[i % 2]
            k.dma("sp", t[:], sc["xmid"][i * TT:(i + 1) * TT, :].rearrange("(s p) d -> p s d", p=128), b)

        def rms(x_ap, bx, col, g_t, bg, out_ap, bout):
            k.act(junk[:], x_ap, AF.Square, r=[bx], w=[bjunk, bss], accum=ss[:, col:col + 1])
            k.ts("dve", ms[:, col:col + 1], ss[:, col:col + 1], 1.0 / D, ALU.mult, r=[bss], w=[bms], s2=EPS, op1=ALU.add)
            k.act(sd[:, col:col + 1], ms[:, col:col + 1], AF.Sqrt, r=[bms], w=[bsd])
            k.recip(rstd[:, col:col + 1], sd[:, col:col + 1], r=[bsd], w=[brstd])
            k.stt(out_ap, x_ap, rstd[:, col:col + 1], g_t[:], ALU.mult, ALU.mult, r=[bx, brstd, bg], w=[bout])

        load_tile(0)
        for i in range(NTT):
            if i + 1 < NTT:
                load_tile(i + 1)
            x_t, bx = xt[i % 2]
            for s_ in range(NSB):
                h_t, bh = hh[s_ % 2]
                rms(x_t[:, s_, :], bx, s_, gam, bgam, h_t[:], bh)
                ps, bp = ring.get()
                pbf = ps[:].bitcast(BF16)
                for c in range(8):
                    k.tr(pbf[:, c * 128:(c + 1) * 128], h_t[:, c * 128:(c + 1) * 128], ident[:], r=[bh, bid], w=[bp])
                k.copy("act", hT[:, :, s_ * 128:(s_ + 1) * 128], pbf.rearrange("p (c t) -> p c t", c=8), r=[bp], w=[bhT])
            for fc in range(NFC):
                psa, bpa = ring.get()
                for c in range(8):
                    k.mm(psa[:, 0:TT], w_fin[:, c, fc * 128:(fc + 1) * 128], hT[:, c, :], c == 0, c == 7, r=[bw_fin[c], bhT], w=[bpa])
                psb, bpb = ring.get()
                for c in range(8):
                    k.mm(psb[:, 0:TT], w_fin[:, c, DFF + fc * 128:DFF + (fc + 1) * 128], hT[:, c, :], c == 0, c == 7, r=[bw_fin[c], bhT], w=[bpb])
                a_t, ba = a_sb[fc % 2]
                c_t, bc = cv[fc % 2]
                g_t, bg = gl[fc % 2]
                k.copy("pool", a_t[:, 0:2], halo[:, fc, :], r=[bhalo], w=[ba])
                k.copy("act", a_t[:, 2:2 + TT], psa[:, 0:TT], r=[bpa], w=[ba])
                k.copy("pool", halo[:, fc, :], a_t[:, TT:TT + 2], r=[ba], w=[bhalo])
                k.ts("dve", c_t[:], a_t[:, 2:2 + TT], cw[:, fc, 2:3], ALU.mult, r=[ba, bcw, bcb], w=[bc], s2=cb[:, fc:fc + 1], op1=ALU.add)
                k.stt(c_t[:], a_t[:, 1:1 + TT], cw[:, fc, 1:2], c_t[:], ALU.mult, ALU.add, r=[ba, bcw, bc], w=[bc])
                k.stt(c_t[:], a_t[:, 0:TT], cw[:, fc, 0:1], c_t[:], ALU.mult, ALU.add, r=[ba, bcw, bc], w=[bc])
                k.act(g_t[:], c_t[:], AF.Gelu_apprx_tanh, r=[bc], w=[bg])
                k.tt("dve", actT[:, fc, :], psb[:, 0:TT], g_t[:], ALU.mult, r=[bpb, bg], w=[bactT])
            xo_t, bxo = xo[i % 2]
            for s_ in range(NSB):
                for half in range(2):
                    ps, bp = ring.get()
                    for fc in range(NFC):
                        k.mm(ps[:, 0:512], actT[:, fc, s_ * 128:(s_ + 1) * 128], w_fo[:, fc, half * 512:(half + 1) * 512], fc == 0, fc == NFC - 1,
                             r=[bactT, bw_fo[fc]], w=[bp])
                    k.tt("dve", xo_t[:, s_, half * 512:(half + 1) * 512], ps[:, 0:512], x_t[:, s_, half * 512:(half + 1) * 512], ALU.add, r=[bp, bx], w=[bxo])
            if last:
                for s_ in range(NSB):
                    rms(xo_t[:, s_, :], bxo, NSB + s_, gfin, bgfin, xo_t[:, s_, :], bxo)
            k.dma("sp", dst[i * TT:(i + 1) * TT, :].rearrange("(s p) d -> p s d", p=128), xo_t[:], bxo)
    P.barrier()


INPUT_SHAPES = None


def build(S, L, TS=128, dbg=False, phases=("p1", "p2", "p3", "p4")):
    nc = bass.Bass("TRN2", target_bir_lowering=False)
    NT = S // 128
    di = {}

    def din(name, shape):
        di[name] = nc.dram_tensor(name, list(shape), F32, kind="ExternalInput").ap()
    din("x", [S, D])
    for nm, shp in (("g_mix", [L, 128, D]), ("g_ffn", [L, 128, D]), ("g_fin", [128, D]),
                    ("w_in", [L, D, INW]), ("w_sw", [L, D, 896]),
                    ("s_are", [L, 128, 16]), ("s_aim", [L, 128, 16]), ("s_ldt", [L, 128, 16]),
                    ("s_bre", [L, 128, 16, 128]), ("s_bim", [L, 128, 16, 128]),
                    ("s_cre", [L, 128, 16, 128]), ("s_cim", [L, 128, 16, 128]), ("s_d", [L, 128, 4]),
                    ("w_glu", [L, 512, 1024]), ("w_bssm", [L, 512, 1024]), ("w_bnsa", [L, 512, 1024]),
                    ("w_out", [L, D, D]),
                    ("c_pek", [L, 64, 32]), ("c_w1k", [L, 2048, 256]), ("c_b1k", [L, 128, 2]), ("c_w2k", [L, 256, 64]),
                    ("c_pev", [L, 64, 32]), ("c_w1v", [L, 2048, 256]), ("c_b1v", [L, 128, 2]), ("c_w2v", [L, 256, 64]),
                    ("w_fin", [L, D, 2 * DFF]), ("w_fout", [L, DFF, D]), ("f_cw", [L, 128, NFC, 3]), ("f_cb", [L, 128, NFC]),
                    ("c_cos", [128, S]), ("c_sin", [128, S]), ("c_ident", [128, 128]), ("c_tau", [128, TS]),
                    ("c_caus", [128, 128]), ("c_low", [128, 128]), ("c_mgen", [128, 1024]), ("c_mt", [128, 16, 128]),
                    ("c_g", [128, 256]), ("c_ind", [64, S])):
        din(nm, shp)
    out = nc.dram_tensor("out", [S, D], F32, kind="ExternalOutput").ap()
    skind = "ExternalOutput" if dbg else "Internal"
    sc = {}

    def scr(name, shape, dt):
        sc[name] = nc.dram_tensor(name, list(shape), dt, kind=skind).ap()
    scr("qT", [512, S], BF16)
    for nm in ("kcT", "vcT", "ksT", "kwT"):
        scr(nm, [128, S], BF16)
    scr("vs", [S, 2, 65], BF16)
    scr("vw", [S, 2, 65], BF16)
    scr("sgT", [24, S], F32)
    scr("sgnT", [1024, S], BF16)
    scr("gssT", [1024, S], BF16)
    scr("xmid", [S, D], F32)
    if dbg:
        scr("dbg_o", [2, 64, NT, 512], BF16)
        scr("dbg_kc", [2, 64, S // 16], BF16)
        scr("dbg_vc", [128, S // 2048, 2, 65], BF16)
    scr("x1", [S, D], F32)
    with ExitStack() as st:
        P = Prog(nc)
        k = K(nc, P)
        cst = {}
        ident, bid = k.sb(st, [128, 128], BF16, "ident")
        k.dma("pool", ident[:], di["c_ident"], bid)
        cst["ident"] = (ident, bid)
        x_src = di["x"]
        for l in range(L):
            last = l == L - 1
            if "p1" in phases:
                phase1(k, l, S, TS, x_src, di, sc, cst)
            if "p2" in phases:
                pers = ExitStack()
                cmp_t = phase2(k, pers, l, S, di, sc, cst)
            if "p3" in phases:
                phase3(k, l, S, x_src, di, sc, cst, cmp_t)
            if "p2" in phases:
                pers.close()
                P.barrier()
            if "p4" in phases:
                phase4(k, l, S, di, sc, cst, out if last else sc["x1"], last)
            x_src = sc["x1"]
        P.barrier()
        P.emit(st)
    return nc


_NC_CACHE = {}


def kernel(**inputs):
    S, L, NCORES = 8192, 2, 8
    inp = {k_: np.asarray(v) for k_, v in inputs.items()}
    hl = host_layout(inp, L)
    hc = host_consts(S, 128)
    common = {}
    common.update(hl)
    common.update(hc)
    common = {k_: np.ascontiguousarray(v, dtype=np.float32) for k_, v in common.items()}
    if "nc" not in _NC_CACHE:
        _NC_CACHE["nc"] = build(S, L)
    nc = _NC_CACHE["nc"]
    x = np.asarray(inp["x"], dtype=np.float32)
    in_maps = []
    for b in range(NCORES):
        m = dict(common)
        m["x"] = np.ascontiguousarray(x[b])
        in_maps.append(m)
    res = run_bass_kernel_spmd(nc, in_maps, core_ids=list(range(NCORES)))
    return np.stack([np.asarray(r["out"], dtype=np.float32) for r in res.results], axis=0)
```

```python
from contextlib import ExitStack
import numpy as np
import ml_dtypes
import concourse.bass as bass
import concourse.mybir as mybir
from concourse.bass_utils import run_bass_kernel_spmd

F32 = mybir.dt.float32
BF16 = mybir.dt.bfloat16
I32 = mybir.dt.int32
ALU = mybir.AluOpType
AF = mybir.ActivationFunctionType
AX = mybir.AxisListType

D = 1024
DFF = 2816
NFC = DFF // 128
INW = 3864
EPS = 1e-6
NEG = -30000.0
TWO_PI = float(2 * np.pi)
SIN_SCALE = TWO_PI * 0.999999


class Buf:
    __slots__ = ("name", "w", "rs", "sem", "cnt", "slot", "base", "uid")

    def __init__(self, name="b"):
        self.name = name
        self.w = None
        self.rs = {}
        self.sem = None
        self.cnt = 0
        self.slot = None
        self.base = 0
        self.uid = None


class Op:
    __slots__ = ("eng", "fn", "deps", "key", "val", "signal", "sigval", "dma", "slot", "semval")


ENGS = ("pe", "act", "dve", "pool", "sp")


class Prog:
    def __init__(self, nc):
        self.nc = nc
        self.ops = {e: [] for e in ENGS}
        self.seen = {e: {} for e in ENGS}
        self.dma_bufs = []
        self.last = {}
        self.slot_base = []
        self.free_slots = []
        self.live = []
        self.uid = 0

    def _get_slot(self, buf):
        if self.free_slots:
            sl = self.free_slots.pop()
        else:
            sl = len(self.slot_base)
            self.slot_base.append(0)
        self.uid += 1
        buf.sem = True
        buf.slot = sl
        buf.base = self.slot_base[sl]
        buf.cnt = 0
        buf.uid = self.uid
        self.live.append(buf)

    def barrier(self):
        lasts = list(self.last.values())
        self._barrier_ops(lasts)
        for b in self.live:
            self.slot_base[b.slot] = b.base + b.cnt
            self.free_slots.append(b.slot)
            b.sem = None
        self.live = []
        self.last = {kk: v for kk, v in self.last.items() if not isinstance(kk, tuple)}

    def _barrier_ops(self, lasts):
        for e in ENGS:
            o = Op()
            o.eng = e
            o.fn = None
            o.deps = []
            o.signal = False
            o.sigval = None
            o.dma = None
            o.key = e
            o.val = len(self.ops[e])
            for d in lasts:
                if d.key == e:
                    continue
                if self.seen[e].get(d.key, -1) >= d.val:
                    continue
                self.seen[e][d.key] = d.val
                o.deps.append(d)
            self.ops[e].append(o)

    def _dep(self, eng, d, deps, same_ok):
        if d is None:
            return
        key = d.key
        if key == eng:
            if eng == "pe" or same_ok:
                return
        if self.seen[eng].get(key, -1) >= d.val:
            return
        self.seen[eng][key] = d.val
        deps.append(d)

    def op(self, eng, fn, reads=(), writes=(), dma=None):
        o = Op()
        o.eng = eng
        o.fn = fn
        o.deps = []
        o.signal = False
        o.sigval = None
        o.dma = dma
        writes = [b for b in writes if b is not None]
        reads = [b for b in reads if b is not None]
        o.slot = None
        o.semval = None
        if dma is not None:
            if dma.sem is None:
                self._get_slot(dma)
            dma.cnt += 1
            o.key = ("dma", dma.uid)
            o.val = dma.cnt
            o.slot = dma.slot
            o.semval = 16 * (dma.base + dma.cnt)
            if dma not in writes:
                writes.append(dma)
            reads = [b for b in reads if b is not dma]
        else:
            o.key = eng
            o.val = len(self.ops[eng])
        for b in reads:
            self._dep(eng, b.w, o.deps, False)
        for b in writes:
            self._dep(eng, b.w, o.deps, True)
            for r in b.rs.values():
                self._dep(eng, r, o.deps, True)
        for b in reads:
            b.rs[o.key] = o
        for b in writes:
            b.w = o
            b.rs = {}
        self.ops[eng].append(o)
        self.last[o.key] = o
        return o

    def emit(self, stack):
        nc = self.nc
        for e in ENGS:
            for o in self.ops[e]:
                for d in o.deps:
                    if d.dma is None:
                        d.signal = True
        for e in ENGS:
            c = 0
            for o in self.ops[e]:
                if o.dma is None and o.signal:
                    c += 1
                    o.sigval = c
        esem = {}
        for e in ("pe", "act", "dve", "pool"):
            esem[e] = stack.enter_context(nc.semaphore("s_" + e))
        dsem = [stack.enter_context(nc.semaphore("d%d" % i)) for i in range(len(self.slot_base))]
        block = stack.enter_context(nc.Block())
        prog = self

        def run(name, eng):
            for o in prog.ops[name]:
                for d in o.deps:
                    if d.dma is not None:
                        eng.wait_ge(dsem[d.slot], d.semval)
                    else:
                        eng.wait_ge(esem[d.key], d.sigval)
                if o.fn is None:
                    continue
                ins = o.fn(eng)
                if o.dma is not None:
                    ins.then_inc(dsem[o.slot], 16)
                elif o.signal:
                    ins.then_inc(esem[name], 1)

        @block.sync
        def _(eng):
            run("sp", eng)

        @block.scalar
        def _(eng):
            run("act", eng)

        @block.vector
        def _(eng):
            run("dve", eng)

        @block.gpsimd
        def _(eng):
            run("pool", eng)

        @block.tensor
        def _(eng):
            run("pe", eng)


class K:
    def __init__(self, nc, P):
        self.nc = nc
        self.P = P
        self.n = 0

    def name(self, s):
        self.n += 1
        return "%s_%d" % (s, self.n)

    def sb(self, st, shape, dt=F32, name="t"):
        t = st.enter_context(self.nc.sbuf_tensor(self.name(name), list(shape), dt))
        return t, Buf(name)

    def dma(self, eng, out, in_, buf, reads=(), writes=()):
        self.P.op(eng, lambda e: e.dma_start(out=out, in_=in_), reads=reads, writes=writes, dma=buf)

    def mm(self, out, lhsT, rhs, start, stop, r, w):
        self.P.op("pe", lambda e: e.matmul(out, lhsT=lhsT, rhs=rhs, start=start, stop=stop), reads=r, writes=w)

    def tr(self, out, in_, ident, r, w):
        self.P.op("pe", lambda e: e.transpose(out=out, in_=in_, identity=ident), reads=r, writes=w)

    def act(self, out, in_, func, r, w, bias=None, scale=None, accum=None):
        kw = {}
        if bias is not None:
            kw["bias"] = bias
        if scale is not None:
            kw["scale"] = scale
        if accum is not None:
            kw["accum_out"] = accum
        self.P.op("act", lambda e: e.activation(out=out, in_=in_, func=func, **kw), reads=r, writes=w)

    def tt(self, eng, out, in0, in1, op, r, w):
        self.P.op(eng, lambda e: e.tensor_tensor(out=out, in0=in0, in1=in1, op=op), reads=r, writes=w)

    def ts(self, eng, out, in0, s1, op0, r, w, s2=None, op1=None):
        if op1 is None:
            self.P.op(eng, lambda e: e.tensor_scalar(out=out, in0=in0, scalar1=s1, scalar2=None, op0=op0), reads=r, writes=w)
        else:
            self.P.op(eng, lambda e: e.tensor_scalar(out=out, in0=in0, scalar1=s1, scalar2=s2, op0=op0, op1=op1), reads=r, writes=w)

    def stt(self, out, in0, scalar, in1, op0, op1, r, w):
        self.P.op("dve", lambda e: e.scalar_tensor_tensor(out=out, in0=in0, scalar=scalar, in1=in1, op0=op0, op1=op1), reads=r, writes=w)

    def copy(self, eng, out, in_, r, w):
        if eng == "act":
            self.P.op("act", lambda e: e.activation(out=out, in_=in_, func=AF.Copy), reads=r, writes=w)
        else:
            self.P.op(eng, lambda e: e.tensor_copy(out=out, in_=in_), reads=r, writes=w)

    def memset(self, eng, ap, val, w):
        self.P.op(eng, lambda e: e.memset(ap, val), writes=w)

    def scan(self, out, d0, d1, init, r, w):
        self.P.op("dve", lambda e: e.tensor_tensor_scan(out=out, data0=d0, data1=d1, initial=init, op0=ALU.mult, op1=ALU.add), reads=r, writes=w)

    def recip(self, out, in_, r, w):
        self.P.op("dve", lambda e: e.reciprocal(out=out, in_=in_), reads=r, writes=w)


class PsumRing:
    def __init__(self, k, st, n=8):
        self.banks = []
        for i in range(n):
            t = st.enter_context(k.nc.psum_tensor(k.name("ps"), [128, 512], F32))
            self.banks.append((t, Buf("ps%d" % i)))
        self.i = 0

    def get(self):
        t, b = self.banks[self.i % len(self.banks)]
        self.i += 1
        return t, b


def _swap_halves(w):
    sh = w.shape
    w4 = w.reshape(sh[:-1] + (sh[-1] // 64, 2, 32))
    return np.ascontiguousarray(w4[..., ::-1, :]).reshape(sh)


def host_consts(S, TS):
    c = {}
    inv = (10000.0 ** (-np.arange(0, 64, 2, dtype=np.float32) / np.float32(64))).astype(np.float32)
    ang = (np.arange(S, dtype=np.float32)[:, None] * inv[None, :]).astype(np.float32)
    cs, sn = np.cos(ang).astype(np.float32), np.sin(ang).astype(np.float32)
    cosT = np.concatenate([cs.T, cs.T], 0)
    sinT = np.concatenate([-sn.T, sn.T], 0)
    c["c_cos"] = np.ascontiguousarray(np.concatenate([cosT, cosT], 0))
    c["c_sin"] = np.ascontiguousarray(np.concatenate([sinT, sinT], 0))
    c["c_ident"] = np.eye(128, dtype=np.float32)
    c["c_tau"] = np.ascontiguousarray(np.broadcast_to(np.arange(TS, dtype=np.float32)[None, :], (128, TS)))
    k = np.arange(128)[:, None]
    q = np.arange(128)[None, :]
    c["c_caus"] = np.where(k <= q, 0.0, NEG).astype(np.float32)
    c["c_low"] = np.where(k > q, 0.0, NEG).astype(np.float32)
    cc = np.arange(1024)[None, :]
    qi = np.arange(128)[:, None]
    c["c_mgen"] = np.where(16 * (cc - 512) + 31 <= qi, 0.0, NEG).astype(np.float32)
    m = np.arange(16)[None, :, None]
    ni = np.arange(128)[:, None, None]
    qq = np.arange(128)[None, None, :]
    c["c_mt"] = np.where(16 * ni + 31 <= 128 * m + qq, 0.0, NEG).astype(np.float32)
    rel = np.arange(256)[None, :] - 126
    cur = (np.arange(128)[:, None] >= 64).astype(np.int64)
    g = np.where(rel > cur, -1e30, 0.0) + np.where((rel == cur) | (rel == cur - 1), 1e4, 0.0)
    c["c_g"] = g.astype(np.float32)
    NT = S // 128
    j = np.arange(128)[:, None, None]
    i = np.arange(NT)[None, :, None]
    kk = np.arange(128)[None, None, :]
    c["c_e"] = (j == 2 * i + (kk >= 64)).astype(np.float32)
    r_ = np.arange(64)[:, None]
    cidx = np.arange(S)[None, :]
    c["c_ind"] = (((cidx // 64) % 64) == r_).astype(np.float32)
    return c


def host_layout(inp, L):
    o = {}
    f = np.float32
    o["g_mix"] = np.ascontiguousarray(np.broadcast_to(inp["norm_mix"][:, None, :], (L, 128, D))).astype(f)
    o["g_ffn"] = np.ascontiguousarray(np.broadcast_to(inp["norm_ffn"][:, None, :], (L, 128, D))).astype(f)
    o["g_fin"] = np.ascontiguousarray(np.broadcast_to(inp["norm_final"][None, :], (128, D))).astype(f)
    w_in = inp["w_in"]
    o["w_in"] = w_in
    sw = np.concatenate([_swap_halves(w_in[:, :, 512:1024]), _swap_halves(w_in[:, :, 1024:1152]),
                         _swap_halves(w_in[:, :, 1280:1408]), _swap_halves(w_in[:, :, 1536:1664])], axis=-1)
    o["w_sw"] = np.ascontiguousarray(sw)

    def pair(a):
        return np.ascontiguousarray(a.reshape(L, 16, 2, 64).transpose(0, 2, 3, 1).reshape(L, 128, 16))
    o["s_are"] = pair(inp["ssm_a_re"])
    o["s_aim"] = pair(inp["ssm_a_im"])
    o["s_ldt"] = pair(np.broadcast_to(inp["ssm_log_dt"][:, :, None], (L, 32, 64)))
    for nm, src in (("s_bre", "ssm_b_re"), ("s_bim", "ssm_b_im")):
        b = inp[src].reshape(L, 16, 2, 64, 16)
        pad = np.zeros((L, 8, 16, 16, 2, 64), f)
        for j in range(16):
            for gl in range(2):
                pad[:, 2 * (j % 4) + gl, :, j, gl, :] = b[:, j, gl].transpose(0, 2, 1)
        o[nm] = pad.reshape(L, 128, 16, 128)
    for nm, src in (("s_cre", "ssm_c_re"), ("s_cim", "ssm_c_im")):
        cmat = inp[src].reshape(L, 16, 2, 16, 64)
        pad = np.zeros((L, 2, 64, 16, 8, 16), f)
        for j in range(16):
            for gl in range(2):
                pad[:, gl, :, j, 2 * (j % 4) + gl, :] = cmat[:, j, gl].transpose(0, 2, 1)
        o[nm] = pad.reshape(L, 128, 16, 128)
    o["s_d"] = np.ascontiguousarray(inp["ssm_d"].reshape(L, 4, 128).transpose(0, 2, 1))
    o["w_glu"] = inp["ssm_w_glu"]
    o["w_bssm"] = inp["w_branch_ssm"]
    o["w_bnsa"] = inp["w_branch_nsa"]
    o["w_out"] = inp["w_out"]
    for t in ("k", "v"):
        o["c_pe" + t] = np.ascontiguousarray(inp["cmp_pe_" + t].transpose(0, 2, 1))
        o["c_w1" + t] = inp["cmp_w1_" + t]
        o["c_b1" + t] = np.ascontiguousarray(inp["cmp_b1_" + t].reshape(L, 2, 128).transpose(0, 2, 1))
        o["c_w2" + t] = inp["cmp_w2_" + t]
    o["w_fin"] = inp["w_ffn_in"]
    o["w_fout"] = inp["w_ffn_out"]
    o["f_cw"] = np.ascontiguousarray(inp["ffn_conv_w"].reshape(L, 3, NFC, 128).transpose(0, 3, 2, 1))
    o["f_cb"] = np.ascontiguousarray(inp["ffn_conv_b"].reshape(L, NFC, 128).transpose(0, 2, 1))
    return o


def load_w(k, st, src2d, nch, ncols, prow=128, eng="pool", name="w"):
    t, _ = k.sb(st, [prow, nch, ncols], BF16, name)
    bufs = []
    for c in range(nch):
        b = Buf(name)
        k.dma(eng, t[:, c, :], src2d[c * prow:(c + 1) * prow, :], b)
        bufs.append(b)
    return t, bufs


def sincos(k, st, arg, n, out_sin=None, out_cos=None, rb=(), wsin=None, wcos=None):
    for (dst, off, wb) in ((out_sin, 0.0, wsin), (out_cos, 0.25, wcos)):
        if dst is None:
            continue
        a2, ba2 = k.sb(st, [128, n], F32, "sc_a")
        ti, bti = k.sb(st, [128, n], I32, "sc_i")
        tf, btf = k.sb(st, [128, n], F32, "sc_f")
        k.ts("dve", a2[:], arg, off, ALU.add, r=list(rb), w=[ba2])
        k.copy("dve", ti[:], a2[:], r=[ba2], w=[bti])
        k.copy("dve", tf[:], ti[:], r=[bti], w=[btf])
        k.tt("dve", a2[:], a2[:], tf[:], ALU.subtract, r=[ba2, btf], w=[ba2])
        k.act(dst, a2[:], AF.Sin, r=[ba2], w=[wb], scale=SIN_SCALE)


def phase1(k, l, S, TS, x_src, di, sc, cst):
    P = k.P
    TT = TS
    NTT = S // TT
    with ExitStack() as st:
        ring = PsumRing(k, st)
        ident, bid = cst["ident"]
        gam, bgam = k.sb(st, [128, D], F32, "gam")
        k.dma("sp", gam[:], di["g_mix"][l], bgam)
        dvec, bdvec = k.sb(st, [128, 4], F32, "dvec")
        k.dma("sp", dvec[:], di["s_d"][l], bdvec)
        RFre, bRFre = k.sb(st, [128, 16, TS], F32, "RFre")
        RFim, bRFim = k.sb(st, [128, 16, TS], F32, "RFim")
        COSb, bCOSb = k.sb(st, [128, 16, TS], BF16, "COSb")
        SINb, bSINb = k.sb(st, [128, 16, TS], BF16, "SINb")
        NSINb, bNSINb = k.sb(st, [128, 16, TS], BF16, "NSINb")
        dec, bdec = k.sb(st, [128, 16], F32, "dec")
        cT, bcT = k.sb(st, [128, 16], F32, "cT")
        sT, bsT = k.sb(st, [128, 16], F32, "sT")
        nsT, bnsT = k.sb(st, [128, 16], F32, "nsT")
        with ExitStack() as s2:
            are, bare = k.sb(s2, [128, 16], F32, "are")
            aim, baim = k.sb(s2, [128, 16], F32, "aim")
            ldt, bldt = k.sb(s2, [128, 16], F32, "ldt")
            tau, btau = k.sb(s2, [128, TS], F32, "tau")
            k.dma("sp", are[:], di["s_are"][l], bare)
            k.dma("sp", aim[:], di["s_aim"][l], baim)
            k.dma("sp", ldt[:], di["s_ldt"][l], bldt)
            k.dma("sp", tau[:], di["c_tau"], btau)
            dt_, bdt = k.sb(s2, [128, 16], F32, "dt")
            k.act(dt_[:], ldt[:], AF.Exp, r=[bldt], w=[bdt])
            rho, brho = k.sb(s2, [128, 16], F32, "rho")
            thn, bthn = k.sb(s2, [128, 16], F32, "thn")
            k.tt("dve", rho[:], are[:], dt_[:], ALU.mult, r=[bare, bdt], w=[brho])
            k.tt("dve", thn[:], aim[:], dt_[:], ALU.mult, r=[baim, bdt], w=[bthn])
            k.ts("dve", thn[:], thn[:], 1.0 / TWO_PI, ALU.mult, r=[bthn], w=[bthn])
            k.act(dec[:], rho[:], AF.Exp, r=[brho], w=[bdec])
            s1, bs1 = k.sb(s2, [128, 16], F32, "s1")
            c1, bc1 = k.sb(s2, [128, 16], F32, "c1")
            sincos(k, s2, thn[:], 16, s1[:], c1[:], rb=[bthn], wsin=bs1, wcos=bc1)
            abre, babre = k.sb(s2, [128, 16], F32, "abre")
            abim, babim = k.sb(s2, [128, 16], F32, "abim")
            k.tt("dve", abre[:], dec[:], c1[:], ALU.mult, r=[bdec, bc1], w=[babre])
            k.ts("dve", abre[:], abre[:], -1.0, ALU.add, r=[babre], w=[babre])
            k.tt("dve", abim[:], dec[:], s1[:], ALU.mult, r=[bdec, bs1], w=[babim])
            den, bden = k.sb(s2, [128, 16], F32, "den")
            t0, bt0 = k.sb(s2, [128, 16], F32, "t0")
            k.tt("dve", den[:], are[:], are[:], ALU.mult, r=[bare], w=[bden])
            k.tt("dve", t0[:], aim[:], aim[:], ALU.mult, r=[baim], w=[bt0])
            k.tt("dve", den[:], den[:], t0[:], ALU.add, r=[bden, bt0], w=[bden])
            k.recip(den[:], den[:], r=[bden], w=[bden])
            fre, bfre = k.sb(s2, [128, 16], F32, "fre")
            fim, bfim = k.sb(s2, [128, 16], F32, "fim")
            t1, bt1 = k.sb(s2, [128, 16], F32, "t1")
            k.tt("dve", fre[:], abre[:], are[:], ALU.mult, r=[babre, bare], w=[bfre])
            k.tt("dve", t1[:], abim[:], aim[:], ALU.mult, r=[babim, baim], w=[bt1])
            k.tt("dve", fre[:], fre[:], t1[:], ALU.add, r=[bfre, bt1], w=[bfre])
            k.tt("dve", fre[:], fre[:], den[:], ALU.mult, r=[bfre, bden], w=[bfre])
            k.tt("dve", fim[:], abim[:], are[:], ALU.mult, r=[babim, bare], w=[bfim])
            k.tt("dve", t1[:], abre[:], aim[:], ALU.mult, r=[babre, baim], w=[bt1])
            k.tt("dve", fim[:], fim[:], t1[:], ALU.subtract, r=[bfim, bt1], w=[bfim])
            k.tt("dve", fim[:], fim[:], den[:], ALU.mult, r=[bfim, bden], w=[bfim])
            aT, baT = k.sb(s2, [128, 16], F32, "aT")
            k.ts("dve", aT[:], thn[:], float(TS), ALU.mult, r=[bthn], w=[baT])
            sincos(k, s2, aT[:], 16, sT[:], cT[:], rb=[baT], wsin=bsT, wcos=bcT)
            k.ts("dve", nsT[:], sT[:], -1.0, ALU.mult, r=[bsT], w=[bnsT])
            ANG, bANG = k.sb(s2, [128, 16, TS], F32, "ANG")
            SINf, bSINf = k.sb(s2, [128, 16 * TS], F32, "SINf")
            COSf, bCOSf = k.sb(s2, [128, 16 * TS], F32, "COSf")
            for j in range(16):
                k.ts("dve", ANG[:, j, :], tau[:], thn[:, j:j + 1], ALU.mult, r=[btau, bthn], w=[bANG])
            sincos(k, s2, ANG[:].rearrange("p j t -> p (j t)"), 16 * TS, SINf[:], COSf[:], rb=[bANG], wsin=bSINf, wcos=bCOSf)
            SIN3 = SINf[:].rearrange("p (j t) -> p j t", j=16)
            COS3 = COSf[:].rearrange("p (j t) -> p j t", j=16)
            tmp, btmp = k.sb(s2, [128, TS], F32, "tmp")
            for j in range(16):
                k.ts("dve", tmp[:], SIN3[:, j, :], fim[:, j:j + 1], ALU.mult, r=[bSINf, bfim], w=[btmp])
                k.stt(RFre[:, j, :], COS3[:, j, :], fre[:, j:j + 1], tmp[:], ALU.mult, ALU.add, r=[bCOSf, bfre, btmp], w=[bRFre])
                k.ts("dve", tmp[:], SIN3[:, j, :], fre[:, j:j + 1], ALU.mult, r=[bSINf, bfre], w=[btmp])
                k.stt(RFim[:, j, :], COS3[:, j, :], fim[:, j:j + 1], tmp[:], ALU.mult, ALU.subtract, r=[bCOSf, bfim, btmp], w=[bRFim])
            k.copy("dve", COSb[:].rearrange("p j t -> p (j t)"), COSf[:], r=[bCOSf], w=[bCOSb])
            k.copy("dve", SINb[:].rearrange("p j t -> p (j t)"), SINf[:], r=[bSINf], w=[bSINb])
            k.ts("dve", NSINb[:].rearrange("p j t -> p (j t)"), SINf[:], -1.0, ALU.mult, r=[bSINf], w=[bNSINb])
        P.barrier()
        w_in, bw_in = load_w(k, st, di["w_in"][l], 8, INW, name="w_in")
        w_sw, bw_sw = load_w(k, st, di["w_sw"][l], 8, 896, name="w_sw")
        w_glu, bw_glu = load_w(k, st, di["w_glu"][l], 4, 1024, name="w_glu")
        w_bs, bw_bs = load_w(k, st, di["w_bssm"][l], 4, 1024, name="w_bs")
        bre, bbre = load_w(k, st, di["s_bre"][l].rearrange("p j m -> p (j m)"), 1, 2048, name="bre")
        bim, bbim = load_w(k, st, di["s_bim"][l].rearrange("p j m -> p (j m)"), 1, 2048, name="bim")
        cre, bcre = load_w(k, st, di["s_cre"][l].rearrange("p j m -> p (j m)"), 1, 2048, name="cre")
        cim, bcim = load_w(k, st, di["s_cim"][l].rearrange("p j m -> p (j m)"), 1, 2048, name="cim")
        xt = [k.sb(st, [128, TT // 128, D], F32, "xt") for _ in range(2)]
        cosr = [k.sb(st, [128, TT], F32, "cosr") for _ in range(2)]
        sinr = [k.sb(st, [128, TT], F32, "sinr") for _ in range(2)]
        NSB = TT // 128
        junk, bjunk = k.sb(st, [128, D], BF16, "junk")
        ss, bss = k.sb(st, [128, NSB], F32, "ss")
        ms, bms = k.sb(st, [128, NSB], F32, "ms")
        sd, bsd = k.sb(st, [128, NSB], F32, "sd")
        rstd, brstd = k.sb(st, [128, NSB], F32, "rstd")
        hh = [k.sb(st, [128, D], BF16, "h") for _ in range(2)]
        hT, bhT = k.sb(st, [128, 8, TT], BF16, "hT")
        uT2 = [k.sb(st, [128, 4, TT], BF16, "uT") for _ in range(2)]
        qTs, bqTs = k.sb(st, [128, 4, TT], BF16, "qTs")
        kvs = {nm: k.sb(st, [128, TT], BF16, nm) for nm in ("kcT", "vcT", "ksT", "kwT")}
        sgs2 = [k.sb(st, [128, 8, TT], BF16, "sgs") for _ in range(2)]
        sgn, bsgn = k.sb(st, [128, 8, TT], BF16, "sgn")
        sg, bsg = k.sb(st, [24, TT], F32, "sg")
        vsel, bvsel = k.sb(st, [128, NSB, 2, 65], BF16, "vsel")
        vwin, bvwin = k.sb(st, [128, NSB, 2, 65], BF16, "vwin")
        k.memset("pool", vsel[:], 1.0, [bvsel])
        k.memset("pool", vwin[:], 1.0, [bvwin])
        tmps = [k.sb(st, [128, TT], F32, "tmp") for _ in range(12)]
        tmpi = [0]

        def gettmp():
            t = tmps[tmpi[0] % len(tmps)]
            tmpi[0] += 1
            return t
        bsc = [k.sb(st, [128, TT], F32, "bsc") for _ in range(4)]
        wsc = [k.sb(st, [128, TT], F32, "wsc") for _ in range(4)]
        xre, bxre = k.sb(st, [128, 16, TT], BF16, "xre")
        nxim, bnxim = k.sb(st, [128, 16, TT], BF16, "nxim")
        car, bcar = k.sb(st, [128, 2, 16], F32, "car")
        k.memset("dve", car[:], 0.0, [bcar])
        ctmps = [k.sb(st, [128, 2], F32, "ctmp") for _ in range(2)]
        ypre, bypre = k.sb(st, [128, TT], F32, "ypre")
        yT, byT = k.sb(st, [128, 4, TT], BF16, "yT")
        sgz, bsgz = k.sb(st, [128, TT], F32, "sgz")
        zzT, bzzT = k.sb(st, [128, 4, TT], BF16, "zzT")
        gss, bgss = k.sb(st, [128, 8, TT], BF16, "gss")

        def load_tile(i):
            t, b = xt[i % 2]
            k.dma("sp", t[:], x_src[i * TT:(i + 1) * TT, :].rearrange("(s p) d -> p s d", p=128), b)
            k.dma("sp", cosr[i % 2][0][:], di["c_cos"][:, i * TT:(i + 1) * TT], cosr[i % 2][1])
            k.dma("sp", sinr[i % 2][0][:], di["c_sin"][:, i * TT:(i + 1) * TT], sinr[i % 2][1])

        def proj(wt, wb, col0, M=128):
            ps, bp = ring.get()
            for c in range(8):
                k.mm(ps[0:M, 0:TT], wt[:, c, col0:col0 + M], hT[:, c, :], c == 0, c == 7, r=[wb[c], bhT], w=[bp])
            return ps, bp

        def inproj_gen(i):
            uT, buT = uT2[i % 2]
            sgs, bsgs = sgs2[i % 2]
            x_t, bx = xt[i % 2]
            cos_t, bcos = cosr[i % 2]
            sin_t, bsin = sinr[i % 2]
            tok = slice(i * TT, (i + 1) * TT)
            for s_ in range(NSB):
                h_t, bh = hh[s_ % 2]
                k.act(junk[:], x_t[:, s_, :], AF.Square, r=[bx], w=[bjunk, bss], accum=ss[:, s_:s_ + 1])
                k.ts("dve", ms[:, s_:s_ + 1], ss[:, s_:s_ + 1], 1.0 / D, ALU.mult, r=[bss], w=[bms], s2=EPS, op1=ALU.add)
                k.act(sd[:, s_:s_ + 1], ms[:, s_:s_ + 1], AF.Sqrt, r=[bms], w=[bsd])
                k.recip(rstd[:, s_:s_ + 1], sd[:, s_:s_ + 1], r=[bsd], w=[brstd])
                k.stt(h_t[:], x_t[:, s_, :], rstd[:, s_:s_ + 1], gam[:], ALU.mult, ALU.mult, r=[bx, brstd, bgam], w=[bh])
                ps, bp = ring.get()
                pbf = ps[:].bitcast(BF16)
                for c in range(8):
                    k.tr(pbf[:, c * 128:(c + 1) * 128], h_t[:, c * 128:(c + 1) * 128], ident[:], r=[bh, bid], w=[bp])
                k.copy("act", hT[:, :, s_ * 128:(s_ + 1) * 128], pbf.rearrange("p (c t) -> p c t", c=8), r=[bp], w=[bhT])
            for c4 in range(4):
                ps, bp = proj(w_in, bw_in, c4 * 128)
                k.copy("act", uT[:, c4, :], ps[:, 0:TT], r=[bp], w=[buT])
                yield
            def rope(col, swcol, dst, bdst):
                psA, bA = proj(w_in, bw_in, col)
                psB, bB = proj(w_sw, bw_sw, swcol)
                t1_, bt1_ = gettmp()
                t2_, bt2_ = gettmp()
                k.tt("dve", t1_[:], psA[:, 0:TT], cos_t[:], ALU.mult, r=[bA, bcos], w=[bt1_])
                k.tt("dve", t2_[:], psB[:, 0:TT], sin_t[:], ALU.mult, r=[bB, bsin], w=[bt2_])
                k.tt("pool", dst, t1_[:], t2_[:], ALU.add, r=[bt1_, bt2_], w=[bdst])
            for c in range(4):
                rope(512 + c * 128, c * 128, qTs[:, c, :], bqTs)
                yield
            rope(1024, 512, kvs["kcT"][0][:], kvs["kcT"][1])
            yield
            rope(1280, 640, kvs["ksT"][0][:], kvs["ksT"][1])
            yield
            rope(1536, 768, kvs["kwT"][0][:], kvs["kwT"][1])
            yield
            ps, bp = proj(w_in, bw_in, 1152)
            k.copy("act", kvs["vcT"][0][:], ps[:, 0:TT], r=[bp], w=[kvs["vcT"][1]])
            for c in range(8):
                ps, bp = proj(w_in, bw_in, 1816 + c * 128)
                k.act(sgs[:, c, :], ps[:, 0:TT], AF.Sigmoid, r=[bp], w=[bsgs])
                yield
            for c in range(8):
                ps, bp = proj(w_in, bw_in, 2840 + c * 128)
                k.act(sgn[:, c, :], ps[:, 0:TT], AF.Sigmoid, r=[bp], w=[bsgn])
                yield
            ps, bp = proj(w_in, bw_in, 1792, M=24)
            k.act(sg[:], ps[0:24, 0:TT], AF.Sigmoid, r=[bp], w=[bsg])
            for s_ in range(NSB):
                for (col, vt, bv) in ((1408, vsel, bvsel), (1664, vwin, bvwin)):
                    ps, bp = ring.get()
                    for c in range(8):
                        k.mm(ps[:, 0:128], hT[:, c, s_ * 128:(s_ + 1) * 128], w_in[:, c, col:col + 128], c == 0, c == 7, r=[bw_in[c], bhT], w=[bp])
                    k.copy("act", vt[:, s_, :, 0:64], ps[:, 0:128].rearrange("p (h d) -> p h d", h=2), r=[bp], w=[bv])
                    yield
            k.dma("sp", sc["qT"].rearrange("(c p) s -> p c s", p=128)[:, :, tok], qTs[:], bqTs)
            for nm in ("kcT", "vcT", "ksT", "kwT"):
                k.dma("sp", sc[nm][:, tok], kvs[nm][0][:], kvs[nm][1])
            k.dma("sp", sc["sgnT"].rearrange("(c p) s -> p c s", p=128)[:, :, tok], sgn[:], bsgn)
            k.dma("sp", sc["sgT"][:, tok], sg[:], bsg)
            k.dma("sp", sc["vs"][tok].rearrange("(s p) h c -> p s h c", p=128), vsel[:], bvsel)
            k.dma("sp", sc["vw"][tok].rearrange("(s p) h c -> p s h c", p=128), vwin[:], bvwin)
        def s5_gen(i):
            uT, buT = uT2[i % 2]
            sgs, bsgs = sgs2[i % 2]
            tok = slice(i * TT, (i + 1) * TT)
            def stageA(j):
                c4 = j // 4
                psr, bpr = ring.get()
                psi, bpi = ring.get()
                k.mm(psr[:, 0:TT], bre[:, 0, j * 128:(j + 1) * 128], uT[:, c4, :], True, True, r=[bbre[0], buT], w=[bpr])
                k.mm(psi[:, 0:TT], bim[:, 0, j * 128:(j + 1) * 128], uT[:, c4, :], True, True, r=[bbim[0], buT], w=[bpi])
                b_re, bb_re = bsc[(2 * j) % 4]
                b_im, bb_im = bsc[(2 * j + 1) % 4]
                t1_, bt1_ = gettmp()
                t2_, bt2_ = gettmp()
                k.tt("dve", t1_[:], psr[:, 0:TT], RFre[:, j, :], ALU.mult, r=[bpr, bRFre], w=[bt1_])
                k.tt("dve", t2_[:], psi[:, 0:TT], RFim[:, j, :], ALU.mult, r=[bpi, bRFim], w=[bt2_])
                k.tt("pool", b_re[:], t1_[:], t2_[:], ALU.subtract, r=[bt1_, bt2_], w=[bb_re])
                t3_, bt3_ = gettmp()
                t4_, bt4_ = gettmp()
                k.tt("dve", t3_[:], psi[:, 0:TT], RFre[:, j, :], ALU.mult, r=[bpi, bRFre], w=[bt3_])
                k.tt("dve", t4_[:], psr[:, 0:TT], RFim[:, j, :], ALU.mult, r=[bpr, bRFim], w=[bt4_])
                k.tt("pool", b_im[:], t3_[:], t4_[:], ALU.add, r=[bt3_, bt4_], w=[bb_im])

            def stageB(j):
                b_re, bb_re = bsc[(2 * j) % 4]
                b_im, bb_im = bsc[(2 * j + 1) % 4]
                w_re, bw_re = wsc[(2 * j) % 4]
                w_im, bw_im = wsc[(2 * j + 1) % 4]
                dj = dec[:, j:j + 1].to_broadcast([128, TT])
                k.scan(w_re[:], dj, b_re[:], car[:, 0, j:j + 1], r=[bdec, bb_re, bcar], w=[bw_re])
                k.scan(w_im[:], dj, b_im[:], car[:, 1, j:j + 1], r=[bdec, bb_im, bcar], w=[bw_im])
                t5_, bt5_ = gettmp()
                t6_, bt6_ = gettmp()
                k.tt("dve", t5_[:], w_re[:], COSb[:, j, :], ALU.mult, r=[bw_re, bCOSb], w=[bt5_])
                k.tt("dve", t6_[:], w_im[:], SINb[:, j, :], ALU.mult, r=[bw_im, bSINb], w=[bt6_])
                k.tt("pool", xre[:, j, :], t5_[:], t6_[:], ALU.subtract, r=[bt5_, bt6_], w=[bxre])
                t7_, bt7_ = gettmp()
                t8_, bt8_ = gettmp()
                k.tt("dve", t7_[:], w_re[:], NSINb[:, j, :], ALU.mult, r=[bw_re, bNSINb], w=[bt7_])
                k.tt("dve", t8_[:], w_im[:], COSb[:, j, :], ALU.mult, r=[bw_im, bCOSb], w=[bt8_])
                k.tt("pool", nxim[:, j, :], t7_[:], t8_[:], ALU.subtract, r=[bt7_, bt8_], w=[bnxim])
                ct_, bct_ = ctmps[j % 2]
                k.ts("dve", ct_[:, 0:1], w_re[:, TT - 1:TT], cT[:, j:j + 1], ALU.mult, r=[bw_re, bcT], w=[bct_])
                k.ts("dve", ct_[:, 1:2], w_im[:, TT - 1:TT], cT[:, j:j + 1], ALU.mult, r=[bw_im, bcT], w=[bct_])
                k.stt(car[:, 0, j:j + 1], w_im[:, TT - 1:TT], nsT[:, j:j + 1], ct_[:, 0:1], ALU.mult, ALU.add, r=[bw_im, bnsT, bct_], w=[bcar])
                k.stt(car[:, 1, j:j + 1], w_re[:, TT - 1:TT], sT[:, j:j + 1], ct_[:, 1:2], ALU.mult, ALU.add, r=[bw_re, bsT, bct_], w=[bcar])

            stageA(0)
            for j in range(16):
                if j + 1 < 16:
                    stageA(j + 1)
                stageB(j)
                yield
            for c4 in range(4):
                ps, bp = ring.get()
                for jj in range(4):
                    j = 4 * c4 + jj
                    k.mm(ps[:, 0:TT], cre[:, 0, j * 128:(j + 1) * 128], xre[:, j, :], jj == 0, False, r=[bcre[0], bxre], w=[bp])
                    k.mm(ps[:, 0:TT], cim[:, 0, j * 128:(j + 1) * 128], nxim[:, j, :], False, jj == 3, r=[bcim[0], bnxim], w=[bp])
                k.stt(ypre[:], uT[:, c4, :], dvec[:, c4:c4 + 1], ps[:, 0:TT], ALU.mult, ALU.add, r=[buT, bdvec, bp], w=[bypre])
                k.act(yT[:, c4, :], ypre[:], AF.Gelu_apprx_tanh, r=[bypre], w=[byT])
                yield
            for kk in range(4):
                psg, bpg = ring.get()
                for c4 in range(4):
                    k.mm(psg[:, 0:TT], w_glu[:, c4, (4 + kk) * 128:(5 + kk) * 128], yT[:, c4, :], c4 == 0, c4 == 3, r=[bw_glu[c4], byT], w=[bpg])
                k.act(sgz[:], psg[:, 0:TT], AF.Sigmoid, r=[bpg], w=[bsgz])
                psv, bpv = ring.get()
                for c4 in range(4):
                    k.mm(psv[:, 0:TT], w_glu[:, c4, kk * 128:(kk + 1) * 128], yT[:, c4, :], c4 == 0, c4 == 3, r=[bw_glu[c4], byT], w=[bpv])
                k.tt("dve", zzT[:, kk, :], psv[:, 0:TT], sgz[:], ALU.mult, r=[bpv, bsgz], w=[bzzT])
                yield
            for fc in range(8):
                ps, bp = ring.get()
                for kk in range(4):
                    k.mm(ps[:, 0:TT], w_bs[:, kk, fc * 128:(fc + 1) * 128], zzT[:, kk, :], kk == 0, kk == 3, r=[bw_bs[kk], bzzT], w=[bp])
                k.tt("dve", gss[:, fc, :], ps[:, 0:TT], sgs[:, fc, :], ALU.mult, r=[bp, bsgs], w=[bgss])
                yield
            k.dma("sp", sc["gssT"].rearrange("(c p) s -> p c s", p=128)[:, :, tok], gss[:], bgss)
        def step(g):
            try:
                next(g)
                return True
            except StopIteration:
                return False

        load_tile(0)
        if NTT > 1:
            load_tile(1)
        for _ in inproj_gen(0):
            pass
        for i in range(NTT):
            if i + 2 < NTT:
                load_tile(i + 2)
            gs = s5_gen(i)
            gi = inproj_gen(i + 1) if i + 1 < NTT else iter(())
            alive_s, alive_i = True, True
            n_ = 0
            while alive_s or alive_i:
                if alive_s:
                    alive_s = step(gs)
                for _ in range(1 + (n_ % 2)):
                    if alive_i:
                        alive_i = step(gi)
                n_ += 1
    P.barrier()


def phase2(k, pers, l, S, di, sc, cst):
    P = k.P
    NC = S // 16 - 1
    NCP = S // 16
    NCT = NCP // 128
    KcT, bKcT = k.sb(pers, [128, NCP], BF16, "KcT")
    Vc, bVc = k.sb(pers, [128, NCT, 2, 65], BF16, "Vc")
    k.memset("pool", KcT[:], 0.0, [bKcT])
    k.memset("pool", Vc[:], 1.0, [bVc])
    with ExitStack() as st:
        ring = PsumRing(k, st)
        for typ in ("k", "v"):
            with ExitStack() as s2:
                xT, bxT = k.sb(s2, [128, S], BF16, "cxT")
                k.dma("sp", xT[:], sc["kcT" if typ == "k" else "vcT"], bxT)
                w1, bw1 = k.sb(s2, [128, 32, 256], BF16, "w1")
                bw1b = Buf("w1b")
                src = di["c_w1" + typ][l].rearrange("(l d) c -> d l c", d=64)
                k.dma("pool", w1[0:64], src, bw1)
                k.dma("pool", w1[64:128], src, bw1b)
                pe2, bpe2 = k.sb(s2, [64, 32, 2], BF16, "pe2")
                pe_f, bpe_f = k.sb(s2, [64, 32], F32, "pe_f")
                k.dma("sp", pe_f[:], di["c_pe" + typ][l], bpe_f)
                k.copy("dve", pe2[:, :, 0], pe_f[:], r=[bpe_f], w=[bpe2])
                k.copy("dve", pe2[:, :, 1], pe_f[:], r=[bpe_f], w=[bpe2])
                b1, bb1 = k.sb(s2, [128, 2], F32, "b1")
                k.dma("sp", b1[:], di["c_b1" + typ][l], bb1)
                w2, bw2 = k.sb(s2, [128, 2, 64], BF16, "w2")
                k.dma("pool", w2[:], di["c_w2" + typ][l].rearrange("(c p) d -> p c d", p=128), bw2)
                bias, bbias = k.sb(s2, [128, 2], F32, "bias")
                for cc in range(2):
                    ps, bp = ring.get()
                    for li in range(32):
                        k.mm(ps[:, 0:2], w1[0:64, li, cc * 128:(cc + 1) * 128], pe2[:, li, :], li == 0, li == 31, r=[bw1, bpe2], w=[bp])
                    k.tt("dve", bias[:, cc:cc + 1], ps[:, 0:1], b1[:, cc:cc + 1], ALU.add, r=[bp, bb1], w=[bbias])
                for hk in range(2):
                    hid, bhid = k.sb(s2, [128, 2, NCP], BF16, "hid")
                    k.memset("pool", hid[:], 0.0, [bhid])
                    bw = bw1 if hk == 0 else bw1b
                    for cc in range(2):
                        ps, bp = ring.get()
                        for li in range(32):
                            k.mm(ps[:, 0:NC], w1[hk * 64:(hk + 1) * 64, li, cc * 128:(cc + 1) * 128],
                                 xT[hk * 64:(hk + 1) * 64, li:li + 16 * (NC - 1) + 1:16], li == 0, li == 31, r=[bw, bxT], w=[bp])
                        k.act(hid[:, cc, 0:NC], ps[:, 0:NC], AF.Gelu_apprx_tanh, r=[bp, bbias], w=[bhid], bias=bias[:, cc:cc + 1])
                    if typ == "k":
                        ps, bp = ring.get()
                        for cc in range(2):
                            k.mm(ps[hk * 64:(hk + 1) * 64, 0:NC], w2[:, cc, :], hid[:, cc, 0:NC], cc == 0, cc == 1, r=[bw2, bhid], w=[bp])
                        k.copy("act", KcT[hk * 64:(hk + 1) * 64, 0:NC], ps[hk * 64:(hk + 1) * 64, 0:NC], r=[bp], w=[bKcT])
                    else:
                        for nt in range(NCT):
                            ps, bp = ring.get()
                            for cc in range(2):
                                k.mm(ps[:, 0:64], hid[:, cc, nt * 128:(nt + 1) * 128], w2[:, cc, :], cc == 0, cc == 1, r=[bhid, bw2], w=[bp])
                            k.copy("act", Vc[:, nt, hk, 0:64], ps[:, 0:64], r=[bp], w=[bVc])
            P.barrier()
    if "dbg_kc" in sc:
        k.dma("sp", sc["dbg_kc"].rearrange("h d n -> (h d) n"), KcT[:], bKcT)
        k.dma("sp", sc["dbg_vc"], Vc[:], bVc)
    return (KcT, bKcT), (Vc, bVc)


def phase3(k, l, S, x_src, di, sc, cst, cmp_t):
    P = k.P
    NT = S // 128
    NCP = S // 16
    NCT = NCP // 128
    NB = S // 64
    (KcT, bKcT), (Vc, bVc) = cmp_t
    ident, bid = cst["ident"]
    with ExitStack() as st:
        ring = PsumRing(k, st, 3)
        ringO = PsumRing(k, st, 3)
        ringM = PsumRing(k, st, 2)
        KsM = []
        for hk in range(2):
            t, b = k.sb(st, [128, S], BF16, "KsM")
            b2 = Buf("KsMi")
            k.dma("sp", t[hk * 64:(hk + 1) * 64], sc["ksT"][hk * 64:(hk + 1) * 64, :], b)
            k.dma("pool", t[(1 - hk) * 64:(2 - hk) * 64], di["c_ind"], b2)
            KsM.append((t, b, b2))
        Vs, bVs = k.sb(st, [128, NT, 2, 65], BF16, "Vs")
        k.dma("sp", Vs[:], sc["vs"].rearrange("(n p) h c -> p n h c", p=128), bVs)

        def cload(name, shape, src, dt=BF16):
            t, b = k.sb(st, shape, dt, name)
            k.dma("pool" if dt == BF16 else "sp", t[:], src, b)
            return t, b
        caus, bcaus = cload("caus", [128, 128], di["c_caus"])
        low, blow = cload("low", [128, 128], di["c_low"])
        mgen, bmgen = cload("mgen", [128, 1024], di["c_mgen"])
        mt, bmt = cload("mt", [128, 16, 128], di["c_mt"])
        G, bG = cload("G", [128, 256], di["c_g"], F32)
        wbn, bwbn = cload("wbn", [64, 8, 1024], di["w_bnsa"][l].rearrange("(h d) n -> d h n", d=64))
        w_out, bw_out = load_w(k, st, di["w_out"][l], 8, 1024, name="w_out")
        ones16, bones = k.sb(st, [128, 64], BF16, "ones16")
        k.memset("pool", ones16[:], 1.0, [bones])
        rhis = [k.sb(st, [65, 512], BF16, "rhi") for _ in range(4)]
        rlos = [k.sb(st, [65, 512], BF16, "rlo") for _ in range(4)]
        QT = [[k.sb(st, [128, 4, 128], BF16, "QT") for _ in range(2)] for _ in range(2)]
        for pb_ in range(2):
            for hk_ in range(2):
                k.memset("pool", QT[pb_][hk_][0][:], 0.0, [QT[pb_][hk_][1]])
        NHALF = max(1, NB // 64)
        Qsel = [[[k.sb(st, [128, 4, 128], BF16, "Qsel") + (Buf("Qselm"),) for _ in range(NHALF)] for _ in range(2)] for _ in range(2)]
        negm_sw, bnegm_sw = k.sb(st, [128, 128], BF16, "negm_sw")
        k.memset("pool", negm_sw[:], 0.0, [bnegm_sw])
        KwT = [k.sb(st, [128, 640], BF16, "KwT") for _ in range(2)]
        Vw = [k.sb(st, [128, 5, 2, 65], BF16, "Vw") for _ in range(2)]
        grow = [k.sb(st, [65, 2, 3, 4, 128], F32, "grow") for _ in range(2)]
        sgn = [k.sb(st, [128, 8, 128], BF16, "sgn") for _ in range(2)]
        gss = [k.sb(st, [128, 8, 128], BF16, "gss") for _ in range(2)]
        xin = [k.sb(st, [128, D], F32, "xin") for _ in range(2)]
        eg = [k.sb(st, [128, NCP], F32, "eg") for _ in range(4)]
        den4, bden4 = k.sb(st, [128, 4], F32, "den4")
        rden4, brden4 = k.sb(st, [128, 4], F32, "rden4")
        pg, bpg = k.sb(st, [128, NCP + 8], F32, "pg")
        k.memset("pool", pg[:], 0.0, [bpg])
        blk, bblk = k.sb(st, [128, NB], F32, "blk")
        blk2, bblk2 = k.sb(st, [128, NB], F32, "blk2")
        m8, bm8 = k.sb(st, [128, 16], F32, "m8")
        negm, bnegm = k.sb(st, [128, 128], BF16, "negm")
        k.memset("pool", negm[:], 0.0, [bnegm])
        pTs = [k.sb(st, [128, 512], BF16, "pT") for _ in range(5)]
        pti = [0]
        rrs = [k.sb(st, [65, 512], F32, "rr") for _ in range(4)]
        osbs = [k.sb(st, [64, 512], F32, "osb") for _ in range(4)]
        oacc, boacc = k.sb(st, [64, 512], F32, "oacc")
        otmp, botmp = k.sb(st, [64, 512], F32, "otmp")
        oTb = [k.sb(st, [64, 4, 128], BF16, "oTb") for _ in range(2)]
        mrg, bmrg = k.sb(st, [128, 8, 128], BF16, "mrg")
        xm = [k.sb(st, [128, D], F32, "xm") for _ in range(2)]

        def loads(qb):
            s0 = qb * 128
            pb = qb % 2
            qv = sc["qT"].rearrange("(h d) s -> d h s", d=64)
            for hk in range(2):
                t, b = QT[pb][hk]
                k.dma("sp", t[hk * 64:(hk + 1) * 64], qv[:, hk * 4:(hk + 1) * 4, s0:s0 + 128], b)
                for hf in range(NHALF):
                    if hf * 32 <= qb:
                        t, b, _ = Qsel[pb][hk][hf]
                        k.dma("sp", t[hk * 64:(hk + 1) * 64], qv[:, hk * 4:(hk + 1) * 4, s0:s0 + 128], b)
            lo = max(0, s0 - 512)
            t, b = KwT[pb]
            k.dma("sp", t[:, 640 - (s0 + 128 - lo):640], sc["kwT"][:, lo:s0 + 128], b)
            nw = (s0 + 128 - lo) // 128
            t, b = Vw[pb]
            k.dma("sp", t[:, 5 - nw:5], sc["vw"][lo:s0 + 128].rearrange("(n p) h c -> p n h c", p=128), b)
            t, b = grow[pb]
            gv = sc["sgT"].rearrange("(hk g br) s -> hk br g s", hk=2, g=4, br=3)
            for hk in range(2):
                for br in range(3):
                    k.dma("sp", t[64:65, hk, br], gv[hk, br:br + 1, :, s0:s0 + 128], b)
            t, b = sgn[pb]
            k.dma("sp", t[:], sc["sgnT"].rearrange("(c p) s -> p c s", p=128)[:, :, s0:s0 + 128], b)
            t, b = gss[pb]
            k.dma("sp", t[:], sc["gssT"].rearrange("(c p) s -> p c s", p=128)[:, :, s0:s0 + 128], b)
            t, b = xin[pb]
            k.dma("sp", t[:], x_src[s0:s0 + 128, :], b)

        DEPTH = 2
        pipe = []
        delayed = []

        def tick():
            for d in delayed:
                d[0] -= 1
            while delayed and delayed[0][0] <= 0:
                delayed.pop(0)[1]()

        cur_tag = [0]

        def push(score_fn, pv_fn, after=None):
            tok_ = score_fn()
            pipe.append((pv_fn, tok_, after, cur_tag[0]))
            if len(pipe) > DEPTH:
                pv, tk, af, _ = pipe.pop(0)
                pv(tk)
                if af is not None:
                    af()
            tick()

        def flush():
            while pipe:
                pv, tk, af, _ = pipe.pop(0)
                pv(tk)
                if af is not None:
                    af()
            while delayed:
                delayed.pop(0)[1]()

        def attn_tile(Ops, bO, first, last_, KT_ap, bKT, V_ap, bV, Q2, bQ, smask=None, emask=None, after=None):
            def score():
                psS, bS = ring.get()
                nmask = (4 if smask is not None else 0) + (1 if emask is not None else 0)
                rl = (bKT if isinstance(bKT, list) else [bKT]) + (bQ if isinstance(bQ, list) else [bQ])
                k.mm(psS[:, 0:512], KT_ap, Q2, True, nmask == 0, r=rl, w=[bS])
                done = 0
                assert emask is None
                if smask is not None:
                    m_ap, bm = smask
                    for g in range(4):
                        done += 1
                        k.mm(psS[:, g * 128:(g + 1) * 128], ident[:], m_ap, False, done == nmask, r=[bid, bm], w=[bS])
                pT, bpT = pTs[pti[0] % len(pTs)]
                pti[0] += 1
                k.act(pT[:], psS[:, 0:512], AF.Exp, r=[bS], w=[bpT], scale=0.125)
                return (pT, bpT)

            def pv(tk):
                pT, bpT = tk
                k.mm(Ops[0:65, 0:512], V_ap, pT[:], first, last_, r=[bV, bpT], w=[bO])
            push(score, pv, after)

        fin_i = [0]

        def finalize(Ops, bO, gate_ap, bgate, first_branch, out_final=None, bout=None):
            tag_ = cur_tag[0]

            def stage_a():
                rr, brr = rrs[fin_i[0] % 4]
                rhi, brhi = rhis[fin_i[0] % 4]
                rlo, brlo = rlos[fin_i[0] % 4]
                osb, bosb = osbs[fin_i[0] % 4]
                fin_i[0] += 1
                k.ts("dve", rr[64:65, :], Ops[64:65, 0:512], 1e-20, ALU.max, r=[bO], w=[brr])
                k.recip(rr[64:65, :], rr[64:65, :], r=[brr], w=[brr])
                k.tt("dve", rr[64:65, :], rr[64:65, :], gate_ap, ALU.mult, r=[brr, bgate], w=[brr])
                k.copy("dve", rhi[64:65, :], rr[64:65, :], r=[brr], w=[brhi])
                k.tt("dve", rlo[64:65, :], rr[64:65, :], rhi[64:65, :], ALU.subtract, r=[brr, brhi], w=[brlo])
                k.copy("act", osb[:], Ops[0:64, 0:512], r=[bO], w=[bosb])

                def stage_b():
                    psb, bpb = ringM.get()
                    k.mm(psb[0:64, 0:512], ones16[64:65, 0:64], rhi[64:65, :], True, False, r=[bones, brhi], w=[bpb])
                    k.mm(psb[0:64, 0:512], ones16[64:65, 0:64], rlo[64:65, :], False, True, r=[bones, brlo], w=[bpb])
                    if first_branch:
                        k.tt("dve", oacc[:], osb[:], psb[0:64, 0:512], ALU.mult, r=[bosb, bpb], w=[boacc])
                    else:
                        k.tt("dve", otmp[:], osb[:], psb[0:64, 0:512], ALU.mult, r=[bosb, bpb], w=[botmp])
                        if out_final is None:
                            k.tt("pool", oacc[:], oacc[:], otmp[:], ALU.add, r=[boacc, botmp], w=[boacc])
                        else:
                            k.tt("pool", out_final, oacc[:], otmp[:], ALU.add, r=[boacc, botmp], w=[bout])
                delayed.append([3, stage_b, tag_])
            return stage_a

        mtmps = [k.sb(st, [128, 128], F32, "mtmp") for _ in range(4)]

        def epilogue1(qb):
            pb = qb % 2
            if "dbg_o" in sc:
                for hk in range(2):
                    k.dma("sp", sc["dbg_o"][hk, :, qb], oTb[hk][0][:].rearrange("d g q -> d (g q)"), oTb[hk][1])
            sg_t, bsgn_ = sgn[pb]
            gs_t, bgs_ = gss[pb]
            for half in range(2):
                ps, bp = ringM.get()
                for f4 in range(4):
                    fc = half * 4 + f4
                    for h in range(8):
                        k.mm(ps[:, f4 * 128:(f4 + 1) * 128], wbn[:, h, fc * 128:(fc + 1) * 128], oTb[h // 4][0][:, h % 4, :],
                             h == 0, h == 7, r=[bwbn, oTb[h // 4][1]], w=[bp])
                for f4 in range(4):
                    fc = half * 4 + f4
                    mt_, bmt_ = mtmps[fc % 4]
                    k.tt("dve", mt_[:], ps[:, f4 * 128:(f4 + 1) * 128], sg_t[:, fc, :], ALU.mult, r=[bp, bsgn_], w=[bmt_])
                    k.tt("pool", mrg[:, fc, :], mt_[:], gs_t[:, fc, :], ALU.add, r=[bmt_, bgs_], w=[bmrg])
            delayed.append([8, lambda: epilogue2(qb), qb])

        def epilogue2(qb):
            s0 = qb * 128
            pb = qb % 2
            x_t, bx = xin[pb]
            xm_t, bxm = xm[qb % 2]
            for half in range(2):
                ps, bp = ringM.get()
                for fc in range(8):
                    k.mm(ps[:, 0:512], mrg[:, fc, :], w_out[:, fc, half * 512:(half + 1) * 512], fc == 0, fc == 7, r=[bmrg, bw_out[fc]], w=[bp])
                k.tt("dve", xm_t[:, half * 512:(half + 1) * 512], ps[:, 0:512], x_t[:, half * 512:(half + 1) * 512], ALU.add, r=[bp, bx], w=[bxm])
            k.dma("sp", sc["xmid"][s0:s0 + 128, :], xm_t[:], bxm)

        def force(tag_max):
            while pipe and pipe[0][3] <= tag_max:
                pv, tk, af, _ = pipe.pop(0)
                pv(tk)
                if af is not None:
                    af()
            progressed = True
            while progressed:
                progressed = False
                for idx_, d_ in enumerate(delayed):
                    if d_[2] <= tag_max:
                        delayed.pop(idx_)
                        d_[1]()
                        progressed = True
                        break

        def chainA(qb, hk):
            pb = qb % 2
            Qt, bQ = QT[pb][hk]
            for g in range(4):
                ps, bp = ringM.get()
                k.mm(ps[:, 0:NCP], Qt[:, g, :], KcT[:, 0:NCP], True, False, r=[bQ, bKcT], w=[bp])
                k.mm(ps[:, 0:NCP], ident[:], mgen[:, 512 - 8 * qb:512 - 8 * qb + NCP], False, True, r=[bid, bmgen], w=[bp])
                k.act(eg[g][0][:], ps[:, 0:NCP], AF.Exp, r=[bp], w=[eg[g][1], bden4], scale=0.125, accum=den4[:, g:g + 1])
            k.ts("dve", rden4[:], den4[:], 1e-20, ALU.max, r=[bden4], w=[brden4])
            k.recip(rden4[:], rden4[:], r=[brden4], w=[brden4])
            k.ts("dve", pg[:, 1:1 + NCP], eg[0][0][:], rden4[:, 0:1], ALU.mult, r=[eg[0][1], brden4], w=[bpg])
            for g in range(1, 4):
                k.stt(pg[:, 1:1 + NCP], eg[g][0][:], rden4[:, g:g + 1], pg[:, 1:1 + NCP], ALU.mult, ALU.add, r=[eg[g][1], brden4, bpg], w=[bpg])
            P.op("dve", lambda e: e.tensor_reduce(out=blk[:], in_=pg[:, 0:NCP].rearrange("p (j o) -> p j o", o=4), axis=AX.X, op=ALU.add), reads=[bpg], writes=[bblk])
            k.tt("dve", blk[:], blk[:], pg[:, 4:4 + 4 * NB:4], ALU.add, r=[bblk, bpg], w=[bblk])
            k.tt("dve", blk[:], blk[:], G[:, 126 - 2 * qb:126 - 2 * qb + NB], ALU.add, r=[bblk, bG], w=[bblk])
            if qb >= 1:
                k.ts("dve", blk[:, 0:1], blk[:, 0:1], 1e4, ALU.add, r=[bblk], w=[bblk])
            P.op("dve", lambda e: e.max(out=m8[:, 0:8], in_=blk[:]), reads=[bblk], writes=[bm8])
            P.op("dve", lambda e: e.match_replace(out=blk2[:], in_to_replace=m8[:, 0:8], in_values=blk[:], imm_value=-3e38), reads=[bblk, bm8], writes=[bblk2])
            P.op("dve", lambda e: e.max(out=m8[:, 8:16], in_=blk2[:]), reads=[bblk2], writes=[bm8])
            nhalf_used = 1 if qb < 32 else NHALF
            need_nat = (hk == 1) or nhalf_used > 1
            need_sw = (hk == 0) or nhalf_used > 1
            if need_nat:
                k.ts("dve", negm[:, 0:NB], blk[:], m8[:, 15:16], ALU.is_lt, r=[bblk, bm8], w=[bnegm], s2=NEG, op1=ALU.mult)
            if need_sw:
                n0 = min(NB, 64)
                k.ts("dve", negm_sw[:, 64:64 + n0], blk[:, 0:n0], m8[:, 15:16], ALU.is_lt, r=[bblk, bm8], w=[bnegm_sw], s2=NEG, op1=ALU.mult)
                if NB > 64:
                    k.ts("dve", negm_sw[:, 0:NB - 64], blk[:, 64:NB], m8[:, 15:16], ALU.is_lt, r=[bblk, bm8], w=[bnegm_sw], s2=NEG, op1=ALU.mult)

        def chainB(qb, hk):
            pb = qb % 2
            os_ = slice((1 - hk) * 64, (2 - hk) * 64)
            nhalf_used = 1 if qb < 32 else NHALF
            for hf in range(nhalf_used):
                use_sw = (hk == 0 and hf == 0) or (hk == 1 and hf == 1)
                src_t, bsrc = (negm_sw, bnegm_sw) if use_sw else (negm, bnegm)
                ps, bp = ringM.get()
                pbf = ps[:].bitcast(BF16)
                k.tr(pbf[:, 0:128], src_t[:], ident[:], r=[bsrc, bid], w=[bp])
                qs_t, _, bqm = Qsel[pb][hk][hf]
                for g in range(4):
                    k.copy("dve", qs_t[os_, g, :], pbf[os_, 0:128], r=[bp], w=[bqm])

        loads(0)
        chainA(0, 0)
        for qb in range(NT):
            s0 = qb * 128
            pb = qb % 2
            for hk in range(2):
                cur_tag[0] = qb
                Qt, bQ = QT[pb][hk]
                Q2 = Qt[:].rearrange("d g q -> d (g q)")
                gr, bgr = grow[pb]
                chainB(qb, hk)
                Ops, bO = ringO.get()
                fa = finalize(Ops, bO, gr[64:65, hk, 0].rearrange("o g q -> o (g q)"), bgr, True)
                tiles = [nt for nt in range(NCT) if qb - 16 * nt >= 0]
                for idx, nt in enumerate(tiles):
                    m = qb - 16 * nt
                    sm = (mt[:, m, :], bmt) if m < 16 else None
                    attn_tile(Ops, bO, idx == 0, idx == len(tiles) - 1, KcT[:, nt * 128:(nt + 1) * 128], bKcT,
                              Vc[:, nt, hk, :], bVc, Q2, bQ, smask=sm, after=fa if idx == len(tiles) - 1 else None)
                Ops, bO = ringO.get()
                fa = finalize(Ops, bO, gr[64:65, hk, 2].rearrange("o g q -> o (g q)"), bgr, False)
                tiles = [wt for wt in range(5) if s0 - 512 + 128 * wt >= 0]
                kw_full, bkw = KwT[pb]
                kw_t = kw_full
                vw_t, bvw = Vw[pb]
                for idx, wt in enumerate(tiles):
                    sm = (low[:], blow) if wt == 0 else ((caus[:], bcaus) if wt == 4 else None)
                    attn_tile(Ops, bO, idx == 0, idx == len(tiles) - 1, kw_t[:, wt * 128:(wt + 1) * 128], bkw,
                              vw_t[:, wt, hk, :], bvw, Q2, bQ, smask=sm, after=fa if idx == len(tiles) - 1 else None)
                if hk == 1 and qb + 1 < NT:
                    force(qb - 1)
                    loads(qb + 1)
                if hk == 0:
                    chainA(qb, 1)
                elif qb + 1 < NT:
                    chainA(qb + 1, 0)
                Ops, bO = ringO.get()
                fa = finalize(Ops, bO, gr[64:65, hk, 1].rearrange("o g q -> o (g q)"), bgr, False,
                              out_final=oTb[hk][0][:].rearrange("d g q -> d (g q)"), bout=oTb[hk][1])
                for i in range(qb + 1):
                    sm = (caus[:], bcaus) if i == qb else None
                    qs_t, bqs, bqm = Qsel[pb][hk][i // 32]
                    attn_tile(Ops, bO, i == 0, i == qb, KsM[hk][0][:, i * 128:(i + 1) * 128], [KsM[hk][1], KsM[hk][2]],
                              Vs[:, i, hk, :], bVs, qs_t[:].rearrange("d g q -> d (g q)"), [bqs, bqm], smask=sm, after=fa if i == qb else None)
                if hk == 1:
                    delayed_ep = (lambda q_=qb: (lambda: delayed.append([4, lambda: epilogue1(q_), q_])))(qb)
                    pipe[-1] = (pipe[-1][0], pipe[-1][1], (lambda f1=pipe[-1][2], f2=delayed_ep: (f1(), f2())), pipe[-1][3])
        flush()
    P.barrier()


def phase4(k, l, S, di, sc, cst, dst, last):
    P = k.P
    TT = 256
    NTT = S // TT
    NSB = TT // 128
    ident, bid = cst["ident"]
    with ExitStack() as st:
        ring = PsumRing(k, st)
        w_fin, bw_fin = load_w(k, st, di["w_fin"][l], 8, 2 * DFF, name="w_fin")
        w_fo, bw_fo = load_w(k, st, di["w_fout"][l], NFC, D, name="w_fo")
        gam, bgam = k.sb(st, [128, D], F32, "gam")
        k.dma("sp", gam[:], di["g_ffn"][l], bgam)
        cw, bcw = k.sb(st, [128, NFC, 3], F32, "cw")
        k.dma("sp", cw[:], di["f_cw"][l], bcw)
        cb, bcb = k.sb(st, [128, NFC], F32, "cb")
        k.dma("sp", cb[:], di["f_cb"][l], bcb)
        if last:
            gfin, bgfin = k.sb(st, [128, D], F32, "gfin")
            k.dma("sp", gfin[:], di["g_fin"], bgfin)
        halo, bhalo = k.sb(st, [128, NFC, 2], F32, "halo")
        k.memset("pool", halo[:], 0.0, [bhalo])
        xt = [k.sb(st, [128, NSB, D], F32, "xt") for _ in range(2)]
        junk, bjunk = k.sb(st, [128, D], BF16, "junk")
        ss, bss = k.sb(st, [128, 2 * NSB], F32, "ss")
        ms, bms = k.sb(st, [128, 2 * NSB], F32, "ms")
        sd, bsd = k.sb(st, [128, 2 * NSB], F32, "sd")
        rstd, brstd = k.sb(st, [128, 2 * NSB], F32, "rstd")
        hh = [k.sb(st, [128, D], BF16, "h") for _ in range(2)]
        hTs = [k.sb(st, [128, 8, TT], BF16, "hT") for _ in range(2)]
        a_sb = [k.sb(st, [128, TT + 2], F32, "a_sb") for _ in range(2)]
        cv = [k.sb(st, [128, TT], F32, "cv") for _ in range(2)]
        gl = [k.sb(st, [128, TT], F32, "gl") for _ in range(2)]
        actTs = [k.sb(st, [128, NFC, TT], BF16, "actT") for _ in range(2)]
        xo = [k.sb(st, [128, NSB, D], F32, "xo") for _ in range(1)]

        def load_tile(i):
            t, b = xt[i % 2]
            k.dma("sp", t[:], sc["xmid"][i * TT:(i + 1) * TT, :].rearrange("(s p) d -> p s d", p=128), b)

        def rms(x_ap, bx, col, g_t, bg, out_ap, bout):
            k.act(junk[:], x_ap, AF.Square, r=[bx], w=[bjunk, bss], accum=ss[:, col:col + 1])
            k.ts("dve", ms[:, col:col + 1], ss[:, col:col + 1], 1.0 / D, ALU.mult, r=[bss], w=[bms], s2=EPS, op1=ALU.add)
            k.act(sd[:, col:col + 1], ms[:, col:col + 1], AF.Sqrt, r=[bms], w=[bsd])
            k.recip(rstd[:, col:col + 1], sd[:, col:col + 1], r=[bsd], w=[brstd])
            k.stt(out_ap, x_ap, rstd[:, col:col + 1], g_t[:], ALU.mult, ALU.mult, r=[bx, brstd, bg], w=[bout])

        def prep(i):
            x_t, bx = xt[i % 2]
            hT, bhT = hTs[i % 2]
            for s_ in range(NSB):
                h_t, bh = hh[s_ % 2]
                rms(x_t[:, s_, :], bx, s_, gam, bgam, h_t[:], bh)
                ps, bp = ring.get()
                pbf = ps[:].bitcast(BF16)
                for c in range(8):
                    k.tr(pbf[:, c * 128:(c + 1) * 128], h_t[:, c * 128:(c + 1) * 128], ident[:], r=[bh, bid], w=[bp])
                k.copy("act", hT[:, :, s_ * 128:(s_ + 1) * 128], pbf.rearrange("p (c t) -> p c t", c=8), r=[bp], w=[bhT])

        def inproj(i):
            hT, bhT = hTs[i % 2]
            actT, bactT = actTs[i % 2]
            for fc in range(NFC):
                psa, bpa = ring.get()
                for c in range(8):
                    k.mm(psa[:, 0:TT], w_fin[:, c, fc * 128:(fc + 1) * 128], hT[:, c, :], c == 0, c == 7, r=[bw_fin[c], bhT], w=[bpa])
                psb, bpb = ring.get()
                for c in range(8):
                    k.mm(psb[:, 0:TT], w_fin[:, c, DFF + fc * 128:DFF + (fc + 1) * 128], hT[:, c, :], c == 0, c == 7, r=[bw_fin[c], bhT], w=[bpb])
                a_t, ba = a_sb[fc % 2]
                c_t, bc = cv[fc % 2]
                g_t, bg = gl[fc % 2]
                k.copy("pool", a_t[:, 0:2], halo[:, fc, :], r=[bhalo], w=[ba])
                k.copy("act", a_t[:, 2:2 + TT], psa[:, 0:TT], r=[bpa], w=[ba])
                k.copy("pool", halo[:, fc, :], a_t[:, TT:TT + 2], r=[ba], w=[bhalo])
                k.ts("dve", c_t[:], a_t[:, 2:2 + TT], cw[:, fc, 2:3], ALU.mult, r=[ba, bcw, bcb], w=[bc], s2=cb[:, fc:fc + 1], op1=ALU.add)
                k.stt(c_t[:], a_t[:, 1:1 + TT], cw[:, fc, 1:2], c_t[:], ALU.mult, ALU.add, r=[ba, bcw, bc], w=[bc])
                k.stt(c_t[:], a_t[:, 0:TT], cw[:, fc, 0:1], c_t[:], ALU.mult, ALU.add, r=[ba, bcw, bc], w=[bc])
                k.act(g_t[:], c_t[:], AF.Gelu_apprx_tanh, r=[bc], w=[bg])
                k.tt("dve", actT[:, fc, :], psb[:, 0:TT], g_t[:], ALU.mult, r=[bpb, bg], w=[bactT])

        def outproj(i):
            x_t, bx = xt[i % 2]
            actT, bactT = actTs[i % 2]
            xo_t, bxo = xo[0]
            for s_ in range(NSB):
                for half in range(2):
                    ps, bp = ring.get()
                    for fc in range(NFC):
                        k.mm(ps[:, 0:512], actT[:, fc, s_ * 128:(s_ + 1) * 128], w_fo[:, fc, half * 512:(half + 1) * 512], fc == 0, fc == NFC - 1,
                             r=[bactT, bw_fo[fc]], w=[bp])
                    k.tt("dve", xo_t[:, s_, half * 512:(half + 1) * 512], ps[:, 0:512], x_t[:, s_, half * 512:(half + 1) * 512], ALU.add, r=[bp, bx], w=[bxo])
            if last:
                for s_ in range(NSB):
                    rms(xo_t[:, s_, :], bxo, NSB + s_, gfin, bgfin, xo_t[:, s_, :], bxo)
            k.dma("sp", dst[i * TT:(i + 1) * TT, :].rearrange("(s p) d -> p s d", p=128), xo_t[:], bxo)

        load_tile(0)
        if NTT > 1:
            load_tile(1)
        prep(0)
        for i in range(NTT):
            inproj(i)
            if i + 1 < NTT:
                prep(i + 1)
            outproj(i)
            if i + 2 < NTT:
                load_tile(i + 2)
    P.barrier()


INPUT_SHAPES = None


def build(S, L, TS=128, dbg=False, phases=("p1", "p2", "p3", "p4")):
    nc = bass.Bass("TRN2", target_bir_lowering=False)
    NT = S // 128
    di = {}

    def din(name, shape):
        di[name] = nc.dram_tensor(name, list(shape), F32, kind="ExternalInput").ap()
    din("x", [S, D])
    for nm, shp in (("g_mix", [L, 128, D]), ("g_ffn", [L, 128, D]), ("g_fin", [128, D]),
                    ("w_in", [L, D, INW]), ("w_sw", [L, D, 896]),
                    ("s_are", [L, 128, 16]), ("s_aim", [L, 128, 16]), ("s_ldt", [L, 128, 16]),
                    ("s_bre", [L, 128, 16, 128]), ("s_bim", [L, 128, 16, 128]),
                    ("s_cre", [L, 128, 16, 128]), ("s_cim", [L, 128, 16, 128]), ("s_d", [L, 128, 4]),
                    ("w_glu", [L, 512, 1024]), ("w_bssm", [L, 512, 1024]), ("w_bnsa", [L, 512, 1024]),
                    ("w_out", [L, D, D]),
                    ("c_pek", [L, 64, 32]), ("c_w1k", [L, 2048, 256]), ("c_b1k", [L, 128, 2]), ("c_w2k", [L, 256, 64]),
                    ("c_pev", [L, 64, 32]), ("c_w1v", [L, 2048, 256]), ("c_b1v", [L, 128, 2]), ("c_w2v", [L, 256, 64]),
                    ("w_fin", [L, D, 2 * DFF]), ("w_fout", [L, DFF, D]), ("f_cw", [L, 128, NFC, 3]), ("f_cb", [L, 128, NFC]),
                    ("c_cos", [128, S]), ("c_sin", [128, S]), ("c_ident", [128, 128]), ("c_tau", [128, TS]),
                    ("c_caus", [128, 128]), ("c_low", [128, 128]), ("c_mgen", [128, 1024]), ("c_mt", [128, 16, 128]),
                    ("c_g", [128, 256]), ("c_ind", [64, S])):
        din(nm, shp)
    out = nc.dram_tensor("out", [S, D], F32, kind="ExternalOutput").ap()
    skind = "ExternalOutput" if dbg else "Internal"
    sc = {}

    def scr(name, shape, dt):
        sc[name] = nc.dram_tensor(name, list(shape), dt, kind=skind).ap()
    scr("qT", [512, S], BF16)
    for nm in ("kcT", "vcT", "ksT", "kwT"):
        scr(nm, [128, S], BF16)
    scr("vs", [S, 2, 65], BF16)
    scr("vw", [S, 2, 65], BF16)
    scr("sgT", [24, S], F32)
    scr("sgnT", [1024, S], BF16)
    scr("gssT", [1024, S], BF16)
    scr("xmid", [S, D], F32)
    if dbg:
        scr("dbg_o", [2, 64, NT, 512], BF16)
        scr("dbg_kc", [2, 64, S // 16], BF16)
        scr("dbg_vc", [128, S // 2048, 2, 65], BF16)
    scr("x1", [S, D], F32)
    with ExitStack() as st:
        P = Prog(nc)
        k = K(nc, P)
        cst = {}
        ident, bid = k.sb(st, [128, 128], BF16, "ident")
        k.dma("pool", ident[:], di["c_ident"], bid)
        cst["ident"] = (ident, bid)
        x_src = di["x"]
        for l in range(L):
            last = l == L - 1
            if "p1" in phases:
                phase1(k, l, S, TS, x_src, di, sc, cst)
            if "p2" in phases:
                pers = ExitStack()
                cmp_t = phase2(k, pers, l, S, di, sc, cst)
            if "p3" in phases:
                phase3(k, l, S, x_src, di, sc, cst, cmp_t)
            if "p2" in phases:
                pers.close()
                P.barrier()
            if "p4" in phases:
                phase4(k, l, S, di, sc, cst, out if last else sc["x1"], last)
            x_src = sc["x1"]
        P.barrier()
        P.emit(st)
    return nc


_NC_CACHE = {}


def kernel(**inputs):
    S, L, NCORES = 8192, 2, 8
    inp = {k_: np.asarray(v) for k_, v in inputs.items()}
    hl = host_layout(inp, L)
    hc = host_consts(S, 128)
    common = {}
    common.update(hl)
    common.update(hc)
    common = {k_: np.ascontiguousarray(v, dtype=np.float32) for k_, v in common.items()}
    if "nc" not in _NC_CACHE:
        _NC_CACHE["nc"] = build(S, L)
    nc = _NC_CACHE["nc"]
    x = np.asarray(inp["x"], dtype=np.float32)
    in_maps = []
    for b in range(NCORES):
        m = dict(common)
        m["x"] = np.ascontiguousarray(x[b])
        in_maps.append(m)
    res = run_bass_kernel_spmd(nc, in_maps, core_ids=list(range(NCORES)))
    return np.stack([np.asarray(r["out"], dtype=np.float32) for r in res.results], axis=0)
```

```python
from contextlib import ExitStack
import numpy as np
import ml_dtypes
import concourse.bass as bass
import concourse.mybir as mybir
from concourse.bass_utils import run_bass_kernel_spmd

F32 = mybir.dt.float32
BF16 = mybir.dt.bfloat16
I32 = mybir.dt.int32
ALU = mybir.AluOpType
AF = mybir.ActivationFunctionType
AX = mybir.AxisListType

D = 1024
DFF = 2816
NFC = DFF // 128
INW = 3864
EPS = 1e-6
NEG = -30000.0
TWO_PI = float(2 * np.pi)
SIN_SCALE = TWO_PI * 0.999999


class Buf:
    __slots__ = ("name", "w", "rs", "sem", "cnt", "slot", "base", "uid")

    def __init__(self, name="b"):
        self.name = name
        self.w = None
        self.rs = {}
        self.sem = None
        self.cnt = 0
        self.slot = None
        self.base = 0
        self.uid = None


class Op:
    __slots__ = ("eng", "fn", "deps", "key", "val", "signal", "sigval", "dma", "slot", "semval")


ENGS = ("pe", "act", "dve", "pool", "sp")


class Prog:
    def __init__(self, nc):
        self.nc = nc
        self.ops = {e: [] for e in ENGS}
        self.seen = {e: {} for e in ENGS}
        self.dma_bufs = []
        self.last = {}
        self.slot_base = []
        self.free_slots = []
        self.live = []
        self.uid = 0

    def _get_slot(self, buf):
        if self.free_slots:
            sl = self.free_slots.pop()
        else:
            sl = len(self.slot_base)
            self.slot_base.append(0)
        self.uid += 1
        buf.sem = True
        buf.slot = sl
        buf.base = self.slot_base[sl]
        buf.cnt = 0
        buf.uid = self.uid
        self.live.append(buf)

    def barrier(self):
        lasts = list(self.last.values())
        self._barrier_ops(lasts)
        for b in self.live:
            self.slot_base[b.slot] = b.base + b.cnt
            self.free_slots.append(b.slot)
            b.sem = None
        self.live = []
        self.last = {kk: v for kk, v in self.last.items() if not isinstance(kk, tuple)}

    def _barrier_ops(self, lasts):
        for e in ENGS:
            o = Op()
            o.eng = e
            o.fn = None
            o.deps = []
            o.signal = False
            o.sigval = None
            o.dma = None
            o.key = e
            o.val = len(self.ops[e])
            for d in lasts:
                if d.key == e:
                    continue
                if self.seen[e].get(d.key, -1) >= d.val:
                    continue
                self.seen[e][d.key] = d.val
                o.deps.append(d)
            self.ops[e].append(o)

    def _dep(self, eng, d, deps, same_ok):
        if d is None:
            return
        key = d.key
        if key == eng:
            if eng == "pe" or same_ok:
                return
        if self.seen[eng].get(key, -1) >= d.val:
            return
        self.seen[eng][key] = d.val
        deps.append(d)

    def op(self, eng, fn, reads=(), writes=(), dma=None):
        o = Op()
        o.eng = eng
        o.fn = fn
        o.deps = []
        o.signal = False
        o.sigval = None
        o.dma = dma
        writes = [b for b in writes if b is not None]
        reads = [b for b in reads if b is not None]
        o.slot = None
        o.semval = None
        if dma is not None:
            if dma.sem is None:
                self._get_slot(dma)
            dma.cnt += 1
            o.key = ("dma", dma.uid)
            o.val = dma.cnt
            o.slot = dma.slot
            o.semval = 16 * (dma.base + dma.cnt)
            if dma not in writes:
                writes.append(dma)
            reads = [b for b in reads if b is not dma]
        else:
            o.key = eng
            o.val = len(self.ops[eng])
        for b in reads:
            self._dep(eng, b.w, o.deps, False)
        for b in writes:
            self._dep(eng, b.w, o.deps, True)
            for r in b.rs.values():
                self._dep(eng, r, o.deps, True)
        for b in reads:
            b.rs[o.key] = o
        for b in writes:
            b.w = o
            b.rs = {}
        self.ops[eng].append(o)
        self.last[o.key] = o
        return o

    def emit(self, stack):
        nc = self.nc
        for e in ENGS:
            for o in self.ops[e]:
                for d in o.deps:
                    if d.dma is None:
                        d.signal = True
        for e in ENGS:
            c = 0
            for o in self.ops[e]:
                if o.dma is None and o.signal:
                    c += 1
                    o.sigval = c
        esem = {}
        for e in ("pe", "act", "dve", "pool"):
            esem[e] = stack.enter_context(nc.semaphore("s_" + e))
        dsem = [stack.enter_context(nc.semaphore("d%d" % i)) for i in range(len(self.slot_base))]
        block = stack.enter_context(nc.Block())
        prog = self

        def run(name, eng):
            for o in prog.ops[name]:
                for d in o.deps:
                    if d.dma is not None:
                        eng.wait_ge(dsem[d.slot], d.semval)
                    else:
                        eng.wait_ge(esem[d.key], d.sigval)
                if o.fn is None:
                    continue
                ins = o.fn(eng)
                if o.dma is not None:
                    ins.then_inc(dsem[o.slot], 16)
                elif o.signal:
                    ins.then_inc(esem[name], 1)

        @block.sync
        def _(eng):
            run("sp", eng)

        @block.scalar
        def _(eng):
            run("act", eng)

        @block.vector
        def _(eng):
            run("dve", eng)

        @block.gpsimd
        def _(eng):
            run("pool", eng)

        @block.tensor
        def _(eng):
            run("pe", eng)


class K:
    def __init__(self, nc, P):
        self.nc = nc
        self.P = P
        self.n = 0

    def name(self, s):
        self.n += 1
        return "%s_%d" % (s, self.n)

    def sb(self, st, shape, dt=F32, name="t"):
        t = st.enter_context(self.nc.sbuf_tensor(self.name(name), list(shape), dt))
        return t, Buf(name)

    def dma(self, eng, out, in_, buf, reads=(), writes=()):
        self.P.op(eng, lambda e: e.dma_start(out=out, in_=in_), reads=reads, writes=writes, dma=buf)

    def mm(self, out, lhsT, rhs, start, stop, r, w):
        self.P.op("pe", lambda e: e.matmul(out, lhsT=lhsT, rhs=rhs, start=start, stop=stop), reads=r, writes=w)

    def tr(self, out, in_, ident, r, w):
        self.P.op("pe", lambda e: e.transpose(out=out, in_=in_, identity=ident), reads=r, writes=w)

    def act(self, out, in_, func, r, w, bias=None, scale=None, accum=None):
        kw = {}
        if bias is not None:
            kw["bias"] = bias
        if scale is not None:
            kw["scale"] = scale
        if accum is not None:
            kw["accum_out"] = accum
        self.P.op("act", lambda e: e.activation(out=out, in_=in_, func=func, **kw), reads=r, writes=w)

    def tt(self, eng, out, in0, in1, op, r, w):
        self.P.op(eng, lambda e: e.tensor_tensor(out=out, in0=in0, in1=in1, op=op), reads=r, writes=w)

    def ts(self, eng, out, in0, s1, op0, r, w, s2=None, op1=None):
        if op1 is None:
            self.P.op(eng, lambda e: e.tensor_scalar(out=out, in0=in0, scalar1=s1, scalar2=None, op0=op0), reads=r, writes=w)
        else:
            self.P.op(eng, lambda e: e.tensor_scalar(out=out, in0=in0, scalar1=s1, scalar2=s2, op0=op0, op1=op1), reads=r, writes=w)

    def stt(self, out, in0, scalar, in1, op0, op1, r, w):
        self.P.op("dve", lambda e: e.scalar_tensor_tensor(out=out, in0=in0, scalar=scalar, in1=in1, op0=op0, op1=op1), reads=r, writes=w)

    def copy(self, eng, out, in_, r, w):
        if eng == "act":
            self.P.op("act", lambda e: e.activation(out=out, in_=in_, func=AF.Copy), reads=r, writes=w)
        else:
            self.P.op(eng, lambda e: e.tensor_copy(out=out, in_=in_), reads=r, writes=w)

    def memset(self, eng, ap, val, w):
        self.P.op(eng, lambda e: e.memset(ap, val), writes=w)

    def scan(self, out, d0, d1, init, r, w):
        self.P.op("dve", lambda e: e.tensor_tensor_scan(out=out, data0=d0, data1=d1, initial=init, op0=ALU.mult, op1=ALU.add), reads=r, writes=w)

    def recip(self, out, in_, r, w):
        self.P.op("dve", lambda e: e.reciprocal(out=out, in_=in_), reads=r, writes=w)


class PsumRing:
    def __init__(self, k, st, n=8):
        self.banks = []
        for i in range(n):
            t = st.enter_context(k.nc.psum_tensor(k.name("ps"), [128, 512], F32))
            self.banks.append((t, Buf("ps%d" % i)))
        self.i = 0

    def get(self):
        t, b = self.banks[self.i % len(self.banks)]
        self.i += 1
        return t, b


def _swap_halves(w):
    sh = w.shape
    w4 = w.reshape(sh[:-1] + (sh[-1] // 64, 2, 32))
    return np.ascontiguousarray(w4[..., ::-1, :]).reshape(sh)


def host_consts(S, TS):
    c = {}
    inv = (10000.0 ** (-np.arange(0, 64, 2, dtype=np.float32) / np.float32(64))).astype(np.float32)
    ang = (np.arange(S, dtype=np.float32)[:, None] * inv[None, :]).astype(np.float32)
    cs, sn = np.cos(ang).astype(np.float32), np.sin(ang).astype(np.float32)
    cosT = np.concatenate([cs.T, cs.T], 0)
    sinT = np.concatenate([-sn.T, sn.T], 0)
    c["c_cos"] = np.ascontiguousarray(np.concatenate([cosT, cosT], 0))
    c["c_sin"] = np.ascontiguousarray(np.concatenate([sinT, sinT], 0))
    c["c_ident"] = np.eye(128, dtype=np.float32)
    c["c_tau"] = np.ascontiguousarray(np.broadcast_to(np.arange(TS, dtype=np.float32)[None, :], (128, TS)))
    k = np.arange(128)[:, None]
    q = np.arange(128)[None, :]
    c["c_caus"] = np.where(k <= q, 0.0, NEG).astype(np.float32)
    c["c_low"] = np.where(k > q, 0.0, NEG).astype(np.float32)
    cc = np.arange(1024)[None, :]
    qi = np.arange(128)[:, None]
    c["c_mgen"] = np.where(16 * (cc - 512) + 31 <= qi, 0.0, NEG).astype(np.float32)
    m = np.arange(16)[None, :, None]
    ni = np.arange(128)[:, None, None]
    qq = np.arange(128)[None, None, :]
    c["c_mt"] = np.where(16 * ni + 31 <= 128 * m + qq, 0.0, NEG).astype(np.float32)
    rel = np.arange(256)[None, :] - 126
    cur = (np.arange(128)[:, None] >= 64).astype(np.int64)
    g = np.where(rel > cur, -1e30, 0.0) + np.where((rel == cur) | (rel == cur - 1), 1e4, 0.0)
    c["c_g"] = g.astype(np.float32)
    NT = S // 128
    j = np.arange(128)[:, None, None]
    i = np.arange(NT)[None, :, None]
    kk = np.arange(128)[None, None, :]
    c["c_e"] = (j == 2 * i + (kk >= 64)).astype(np.float32)
    r_ = np.arange(64)[:, None]
    cidx = np.arange(S)[None, :]
    c["c_ind"] = (((cidx // 64) % 64) == r_).astype(np.float32)
    return c


def host_layout(inp, L):
    o = {}
    f = np.float32
    o["g_mix"] = np.ascontiguousarray(np.broadcast_to(inp["norm_mix"][:, None, :], (L, 128, D))).astype(f)
    o["g_ffn"] = np.ascontiguousarray(np.broadcast_to(inp["norm_ffn"][:, None, :], (L, 128, D))).astype(f)
    o["g_fin"] = np.ascontiguousarray(np.broadcast_to(inp["norm_final"][None, :], (128, D))).astype(f)
    w_in = inp["w_in"]
    o["w_in"] = w_in
    sw = np.concatenate([_swap_halves(w_in[:, :, 512:1024]), _swap_halves(w_in[:, :, 1024:1152]),
                         _swap_halves(w_in[:, :, 1280:1408]), _swap_halves(w_in[:, :, 1536:1664])], axis=-1)
    o["w_sw"] = np.ascontiguousarray(sw)

    def pair(a):
        return np.ascontiguousarray(a.reshape(L, 16, 2, 64).transpose(0, 2, 3, 1).reshape(L, 128, 16))
    o["s_are"] = pair(inp["ssm_a_re"])
    o["s_aim"] = pair(inp["ssm_a_im"])
    o["s_ldt"] = pair(np.broadcast_to(inp["ssm_log_dt"][:, :, None], (L, 32, 64)))
    for nm, src in (("s_bre", "ssm_b_re"), ("s_bim", "ssm_b_im")):
        b = inp[src].reshape(L, 16, 2, 64, 16)
        pad = np.zeros((L, 8, 16, 16, 2, 64), f)
        for j in range(16):
            for gl in range(2):
                pad[:, 2 * (j % 4) + gl, :, j, gl, :] = b[:, j, gl].transpose(0, 2, 1)
        o[nm] = pad.reshape(L, 128, 16, 128)
    for nm, src in (("s_cre", "ssm_c_re"), ("s_cim", "ssm_c_im")):
        cmat = inp[src].reshape(L, 16, 2, 16, 64)
        pad = np.zeros((L, 2, 64, 16, 8, 16), f)
        for j in range(16):
            for gl in range(2):
                pad[:, gl, :, j, 2 * (j % 4) + gl, :] = cmat[:, j, gl].transpose(0, 2, 1)
        o[nm] = pad.reshape(L, 128, 16, 128)
    o["s_d"] = np.ascontiguousarray(inp["ssm_d"].reshape(L, 4, 128).transpose(0, 2, 1))
    o["w_glu"] = inp["ssm_w_glu"]
    o["w_bssm"] = inp["w_branch_ssm"]
    o["w_bnsa"] = inp["w_branch_nsa"]
    o["w_out"] = inp["w_out"]
    for t in ("k", "v"):
        o["c_pe" + t] = np.ascontiguousarray(inp["cmp_pe_" + t].transpose(0, 2, 1))
        o["c_w1" + t] = inp["cmp_w1_" + t]
        o["c_b1" + t] = np.ascontiguousarray(inp["cmp_b1_" + t].reshape(L, 2, 128).transpose(0, 2, 1))
        o["c_w2" + t] = inp["cmp_w2_" + t]
    o["w_fin"] = inp["w_ffn_in"]
    o["w_fout"] = inp["w_ffn_out"]
    o["f_cw"] = np.ascontiguousarray(inp["ffn_conv_w"].reshape(L, 3, NFC, 128).transpose(0, 3, 2, 1))
    o["f_cb"] = np.ascontiguousarray(inp["ffn_conv_b"].reshape(L, NFC, 128).transpose(0, 2, 1))
    return o


def load_w(k, st, src2d, nch, ncols, prow=128, eng="pool", name="w"):
    t, _ = k.sb(st, [prow, nch, ncols], BF16, name)
    bufs = []
    for c in range(nch):
        b = Buf(name)
        k.dma(eng, t[:, c, :], src2d[c * prow:(c + 1) * prow, :], b)
        bufs.append(b)
    return t, bufs


def sincos(k, st, arg, n, out_sin=None, out_cos=None, rb=(), wsin=None, wcos=None):
    for (dst, off, wb) in ((out_sin, 0.0, wsin), (out_cos, 0.25, wcos)):
        if dst is None:
            continue
        a2, ba2 = k.sb(st, [128, n], F32, "sc_a")
        ti, bti = k.sb(st, [128, n], I32, "sc_i")
        tf, btf = k.sb(st, [128, n], F32, "sc_f")
        k.ts("dve", a2[:], arg, off, ALU.add, r=list(rb), w=[ba2])
        k.copy("dve", ti[:], a2[:], r=[ba2], w=[bti])
        k.copy("dve", tf[:], ti[:], r=[bti], w=[btf])
        k.tt("dve", a2[:], a2[:], tf[:], ALU.subtract, r=[ba2, btf], w=[ba2])
        k.act(dst, a2[:], AF.Sin, r=[ba2], w=[wb], scale=SIN_SCALE)


def phase1(k, l, S, TS, x_src, di, sc, cst):
    P = k.P
    TT = TS
    NTT = S // TT
    with ExitStack() as st:
        ring = PsumRing(k, st)
        ident, bid = cst["ident"]
        gam, bgam = k.sb(st, [128, D], F32, "gam")
        k.dma("sp", gam[:], di["g_mix"][l], bgam)
        dvec, bdvec = k.sb(st, [128, 4], F32, "dvec")
        k.dma("sp", dvec[:], di["s_d"][l], bdvec)
        RFre, bRFre = k.sb(st, [128, 16, TS], F32, "RFre")
        RFim, bRFim = k.sb(st, [128, 16, TS], F32, "RFim")
        COSb, bCOSb = k.sb(st, [128, 16, TS], BF16, "COSb")
        SINb, bSINb = k.sb(st, [128, 16, TS], BF16, "SINb")
        NSINb, bNSINb = k.sb(st, [128, 16, TS], BF16, "NSINb")
        dec, bdec = k.sb(st, [128, 16], F32, "dec")
        cT, bcT = k.sb(st, [128, 16], F32, "cT")
        sT, bsT = k.sb(st, [128, 16], F32, "sT")
        nsT, bnsT = k.sb(st, [128, 16], F32, "nsT")
        with ExitStack() as s2:
            are, bare = k.sb(s2, [128, 16], F32, "are")
            aim, baim = k.sb(s2, [128, 16], F32, "aim")
            ldt, bldt = k.sb(s2, [128, 16], F32, "ldt")
            tau, btau = k.sb(s2, [128, TS], F32, "tau")
            k.dma("sp", are[:], di["s_are"][l], bare)
            k.dma("sp", aim[:], di["s_aim"][l], baim)
            k.dma("sp", ldt[:], di["s_ldt"][l], bldt)
            k.dma("sp", tau[:], di["c_tau"], btau)
            dt_, bdt = k.sb(s2, [128, 16], F32, "dt")
            k.act(dt_[:], ldt[:], AF.Exp, r=[bldt], w=[bdt])
            rho, brho = k.sb(s2, [128, 16], F32, "rho")
            thn, bthn = k.sb(s2, [128, 16], F32, "thn")
            k.tt("dve", rho[:], are[:], dt_[:], ALU.mult, r=[bare, bdt], w=[brho])
            k.tt("dve", thn[:], aim[:], dt_[:], ALU.mult, r=[baim, bdt], w=[bthn])
            k.ts("dve", thn[:], thn[:], 1.0 / TWO_PI, ALU.mult, r=[bthn], w=[bthn])
            k.act(dec[:], rho[:], AF.Exp, r=[brho], w=[bdec])
            s1, bs1 = k.sb(s2, [128, 16], F32, "s1")
            c1, bc1 = k.sb(s2, [128, 16], F32, "c1")
            sincos(k, s2, thn[:], 16, s1[:], c1[:], rb=[bthn], wsin=bs1, wcos=bc1)
            abre, babre = k.sb(s2, [128, 16], F32, "abre")
            abim, babim = k.sb(s2, [128, 16], F32, "abim")
            k.tt("dve", abre[:], dec[:], c1[:], ALU.mult, r=[bdec, bc1], w=[babre])
            k.ts("dve", abre[:], abre[:], -1.0, ALU.add, r=[babre], w=[babre])
            k.tt("dve", abim[:], dec[:], s1[:], ALU.mult, r=[bdec, bs1], w=[babim])
            den, bden = k.sb(s2, [128, 16], F32, "den")
            t0, bt0 = k.sb(s2, [128, 16], F32, "t0")
            k.tt("dve", den[:], are[:], are[:], ALU.mult, r=[bare], w=[bden])
            k.tt("dve", t0[:], aim[:], aim[:], ALU.mult, r=[baim], w=[bt0])
            k.tt("dve", den[:], den[:], t0[:], ALU.add, r=[bden, bt0], w=[bden])
            k.recip(den[:], den[:], r=[bden], w=[bden])
            fre, bfre = k.sb(s2, [128, 16], F32, "fre")
            fim, bfim = k.sb(s2, [128, 16], F32, "fim")
            t1, bt1 = k.sb(s2, [128, 16], F32, "t1")
            k.tt("dve", fre[:], abre[:], are[:], ALU.mult, r=[babre, bare], w=[bfre])
            k.tt("dve", t1[:], abim[:], aim[:], ALU.mult, r=[babim, baim], w=[bt1])
            k.tt("dve", fre[:], fre[:], t1[:], ALU.add, r=[bfre, bt1], w=[bfre])
            k.tt("dve", fre[:], fre[:], den[:], ALU.mult, r=[bfre, bden], w=[bfre])
            k.tt("dve", fim[:], abim[:], are[:], ALU.mult, r=[babim, bare], w=[bfim])
            k.tt("dve", t1[:], abre[:], aim[:], ALU.mult, r=[babre, baim], w=[bt1])
            k.tt("dve", fim[:], fim[:], t1[:], ALU.subtract, r=[bfim, bt1], w=[bfim])
            k.tt("dve", fim[:], fim[:], den[:], ALU.mult, r=[bfim, bden], w=[bfim])
            aT, baT = k.sb(s2, [128, 16], F32, "aT")
            k.ts("dve", aT[:], thn[:], float(TS), ALU.mult, r=[bthn], w=[baT])
            sincos(k, s2, aT[:], 16, sT[:], cT[:], rb=[baT], wsin=bsT, wcos=bcT)
            k.ts("dve", nsT[:], sT[:], -1.0, ALU.mult, r=[bsT], w=[bnsT])
            ANG, bANG = k.sb(s2, [128, 16, TS], F32, "ANG")
            SINf, bSINf = k.sb(s2, [128, 16 * TS], F32, "SINf")
            COSf, bCOSf = k.sb(s2, [128, 16 * TS], F32, "COSf")
            for j in range(16):
                k.ts("dve", ANG[:, j, :], tau[:], thn[:, j:j + 1], ALU.mult, r=[btau, bthn], w=[bANG])
            sincos(k, s2, ANG[:].rearrange("p j t -> p (j t)"), 16 * TS, SINf[:], COSf[:], rb=[bANG], wsin=bSINf, wcos=bCOSf)
            SIN3 = SINf[:].rearrange("p (j t) -> p j t", j=16)
            COS3 = COSf[:].rearrange("p (j t) -> p j t", j=16)
            tmp, btmp = k.sb(s2, [128, TS], F32, "tmp")
            for j in range(16):
                k.ts("dve", tmp[:], SIN3[:, j, :], fim[:, j:j + 1], ALU.mult, r=[bSINf, bfim], w=[btmp])
                k.stt(RFre[:, j, :], COS3[:, j, :], fre[:, j:j + 1], tmp[:], ALU.mult, ALU.add, r=[bCOSf, bfre, btmp], w=[bRFre])
                k.ts("dve", tmp[:], SIN3[:, j, :], fre[:, j:j + 1], ALU.mult, r=[bSINf, bfre], w=[btmp])
                k.stt(RFim[:, j, :], COS3[:, j, :], fim[:, j:j + 1], tmp[:], ALU.mult, ALU.subtract, r=[bCOSf, bfim, btmp], w=[bRFim])
            k.copy("dve", COSb[:].rearrange("p j t -> p (j t)"), COSf[:], r=[bCOSf], w=[bCOSb])
            k.copy("dve", SINb[:].rearrange("p j t -> p (j t)"), SINf[:], r=[bSINf], w=[bSINb])
            k.ts("dve", NSINb[:].rearrange("p j t -> p (j t)"), SINf[:], -1.0, ALU.mult, r=[bSINf], w=[bNSINb])
        P.barrier()
        w_in, bw_in = load_w(k, st, di["w_in"][l], 8, INW, name="w_in")
        w_sw, bw_sw = load_w(k, st, di["w_sw"][l], 8, 896, name="w_sw")
        w_glu, bw_glu = load_w(k, st, di["w_glu"][l], 4, 1024, name="w_glu")
        w_bs, bw_bs = load_w(k, st, di["w_bssm"][l], 4, 1024, name="w_bs")
        bre, bbre = load_w(k, st, di["s_bre"][l].rearrange("p j m -> p (j m)"), 1, 2048, name="bre")
        bim, bbim = load_w(k, st, di["s_bim"][l].rearrange("p j m -> p (j m)"), 1, 2048, name="bim")
        cre, bcre = load_w(k, st, di["s_cre"][l].rearrange("p j m -> p (j m)"), 1, 2048, name="cre")
        cim, bcim = load_w(k, st, di["s_cim"][l].rearrange("p j m -> p (j m)"), 1, 2048, name="cim")
        xt = [k.sb(st, [128, TT // 128, D], F32, "xt") for _ in range(2)]
        cosr = [k.sb(st, [128, TT], F32, "cosr") for _ in range(2)]
        sinr = [k.sb(st, [128, TT], F32, "sinr") for _ in range(2)]
        NSB = TT // 128
        junk, bjunk = k.sb(st, [128, D], BF16, "junk")
        ss, bss = k.sb(st, [128, NSB], F32, "ss")
        ms, bms = k.sb(st, [128, NSB], F32, "ms")
        sd, bsd = k.sb(st, [128, NSB], F32, "sd")
        rstd, brstd = k.sb(st, [128, NSB], F32, "rstd")
        hh = [k.sb(st, [128, D], BF16, "h") for _ in range(2)]
        hT, bhT = k.sb(st, [128, 8, TT], BF16, "hT")
        uT2 = [k.sb(st, [128, 4, TT], BF16, "uT") for _ in range(2)]
        qTs, bqTs = k.sb(st, [128, 4, TT], BF16, "qTs")
        kvs = {nm: k.sb(st, [128, TT], BF16, nm) for nm in ("kcT", "vcT", "ksT", "kwT")}
        sgs2 = [k.sb(st, [128, 8, TT], BF16, "sgs") for _ in range(2)]
        sgn, bsgn = k.sb(st, [128, 8, TT], BF16, "sgn")
        sg, bsg = k.sb(st, [24, TT], F32, "sg")
        vsel, bvsel = k.sb(st, [128, NSB, 2, 65], BF16, "vsel")
        vwin, bvwin = k.sb(st, [128, NSB, 2, 65], BF16, "vwin")
        k.memset("pool", vsel[:], 1.0, [bvsel])
        k.memset("pool", vwin[:], 1.0, [bvwin])
        tmps = [k.sb(st, [128, TT], F32, "tmp") for _ in range(12)]
        tmpi = [0]

        def gettmp():
            t = tmps[tmpi[0] % len(tmps)]
            tmpi[0] += 1
            return t
        bsc = [k.sb(st, [128, TT], F32, "bsc") for _ in range(4)]
        wsc = [k.sb(st, [128, TT], F32, "wsc") for _ in range(4)]
        xre, bxre = k.sb(st, [128, 16, TT], BF16, "xre")
        nxim, bnxim = k.sb(st, [128, 16, TT], BF16, "nxim")
        car, bcar = k.sb(st, [128, 2, 16], F32, "car")
        k.memset("dve", car[:], 0.0, [bcar])
        ctmps = [k.sb(st, [128, 2], F32, "ctmp") for _ in range(2)]
        ypre, bypre = k.sb(st, [128, TT], F32, "ypre")
        yT, byT = k.sb(st, [128, 4, TT], BF16, "yT")
        sgz, bsgz = k.sb(st, [128, TT], F32, "sgz")
        zzT, bzzT = k.sb(st, [128, 4, TT], BF16, "zzT")
        gss, bgss = k.sb(st, [128, 8, TT], BF16, "gss")

        def load_tile(i):
            t, b = xt[i % 2]
            k.dma("sp", t[:], x_src[i * TT:(i + 1) * TT, :].rearrange("(s p) d -> p s d", p=128), b)
            k.dma("sp", cosr[i % 2][0][:], di["c_cos"][:, i * TT:(i + 1) * TT], cosr[i % 2][1])
            k.dma("sp", sinr[i % 2][0][:], di["c_sin"][:, i * TT:(i + 1) * TT], sinr[i % 2][1])

        def proj(wt, wb, col0, M=128):
            ps, bp = ring.get()
            for c in range(8):
                k.mm(ps[0:M, 0:TT], wt[:, c, col0:col0 + M], hT[:, c, :], c == 0, c == 7, r=[wb[c], bhT], w=[bp])
            return ps, bp

        def inproj_gen(i):
            uT, buT = uT2[i % 2]
            sgs, bsgs = sgs2[i % 2]
            x_t, bx = xt[i % 2]
            cos_t, bcos = cosr[i % 2]
            sin_t, bsin = sinr[i % 2]
            tok = slice(i * TT, (i + 1) * TT)
            for s_ in range(NSB):
                h_t, bh = hh[s_ % 2]
                k.act(junk[:], x_t[:, s_, :], AF.Square, r=[bx], w=[bjunk, bss], accum=ss[:, s_:s_ + 1])
                k.ts("dve", ms[:, s_:s_ + 1], ss[:, s_:s_ + 1], 1.0 / D, ALU.mult, r=[bss], w=[bms], s2=EPS, op1=ALU.add)
                k.act(sd[:, s_:s_ + 1], ms[:, s_:s_ + 1], AF.Sqrt, r=[bms], w=[bsd])
                k.recip(rstd[:, s_:s_ + 1], sd[:, s_:s_ + 1], r=[bsd], w=[brstd])
                k.stt(h_t[:], x_t[:, s_, :], rstd[:, s_:s_ + 1], gam[:], ALU.mult, ALU.mult, r=[bx, brstd, bgam], w=[bh])
                ps, bp = ring.get()
                pbf = ps[:].bitcast(BF16)
                for c in range(8):
                    k.tr(pbf[:, c * 128:(c + 1) * 128], h_t[:, c * 128:(c + 1) * 128], ident[:], r=[bh, bid], w=[bp])
                k.copy("act", hT[:, :, s_ * 128:(s_ + 1) * 128], pbf.rearrange("p (c t) -> p c t", c=8), r=[bp], w=[bhT])
            for c4 in range(4):
                ps, bp = proj(w_in, bw_in, c4 * 128)
                k.copy("act", uT[:, c4, :], ps[:, 0:TT], r=[bp], w=[buT])
                yield
            def rope(col, swcol, dst, bdst):
                psA, bA = proj(w_in, bw_in, col)
                psB, bB = proj(w_sw, bw_sw, swcol)
                t1_, bt1_ = gettmp()
                t2_, bt2_ = gettmp()
                k.tt("dve", t1_[:], psA[:, 0:TT], cos_t[:], ALU.mult, r=[bA, bcos], w=[bt1_])
                k.tt("dve", t2_[:], psB[:, 0:TT], sin_t[:], ALU.mult, r=[bB, bsin], w=[bt2_])
                k.tt("pool", dst, t1_[:], t2_[:], ALU.add, r=[bt1_, bt2_], w=[bdst])
            for c in range(4):
                rope(512 + c * 128, c * 128, qTs[:, c, :], bqTs)
                yield
            rope(1024, 512, kvs["kcT"][0][:], kvs["kcT"][1])
            yield
            rope(1280, 640, kvs["ksT"][0][:], kvs["ksT"][1])
            yield
            rope(1536, 768, kvs["kwT"][0][:], kvs["kwT"][1])
            yield
            ps, bp = proj(w_in, bw_in, 1152)
            k.copy("act", kvs["vcT"][0][:], ps[:, 0:TT], r=[bp], w=[kvs["vcT"][1]])
            for c in range(8):
                ps, bp = proj(w_in, bw_in, 1816 + c * 128)
                k.act(sgs[:, c, :], ps[:, 0:TT], AF.Sigmoid, r=[bp], w=[bsgs])
                yield
            for c in range(8):
                ps, bp = proj(w_in, bw_in, 2840 + c * 128)
                k.act(sgn[:, c, :], ps[:, 0:TT], AF.Sigmoid, r=[bp], w=[bsgn])
                yield
            ps, bp = proj(w_in, bw_in, 1792, M=24)
            k.act(sg[:], ps[0:24, 0:TT], AF.Sigmoid, r=[bp], w=[bsg])
            for s_ in range(NSB):
                for (col, vt, bv) in ((1408, vsel, bvsel), (1664, vwin, bvwin)):
                    ps, bp = ring.get()
                    for c in range(8):
                        k.mm(ps[:, 0:128], hT[:, c, s_ * 128:(s_ + 1) * 128], w_in[:, c, col:col + 128], c == 0, c == 7, r=[bw_in[c], bhT], w=[bp])
                    k.copy("act", vt[:, s_, :, 0:64], ps[:, 0:128].rearrange("p (h d) -> p h d", h=2), r=[bp], w=[bv])
                    yield
            k.dma("sp", sc["qT"].rearrange("(c p) s -> p c s", p=128)[:, :, tok], qTs[:], bqTs)
            for nm in ("kcT", "vcT", "ksT", "kwT"):
                k.dma("sp", sc[nm][:, tok], kvs[nm][0][:], kvs[nm][1])
            k.dma("sp", sc["sgnT"].rearrange("(c p) s -> p c s", p=128)[:, :, tok], sgn[:], bsgn)
            k.dma("sp", sc["sgT"][:, tok], sg[:], bsg)
            k.dma("sp", sc["vs"][tok].rearrange("(s p) h c -> p s h c", p=128), vsel[:], bvsel)
            k.dma("sp", sc["vw"][tok].rearrange("(s p) h c -> p s h c", p=128), vwin[:], bvwin)
        def s5_gen(i):
            uT, buT = uT2[i % 2]
            sgs, bsgs = sgs2[i % 2]
            tok = slice(i * TT, (i + 1) * TT)
            def stageA(j):
                c4 = j // 4
                psr, bpr = ring.get()
                psi, bpi = ring.get()
                k.mm(psr[:, 0:TT], bre[:, 0, j * 128:(j + 1) * 128], uT[:, c4, :], True, True, r=[bbre[0], buT], w=[bpr])
                k.mm(psi[:, 0:TT], bim[:, 0, j * 128:(j + 1) * 128], uT[:, c4, :], True, True, r=[bbim[0], buT], w=[bpi])
                b_re, bb_re = bsc[(2 * j) % 4]
                b_im, bb_im = bsc[(2 * j + 1) % 4]
                t1_, bt1_ = gettmp()
                t2_, bt2_ = gettmp()
                k.tt("dve", t1_[:], psr[:, 0:TT], RFre[:, j, :], ALU.mult, r=[bpr, bRFre], w=[bt1_])
                k.tt("dve", t2_[:], psi[:, 0:TT], RFim[:, j, :], ALU.mult, r=[bpi, bRFim], w=[bt2_])
                k.tt("pool", b_re[:], t1_[:], t2_[:], ALU.subtract, r=[bt1_, bt2_], w=[bb_re])
                t3_, bt3_ = gettmp()
                t4_, bt4_ = gettmp()
                k.tt("dve", t3_[:], psi[:, 0:TT], RFre[:, j, :], ALU.mult, r=[bpi, bRFre], w=[bt3_])
                k.tt("dve", t4_[:], psr[:, 0:TT], RFim[:, j, :], ALU.mult, r=[bpr, bRFim], w=[bt4_])
                k.tt("pool", b_im[:], t3_[:], t4_[:], ALU.add, r=[bt3_, bt4_], w=[bb_im])

            def stageB(j):
                b_re, bb_re = bsc[(2 * j) % 4]
                b_im, bb_im = bsc[(2 * j + 1) % 4]
                w_re, bw_re = wsc[(2 * j) % 4]
                w_im, bw_im = wsc[(2 * j + 1) % 4]
                dj = dec[:, j:j + 1].to_broadcast([128, TT])
                k.scan(w_re[:], dj, b_re[:], car[:, 0, j:j + 1], r=[bdec, bb_re, bcar], w=[bw_re])
                k.scan(w_im[:], dj, b_im[:], car[:, 1, j:j + 1], r=[bdec, bb_im, bcar], w=[bw_im])
                t5_, bt5_ = gettmp()
                t6_, bt6_ = gettmp()
                k.tt("dve", t5_[:], w_re[:], COSb[:, j, :], ALU.mult, r=[bw_re, bCOSb], w=[bt5_])
                k.tt("dve", t6_[:], w_im[:], SINb[:, j, :], ALU.mult, r=[bw_im, bSINb], w=[bt6_])
                k.tt("pool", xre[:, j, :], t5_[:], t6_[:], ALU.subtract, r=[bt5_, bt6_], w=[bxre])
                t7_, bt7_ = gettmp()
                t8_, bt8_ = gettmp()
                k.tt("dve", t7_[:], w_re[:], NSINb[:, j, :], ALU.mult, r=[bw_re, bNSINb], w=[bt7_])
                k.tt("dve", t8_[:], w_im[:], COSb[:, j, :], ALU.mult, r=[bw_im, bCOSb], w=[bt8_])
                k.tt("pool", nxim[:, j, :], t7_[:], t8_[:], ALU.subtract, r=[bt7_, bt8_], w=[bnxim])
                ct_, bct_ = ctmps[j % 2]
                k.ts("dve", ct_[:, 0:1], w_re[:, TT - 1:TT], cT[:, j:j + 1], ALU.mult, r=[bw_re, bcT], w=[bct_])
                k.ts("dve", ct_[:, 1:2], w_im[:, TT - 1:TT], cT[:, j:j + 1], ALU.mult, r=[bw_im, bcT], w=[bct_])
                k.stt(car[:, 0, j:j + 1], w_im[:, TT - 1:TT], nsT[:, j:j + 1], ct_[:, 0:1], ALU.mult, ALU.add, r=[bw_im, bnsT, bct_], w=[bcar])
                k.stt(car[:, 1, j:j + 1], w_re[:, TT - 1:TT], sT[:, j:j + 1], ct_[:, 1:2], ALU.mult, ALU.add, r=[bw_re, bsT, bct_], w=[bcar])

            stageA(0)
            for j in range(16):
                if j + 1 < 16:
                    stageA(j + 1)
                stageB(j)
                yield
            for c4 in range(4):
                ps, bp = ring.get()
                for jj in range(4):
                    j = 4 * c4 + jj
                    k.mm(ps[:, 0:TT], cre[:, 0, j * 128:(j + 1) * 128], xre[:, j, :], jj == 0, False, r=[bcre[0], bxre], w=[bp])
                    k.mm(ps[:, 0:TT], cim[:, 0, j * 128:(j + 1) * 128], nxim[:, j, :], False, jj == 3, r=[bcim[0], bnxim], w=[bp])
                k.stt(ypre[:], uT[:, c4, :], dvec[:, c4:c4 + 1], ps[:, 0:TT], ALU.mult, ALU.add, r=[buT, bdvec, bp], w=[bypre])
                k.act(yT[:, c4, :], ypre[:], AF.Gelu_apprx_tanh, r=[bypre], w=[byT])
                yield
            for kk in range(4):
                psg, bpg = ring.get()
                for c4 in range(4):
                    k.mm(psg[:, 0:TT], w_glu[:, c4, (4 + kk) * 128:(5 + kk) * 128], yT[:, c4, :], c4 == 0, c4 == 3, r=[bw_glu[c4], byT], w=[bpg])
                k.act(sgz[:], psg[:, 0:TT], AF.Sigmoid, r=[bpg], w=[bsgz])
                psv, bpv = ring.get()
                for c4 in range(4):
                    k.mm(psv[:, 0:TT], w_glu[:, c4, kk * 128:(kk + 1) * 128], yT[:, c4, :], c4 == 0, c4 == 3, r=[bw_glu[c4], byT], w=[bpv])
                k.tt("dve", zzT[:, kk, :], psv[:, 0:TT], sgz[:], ALU.mult, r=[bpv, bsgz], w=[bzzT])
                yield
            for fc in range(8):
                ps, bp = ring.get()
                for kk in range(4):
                    k.mm(ps[:, 0:TT], w_bs[:, kk, fc * 128:(fc + 1) * 128], zzT[:, kk, :], kk == 0, kk == 3, r=[bw_bs[kk], bzzT], w=[bp])
                k.tt("dve", gss[:, fc, :], ps[:, 0:TT], sgs[:, fc, :], ALU.mult, r=[bp, bsgs], w=[bgss])
                yield
            k.dma("sp", sc["gssT"].rearrange("(c p) s -> p c s", p=128)[:, :, tok], gss[:], bgss)
        def step(g):
            try:
                next(g)
                return True
            except StopIteration:
                return False

        load_tile(0)
        if NTT > 1:
            load_tile(1)
        for _ in inproj_gen(0):
            pass
        for i in range(NTT):
            if i + 2 < NTT:
                load_tile(i + 2)
            gs = s5_gen(i)
            gi = inproj_gen(i + 1) if i + 1 < NTT else iter(())
            alive_s, alive_i = True, True
            n_ = 0
            while alive_s or alive_i:
                if alive_s:
                    alive_s = step(gs)
                for _ in range(1 + (n_ % 2)):
                    if alive_i:
                        alive_i = step(gi)
                n_ += 1
    P.barrier()


def phase2(k, pers, l, S, di, sc, cst):
    P = k.P
    NC = S // 16 - 1
    NCP = S // 16
    NCT = NCP // 128
    KcT, bKcT = k.sb(pers, [128, NCP], BF16, "KcT")
    Vc, bVc = k.sb(pers, [128, NCT, 2, 65], BF16, "Vc")
    k.memset("pool", KcT[:], 0.0, [bKcT])
    k.memset("pool", Vc[:], 1.0, [bVc])
    with ExitStack() as st:
        ring = PsumRing(k, st)
        for typ in ("k", "v"):
            with ExitStack() as s2:
                xT, bxT = k.sb(s2, [128, S], BF16, "cxT")
                k.dma("sp", xT[:], sc["kcT" if typ == "k" else "vcT"], bxT)
                w1, bw1 = k.sb(s2, [128, 32, 256], BF16, "w1")
                bw1b = Buf("w1b")
                src = di["c_w1" + typ][l].rearrange("(l d) c -> d l c", d=64)
                k.dma("pool", w1[0:64], src, bw1)
                k.dma("pool", w1[64:128], src, bw1b)
                pe2, bpe2 = k.sb(s2, [64, 32, 2], BF16, "pe2")
                pe_f, bpe_f = k.sb(s2, [64, 32], F32, "pe_f")
                k.dma("sp", pe_f[:], di["c_pe" + typ][l], bpe_f)
                k.copy("dve", pe2[:, :, 0], pe_f[:], r=[bpe_f], w=[bpe2])
                k.copy("dve", pe2[:, :, 1], pe_f[:], r=[bpe_f], w=[bpe2])
                b1, bb1 = k.sb(s2, [128, 2], F32, "b1")
                k.dma("sp", b1[:], di["c_b1" + typ][l], bb1)
                w2, bw2 = k.sb(s2, [128, 2, 64], BF16, "w2")
                k.dma("pool", w2[:], di["c_w2" + typ][l].rearrange("(c p) d -> p c d", p=128), bw2)
                bias, bbias = k.sb(s2, [128, 2], F32, "bias")
                for cc in range(2):
                    ps, bp = ring.get()
                    for li in range(32):
                        k.mm(ps[:, 0:2], w1[0:64, li, cc * 128:(cc + 1) * 128], pe2[:, li, :], li == 0, li == 31, r=[bw1, bpe2], w=[bp])
                    k.tt("dve", bias[:, cc:cc + 1], ps[:, 0:1], b1[:, cc:cc + 1], ALU.add, r=[bp, bb1], w=[bbias])
                for hk in range(2):
                    hid, bhid = k.sb(s2, [128, 2, NCP], BF16, "hid")
                    k.memset("pool", hid[:], 0.0, [bhid])
                    bw = bw1 if hk == 0 else bw1b
                    for cc in range(2):
                        ps, bp = ring.get()
                        for li in range(32):
                            k.mm(ps[:, 0:NC], w1[hk * 64:(hk + 1) * 64, li, cc * 128:(cc + 1) * 128],
                                 xT[hk * 64:(hk + 1) * 64, li:li + 16 * (NC - 1) + 1:16], li == 0, li == 31, r=[bw, bxT], w=[bp])
                        k.act(hid[:, cc, 0:NC], ps[:, 0:NC], AF.Gelu_apprx_tanh, r=[bp, bbias], w=[bhid], bias=bias[:, cc:cc + 1])
                    if typ == "k":
                        ps, bp = ring.get()
                        for cc in range(2):
                            k.mm(ps[hk * 64:(hk + 1) * 64, 0:NC], w2[:, cc, :], hid[:, cc, 0:NC], cc == 0, cc == 1, r=[bw2, bhid], w=[bp])
                        k.copy("act", KcT[hk * 64:(hk + 1) * 64, 0:NC], ps[hk * 64:(hk + 1) * 64, 0:NC], r=[bp], w=[bKcT])
                    else:
                        for nt in range(NCT):
                            ps, bp = ring.get()
                            for cc in range(2):
                                k.mm(ps[:, 0:64], hid[:, cc, nt * 128:(nt + 1) * 128], w2[:, cc, :], cc == 0, cc == 1, r=[bhid, bw2], w=[bp])
                            k.copy("act", Vc[:, nt, hk, 0:64], ps[:, 0:64], r=[bp], w=[bVc])
            P.barrier()
    if "dbg_kc" in sc:
        k.dma("sp", sc["dbg_kc"].rearrange("h d n -> (h d) n"), KcT[:], bKcT)
        k.dma("sp", sc["dbg_vc"], Vc[:], bVc)
    return (KcT, bKcT), (Vc, bVc)


def phase3(k, l, S, x_src, di, sc, cst, cmp_t):
    P = k.P
    NT = S // 128
    NCP = S // 16
    NCT = NCP // 128
    NB = S // 64
    (KcT, bKcT), (Vc, bVc) = cmp_t
    ident, bid = cst["ident"]
    with ExitStack() as st:
        ring = PsumRing(k, st, 3)
        ringO = PsumRing(k, st, 3)
        ringM = PsumRing(k, st, 2)
        KsM = []
        for hk in range(2):
            t, b = k.sb(st, [128, S], BF16, "KsM")
            b2 = Buf("KsMi")
            k.dma("sp", t[hk * 64:(hk + 1) * 64], sc["ksT"][hk * 64:(hk + 1) * 64, :], b)
            k.dma("pool", t[(1 - hk) * 64:(2 - hk) * 64], di["c_ind"], b2)
            KsM.append((t, b, b2))
        Vs, bVs = k.sb(st, [128, NT, 2, 65], BF16, "Vs")
        k.dma("sp", Vs[:], sc["vs"].rearrange("(n p) h c -> p n h c", p=128), bVs)

        def cload(name, shape, src, dt=BF16):
            t, b = k.sb(st, shape, dt, name)
            k.dma("pool" if dt == BF16 else "sp", t[:], src, b)
            return t, b
        caus, bcaus = cload("caus", [128, 128], di["c_caus"])
        low, blow = cload("low", [128, 128], di["c_low"])
        mgen, bmgen = cload("mgen", [128, 1024], di["c_mgen"])
        mt, bmt = cload("mt", [128, 16, 128], di["c_mt"])
        G, bG = cload("G", [128, 256], di["c_g"], F32)
        wbn, bwbn = cload("wbn", [64, 8, 1024], di["w_bnsa"][l].rearrange("(h d) n -> d h n", d=64))
        w_out, bw_out = load_w(k, st, di["w_out"][l], 8, 1024, name="w_out")
        ones16, bones = k.sb(st, [128, 64], BF16, "ones16")
        k.memset("pool", ones16[:], 1.0, [bones])
        rhis = [k.sb(st, [65, 512], BF16, "rhi") for _ in range(5)]
        rlos = [k.sb(st, [65, 512], BF16, "rlo") for _ in range(5)]
        QT = [[k.sb(st, [128, 4, 128], BF16, "QT") for _ in range(2)] for _ in range(2)]
        for pb_ in range(2):
            for hk_ in range(2):
                k.memset("pool", QT[pb_][hk_][0][:], 0.0, [QT[pb_][hk_][1]])
        NHALF = max(1, NB // 64)
        Qsel = [[[k.sb(st, [128, 4, 128], BF16, "Qsel") + (Buf("Qselm"),) for _ in range(NHALF)] for _ in range(2)] for _ in range(2)]
        negm_sw, bnegm_sw = k.sb(st, [128, 128], BF16, "negm_sw")
        k.memset("pool", negm_sw[:], 0.0, [bnegm_sw])
        KwT = [k.sb(st, [128, 640], BF16, "KwT") for _ in range(2)]
        Vw = [k.sb(st, [128, 5, 2, 65], BF16, "Vw") for _ in range(2)]
        grow = [k.sb(st, [65, 2, 3, 4, 128], F32, "grow") for _ in range(2)]
        sgn = [k.sb(st, [128, 8, 128], BF16, "sgn") for _ in range(2)]
        gss = [k.sb(st, [128, 8, 128], BF16, "gss") for _ in range(2)]
        xin = [k.sb(st, [128, D], F32, "xin") for _ in range(2)]
        eg = [k.sb(st, [128, NCP], F32, "eg") for _ in range(4)]
        den4, bden4 = k.sb(st, [128, 4], F32, "den4")
        rden4, brden4 = k.sb(st, [128, 4], F32, "rden4")
        pg, bpg = k.sb(st, [128, NCP + 8], F32, "pg")
        k.memset("pool", pg[:], 0.0, [bpg])
        blk, bblk = k.sb(st, [128, NB], F32, "blk")
        blk2, bblk2 = k.sb(st, [128, NB], F32, "blk2")
        m8, bm8 = k.sb(st, [128, 16], F32, "m8")
        negm, bnegm = k.sb(st, [128, 128], BF16, "negm")
        k.memset("pool", negm[:], 0.0, [bnegm])
        pTs = [k.sb(st, [128, 512], BF16, "pT") for _ in range(4)]
        pti = [0]
        rrs = [k.sb(st, [65, 512], F32, "rr") for _ in range(5)]
        osbs = [k.sb(st, [64, 512], F32, "osb") for _ in range(5)]
        oacc, boacc = k.sb(st, [64, 512], F32, "oacc")
        otmp, botmp = k.sb(st, [64, 512], F32, "otmp")
        oTb = [k.sb(st, [64, 4, 128], BF16, "oTb") for _ in range(2)]
        mrg, bmrg = k.sb(st, [128, 8, 128], BF16, "mrg")
        xm = [k.sb(st, [128, D], F32, "xm") for _ in range(1)]

        def loads(qb):
            s0 = qb * 128
            pb = qb % 2
            qv = sc["qT"].rearrange("(h d) s -> d h s", d=64)
            for hk in range(2):
                t, b = QT[pb][hk]
                k.dma("sp", t[hk * 64:(hk + 1) * 64], qv[:, hk * 4:(hk + 1) * 4, s0:s0 + 128], b)
                for hf in range(NHALF):
                    if hf * 32 <= qb:
                        t, b, _ = Qsel[pb][hk][hf]
                        k.dma("sp", t[hk * 64:(hk + 1) * 64], qv[:, hk * 4:(hk + 1) * 4, s0:s0 + 128], b)
            lo = max(0, s0 - 512)
            t, b = KwT[pb]
            k.dma("sp", t[:, 640 - (s0 + 128 - lo):640], sc["kwT"][:, lo:s0 + 128], b)
            nw = (s0 + 128 - lo) // 128
            t, b = Vw[pb]
            k.dma("sp", t[:, 5 - nw:5], sc["vw"][lo:s0 + 128].rearrange("(n p) h c -> p n h c", p=128), b)
            t, b = grow[pb]
            gv = sc["sgT"].rearrange("(hk g br) s -> hk br g s", hk=2, g=4, br=3)
            for hk in range(2):
                for br in range(3):
                    k.dma("sp", t[64:65, hk, br], gv[hk, br:br + 1, :, s0:s0 + 128], b)
            t, b = sgn[pb]
            k.dma("sp", t[:], sc["sgnT"].rearrange("(c p) s -> p c s", p=128)[:, :, s0:s0 + 128], b)
            t, b = gss[pb]
            k.dma("sp", t[:], sc["gssT"].rearrange("(c p) s -> p c s", p=128)[:, :, s0:s0 + 128], b)
            t, b = xin[pb]
            k.dma("sp", t[:], x_src[s0:s0 + 128, :], b)

        DEPTH = 2
        pipe = []
        delayed = []

        def tick():
            for d in delayed:
                d[0] -= 1
            while delayed and delayed[0][0] <= 0:
                delayed.pop(0)[1]()

        cur_tag = [0]

        def push(score_fn, pv_fn, after=None):
            tok_ = score_fn()
            pipe.append((pv_fn, tok_, after, cur_tag[0]))
            if len(pipe) > DEPTH:
                pv, tk, af, _ = pipe.pop(0)
                pv(tk)
                if af is not None:
                    af()
            tick()

        def flush():
            while pipe:
                pv, tk, af, _ = pipe.pop(0)
                pv(tk)
                if af is not None:
                    af()
            while delayed:
                delayed.pop(0)[1]()

        def attn_tile(Ops, bO, first, last_, KT_ap, bKT, V_ap, bV, Q2, bQ, smask=None, emask=None, after=None):
            def score():
                psS, bS = ring.get()
                nmask = (4 if smask is not None else 0) + (1 if emask is not None else 0)
                rl = (bKT if isinstance(bKT, list) else [bKT]) + (bQ if isinstance(bQ, list) else [bQ])
                k.mm(psS[:, 0:512], KT_ap, Q2, True, nmask == 0, r=rl, w=[bS])
                done = 0
                assert emask is None
                if smask is not None:
                    m_ap, bm = smask
                    for g in range(4):
                        done += 1
                        k.mm(psS[:, g * 128:(g + 1) * 128], ident[:], m_ap, False, done == nmask, r=[bid, bm], w=[bS])
                pT, bpT = pTs[pti[0] % len(pTs)]
                pti[0] += 1
                k.act(pT[:], psS[:, 0:512], AF.Exp, r=[bS], w=[bpT], scale=0.125)
                return (pT, bpT)

            def pv(tk):
                pT, bpT = tk
                k.mm(Ops[0:65, 0:512], V_ap, pT[:], first, last_, r=[bV, bpT], w=[bO])
            push(score, pv, after)

        fin_i = [0]

        def finalize(Ops, bO, gate_ap, bgate, first_branch, out_final=None, bout=None):
            tag_ = cur_tag[0]

            def stage_a():
                while sum(1 for d_ in delayed if len(d_) > 3) >= 4:
                    delayed.pop(0)[1]()
                rr, brr = rrs[fin_i[0] % 5]
                rhi, brhi = rhis[fin_i[0] % 5]
                rlo, brlo = rlos[fin_i[0] % 5]
                osb, bosb = osbs[fin_i[0] % 5]
                fin_i[0] += 1
                k.ts("dve", rr[64:65, :], Ops[64:65, 0:512], 1e-20, ALU.max, r=[bO], w=[brr])
                k.recip(rr[64:65, :], rr[64:65, :], r=[brr], w=[brr])
                k.tt("dve", rr[64:65, :], rr[64:65, :], gate_ap, ALU.mult, r=[brr, bgate], w=[brr])
                k.copy("dve", rhi[64:65, :], rr[64:65, :], r=[brr], w=[brhi])
                k.tt("dve", rlo[64:65, :], rr[64:65, :], rhi[64:65, :], ALU.subtract, r=[brr, brhi], w=[brlo])
                k.copy("act", osb[:], Ops[0:64, 0:512], r=[bO], w=[bosb])

                def stage_b():
                    psb, bpb = ringM.get()
                    k.mm(psb[0:64, 0:512], ones16[64:65, 0:64], rhi[64:65, :], True, False, r=[bones, brhi], w=[bpb])
                    k.mm(psb[0:64, 0:512], ones16[64:65, 0:64], rlo[64:65, :], False, True, r=[bones, brlo], w=[bpb])
                    if first_branch:
                        k.tt("dve", oacc[:], osb[:], psb[0:64, 0:512], ALU.mult, r=[bosb, bpb], w=[boacc])
                    else:
                        k.tt("dve", otmp[:], osb[:], psb[0:64, 0:512], ALU.mult, r=[bosb, bpb], w=[botmp])
                        if out_final is None:
                            k.tt("pool", oacc[:], oacc[:], otmp[:], ALU.add, r=[boacc, botmp], w=[boacc])
                        else:
                            k.tt("pool", out_final, oacc[:], otmp[:], ALU.add, r=[boacc, botmp], w=[bout])
                delayed.append([12, stage_b, tag_, "fin"])
            return stage_a

        mtmps = [k.sb(st, [128, 128], F32, "mtmp") for _ in range(2)]

        def epilogue1(qb):
            pb = qb % 2
            if "dbg_o" in sc:
                for hk in range(2):
                    k.dma("sp", sc["dbg_o"][hk, :, qb], oTb[hk][0][:].rearrange("d g q -> d (g q)"), oTb[hk][1])
            sg_t, bsgn_ = sgn[pb]
            gs_t, bgs_ = gss[pb]
            for half in range(2):
                ps, bp = ringM.get()
                for f4 in range(4):
                    fc = half * 4 + f4
                    for h in range(8):
                        k.mm(ps[:, f4 * 128:(f4 + 1) * 128], wbn[:, h, fc * 128:(fc + 1) * 128], oTb[h // 4][0][:, h % 4, :],
                             h == 0, h == 7, r=[bwbn, oTb[h // 4][1]], w=[bp])
                for f4 in range(4):
                    fc = half * 4 + f4
                    mt_, bmt_ = mtmps[fc % 2]
                    k.tt("dve", mt_[:], ps[:, f4 * 128:(f4 + 1) * 128], sg_t[:, fc, :], ALU.mult, r=[bp, bsgn_], w=[bmt_])
                    k.tt("pool", mrg[:, fc, :], mt_[:], gs_t[:, fc, :], ALU.add, r=[bmt_, bgs_], w=[bmrg])
            delayed.append([8, lambda: epilogue2(qb), qb])

        def epilogue2(qb):
            s0 = qb * 128
            pb = qb % 2
            x_t, bx = xin[pb]
            xm_t, bxm = xm[0]
            for half in range(2):
                ps, bp = ringM.get()
                for fc in range(8):
                    k.mm(ps[:, 0:512], mrg[:, fc, :], w_out[:, fc, half * 512:(half + 1) * 512], fc == 0, fc == 7, r=[bmrg, bw_out[fc]], w=[bp])
                k.tt("dve", xm_t[:, half * 512:(half + 1) * 512], ps[:, 0:512], x_t[:, half * 512:(half + 1) * 512], ALU.add, r=[bp, bx], w=[bxm])
            k.dma("sp", sc["xmid"][s0:s0 + 128, :], xm_t[:], bxm)

        def force(tag_max):
            while pipe and pipe[0][3] <= tag_max:
                pv, tk, af, _ = pipe.pop(0)
                pv(tk)
                if af is not None:
                    af()
            progressed = True
            while progressed:
                progressed = False
                for idx_, d_ in enumerate(delayed):
                    if d_[2] <= tag_max:
                        delayed.pop(idx_)
                        d_[1]()
                        progressed = True
                        break

        def chainA(qb, hk):
            pb = qb % 2
            Qt, bQ = QT[pb][hk]
            for g in range(4):
                ps, bp = ringM.get()
                k.mm(ps[:, 0:NCP], Qt[:, g, :], KcT[:, 0:NCP], True, False, r=[bQ, bKcT], w=[bp])
                k.mm(ps[:, 0:NCP], ident[:], mgen[:, 512 - 8 * qb:512 - 8 * qb + NCP], False, True, r=[bid, bmgen], w=[bp])
                k.act(eg[g][0][:], ps[:, 0:NCP], AF.Exp, r=[bp], w=[eg[g][1], bden4], scale=0.125, accum=den4[:, g:g + 1])
            k.ts("dve", rden4[:], den4[:], 1e-20, ALU.max, r=[bden4], w=[brden4])
            k.recip(rden4[:], rden4[:], r=[brden4], w=[brden4])
            k.ts("dve", pg[:, 1:1 + NCP], eg[0][0][:], rden4[:, 0:1], ALU.mult, r=[eg[0][1], brden4], w=[bpg])
            for g in range(1, 4):
                k.stt(pg[:, 1:1 + NCP], eg[g][0][:], rden4[:, g:g + 1], pg[:, 1:1 + NCP], ALU.mult, ALU.add, r=[eg[g][1], brden4, bpg], w=[bpg])
            P.op("dve", lambda e: e.tensor_reduce(out=blk[:], in_=pg[:, 0:NCP].rearrange("p (j o) -> p j o", o=4), axis=AX.X, op=ALU.add), reads=[bpg], writes=[bblk])
            k.tt("dve", blk[:], blk[:], pg[:, 4:4 + 4 * NB:4], ALU.add, r=[bblk, bpg], w=[bblk])
            k.tt("dve", blk[:], blk[:], G[:, 126 - 2 * qb:126 - 2 * qb + NB], ALU.add, r=[bblk, bG], w=[bblk])
            if qb >= 1:
                k.ts("dve", blk[:, 0:1], blk[:, 0:1], 1e4, ALU.add, r=[bblk], w=[bblk])
            P.op("dve", lambda e: e.max(out=m8[:, 0:8], in_=blk[:]), reads=[bblk], writes=[bm8])
            P.op("dve", lambda e: e.match_replace(out=blk2[:], in_to_replace=m8[:, 0:8], in_values=blk[:], imm_value=-3e38), reads=[bblk, bm8], writes=[bblk2])
            P.op("dve", lambda e: e.max(out=m8[:, 8:16], in_=blk2[:]), reads=[bblk2], writes=[bm8])
            nhalf_used = 1 if qb < 32 else NHALF
            need_nat = (hk == 1) or nhalf_used > 1
            need_sw = (hk == 0) or nhalf_used > 1
            if need_nat:
                k.ts("dve", negm[:, 0:NB], blk[:], m8[:, 15:16], ALU.is_lt, r=[bblk, bm8], w=[bnegm], s2=NEG, op1=ALU.mult)
            if need_sw:
                n0 = min(NB, 64)
                k.ts("dve", negm_sw[:, 64:64 + n0], blk[:, 0:n0], m8[:, 15:16], ALU.is_lt, r=[bblk, bm8], w=[bnegm_sw], s2=NEG, op1=ALU.mult)
                if NB > 64:
                    k.ts("dve", negm_sw[:, 0:NB - 64], blk[:, 64:NB], m8[:, 15:16], ALU.is_lt, r=[bblk, bm8], w=[bnegm_sw], s2=NEG, op1=ALU.mult)

        def chainB(qb, hk):
            pb = qb % 2
            os_ = slice((1 - hk) * 64, (2 - hk) * 64)
            nhalf_used = 1 if qb < 32 else NHALF
            for hf in range(nhalf_used):
                use_sw = (hk == 0 and hf == 0) or (hk == 1 and hf == 1)
                src_t, bsrc = (negm_sw, bnegm_sw) if use_sw else (negm, bnegm)
                ps, bp = ringM.get()
                pbf = ps[:].bitcast(BF16)
                k.tr(pbf[:, 0:128], src_t[:], ident[:], r=[bsrc, bid], w=[bp])
                qs_t, _, bqm = Qsel[pb][hk][hf]
                for g in range(4):
                    k.copy("dve", qs_t[os_, g, :], pbf[os_, 0:128], r=[bp], w=[bqm])

        loads(0)
        chainA(0, 0)
        for qb in range(NT):
            s0 = qb * 128
            pb = qb % 2
            for hk in range(2):
                cur_tag[0] = qb
                Qt, bQ = QT[pb][hk]
                Q2 = Qt[:].rearrange("d g q -> d (g q)")
                gr, bgr = grow[pb]
                chainB(qb, hk)
                Ops, bO = ringO.get()
                fa = finalize(Ops, bO, gr[64:65, hk, 0].rearrange("o g q -> o (g q)"), bgr, True)
                tiles = [nt for nt in range(NCT) if qb - 16 * nt >= 0]
                for idx, nt in enumerate(tiles):
                    m = qb - 16 * nt
                    sm = (mt[:, m, :], bmt) if m < 16 else None
                    attn_tile(Ops, bO, idx == 0, idx == len(tiles) - 1, KcT[:, nt * 128:(nt + 1) * 128], bKcT,
                              Vc[:, nt, hk, :], bVc, Q2, bQ, smask=sm, after=fa if idx == len(tiles) - 1 else None)
                Ops, bO = ringO.get()
                fa = finalize(Ops, bO, gr[64:65, hk, 2].rearrange("o g q -> o (g q)"), bgr, False)
                tiles = [wt for wt in range(5) if s0 - 512 + 128 * wt >= 0]
                kw_full, bkw = KwT[pb]
                kw_t = kw_full
                vw_t, bvw = Vw[pb]
                for idx, wt in enumerate(tiles):
                    sm = (low[:], blow) if wt == 0 else ((caus[:], bcaus) if wt == 4 else None)
                    attn_tile(Ops, bO, idx == 0, idx == len(tiles) - 1, kw_t[:, wt * 128:(wt + 1) * 128], bkw,
                              vw_t[:, wt, hk, :], bvw, Q2, bQ, smask=sm, after=fa if idx == len(tiles) - 1 else None)
                if hk == 1 and qb + 1 < NT:
                    force(qb - 1)
                    loads(qb + 1)
                if hk == 0:
                    chainA(qb, 1)
                elif qb + 1 < NT:
                    chainA(qb + 1, 0)
                Ops, bO = ringO.get()
                fa = finalize(Ops, bO, gr[64:65, hk, 1].rearrange("o g q -> o (g q)"), bgr, False,
                              out_final=oTb[hk][0][:].rearrange("d g q -> d (g q)"), bout=oTb[hk][1])
                for i in range(qb + 1):
                    sm = (caus[:], bcaus) if i == qb else None
                    qs_t, bqs, bqm = Qsel[pb][hk][i // 32]
                    attn_tile(Ops, bO, i == 0, i == qb, KsM[hk][0][:, i * 128:(i + 1) * 128], [KsM[hk][1], KsM[hk][2]],
                              Vs[:, i, hk, :], bVs, qs_t[:].rearrange("d g q -> d (g q)"), [bqs, bqm], smask=sm, after=fa if i == qb else None)
                if hk == 1:
                    delayed_ep = (lambda q_=qb: (lambda: delayed.append([14, lambda: epilogue1(q_), q_])))(qb)
                    pipe[-1] = (pipe[-1][0], pipe[-1][1], (lambda f1=pipe[-1][2], f2=delayed_ep: (f1(), f2())), pipe[-1][3])
        flush()
    P.barrier()


def phase4(k, l, S, di, sc, cst, dst, last):
    P = k.P
    TT = 256
    NTT = S // TT
    NSB = TT // 128
    ident, bid = cst["ident"]
    with ExitStack() as st:
        ring = PsumRing(k, st)
        w_fin, bw_fin = load_w(k, st, di["w_fin"][l], 8, 2 * DFF, name="w_fin")
        w_fo, bw_fo = load_w(k, st, di["w_fout"][l], NFC, D, name="w_fo")
        gam, bgam = k.sb(st, [128, D], F32, "gam")
        k.dma("sp", gam[:], di["g_ffn"][l], bgam)
        cw, bcw = k.sb(st, [128, NFC, 3], F32, "cw")
        k.dma("sp", cw[:], di["f_cw"][l], bcw)
        cb, bcb = k.sb(st, [128, NFC], F32, "cb")
        k.dma("sp", cb[:], di["f_cb"][l], bcb)
        if last:
            gfin, bgfin = k.sb(st, [128, D], F32, "gfin")
            k.dma("sp", gfin[:], di["g_fin"], bgfin)
        halo, bhalo = k.sb(st, [128, NFC, 2], F32, "halo")
        k.memset("pool", halo[:], 0.0, [bhalo])
        xt = [k.sb(st, [128, NSB, D], F32, "xt") for _ in range(2)]
        junk, bjunk = k.sb(st, [128, D], BF16, "junk")
        ss, bss = k.sb(st, [128, 2 * NSB], F32, "ss")
        ms, bms = k.sb(st, [128, 2 * NSB], F32, "ms")
        sd, bsd = k.sb(st, [128, 2 * NSB], F32, "sd")
        rstd, brstd = k.sb(st, [128, 2 * NSB], F32, "rstd")
        hh = [k.sb(st, [128, D], BF16, "h") for _ in range(2)]
        hTs = [k.sb(st, [128, 8, TT], BF16, "hT") for _ in range(2)]
        a_sb = [k.sb(st, [128, TT + 2], F32, "a_sb") for _ in range(2)]
        cv = [k.sb(st, [128, TT], F32, "cv") for _ in range(2)]
        gl = [k.sb(st, [128, TT], F32, "gl") for _ in range(2)]
        actTs = [k.sb(st, [128, NFC, TT], BF16, "actT") for _ in range(2)]
        xo = [k.sb(st, [128, NSB, D], F32, "xo") for _ in range(1)]

        def load_tile(i):
            t, b = xt[i % 2]
            k.dma("sp", t[:], sc["xmid"][i * TT:(i + 1) * TT, :].rearrange("(s p) d -> p s d", p=128), b)

        def rms(x_ap, bx, col, g_t, bg, out_ap, bout):
            k.act(junk[:], x_ap, AF.Square, r=[bx], w=[bjunk, bss], accum=ss[:, col:col + 1])
            k.ts("dve", ms[:, col:col + 1], ss[:, col:col + 1], 1.0 / D, ALU.mult, r=[bss], w=[bms], s2=EPS, op1=ALU.add)
            k.act(sd[:, col:col + 1], ms[:, col:col + 1], AF.Sqrt, r=[bms], w=[bsd])
            k.recip(rstd[:, col:col + 1], sd[:, col:col + 1], r=[bsd], w=[brstd])
            k.stt(out_ap, x_ap, rstd[:, col:col + 1], g_t[:], ALU.mult, ALU.mult, r=[bx, brstd, bg], w=[bout])

        def prep(i):
            x_t, bx = xt[i % 2]
            hT, bhT = hTs[i % 2]
            for s_ in range(NSB):
                h_t, bh = hh[s_ % 2]
                rms(x_t[:, s_, :], bx, s_, gam, bgam, h_t[:], bh)
                ps, bp = ring.get()
                pbf = ps[:].bitcast(BF16)
                for c in range(8):
                    k.tr(pbf[:, c * 128:(c + 1) * 128], h_t[:, c * 128:(c + 1) * 128], ident[:], r=[bh, bid], w=[bp])
                k.copy("act", hT[:, :, s_ * 128:(s_ + 1) * 128], pbf.rearrange("p (c t) -> p c t", c=8), r=[bp], w=[bhT])

        def inproj(i):
            hT, bhT = hTs[i % 2]
            actT, bactT = actTs[i % 2]
            for fc in range(NFC):
                psa, bpa = ring.get()
                for c in range(8):
                    k.mm(psa[:, 0:TT], w_fin[:, c, fc * 128:(fc + 1) * 128], hT[:, c, :], c == 0, c == 7, r=[bw_fin[c], bhT], w=[bpa])
                psb, bpb = ring.get()
                for c in range(8):
                    k.mm(psb[:, 0:TT], w_fin[:, c, DFF + fc * 128:DFF + (fc + 1) * 128], hT[:, c, :], c == 0, c == 7, r=[bw_fin[c], bhT], w=[bpb])
                a_t, ba = a_sb[fc % 2]
                c_t, bc = cv[fc % 2]
                g_t, bg = gl[fc % 2]
                k.copy("pool", a_t[:, 0:2], halo[:, fc, :], r=[bhalo], w=[ba])
                k.copy("act", a_t[:, 2:2 + TT], psa[:, 0:TT], r=[bpa], w=[ba])
                k.copy("pool", halo[:, fc, :], a_t[:, TT:TT + 2], r=[ba], w=[bhalo])
                k.act(c_t[:], psa[:, 0:TT], AF.Identity, r=[bpa, bcw, bcb], w=[bc], scale=cw[:, fc, 2:3], bias=cb[:, fc:fc + 1])
                k.stt(c_t[:], a_t[:, 1:1 + TT], cw[:, fc, 1:2], c_t[:], ALU.mult, ALU.add, r=[ba, bcw, bc], w=[bc])
                k.stt(c_t[:], a_t[:, 0:TT], cw[:, fc, 0:1], c_t[:], ALU.mult, ALU.add, r=[ba, bcw, bc], w=[bc])
                k.act(g_t[:], c_t[:], AF.Gelu_apprx_tanh, r=[bc], w=[bg])
                k.tt("dve", actT[:, fc, :], psb[:, 0:TT], g_t[:], ALU.mult, r=[bpb, bg], w=[bactT])

        def outproj(i):
            x_t, bx = xt[i % 2]
            actT, bactT = actTs[i % 2]
            xo_t, bxo = xo[0]
            for s_ in range(NSB):
                for half in range(2):
                    ps, bp = ring.get()
                    for fc in range(NFC):
                        k.mm(ps[:, 0:512], actT[:, fc, s_ * 128:(s_ + 1) * 128], w_fo[:, fc, half * 512:(half + 1) * 512], fc == 0, fc == NFC - 1,
                             r=[bactT, bw_fo[fc]], w=[bp])
                    k.tt("dve", xo_t[:, s_, half * 512:(half + 1) * 512], ps[:, 0:512], x_t[:, s_, half * 512:(half + 1) * 512], ALU.add, r=[bp, bx], w=[bxo])
            if last:
                for s_ in range(NSB):
                    rms(xo_t[:, s_, :], bxo, NSB + s_, gfin, bgfin, xo_t[:, s_, :], bxo)
            k.dma("sp", dst[i * TT:(i + 1) * TT, :].rearrange("(s p) d -> p s d", p=128), xo_t[:], bxo)

        load_tile(0)
        if NTT > 1:
            load_tile(1)
        prep(0)
        for i in range(NTT):
            inproj(i)
            if i + 1 < NTT:
                prep(i + 1)
            outproj(i)
            if i + 2 < NTT:
                load_tile(i + 2)
    P.barrier()


INPUT_SHAPES = None


def build(S, L, TS=128, dbg=False, phases=("p1", "p2", "p3", "p4")):
    nc = bass.Bass("TRN2", target_bir_lowering=False)
    NT = S // 128
    di = {}

    def din(name, shape):
        di[name] = nc.dram_tensor(name, list(shape), F32, kind="ExternalInput").ap()
    din("x", [S, D])
    for nm, shp in (("g_mix", [L, 128, D]), ("g_ffn", [L, 128, D]), ("g_fin", [128, D]),
                    ("w_in", [L, D, INW]), ("w_sw", [L, D, 896]),
                    ("s_are", [L, 128, 16]), ("s_aim", [L, 128, 16]), ("s_ldt", [L, 128, 16]),
                    ("s_bre", [L, 128, 16, 128]), ("s_bim", [L, 128, 16, 128]),
                    ("s_cre", [L, 128, 16, 128]), ("s_cim", [L, 128, 16, 128]), ("s_d", [L, 128, 4]),
                    ("w_glu", [L, 512, 1024]), ("w_bssm", [L, 512, 1024]), ("w_bnsa", [L, 512, 1024]),
                    ("w_out", [L, D, D]),
                    ("c_pek", [L, 64, 32]), ("c_w1k", [L, 2048, 256]), ("c_b1k", [L, 128, 2]), ("c_w2k", [L, 256, 64]),
                    ("c_pev", [L, 64, 32]), ("c_w1v", [L, 2048, 256]), ("c_b1v", [L, 128, 2]), ("c_w2v", [L, 256, 64]),
                    ("w_fin", [L, D, 2 * DFF]), ("w_fout", [L, DFF, D]), ("f_cw", [L, 128, NFC, 3]), ("f_cb", [L, 128, NFC]),
                    ("c_cos", [128, S]), ("c_sin", [128, S]), ("c_ident", [128, 128]), ("c_tau", [128, TS]),
                    ("c_caus", [128, 128]), ("c_low", [128, 128]), ("c_mgen", [128, 1024]), ("c_mt", [128, 16, 128]),
                    ("c_g", [128, 256]), ("c_ind", [64, S])):
        din(nm, shp)
    out = nc.dram_tensor("out", [S, D], F32, kind="ExternalOutput").ap()
    skind = "ExternalOutput" if dbg else "Internal"
    sc = {}

    def scr(name, shape, dt):
        sc[name] = nc.dram_tensor(name, list(shape), dt, kind=skind).ap()
    scr("qT", [512, S], BF16)
    for nm in ("kcT", "vcT", "ksT", "kwT"):
        scr(nm, [128, S], BF16)
    scr("vs", [S, 2, 65], BF16)
    scr("vw", [S, 2, 65], BF16)
    scr("sgT", [24, S], F32)
    scr("sgnT", [1024, S], BF16)
    scr("gssT", [1024, S], BF16)
    scr("xmid", [S, D], F32)
    if dbg:
        scr("dbg_o", [2, 64, NT, 512], BF16)
        scr("dbg_kc", [2, 64, S // 16], BF16)
        scr("dbg_vc", [128, S // 2048, 2, 65], BF16)
    scr("x1", [S, D], F32)
    with ExitStack() as st:
        P = Prog(nc)
        k = K(nc, P)
        cst = {}
        ident, bid = k.sb(st, [128, 128], BF16, "ident")
        k.dma("pool", ident[:], di["c_ident"], bid)
        cst["ident"] = (ident, bid)
        x_src = di["x"]
        for l in range(L):
            last = l == L - 1
            if "p1" in phases:
                phase1(k, l, S, TS, x_src, di, sc, cst)
            if "p2" in phases:
                pers = ExitStack()
                cmp_t = phase2(k, pers, l, S, di, sc, cst)
            if "p3" in phases:
                phase3(k, l, S, x_src, di, sc, cst, cmp_t)
            if "p2" in phases:
                pers.close()
                P.barrier()
            if "p4" in phases:
                phase4(k, l, S, di, sc, cst, out if last else sc["x1"], last)
            x_src = sc["x1"]
        P.barrier()
        P.emit(st)
    return nc


_NC_CACHE = {}


def kernel(**inputs):
    S, L, NCORES = 8192, 2, 8
    inp = {k_: np.asarray(v) for k_, v in inputs.items()}
    hl = host_layout(inp, L)
    hc = host_consts(S, 128)
    common = {}
    common.update(hl)
    common.update(hc)
    common = {k_: np.ascontiguousarray(v, dtype=np.float32) for k_, v in common.items()}
    if "nc" not in _NC_CACHE:
        _NC_CACHE["nc"] = build(S, L)
    nc = _NC_CACHE["nc"]
    x = np.asarray(inp["x"], dtype=np.float32)
    in_maps = []
    for b in range(NCORES):
        m = dict(common)
        m["x"] = np.ascontiguousarray(x[b])
        in_maps.append(m)
    res = run_bass_kernel_spmd(nc, in_maps, core_ids=list(range(NCORES)))
    return np.stack([np.asarray(r["out"], dtype=np.float32) for r in res.results], axis=0)
```

```python
from contextlib import ExitStack
import numpy as np
import ml_dtypes
import concourse.bass as bass
import concourse.mybir as mybir
from concourse.bass_utils import run_bass_kernel_spmd

F32 = mybir.dt.float32
BF16 = mybir.dt.bfloat16
I32 = mybir.dt.int32
ALU = mybir.AluOpType
AF = mybir.ActivationFunctionType
AX = mybir.AxisListType

D = 1024
DFF = 2816
NFC = DFF // 128
INW = 3864
EPS = 1e-6
NEG = -30000.0
TWO_PI = float(2 * np.pi)
SIN_SCALE = TWO_PI * 0.999999


class Buf:
    __slots__ = ("name", "w", "rs", "sem", "cnt", "slot", "base", "uid")

    def __init__(self, name="b"):
        self.name = name
        self.w = None
        self.rs = {}
        self.sem = None
        self.cnt = 0
        self.slot = None
        self.base = 0
        self.uid = None


class Op:
    __slots__ = ("eng", "fn", "deps", "key", "val", "signal", "sigval", "dma", "slot", "semval")


ENGS = ("pe", "act", "dve", "pool", "sp")


class Prog:
    def __init__(self, nc):
        self.nc = nc
        self.ops = {e: [] for e in ENGS}
        self.seen = {e: {} for e in ENGS}
        self.dma_bufs = []
        self.last = {}
        self.slot_base = []
        self.free_slots = []
        self.live = []
        self.uid = 0

    def _get_slot(self, buf):
        if self.free_slots:
            sl = self.free_slots.pop()
        else:
            sl = len(self.slot_base)
            self.slot_base.append(0)
        self.uid += 1
        buf.sem = True
        buf.slot = sl
        buf.base = self.slot_base[sl]
        buf.cnt = 0
        buf.uid = self.uid
        self.live.append(buf)

    def barrier(self):
        lasts = list(self.last.values())
        self._barrier_ops(lasts)
        for b in self.live:
            self.slot_base[b.slot] = b.base + b.cnt
            self.free_slots.append(b.slot)
            b.sem = None
        self.live = []
        self.last = {kk: v for kk, v in self.last.items() if not isinstance(kk, tuple)}

    def _barrier_ops(self, lasts):
        for e in ENGS:
            o = Op()
            o.eng = e
            o.fn = None
            o.deps = []
            o.signal = False
            o.sigval = None
            o.dma = None
            o.key = e
            o.val = len(self.ops[e])
            for d in lasts:
                if d.key == e:
                    continue
                if self.seen[e].get(d.key, -1) >= d.val:
                    continue
                self.seen[e][d.key] = d.val
                o.deps.append(d)
            self.ops[e].append(o)

    def _dep(self, eng, d, deps, same_ok):
        if d is None:
            return
        key = d.key
        if key == eng:
            if eng == "pe" or same_ok:
                return
        if self.seen[eng].get(key, -1) >= d.val:
            return
        self.seen[eng][key] = d.val
        deps.append(d)

    def op(self, eng, fn, reads=(), writes=(), dma=None):
        o = Op()
        o.eng = eng
        o.fn = fn
        o.deps = []
        o.signal = False
        o.sigval = None
        o.dma = dma
        writes = [b for b in writes if b is not None]
        reads = [b for b in reads if b is not None]
        o.slot = None
        o.semval = None
        if dma is not None:
            if dma.sem is None:
                self._get_slot(dma)
            dma.cnt += 1
            o.key = ("dma", dma.uid)
            o.val = dma.cnt
            o.slot = dma.slot
            o.semval = 16 * (dma.base + dma.cnt)
            if dma not in writes:
                writes.append(dma)
            reads = [b for b in reads if b is not dma]
        else:
            o.key = eng
            o.val = len(self.ops[eng])
        for b in reads:
            self._dep(eng, b.w, o.deps, False)
        for b in writes:
            self._dep(eng, b.w, o.deps, True)
            for r in b.rs.values():
                self._dep(eng, r, o.deps, True)
        for b in reads:
            b.rs[o.key] = o
        for b in writes:
            b.w = o
            b.rs = {}
        self.ops[eng].append(o)
        self.last[o.key] = o
        return o

    def emit(self, stack):
        nc = self.nc
        for e in ENGS:
            for o in self.ops[e]:
                for d in o.deps:
                    if d.dma is None:
                        d.signal = True
        for e in ENGS:
            c = 0
            for o in self.ops[e]:
                if o.dma is None and o.signal:
                    c += 1
                    o.sigval = c
        esem = {}
        for e in ("pe", "act", "dve", "pool"):
            esem[e] = stack.enter_context(nc.semaphore("s_" + e))
        dsem = [stack.enter_context(nc.semaphore("d%d" % i)) for i in range(len(self.slot_base))]
        block = stack.enter_context(nc.Block())
        prog = self

        def run(name, eng):
            for o in prog.ops[name]:
                for d in o.deps:
                    if d.dma is not None:
                        eng.wait_ge(dsem[d.slot], d.semval)
                    else:
                        eng.wait_ge(esem[d.key], d.sigval)
                if o.fn is None:
                    continue
                ins = o.fn(eng)
                if o.dma is not None:
                    ins.then_inc(dsem[o.slot], 16)
                elif o.signal:
                    ins.then_inc(esem[name], 1)

        @block.sync
        def _(eng):
            run("sp", eng)

        @block.scalar
        def _(eng):
            run("act", eng)

        @block.vector
        def _(eng):
            run("dve", eng)

        @block.gpsimd
        def _(eng):
            run("pool", eng)

        @block.tensor
        def _(eng):
            run("pe", eng)


class K:
    def __init__(self, nc, P):
        self.nc = nc
        self.P = P
        self.n = 0

    def name(self, s):
        self.n += 1
        return "%s_%d" % (s, self.n)

    def sb(self, st, shape, dt=F32, name="t"):
        t = st.enter_context(self.nc.sbuf_tensor(self.name(name), list(shape), dt))
        return t, Buf(name)

    def dma(self, eng, out, in_, buf, reads=(), writes=()):
        self.P.op(eng, lambda e: e.dma_start(out=out, in_=in_), reads=reads, writes=writes, dma=buf)

    def mm(self, out, lhsT, rhs, start, stop, r, w):
        self.P.op("pe", lambda e: e.matmul(out, lhsT=lhsT, rhs=rhs, start=start, stop=stop), reads=r, writes=w)

    def tr(self, out, in_, ident, r, w):
        self.P.op("pe", lambda e: e.transpose(out=out, in_=in_, identity=ident), reads=r, writes=w)

    def act(self, out, in_, func, r, w, bias=None, scale=None, accum=None):
        kw = {}
        if bias is not None:
            kw["bias"] = bias
        if scale is not None:
            kw["scale"] = scale
        if accum is not None:
            kw["accum_out"] = accum
        self.P.op("act", lambda e: e.activation(out=out, in_=in_, func=func, **kw), reads=r, writes=w)

    def tt(self, eng, out, in0, in1, op, r, w):
        self.P.op(eng, lambda e: e.tensor_tensor(out=out, in0=in0, in1=in1, op=op), reads=r, writes=w)

    def ts(self, eng, out, in0, s1, op0, r, w, s2=None, op1=None):
        if op1 is None:
            self.P.op(eng, lambda e: e.tensor_scalar(out=out, in0=in0, scalar1=s1, scalar2=None, op0=op0), reads=r, writes=w)
        else:
            self.P.op(eng, lambda e: e.tensor_scalar(out=out, in0=in0, scalar1=s1, scalar2=s2, op0=op0, op1=op1), reads=r, writes=w)

    def stt(self, out, in0, scalar, in1, op0, op1, r, w):
        self.P.op("dve", lambda e: e.scalar_tensor_tensor(out=out, in0=in0, scalar=scalar, in1=in1, op0=op0, op1=op1), reads=r, writes=w)

    def copy(self, eng, out, in_, r, w):
        if eng == "act":
            self.P.op("act", lambda e: e.activation(out=out, in_=in_, func=AF.Copy), reads=r, writes=w)
        else:
            self.P.op(eng, lambda e: e.tensor_copy(out=out, in_=in_), reads=r, writes=w)

    def memset(self, eng, ap, val, w):
        self.P.op(eng, lambda e: e.memset(ap, val), writes=w)

    def scan(self, out, d0, d1, init, r, w):
        self.P.op("dve", lambda e: e.tensor_tensor_scan(out=out, data0=d0, data1=d1, initial=init, op0=ALU.mult, op1=ALU.add), reads=r, writes=w)

    def recip(self, out, in_, r, w):
        self.P.op("dve", lambda e: e.reciprocal(out=out, in_=in_), reads=r, writes=w)


class PsumRing:
    def __init__(self, k, st, n=8):
        self.banks = []
        for i in range(n):
            t = st.enter_context(k.nc.psum_tensor(k.name("ps"), [128, 512], F32))
            self.banks.append((t, Buf("ps%d" % i)))
        self.i = 0

    def get(self):
        t, b = self.banks[self.i % len(self.banks)]
        self.i += 1
        return t, b


def _swap_halves(w):
    sh = w.shape
    w4 = w.reshape(sh[:-1] + (sh[-1] // 64, 2, 32))
    return np.ascontiguousarray(w4[..., ::-1, :]).reshape(sh)


def host_consts(S, TS):
    c = {}
    inv = (10000.0 ** (-np.arange(0, 64, 2, dtype=np.float32) / np.float32(64))).astype(np.float32)
    ang = (np.arange(S, dtype=np.float32)[:, None] * inv[None, :]).astype(np.float32)
    cs, sn = np.cos(ang).astype(np.float32), np.sin(ang).astype(np.float32)
    cosT = np.concatenate([cs.T, cs.T], 0)
    sinT = np.concatenate([-sn.T, sn.T], 0)
    c["c_cos"] = np.ascontiguousarray(np.concatenate([cosT, cosT], 0))
    c["c_sin"] = np.ascontiguousarray(np.concatenate([sinT, sinT], 0))
    c["c_ident"] = np.eye(128, dtype=np.float32)
    c["c_tau"] = np.ascontiguousarray(np.broadcast_to(np.arange(TS, dtype=np.float32)[None, :], (128, TS)))
    k = np.arange(128)[:, None]
    q = np.arange(128)[None, :]
    c["c_caus"] = np.where(k <= q, 0.0, NEG).astype(np.float32)
    c["c_low"] = np.where(k > q, 0.0, NEG).astype(np.float32)
    cc = np.arange(1024)[None, :]
    qi = np.arange(128)[:, None]
    c["c_mgen"] = np.where(16 * (cc - 512) + 31 <= qi, 0.0, NEG).astype(np.float32)
    m = np.arange(16)[None, :, None]
    ni = np.arange(128)[:, None, None]
    qq = np.arange(128)[None, None, :]
    c["c_mt"] = np.where(16 * ni + 31 <= 128 * m + qq, 0.0, NEG).astype(np.float32)
    rel = np.arange(256)[None, :] - 126
    cur = (np.arange(128)[:, None] >= 64).astype(np.int64)
    g = np.where(rel > cur, -1e30, 0.0) + np.where((rel == cur) | (rel == cur - 1), 1e4, 0.0)
    c["c_g"] = g.astype(np.float32)
    NT = S // 128
    j = np.arange(128)[:, None, None]
    i = np.arange(NT)[None, :, None]
    kk = np.arange(128)[None, None, :]
    c["c_e"] = (j == 2 * i + (kk >= 64)).astype(np.float32)
    r_ = np.arange(64)[:, None]
    cidx = np.arange(S)[None, :]
    c["c_ind"] = (((cidx // 64) % 64) == r_).astype(np.float32)
    return c


def host_layout(inp, L):
    o = {}
    f = np.float32
    o["g_mix"] = np.ascontiguousarray(np.broadcast_to(inp["norm_mix"][:, None, :], (L, 128, D))).astype(f)
    o["g_ffn"] = np.ascontiguousarray(np.broadcast_to(inp["norm_ffn"][:, None, :], (L, 128, D))).astype(f)
    o["g_fin"] = np.ascontiguousarray(np.broadcast_to(inp["norm_final"][None, :], (128, D))).astype(f)
    w_in = inp["w_in"]
    o["w_in"] = w_in
    sw = np.concatenate([_swap_halves(w_in[:, :, 512:1024]), _swap_halves(w_in[:, :, 1024:1152]),
                         _swap_halves(w_in[:, :, 1280:1408]), _swap_halves(w_in[:, :, 1536:1664])], axis=-1)
    o["w_sw"] = np.ascontiguousarray(sw)

    def pair(a):
        return np.ascontiguousarray(a.reshape(L, 16, 2, 64).transpose(0, 2, 3, 1).reshape(L, 128, 16))
    o["s_are"] = pair(inp["ssm_a_re"])
    o["s_aim"] = pair(inp["ssm_a_im"])
    o["s_ldt"] = pair(np.broadcast_to(inp["ssm_log_dt"][:, :, None], (L, 32, 64)))
    for nm, src in (("s_bre", "ssm_b_re"), ("s_bim", "ssm_b_im")):
        b = inp[src].reshape(L, 16, 2, 64, 16)
        pad = np.zeros((L, 8, 16, 16, 2, 64), f)
        for j in range(16):
            for gl in range(2):
                pad[:, 2 * (j % 4) + gl, :, j, gl, :] = b[:, j, gl].transpose(0, 2, 1)
        o[nm] = pad.reshape(L, 128, 16, 128)
    for nm, src in (("s_cre", "ssm_c_re"), ("s_cim", "ssm_c_im")):
        cmat = inp[src].reshape(L, 16, 2, 16, 64)
        pad = np.zeros((L, 2, 64, 16, 8, 16), f)
        for j in range(16):
            for gl in range(2):
                pad[:, gl, :, j, 2 * (j % 4) + gl, :] = cmat[:, j, gl].transpose(0, 2, 1)
        o[nm] = pad.reshape(L, 128, 16, 128)
    o["s_d"] = np.ascontiguousarray(inp["ssm_d"].reshape(L, 4, 128).transpose(0, 2, 1))
    o["w_glu"] = inp["ssm_w_glu"]
    o["w_bssm"] = inp["w_branch_ssm"]
    o["w_bnsa"] = inp["w_branch_nsa"]
    o["w_out"] = inp["w_out"]
    for t in ("k", "v"):
        o["c_pe" + t] = np.ascontiguousarray(inp["cmp_pe_" + t].transpose(0, 2, 1))
        o["c_w1" + t] = inp["cmp_w1_" + t]
        o["c_b1" + t] = np.ascontiguousarray(inp["cmp_b1_" + t].reshape(L, 2, 128).transpose(0, 2, 1))
        o["c_w2" + t] = inp["cmp_w2_" + t]
    o["w_fin"] = inp["w_ffn_in"]
    o["w_fout"] = inp["w_ffn_out"]
    o["f_cw"] = np.ascontiguousarray(inp["ffn_conv_w"].reshape(L, 3, NFC, 128).transpose(0, 3, 2, 1))
    o["f_cb"] = np.ascontiguousarray(inp["ffn_conv_b"].reshape(L, NFC, 128).transpose(0, 2, 1))
    return o


def load_w(k, st, src2d, nch, ncols, prow=128, eng="pool", name="w"):
    t, _ = k.sb(st, [prow, nch, ncols], BF16, name)
    bufs = []
    for c in range(nch):
        b = Buf(name)
        k.dma(eng, t[:, c, :], src2d[c * prow:(c + 1) * prow, :], b)
        bufs.append(b)
    return t, bufs


def sincos(k, st, arg, n, out_sin=None, out_cos=None, rb=(), wsin=None, wcos=None):
    for (dst, off, wb) in ((out_sin, 0.0, wsin), (out_cos, 0.25, wcos)):
        if dst is None:
            continue
        a2, ba2 = k.sb(st, [128, n], F32, "sc_a")
        ti, bti = k.sb(st, [128, n], I32, "sc_i")
        tf, btf = k.sb(st, [128, n], F32, "sc_f")
        k.ts("dve", a2[:], arg, off, ALU.add, r=list(rb), w=[ba2])
        k.copy("dve", ti[:], a2[:], r=[ba2], w=[bti])
        k.copy("dve", tf[:], ti[:], r=[bti], w=[btf])
        k.tt("dve", a2[:], a2[:], tf[:], ALU.subtract, r=[ba2, btf], w=[ba2])
        k.act(dst, a2[:], AF.Sin, r=[ba2], w=[wb], scale=SIN_SCALE)


def phase1(k, l, S, TS, x_src, di, sc, cst):
    P = k.P
    TT = TS
    NTT = S // TT
    with ExitStack() as st:
        ring = PsumRing(k, st)
        ident, bid = cst["ident"]
        gam, bgam = k.sb(st, [128, D], F32, "gam")
        k.dma("sp", gam[:], di["g_mix"][l], bgam)
        dvec, bdvec = k.sb(st, [128, 4], F32, "dvec")
        k.dma("sp", dvec[:], di["s_d"][l], bdvec)
        RFre, bRFre = k.sb(st, [128, 16, TS], F32, "RFre")
        RFim, bRFim = k.sb(st, [128, 16, TS], F32, "RFim")
        COSb, bCOSb = k.sb(st, [128, 16, TS], BF16, "COSb")
        SINb, bSINb = k.sb(st, [128, 16, TS], BF16, "SINb")
        NSINb, bNSINb = k.sb(st, [128, 16, TS], BF16, "NSINb")
        dec, bdec = k.sb(st, [128, 16], F32, "dec")
        cT, bcT = k.sb(st, [128, 16], F32, "cT")
        sT, bsT = k.sb(st, [128, 16], F32, "sT")
        nsT, bnsT = k.sb(st, [128, 16], F32, "nsT")
        with ExitStack() as s2:
            are, bare = k.sb(s2, [128, 16], F32, "are")
            aim, baim = k.sb(s2, [128, 16], F32, "aim")
            ldt, bldt = k.sb(s2, [128, 16], F32, "ldt")
            tau, btau = k.sb(s2, [128, TS], F32, "tau")
            k.dma("sp", are[:], di["s_are"][l], bare)
            k.dma("sp", aim[:], di["s_aim"][l], baim)
            k.dma("sp", ldt[:], di["s_ldt"][l], bldt)
            k.dma("sp", tau[:], di["c_tau"], btau)
            dt_, bdt = k.sb(s2, [128, 16], F32, "dt")
            k.act(dt_[:], ldt[:], AF.Exp, r=[bldt], w=[bdt])
            rho, brho = k.sb(s2, [128, 16], F32, "rho")
            thn, bthn = k.sb(s2, [128, 16], F32, "thn")
            k.tt("dve", rho[:], are[:], dt_[:], ALU.mult, r=[bare, bdt], w=[brho])
            k.tt("dve", thn[:], aim[:], dt_[:], ALU.mult, r=[baim, bdt], w=[bthn])
            k.ts("dve", thn[:], thn[:], 1.0 / TWO_PI, ALU.mult, r=[bthn], w=[bthn])
            k.act(dec[:], rho[:], AF.Exp, r=[brho], w=[bdec])
            s1, bs1 = k.sb(s2, [128, 16], F32, "s1")
            c1, bc1 = k.sb(s2, [128, 16], F32, "c1")
            sincos(k, s2, thn[:], 16, s1[:], c1[:], rb=[bthn], wsin=bs1, wcos=bc1)
            abre, babre = k.sb(s2, [128, 16], F32, "abre")
            abim, babim = k.sb(s2, [128, 16], F32, "abim")
            k.tt("dve", abre[:], dec[:], c1[:], ALU.mult, r=[bdec, bc1], w=[babre])
            k.ts("dve", abre[:], abre[:], -1.0, ALU.add, r=[babre], w=[babre])
            k.tt("dve", abim[:], dec[:], s1[:], ALU.mult, r=[bdec, bs1], w=[babim])
            den, bden = k.sb(s2, [128, 16], F32, "den")
            t0, bt0 = k.sb(s2, [128, 16], F32, "t0")
            k.tt("dve", den[:], are[:], are[:], ALU.mult, r=[bare], w=[bden])
            k.tt("dve", t0[:], aim[:], aim[:], ALU.mult, r=[baim], w=[bt0])
            k.tt("dve", den[:], den[:], t0[:], ALU.add, r=[bden, bt0], w=[bden])
            k.recip(den[:], den[:], r=[bden], w=[bden])
            fre, bfre = k.sb(s2, [128, 16], F32, "fre")
            fim, bfim = k.sb(s2, [128, 16], F32, "fim")
            t1, bt1 = k.sb(s2, [128, 16], F32, "t1")
            k.tt("dve", fre[:], abre[:], are[:], ALU.mult, r=[babre, bare], w=[bfre])
            k.tt("dve", t1[:], abim[:], aim[:], ALU.mult, r=[babim, baim], w=[bt1])
            k.tt("dve", fre[:], fre[:], t1[:], ALU.add, r=[bfre, bt1], w=[bfre])
            k.tt("dve", fre[:], fre[:], den[:], ALU.mult, r=[bfre, bden], w=[bfre])
            k.tt("dve", fim[:], abim[:], are[:], ALU.mult, r=[babim, bare], w=[bfim])
            k.tt("dve", t1[:], abre[:], aim[:], ALU.mult, r=[babre, baim], w=[bt1])
            k.tt("dve", fim[:], fim[:], t1[:], ALU.subtract, r=[bfim, bt1], w=[bfim])
            k.tt("dve", fim[:], fim[:], den[:], ALU.mult, r=[bfim, bden], w=[bfim])
            aT, baT = k.sb(s2, [128, 16], F32, "aT")
            k.ts("dve", aT[:], thn[:], float(TS), ALU.mult, r=[bthn], w=[baT])
            sincos(k, s2, aT[:], 16, sT[:], cT[:], rb=[baT], wsin=bsT, wcos=bcT)
            k.ts("dve", nsT[:], sT[:], -1.0, ALU.mult, r=[bsT], w=[bnsT])
            ANG, bANG = k.sb(s2, [128, 16, TS], F32, "ANG")
            SINf, bSINf = k.sb(s2, [128, 16 * TS], F32, "SINf")
            COSf, bCOSf = k.sb(s2, [128, 16 * TS], F32, "COSf")
            for j in range(16):
                k.ts("dve", ANG[:, j, :], tau[:], thn[:, j:j + 1], ALU.mult, r=[btau, bthn], w=[bANG])
            sincos(k, s2, ANG[:].rearrange("p j t -> p (j t)"), 16 * TS, SINf[:], COSf[:], rb=[bANG], wsin=bSINf, wcos=bCOSf)
            SIN3 = SINf[:].rearrange("p (j t) -> p j t", j=16)
            COS3 = COSf[:].rearrange("p (j t) -> p j t", j=16)
            tmp, btmp = k.sb(s2, [128, TS], F32, "tmp")
            for j in range(16):
                k.ts("dve", tmp[:], SIN3[:, j, :], fim[:, j:j + 1], ALU.mult, r=[bSINf, bfim], w=[btmp])
                k.stt(RFre[:, j, :], COS3[:, j, :], fre[:, j:j + 1], tmp[:], ALU.mult, ALU.add, r=[bCOSf, bfre, btmp], w=[bRFre])
                k.ts("dve", tmp[:], SIN3[:, j, :], fre[:, j:j + 1], ALU.mult, r=[bSINf, bfre], w=[btmp])
                k.stt(RFim[:, j, :], COS3[:, j, :], fim[:, j:j + 1], tmp[:], ALU.mult, ALU.subtract, r=[bCOSf, bfim, btmp], w=[bRFim])
            k.copy("dve", COSb[:].rearrange("p j t -> p (j t)"), COSf[:], r=[bCOSf], w=[bCOSb])
            k.copy("dve", SINb[:].rearrange("p j t -> p (j t)"), SINf[:], r=[bSINf], w=[bSINb])
            k.ts("dve", NSINb[:].rearrange("p j t -> p (j t)"), SINf[:], -1.0, ALU.mult, r=[bSINf], w=[bNSINb])
        P.barrier()
        w_in, bw_in = load_w(k, st, di["w_in"][l], 8, INW, name="w_in")
        w_sw, bw_sw = load_w(k, st, di["w_sw"][l], 8, 896, name="w_sw")
        w_glu, bw_glu = load_w(k, st, di["w_glu"][l], 4, 1024, name="w_glu")
        w_bs, bw_bs = load_w(k, st, di["w_bssm"][l], 4, 1024, name="w_bs")
        bre, bbre = load_w(k, st, di["s_bre"][l].rearrange("p j m -> p (j m)"), 1, 2048, name="bre")
        bim, bbim = load_w(k, st, di["s_bim"][l].rearrange("p j m -> p (j m)"), 1, 2048, name="bim")
        cre, bcre = load_w(k, st, di["s_cre"][l].rearrange("p j m -> p (j m)"), 1, 2048, name="cre")
        cim, bcim = load_w(k, st, di["s_cim"][l].rearrange("p j m -> p (j m)"), 1, 2048, name="cim")
        xt = [k.sb(st, [128, TT // 128, D], F32, "xt") for _ in range(2)]
        cosr = [k.sb(st, [128, TT], F32, "cosr") for _ in range(2)]
        sinr = [k.sb(st, [128, TT], F32, "sinr") for _ in range(2)]
        NSB = TT // 128
        junk, bjunk = k.sb(st, [128, D], BF16, "junk")
        ss, bss = k.sb(st, [128, NSB], F32, "ss")
        ms, bms = k.sb(st, [128, NSB], F32, "ms")
        sd, bsd = k.sb(st, [128, NSB], F32, "sd")
        rstd, brstd = k.sb(st, [128, NSB], F32, "rstd")
        hh = [k.sb(st, [128, D], BF16, "h") for _ in range(2)]
        hT, bhT = k.sb(st, [128, 8, TT], BF16, "hT")
        uT2 = [k.sb(st, [128, 4, TT], BF16, "uT") for _ in range(2)]
        qTs, bqTs = k.sb(st, [128, 4, TT], BF16, "qTs")
        kvs = {nm: k.sb(st, [128, TT], BF16, nm) for nm in ("kcT", "vcT", "ksT", "kwT")}
        sgs2 = [k.sb(st, [128, 8, TT], BF16, "sgs") for _ in range(2)]
        sgn, bsgn = k.sb(st, [128, 8, TT], BF16, "sgn")
        sg, bsg = k.sb(st, [128, TT // 128, 24], F32, "sg")
        vsel, bvsel = k.sb(st, [128, NSB, 2, 65], BF16, "vsel")
        vwin, bvwin = k.sb(st, [128, NSB, 2, 65], BF16, "vwin")
        k.memset("pool", vsel[:], 1.0, [bvsel])
        k.memset("pool", vwin[:], 1.0, [bvwin])
        tmps = [k.sb(st, [128, TT], F32, "tmp") for _ in range(12)]
        tmpi = [0]

        def gettmp():
            t = tmps[tmpi[0] % len(tmps)]
            tmpi[0] += 1
            return t
        bsc = [k.sb(st, [128, TT], F32, "bsc") for _ in range(4)]
        wall, _ = k.sb(st, [128, 16, 2, TT], F32, "wall")
        bwall = [Buf("wall") for _ in range(16)]
        cwt = [k.sb(st, [128, 16], F32, "cwt") for _ in range(3)]
        xre, bxre = k.sb(st, [128, 16, TT], BF16, "xre")
        nxim, bnxim = k.sb(st, [128, 16, TT], BF16, "nxim")
        car, bcar = k.sb(st, [128, 2, 16], F32, "car")
        k.memset("dve", car[:], 0.0, [bcar])
        ypre, bypre = k.sb(st, [128, TT], F32, "ypre")
        yT, byT = k.sb(st, [128, 4, TT], BF16, "yT")
        sgz, bsgz = k.sb(st, [128, TT], F32, "sgz")
        zzT, bzzT = k.sb(st, [128, 4, TT], BF16, "zzT")
        gss, bgss = k.sb(st, [128, 8, TT], BF16, "gss")

        def load_tile(i):
            t, b = xt[i % 2]
            k.dma("sp", t[:], x_src[i * TT:(i + 1) * TT, :].rearrange("(s p) d -> p s d", p=128), b)
            k.dma("sp", cosr[i % 2][0][:], di["c_cos"][:, i * TT:(i + 1) * TT], cosr[i % 2][1])
            k.dma("sp", sinr[i % 2][0][:], di["c_sin"][:, i * TT:(i + 1) * TT], sinr[i % 2][1])

        def proj(wt, wb, col0, M=128):
            ps, bp = ring.get()
            for c in range(8):
                k.mm(ps[0:M, 0:TT], wt[:, c, col0:col0 + M], hT[:, c, :], c == 0, c == 7, r=[wb[c], bhT], w=[bp])
            return ps, bp

        def inproj_gen(i):
            uT, buT = uT2[i % 2]
            sgs, bsgs = sgs2[i % 2]
            x_t, bx = xt[i % 2]
            cos_t, bcos = cosr[i % 2]
            sin_t, bsin = sinr[i % 2]
            tok = slice(i * TT, (i + 1) * TT)
            for s_ in range(NSB):
                h_t, bh = hh[s_ % 2]
                k.act(junk[:], x_t[:, s_, :], AF.Square, r=[bx], w=[bjunk, bss], accum=ss[:, s_:s_ + 1])
                k.ts("dve", ms[:, s_:s_ + 1], ss[:, s_:s_ + 1], 1.0 / D, ALU.mult, r=[bss], w=[bms], s2=EPS, op1=ALU.add)
                k.act(sd[:, s_:s_ + 1], ms[:, s_:s_ + 1], AF.Sqrt, r=[bms], w=[bsd])
                k.recip(rstd[:, s_:s_ + 1], sd[:, s_:s_ + 1], r=[bsd], w=[brstd])
                k.stt(h_t[:], x_t[:, s_, :], rstd[:, s_:s_ + 1], gam[:], ALU.mult, ALU.mult, r=[bx, brstd, bgam], w=[bh])
                ps, bp = ring.get()
                pbf = ps[:].bitcast(BF16)
                for c in range(8):
                    k.tr(pbf[:, c * 128:(c + 1) * 128], h_t[:, c * 128:(c + 1) * 128], ident[:], r=[bh, bid], w=[bp])
                k.copy("act", hT[:, :, s_ * 128:(s_ + 1) * 128], pbf.rearrange("p (c t) -> p c t", c=8), r=[bp], w=[bhT])
            for c4 in range(4):
                ps, bp = proj(w_in, bw_in, c4 * 128)
                k.copy("act", uT[:, c4, :], ps[:, 0:TT], r=[bp], w=[buT])
                yield
            def rope(col, swcol, dst, bdst):
                psA, bA = proj(w_in, bw_in, col)
                psB, bB = proj(w_sw, bw_sw, swcol)
                t1_, bt1_ = gettmp()
                t2_, bt2_ = gettmp()
                k.tt("dve", t1_[:], psA[:, 0:TT], cos_t[:], ALU.mult, r=[bA, bcos], w=[bt1_])
                k.tt("dve", t2_[:], psB[:, 0:TT], sin_t[:], ALU.mult, r=[bB, bsin], w=[bt2_])
                k.tt("pool", dst, t1_[:], t2_[:], ALU.add, r=[bt1_, bt2_], w=[bdst])
            for c in range(4):
                rope(512 + c * 128, c * 128, qTs[:, c, :], bqTs)
                yield
            rope(1024, 512, kvs["kcT"][0][:], kvs["kcT"][1])
            yield
            rope(1280, 640, kvs["ksT"][0][:], kvs["ksT"][1])
            yield
            rope(1536, 768, kvs["kwT"][0][:], kvs["kwT"][1])
            yield
            ps, bp = proj(w_in, bw_in, 1152)
            k.copy("act", kvs["vcT"][0][:], ps[:, 0:TT], r=[bp], w=[kvs["vcT"][1]])
            for c in range(8):
                ps, bp = proj(w_in, bw_in, 1816 + c * 128)
                k.act(sgs[:, c, :], ps[:, 0:TT], AF.Sigmoid, r=[bp], w=[bsgs])
                yield
            for c in range(8):
                ps, bp = proj(w_in, bw_in, 2840 + c * 128)
                k.act(sgn[:, c, :], ps[:, 0:TT], AF.Sigmoid, r=[bp], w=[bsgn])
                yield
            for s_ in range(NSB):
                ps, bp = ring.get()
                for c in range(8):
                    k.mm(ps[:, 0:24], hT[:, c, s_ * 128:(s_ + 1) * 128], w_in[:, c, 1792:1816], c == 0, c == 7, r=[bw_in[c], bhT], w=[bp])
                k.act(sg[:, s_, :], ps[:, 0:24], AF.Sigmoid, r=[bp], w=[bsg])
            for s_ in range(NSB):
                for (col, vt, bv) in ((1408, vsel, bvsel), (1664, vwin, bvwin)):
                    ps, bp = ring.get()
                    for c in range(8):
                        k.mm(ps[:, 0:128], hT[:, c, s_ * 128:(s_ + 1) * 128], w_in[:, c, col:col + 128], c == 0, c == 7, r=[bw_in[c], bhT], w=[bp])
                    k.copy("act", vt[:, s_, :, 0:64], ps[:, 0:128].rearrange("p (h d) -> p h d", h=2), r=[bp], w=[bv])
                    yield
            k.dma("sp", sc["qT"].rearrange("(c p) s -> p c s", p=128)[:, :, tok], qTs[:], bqTs)
            for nm in ("kcT", "vcT", "ksT", "kwT"):
                k.dma("sp", sc[nm][:, tok], kvs[nm][0][:], kvs[nm][1])
            k.dma("sp", sc["sgnT"].rearrange("(c p) s -> p c s", p=128)[:, :, tok], sgn[:], bsgn)
            k.dma("sp", sc["sgTok"][tok].rearrange("(s p) c -> p s c", p=128), sg[:], bsg)
            k.dma("sp", sc["vs"][tok].rearrange("(s p) h c -> p s h c", p=128), vsel[:], bvsel)
            k.dma("sp", sc["vw"][tok].rearrange("(s p) h c -> p s h c", p=128), vwin[:], bvwin)
        def s5_gen(i):
            uT, buT = uT2[i % 2]
            sgs, bsgs = sgs2[i % 2]
            tok = slice(i * TT, (i + 1) * TT)
            def stageA(j):
                c4 = j // 4
                psr, bpr = ring.get()
                psi, bpi = ring.get()
                k.mm(psr[:, 0:TT], bre[:, 0, j * 128:(j + 1) * 128], uT[:, c4, :], True, True, r=[bbre[0], buT], w=[bpr])
                k.mm(psi[:, 0:TT], bim[:, 0, j * 128:(j + 1) * 128], uT[:, c4, :], True, True, r=[bbim[0], buT], w=[bpi])
                b_re, bb_re = bsc[(2 * j) % 4]
                b_im, bb_im = bsc[(2 * j + 1) % 4]
                t1_, bt1_ = gettmp()
                t2_, bt2_ = gettmp()
                k.tt("dve", t1_[:], psr[:, 0:TT], RFre[:, j, :], ALU.mult, r=[bpr, bRFre], w=[bt1_])
                k.tt("dve", t2_[:], psi[:, 0:TT], RFim[:, j, :], ALU.mult, r=[bpi, bRFim], w=[bt2_])
                k.tt("pool", b_re[:], t1_[:], t2_[:], ALU.subtract, r=[bt1_, bt2_], w=[bb_re])
                t3_, bt3_ = gettmp()
                t4_, bt4_ = gettmp()
                k.tt("dve", t3_[:], psi[:, 0:TT], RFre[:, j, :], ALU.mult, r=[bpi, bRFre], w=[bt3_])
                k.tt("dve", t4_[:], psr[:, 0:TT], RFim[:, j, :], ALU.mult, r=[bpr, bRFim], w=[bt4_])
                k.tt("pool", b_im[:], t3_[:], t4_[:], ALU.add, r=[bt3_, bt4_], w=[bb_im])

            def stageB(j):
                b_re, bb_re = bsc[(2 * j) % 4]
                b_im, bb_im = bsc[(2 * j + 1) % 4]
                w_re, bw_re = wall[:, j, 0, :], bwall[j]
                w_im, bw_im = wall[:, j, 1, :], bwall[j]
                dj = dec[:, j:j + 1].to_broadcast([128, TT])
                k.scan(w_re, dj, b_re[:], car[:, 0, j:j + 1], r=[bdec, bb_re, bcar], w=[bw_re])
                k.scan(w_im, dj, b_im[:], car[:, 1, j:j + 1], r=[bdec, bb_im, bcar], w=[bw_im])
                t5_, bt5_ = gettmp()
                t6_, bt6_ = gettmp()
                k.tt("dve", t5_[:], w_re, COSb[:, j, :], ALU.mult, r=[bw_re, bCOSb], w=[bt5_])
                k.tt("dve", t6_[:], w_im, SINb[:, j, :], ALU.mult, r=[bw_im, bSINb], w=[bt6_])
                k.tt("pool", xre[:, j, :], t5_[:], t6_[:], ALU.subtract, r=[bt5_, bt6_], w=[bxre])
                t7_, bt7_ = gettmp()
                t8_, bt8_ = gettmp()
                k.tt("pool", t7_[:], w_re, NSINb[:, j, :], ALU.mult, r=[bw_re, bNSINb], w=[bt7_])
                k.tt("pool", t8_[:], w_im, COSb[:, j, :], ALU.mult, r=[bw_im, bCOSb], w=[bt8_])
                k.tt("pool", nxim[:, j, :], t7_[:], t8_[:], ALU.subtract, r=[bt7_, bt8_], w=[bnxim])

            stageA(0)
            for j in range(16):
                if j + 1 < 16:
                    stageA(j + 1)
                stageB(j)
                yield
            wl_re = wall[:, :, 0, TT - 1]
            wl_im = wall[:, :, 1, TT - 1]
            (c0, bc0), (c1, bc1), (c2, bc2) = cwt
            k.tt("dve", c0[:], wl_re, cT[:], ALU.mult, r=bwall + [bcT], w=[bc0])
            k.tt("dve", c1[:], wl_im, nsT[:], ALU.mult, r=bwall + [bnsT], w=[bc1])
            k.tt("dve", car[:, 0, :], c0[:], c1[:], ALU.add, r=[bc0, bc1], w=[bcar])
            k.tt("dve", c2[:], wl_im, cT[:], ALU.mult, r=bwall + [bcT], w=[bc2])
            k.tt("dve", c0[:], wl_re, sT[:], ALU.mult, r=bwall + [bsT], w=[bc0])
            k.tt("dve", car[:, 1, :], c2[:], c0[:], ALU.add, r=[bc2, bc0], w=[bcar])
            for c4 in range(4):
                ps, bp = ring.get()
                for jj in range(4):
                    j = 4 * c4 + jj
                    k.mm(ps[:, 0:TT], cre[:, 0, j * 128:(j + 1) * 128], xre[:, j, :], jj == 0, False, r=[bcre[0], bxre], w=[bp])
                    k.mm(ps[:, 0:TT], cim[:, 0, j * 128:(j + 1) * 128], nxim[:, j, :], False, jj == 3, r=[bcim[0], bnxim], w=[bp])
                k.stt(ypre[:], uT[:, c4, :], dvec[:, c4:c4 + 1], ps[:, 0:TT], ALU.mult, ALU.add, r=[buT, bdvec, bp], w=[bypre])
                k.act(yT[:, c4, :], ypre[:], AF.Gelu_apprx_tanh, r=[bypre], w=[byT])
                yield
            for kk in range(4):
                psg, bpg = ring.get()
                for c4 in range(4):
                    k.mm(psg[:, 0:TT], w_glu[:, c4, (4 + kk) * 128:(5 + kk) * 128], yT[:, c4, :], c4 == 0, c4 == 3, r=[bw_glu[c4], byT], w=[bpg])
                k.act(sgz[:], psg[:, 0:TT], AF.Sigmoid, r=[bpg], w=[bsgz])
                psv, bpv = ring.get()
                for c4 in range(4):
                    k.mm(psv[:, 0:TT], w_glu[:, c4, kk * 128:(kk + 1) * 128], yT[:, c4, :], c4 == 0, c4 == 3, r=[bw_glu[c4], byT], w=[bpv])
                k.tt("dve", zzT[:, kk, :], psv[:, 0:TT], sgz[:], ALU.mult, r=[bpv, bsgz], w=[bzzT])
                yield
            for fc in range(8):
                ps, bp = ring.get()
                for kk in range(4):
                    k.mm(ps[:, 0:TT], w_bs[:, kk, fc * 128:(fc + 1) * 128], zzT[:, kk, :], kk == 0, kk == 3, r=[bw_bs[kk], bzzT], w=[bp])
                k.tt("dve", gss[:, fc, :], ps[:, 0:TT], sgs[:, fc, :], ALU.mult, r=[bp, bsgs], w=[bgss])
                yield
            k.dma("sp", sc["gssT"].rearrange("(c p) s -> p c s", p=128)[:, :, tok], gss[:], bgss)
        def step(g):
            try:
                next(g)
                return True
            except StopIteration:
                return False

        load_tile(0)
        if NTT > 1:
            load_tile(1)
        for _ in inproj_gen(0):
            pass
        for i in range(NTT):
            if i + 2 < NTT:
                load_tile(i + 2)
            gs = s5_gen(i)
            gi = inproj_gen(i + 1) if i + 1 < NTT else iter(())
            alive_s, alive_i = True, True
            n_ = 0
            while alive_s or alive_i:
                if alive_s:
                    alive_s = step(gs)
                for _ in range(1 + (n_ % 2)):
                    if alive_i:
                        alive_i = step(gi)
                n_ += 1
    P.barrier()


def phase2(k, pers, l, S, di, sc, cst):
    P = k.P
    NC = S // 16 - 1
    NCP = S // 16
    NCT = NCP // 128
    KcT, bKcT = k.sb(pers, [128, NCP], BF16, "KcT")
    Vc, bVc = k.sb(pers, [128, NCT, 2, 65], BF16, "Vc")
    k.memset("pool", KcT[:], 0.0, [bKcT])
    k.memset("pool", Vc[:], 1.0, [bVc])
    with ExitStack() as st:
        ring = PsumRing(k, st)
        for typ in ("k", "v"):
            with ExitStack() as s2:
                xT, bxT = k.sb(s2, [128, S], BF16, "cxT")
                k.dma("sp", xT[:], sc["kcT" if typ == "k" else "vcT"], bxT)
                w1, bw1 = k.sb(s2, [128, 32, 256], BF16, "w1")
                bw1b = Buf("w1b")
                src = di["c_w1" + typ][l].rearrange("(l d) c -> d l c", d=64)
                k.dma("pool", w1[0:64], src, bw1)
                k.dma("pool", w1[64:128], src, bw1b)
                pe2, bpe2 = k.sb(s2, [64, 32, 2], BF16, "pe2")
                pe_f, bpe_f = k.sb(s2, [64, 32], F32, "pe_f")
                k.dma("sp", pe_f[:], di["c_pe" + typ][l], bpe_f)
                k.copy("dve", pe2[:, :, 0], pe_f[:], r=[bpe_f], w=[bpe2])
                k.copy("dve", pe2[:, :, 1], pe_f[:], r=[bpe_f], w=[bpe2])
                b1, bb1 = k.sb(s2, [128, 2], F32, "b1")
                k.dma("sp", b1[:], di["c_b1" + typ][l], bb1)
                w2, bw2 = k.sb(s2, [128, 2, 64], BF16, "w2")
                k.dma("pool", w2[:], di["c_w2" + typ][l].rearrange("(c p) d -> p c d", p=128), bw2)
                bias, bbias = k.sb(s2, [128, 2], F32, "bias")
                for cc in range(2):
                    ps, bp = ring.get()
                    for li in range(32):
                        k.mm(ps[:, 0:2], w1[0:64, li, cc * 128:(cc + 1) * 128], pe2[:, li, :], li == 0, li == 31, r=[bw1, bpe2], w=[bp])
                    k.tt("dve", bias[:, cc:cc + 1], ps[:, 0:1], b1[:, cc:cc + 1], ALU.add, r=[bp, bb1], w=[bbias])
                for hk in range(2):
                    hid, bhid = k.sb(s2, [128, 2, NCP], BF16, "hid")
                    k.memset("pool", hid[:], 0.0, [bhid])
                    bw = bw1 if hk == 0 else bw1b
                    for cc in range(2):
                        ps, bp = ring.get()
                        for li in range(32):
                            k.mm(ps[:, 0:NC], w1[hk * 64:(hk + 1) * 64, li, cc * 128:(cc + 1) * 128],
                                 xT[hk * 64:(hk + 1) * 64, li:li + 16 * (NC - 1) + 1:16], li == 0, li == 31, r=[bw, bxT], w=[bp])
                        k.act(hid[:, cc, 0:NC], ps[:, 0:NC], AF.Gelu_apprx_tanh, r=[bp, bbias], w=[bhid], bias=bias[:, cc:cc + 1])
                    if typ == "k":
                        ps, bp = ring.get()
                        for cc in range(2):
                            k.mm(ps[hk * 64:(hk + 1) * 64, 0:NC], w2[:, cc, :], hid[:, cc, 0:NC], cc == 0, cc == 1, r=[bw2, bhid], w=[bp])
                        k.copy("act", KcT[hk * 64:(hk + 1) * 64, 0:NC], ps[hk * 64:(hk + 1) * 64, 0:NC], r=[bp], w=[bKcT])
                    else:
                        for nt in range(NCT):
                            ps, bp = ring.get()
                            for cc in range(2):
                                k.mm(ps[:, 0:64], hid[:, cc, nt * 128:(nt + 1) * 128], w2[:, cc, :], cc == 0, cc == 1, r=[bhid, bw2], w=[bp])
                            k.copy("act", Vc[:, nt, hk, 0:64], ps[:, 0:64], r=[bp], w=[bVc])
            P.barrier()
    if "dbg_kc" in sc:
        k.dma("sp", sc["dbg_kc"].rearrange("h d n -> (h d) n"), KcT[:], bKcT)
        k.dma("sp", sc["dbg_vc"], Vc[:], bVc)
    return (KcT, bKcT), (Vc, bVc)


def phase3(k, l, S, x_src, di, sc, cst, cmp_t):
    P = k.P
    NT = S // 128
    NCP = S // 16
    NCT = NCP // 128
    NB = S // 64
    (KcT, bKcT), (Vc, bVc) = cmp_t
    ident, bid = cst["ident"]
    with ExitStack() as st:
        ring = PsumRing(k, st, 3)
        ringO = PsumRing(k, st, 3)
        ringM = PsumRing(k, st, 2)
        KsM = []
        for hk in range(2):
            t, b = k.sb(st, [128, S], BF16, "KsM")
            b2 = Buf("KsMi")
            k.dma("sp", t[hk * 64:(hk + 1) * 64], sc["ksT"][hk * 64:(hk + 1) * 64, :], b)
            k.dma("pool", t[(1 - hk) * 64:(2 - hk) * 64], di["c_ind"], b2)
            KsM.append((t, b, b2))
        Vs, bVs = k.sb(st, [128, NT, 2, 65], BF16, "Vs")
        k.dma("sp", Vs[:], sc["vs"].rearrange("(n p) h c -> p n h c", p=128), bVs)

        def cload(name, shape, src, dt=BF16):
            t, b = k.sb(st, shape, dt, name)
            k.dma("pool" if dt == BF16 else "sp", t[:], src, b)
            return t, b
        caus, bcaus = cload("caus", [128, 128], di["c_caus"])
        low, blow = cload("low", [128, 128], di["c_low"])
        mgen, bmgen = cload("mgen", [128, 1024], di["c_mgen"])
        mt, bmt = cload("mt", [128, 16, 128], di["c_mt"])
        G, bG = cload("G", [128, 256], di["c_g"], F32)
        wbn, bwbn = load_w(k, st, di["w_bnsa"][l], 4, 1024, name="wbn")
        ident32, bid32 = cload("ident32", [128, 128], di["c_ident"], F32)
        w_out, bw_out = load_w(k, st, di["w_out"][l], 8, 1024, name="w_out")
        QT = [[k.sb(st, [128, 4, 128], BF16, "QT") for _ in range(2)] for _ in range(2)]
        for pb_ in range(2):
            for hk_ in range(2):
                k.memset("pool", QT[pb_][hk_][0][:], 0.0, [QT[pb_][hk_][1]])
        NHALF = max(1, NB // 64)
        Qsel = [[[k.sb(st, [128, 4, 128], BF16, "Qsel") + (Buf("Qselm"),) for _ in range(NHALF)] for _ in range(2)] for _ in range(2)]
        negm_sw, bnegm_sw = k.sb(st, [128, 128], BF16, "negm_sw")
        k.memset("pool", negm_sw[:], 0.0, [bnegm_sw])
        KwT = [k.sb(st, [128, 640], BF16, "KwT") for _ in range(2)]
        Vw = [k.sb(st, [128, 5, 2, 65], BF16, "Vw") for _ in range(2)]
        gtok = [k.sb(st, [128, 24], F32, "gtok") for _ in range(2)]
        sgn = [k.sb(st, [128, 8, 128], BF16, "sgn") for _ in range(2)]
        gss = [k.sb(st, [128, 8, 128], BF16, "gss") for _ in range(2)]
        xin = [k.sb(st, [128, D], F32, "xin") for _ in range(2)]
        eg = [k.sb(st, [128, NCP], F32, "eg") for _ in range(4)]
        den4, bden4 = k.sb(st, [128, 4], F32, "den4")
        rden4, brden4 = k.sb(st, [128, 4], F32, "rden4")
        pg, bpg = k.sb(st, [128, NCP + 8], F32, "pg")
        k.memset("pool", pg[:], 0.0, [bpg])
        blk, bblk = k.sb(st, [128, NB], F32, "blk")
        blk2, bblk2 = k.sb(st, [128, NB], F32, "blk2")
        m8, bm8 = k.sb(st, [128, 16], F32, "m8")
        negm, bnegm = k.sb(st, [128, 128], BF16, "negm")
        k.memset("pool", negm[:], 0.0, [bnegm])
        pTs = [k.sb(st, [128, 512], BF16, "pT") for _ in range(4)]
        pti = [0]
        osbs = [k.sb(st, [65, 512], F32, "osb") for _ in range(4)]
        s4s = [k.sb(st, [128, 4], F32, "s4") for _ in range(4)]
        oq = [k.sb(st, [128, 8, 64], F32, "oq") for _ in range(2)]
        oqb, boqb = k.sb(st, [128, 512], BF16, "oqb")
        oT2, boT2 = k.sb(st, [128, 4, 128], BF16, "oT2")
        mrg, bmrg = k.sb(st, [128, 8, 128], BF16, "mrg")
        xm = [k.sb(st, [128, D], F32, "xm") for _ in range(1)]

        def loads(qb):
            s0 = qb * 128
            pb = qb % 2
            qv = sc["qT"].rearrange("(h d) s -> d h s", d=64)
            for hk in range(2):
                t, b = QT[pb][hk]
                k.dma("sp", t[hk * 64:(hk + 1) * 64], qv[:, hk * 4:(hk + 1) * 4, s0:s0 + 128], b)
                for hf in range(NHALF):
                    if hf * 32 <= qb:
                        t, b, _ = Qsel[pb][hk][hf]
                        k.dma("sp", t[hk * 64:(hk + 1) * 64], qv[:, hk * 4:(hk + 1) * 4, s0:s0 + 128], b)
            lo = max(0, s0 - 512)
            t, b = KwT[pb]
            k.dma("sp", t[:, 640 - (s0 + 128 - lo):640], sc["kwT"][:, lo:s0 + 128], b)
            nw = (s0 + 128 - lo) // 128
            t, b = Vw[pb]
            k.dma("sp", t[:, 5 - nw:5], sc["vw"][lo:s0 + 128].rearrange("(n p) h c -> p n h c", p=128), b)
            t, b = gtok[pb]
            k.dma("sp", t[:], sc["sgTok"][s0:s0 + 128, :], b)
            t, b = sgn[pb]
            k.dma("sp", t[:], sc["sgnT"].rearrange("(c p) s -> p c s", p=128)[:, :, s0:s0 + 128], b)
            t, b = gss[pb]
            k.dma("sp", t[:], sc["gssT"].rearrange("(c p) s -> p c s", p=128)[:, :, s0:s0 + 128], b)
            t, b = xin[pb]
            k.dma("sp", t[:], x_src[s0:s0 + 128, :], b)

        DEPTH = 2
        pipe = []
        delayed = []

        def tick():
            for d in delayed:
                d[0] -= 1
            while delayed and delayed[0][0] <= 0:
                delayed.pop(0)[1]()

        cur_tag = [0]

        def push(score_fn, pv_fn, after=None):
            tok_ = score_fn()
            pipe.append((pv_fn, tok_, after, cur_tag[0]))
            if len(pipe) > DEPTH:
                pv, tk, af, _ = pipe.pop(0)
                pv(tk)
                if af is not None:
                    af()
            tick()

        def flush():
            while pipe:
                pv, tk, af, _ = pipe.pop(0)
                pv(tk)
                if af is not None:
                    af()
            while delayed:
                delayed.pop(0)[1]()

        def attn_tile(Ops, bO, first, last_, KT_ap, bKT, V_ap, bV, Q2, bQ, smask=None, emask=None, after=None):
            def score():
                psS, bS = ring.get()
                nmask = (4 if smask is not None else 0) + (1 if emask is not None else 0)
                rl = (bKT if isinstance(bKT, list) else [bKT]) + (bQ if isinstance(bQ, list) else [bQ])
                k.mm(psS[:, 0:512], KT_ap, Q2, True, nmask == 0, r=rl, w=[bS])
                done = 0
                assert emask is None
                if smask is not None:
                    m_ap, bm = smask
                    for g in range(4):
                        done += 1
                        k.mm(psS[:, g * 128:(g + 1) * 128], ident[:], m_ap, False, done == nmask, r=[bid, bm], w=[bS])
                pT, bpT = pTs[pti[0] % len(pTs)]
                pti[0] += 1
                k.act(pT[:], psS[:, 0:512], AF.Exp, r=[bS], w=[bpT], scale=0.125)
                return (pT, bpT)

            def pv(tk):
                pT, bpT = tk
                k.mm(Ops[0:65, 0:512], V_ap, pT[:], first, last_, r=[bV, bpT], w=[bO])
            push(score, pv, after)

        fin_i = [0]

        def finalize(Ops, bO, qb_, hk_, br_, first_branch):
            tag_ = cur_tag[0]

            def stage_a():
                while sum(1 for d_ in delayed if len(d_) > 3) >= 3:
                    delayed.pop(0)[1]()
                osb, bosb = osbs[fin_i[0] % 4]
                s4, bs4 = s4s[fin_i[0] % 4]
                fin_i[0] += 1
                k.copy("act", osb[:], Ops[0:65, 0:512], r=[bO], w=[bosb])

                def stage_b():
                    pst, bpst = ringM.get()
                    for g in range(4):
                        k.tr(pst[:, g * 65:(g + 1) * 65], osb[:, g * 128:(g + 1) * 128], ident32[0:65, 0:65], r=[bosb, bid32], w=[bpst])
                    p3 = pst[:, 0:260].rearrange("p (g c) -> p g c", c=65)
                    gt_, bgt_ = gtok[qb_ % 2]
                    oq_t, boq = oq[qb_ % 2]
                    k.ts("dve", s4[:], p3[:, :, 64], 1e-20, ALU.max, r=[bpst], w=[bs4])
                    k.recip(s4[:], s4[:], r=[bs4], w=[bs4])
                    c0_ = hk_ * 12 + br_
                    k.tt("dve", s4[:], s4[:], gt_[:, c0_:c0_ + 10:3], ALU.mult, r=[bs4, bgt_], w=[bs4])
                    for g in range(4):
                        h_ = hk_ * 4 + g
                        if first_branch:
                            k.ts("dve", oq_t[:, h_, :], p3[:, g, 0:64], s4[:, g:g + 1], ALU.mult, r=[bpst, bs4], w=[boq])
                        else:
                            k.stt(oq_t[:, h_, :], p3[:, g, 0:64], s4[:, g:g + 1], oq_t[:, h_, :], ALU.mult, ALU.add, r=[bpst, bs4, boq], w=[boq])
                delayed.append([3, stage_b, tag_, "fin"])
            return stage_a

        mtmps = [k.sb(st, [128, 128], F32, "mtmp") for _ in range(2)]

        def epilogue1(qb):
            pb = qb % 2
            oq_t, boq = oq[pb]
            if "dbg_o" in sc:
                k.dma("sp", sc["dbg_o"][qb * 128:(qb + 1) * 128, :], oq_t[:].rearrange("p h d -> p (h d)"), boq)
            k.copy("dve", oqb[:], oq_t[:].rearrange("p h d -> p (h d)"), r=[boq], w=[boqb])
            pst, bpst = ringM.get()
            pbf = pst[:].bitcast(BF16)
            for c in range(4):
                k.tr(pbf[:, c * 128:(c + 1) * 128], oqb[:, c * 128:(c + 1) * 128], ident[:], r=[boqb, bid], w=[bpst])
            k.copy("dve", oT2[:].rearrange("p c q -> p (c q)"), pbf[:, 0:512], r=[bpst], w=[boT2])
            sg_t, bsgn_ = sgn[pb]
            gs_t, bgs_ = gss[pb]
            for half in range(2):
                ps, bp = ringM.get()
                for f4 in range(4):
                    fc = half * 4 + f4
                    for c in range(4):
                        k.mm(ps[:, f4 * 128:(f4 + 1) * 128], wbn[:, c, fc * 128:(fc + 1) * 128], oT2[:, c, :],
                             c == 0, c == 3, r=[bwbn[c], boT2], w=[bp])
                for f4 in range(4):
                    fc = half * 4 + f4
                    mt_, bmt_ = mtmps[fc % 2]
                    k.tt("dve", mt_[:], ps[:, f4 * 128:(f4 + 1) * 128], sg_t[:, fc, :], ALU.mult, r=[bp, bsgn_], w=[bmt_])
                    k.tt("pool", mrg[:, fc, :], mt_[:], gs_t[:, fc, :], ALU.add, r=[bmt_, bgs_], w=[bmrg])
            delayed.append([8, lambda: epilogue2(qb), qb])

        def epilogue2(qb):
            s0 = qb * 128
            pb = qb % 2
            x_t, bx = xin[pb]
            xm_t, bxm = xm[0]
            for half in range(2):
                ps, bp = ringM.get()
                for fc in range(8):
                    k.mm(ps[:, 0:512], mrg[:, fc, :], w_out[:, fc, half * 512:(half + 1) * 512], fc == 0, fc == 7, r=[bmrg, bw_out[fc]], w=[bp])
                k.tt("dve", xm_t[:, half * 512:(half + 1) * 512], ps[:, 0:512], x_t[:, half * 512:(half + 1) * 512], ALU.add, r=[bp, bx], w=[bxm])
            k.dma("sp", sc["xmid"][s0:s0 + 128, :], xm_t[:], bxm)

        def force(tag_max):
            while pipe and pipe[0][3] <= tag_max:
                pv, tk, af, _ = pipe.pop(0)
                pv(tk)
                if af is not None:
                    af()
            progressed = True
            while progressed:
                progressed = False
                for idx_, d_ in enumerate(delayed):
                    if d_[2] <= tag_max:
                        delayed.pop(idx_)
                        d_[1]()
                        progressed = True
                        break

        def chainA(qb, hk):
            pb = qb % 2
            Qt, bQ = QT[pb][hk]
            for g in range(4):
                ps, bp = ringM.get()
                k.mm(ps[:, 0:NCP], Qt[:, g, :], KcT[:, 0:NCP], True, False, r=[bQ, bKcT], w=[bp])
                k.mm(ps[:, 0:NCP], ident[:], mgen[:, 512 - 8 * qb:512 - 8 * qb + NCP], False, True, r=[bid, bmgen], w=[bp])
                k.act(eg[g][0][:], ps[:, 0:NCP], AF.Exp, r=[bp], w=[eg[g][1], bden4], scale=0.125, accum=den4[:, g:g + 1])
            k.ts("dve", rden4[:], den4[:], 1e-20, ALU.max, r=[bden4], w=[brden4])
            k.recip(rden4[:], rden4[:], r=[brden4], w=[brden4])
            k.ts("dve", pg[:, 1:1 + NCP], eg[0][0][:], rden4[:, 0:1], ALU.mult, r=[eg[0][1], brden4], w=[bpg])
            for g in range(1, 4):
                k.stt(pg[:, 1:1 + NCP], eg[g][0][:], rden4[:, g:g + 1], pg[:, 1:1 + NCP], ALU.mult, ALU.add, r=[eg[g][1], brden4, bpg], w=[bpg])
            P.op("dve", lambda e: e.tensor_reduce(out=blk[:], in_=pg[:, 0:NCP].rearrange("p (j o) -> p j o", o=4), axis=AX.X, op=ALU.add), reads=[bpg], writes=[bblk])
            k.tt("dve", blk[:], blk[:], pg[:, 4:4 + 4 * NB:4], ALU.add, r=[bblk, bpg], w=[bblk])
            k.tt("dve", blk[:], blk[:], G[:, 126 - 2 * qb:126 - 2 * qb + NB], ALU.add, r=[bblk, bG], w=[bblk])
            if qb >= 1:
                k.ts("dve", blk[:, 0:1], blk[:, 0:1], 1e4, ALU.add, r=[bblk], w=[bblk])
            P.op("dve", lambda e: e.max(out=m8[:, 0:8], in_=blk[:]), reads=[bblk], writes=[bm8])
            P.op("dve", lambda e: e.match_replace(out=blk2[:], in_to_replace=m8[:, 0:8], in_values=blk[:], imm_value=-3e38), reads=[bblk, bm8], writes=[bblk2])
            P.op("dve", lambda e: e.max(out=m8[:, 8:16], in_=blk2[:]), reads=[bblk2], writes=[bm8])
            nhalf_used = 1 if qb < 32 else NHALF
            need_nat = (hk == 1) or nhalf_used > 1
            need_sw = (hk == 0) or nhalf_used > 1
            if need_nat:
                k.ts("dve", negm[:, 0:NB], blk[:], m8[:, 15:16], ALU.is_lt, r=[bblk, bm8], w=[bnegm], s2=NEG, op1=ALU.mult)
            if need_sw:
                n0 = min(NB, 64)
                k.ts("dve", negm_sw[:, 64:64 + n0], blk[:, 0:n0], m8[:, 15:16], ALU.is_lt, r=[bblk, bm8], w=[bnegm_sw], s2=NEG, op1=ALU.mult)
                if NB > 64:
                    k.ts("dve", negm_sw[:, 0:NB - 64], blk[:, 64:NB], m8[:, 15:16], ALU.is_lt, r=[bblk, bm8], w=[bnegm_sw], s2=NEG, op1=ALU.mult)

        def chainB(qb, hk):
            pb = qb % 2
            os_ = slice((1 - hk) * 64, (2 - hk) * 64)
            nhalf_used = 1 if qb < 32 else NHALF
            for hf in range(nhalf_used):
                use_sw = (hk == 0 and hf == 0) or (hk == 1 and hf == 1)
                src_t, bsrc = (negm_sw, bnegm_sw) if use_sw else (negm, bnegm)
                ps, bp = ringM.get()
                pbf = ps[:].bitcast(BF16)
                k.tr(pbf[:, 0:128], src_t[:], ident[:], r=[bsrc, bid], w=[bp])
                qs_t, _, bqm = Qsel[pb][hk][hf]
                for g in range(4):
                    k.copy("dve", qs_t[os_, g, :], pbf[os_, 0:128], r=[bp], w=[bqm])

        loads(0)
        chainA(0, 0)
        for qb in range(NT):
            s0 = qb * 128
            pb = qb % 2
            for hk in range(2):
                cur_tag[0] = qb
                Qt, bQ = QT[pb][hk]
                Q2 = Qt[:].rearrange("d g q -> d (g q)")
                chainB(qb, hk)
                Ops, bO = ringO.get()
                fa = finalize(Ops, bO, qb, hk, 0, True)
                tiles = [nt for nt in range(NCT) if qb - 16 * nt >= 0]
                for idx, nt in enumerate(tiles):
                    m = qb - 16 * nt
                    sm = (mt[:, m, :], bmt) if m < 16 else None
                    attn_tile(Ops, bO, idx == 0, idx == len(tiles) - 1, KcT[:, nt * 128:(nt + 1) * 128], bKcT,
                              Vc[:, nt, hk, :], bVc, Q2, bQ, smask=sm, after=fa if idx == len(tiles) - 1 else None)
                Ops, bO = ringO.get()
                fa = finalize(Ops, bO, qb, hk, 2, False)
                tiles = [wt for wt in range(5) if s0 - 512 + 128 * wt >= 0]
                kw_full, bkw = KwT[pb]
                kw_t = kw_full
                vw_t, bvw = Vw[pb]
                for idx, wt in enumerate(tiles):
                    sm = (low[:], blow) if wt == 0 else ((caus[:], bcaus) if wt == 4 else None)
                    attn_tile(Ops, bO, idx == 0, idx == len(tiles) - 1, kw_t[:, wt * 128:(wt + 1) * 128], bkw,
                              vw_t[:, wt, hk, :], bvw, Q2, bQ, smask=sm, after=fa if idx == len(tiles) - 1 else None)
                if hk == 1 and qb + 1 < NT:
                    force(qb - 1)
                    loads(qb + 1)
                if hk == 0:
                    chainA(qb, 1)
                elif qb + 1 < NT:
                    chainA(qb + 1, 0)
                Ops, bO = ringO.get()
                fa = finalize(Ops, bO, qb, hk, 1, False)
                for i in range(qb + 1):
                    sm = (caus[:], bcaus) if i == qb else None
                    qs_t, bqs, bqm = Qsel[pb][hk][i // 32]
                    attn_tile(Ops, bO, i == 0, i == qb, KsM[hk][0][:, i * 128:(i + 1) * 128], [KsM[hk][1], KsM[hk][2]],
                              Vs[:, i, hk, :], bVs, qs_t[:].rearrange("d g q -> d (g q)"), [bqs, bqm], smask=sm, after=fa if i == qb else None)
                if hk == 1:
                    delayed_ep = (lambda q_=qb: (lambda: delayed.append([6, lambda: epilogue1(q_), q_])))(qb)
                    pipe[-1] = (pipe[-1][0], pipe[-1][1], (lambda f1=pipe[-1][2], f2=delayed_ep: (f1(), f2())), pipe[-1][3])
        flush()
    P.barrier()


def phase4(k, l, S, di, sc, cst, dst, last):
    P = k.P
    TT = 256
    NTT = S // TT
    NSB = TT // 128
    ident, bid = cst["ident"]
    with ExitStack() as st:
        ring = PsumRing(k, st)
        w_fin, bw_fin = load_w(k, st, di["w_fin"][l], 8, 2 * DFF, name="w_fin")
        w_fo, bw_fo = load_w(k, st, di["w_fout"][l], NFC, D, name="w_fo")
        gam, bgam = k.sb(st, [128, D], F32, "gam")
        k.dma("sp", gam[:], di["g_ffn"][l], bgam)
        cw, bcw = k.sb(st, [128, NFC, 3], F32, "cw")
        k.dma("sp", cw[:], di["f_cw"][l], bcw)
        cb, bcb = k.sb(st, [128, NFC], F32, "cb")
        k.dma("sp", cb[:], di["f_cb"][l], bcb)
        if last:
            gfin, bgfin = k.sb(st, [128, D], F32, "gfin")
            k.dma("sp", gfin[:], di["g_fin"], bgfin)
        halo, bhalo = k.sb(st, [128, NFC, 2], F32, "halo")
        k.memset("pool", halo[:], 0.0, [bhalo])
        xt = [k.sb(st, [128, NSB, D], F32, "xt") for _ in range(2)]
        junk, bjunk = k.sb(st, [128, D], BF16, "junk")
        ss, bss = k.sb(st, [128, 2 * NSB], F32, "ss")
        ms, bms = k.sb(st, [128, 2 * NSB], F32, "ms")
        sd, bsd = k.sb(st, [128, 2 * NSB], F32, "sd")
        rstd, brstd = k.sb(st, [128, 2 * NSB], F32, "rstd")
        hh = [k.sb(st, [128, D], BF16, "h") for _ in range(2)]
        hTs = [k.sb(st, [128, 8, TT], BF16, "hT") for _ in range(2)]
        a_sb = [k.sb(st, [128, TT + 2], F32, "a_sb") for _ in range(2)]
        cv = [k.sb(st, [128, TT], F32, "cv") for _ in range(2)]
        gl = [k.sb(st, [128, TT], F32, "gl") for _ in range(2)]
        actTs = [k.sb(st, [128, NFC, TT], BF16, "actT") for _ in range(2)]
        xo = [k.sb(st, [128, NSB, D], F32, "xo") for _ in range(1)]

        def load_tile(i):
            t, b = xt[i % 2]
            k.dma("sp", t[:], sc["xmid"][i * TT:(i + 1) * TT, :].rearrange("(s p) d -> p s d", p=128), b)

        def rms(x_ap, bx, col, g_t, bg, out_ap, bout):
            k.act(junk[:], x_ap, AF.Square, r=[bx], w=[bjunk, bss], accum=ss[:, col:col + 1])
            k.ts("dve", ms[:, col:col + 1], ss[:, col:col + 1], 1.0 / D, ALU.mult, r=[bss], w=[bms], s2=EPS, op1=ALU.add)
            k.act(sd[:, col:col + 1], ms[:, col:col + 1], AF.Sqrt, r=[bms], w=[bsd])
            k.recip(rstd[:, col:col + 1], sd[:, col:col + 1], r=[bsd], w=[brstd])
            k.stt(out_ap, x_ap, rstd[:, col:col + 1], g_t[:], ALU.mult, ALU.mult, r=[bx, brstd, bg], w=[bout])

        def prep(i):
            x_t, bx = xt[i % 2]
            hT, bhT = hTs[i % 2]
            for s_ in range(NSB):
                h_t, bh = hh[s_ % 2]
                rms(x_t[:, s_, :], bx, s_, gam, bgam, h_t[:], bh)
                ps, bp = ring.get()
                pbf = ps[:].bitcast(BF16)
                for c in range(8):
                    k.tr(pbf[:, c * 128:(c + 1) * 128], h_t[:, c * 128:(c + 1) * 128], ident[:], r=[bh, bid], w=[bp])
                k.copy("act", hT[:, :, s_ * 128:(s_ + 1) * 128], pbf.rearrange("p (c t) -> p c t", c=8), r=[bp], w=[bhT])

        def inproj(i):
            hT, bhT = hTs[i % 2]
            actT, bactT = actTs[i % 2]
            for fc in range(NFC):
                psa, bpa = ring.get()
                for c in range(8):
                    k.mm(psa[:, 0:TT], w_fin[:, c, fc * 128:(fc + 1) * 128], hT[:, c, :], c == 0, c == 7, r=[bw_fin[c], bhT], w=[bpa])
                psb, bpb = ring.get()
                for c in range(8):
                    k.mm(psb[:, 0:TT], w_fin[:, c, DFF + fc * 128:DFF + (fc + 1) * 128], hT[:, c, :], c == 0, c == 7, r=[bw_fin[c], bhT], w=[bpb])
                a_t, ba = a_sb[fc % 2]
                c_t, bc = cv[fc % 2]
                g_t, bg = gl[fc % 2]
                k.copy("pool", a_t[:, 0:2], halo[:, fc, :], r=[bhalo], w=[ba])
                k.copy("act", a_t[:, 2:2 + TT], psa[:, 0:TT], r=[bpa], w=[ba])
                k.copy("pool", halo[:, fc, :], a_t[:, TT:TT + 2], r=[ba], w=[bhalo])
                k.act(c_t[:], psa[:, 0:TT], AF.Identity, r=[bpa, bcw, bcb], w=[bc], scale=cw[:, fc, 2:3], bias=cb[:, fc:fc + 1])
                k.stt(c_t[:], a_t[:, 1:1 + TT], cw[:, fc, 1:2], c_t[:], ALU.mult, ALU.add, r=[ba, bcw, bc], w=[bc])
                k.stt(c_t[:], a_t[:, 0:TT], cw[:, fc, 0:1], c_t[:], ALU.mult, ALU.add, r=[ba, bcw, bc], w=[bc])
                k.act(g_t[:], c_t[:], AF.Gelu_apprx_tanh, r=[bc], w=[bg])
                k.tt("dve", actT[:, fc, :], psb[:, 0:TT], g_t[:], ALU.mult, r=[bpb, bg], w=[bactT])

        def outproj(i):
            x_t, bx = xt[i % 2]
            actT, bactT = actTs[i % 2]
            xo_t, bxo = xo[0]
            for s_ in range(NSB):
                for half in range(2):
                    ps, bp = ring.get()
                    for fc in range(NFC):
                        k.mm(ps[:, 0:512], actT[:, fc, s_ * 128:(s_ + 1) * 128], w_fo[:, fc, half * 512:(half + 1) * 512], fc == 0, fc == NFC - 1,
                             r=[bactT, bw_fo[fc]], w=[bp])
                    k.tt("dve", xo_t[:, s_, half * 512:(half + 1) * 512], ps[:, 0:512], x_t[:, s_, half * 512:(half + 1) * 512], ALU.add, r=[bp, bx], w=[bxo])
            if last:
                for s_ in range(NSB):
                    rms(xo_t[:, s_, :], bxo, NSB + s_, gfin, bgfin, xo_t[:, s_, :], bxo)
            k.dma("sp", dst[i * TT:(i + 1) * TT, :].rearrange("(s p) d -> p s d", p=128), xo_t[:], bxo)

        load_tile(0)
        if NTT > 1:
            load_tile(1)
        prep(0)
        for i in range(NTT):
            inproj(i)
            if i + 1 < NTT:
                prep(i + 1)
            outproj(i)
            if i + 2 < NTT:
                load_tile(i + 2)
    P.barrier()


INPUT_SHAPES = None


def build(S, L, TS=128, dbg=False, phases=("p1", "p2", "p3", "p4")):
    nc = bass.Bass("TRN2", target_bir_lowering=False)
    NT = S // 128
    di = {}

    def din(name, shape):
        di[name] = nc.dram_tensor(name, list(shape), F32, kind="ExternalInput").ap()
    din("x", [S, D])
    for nm, shp in (("g_mix", [L, 128, D]), ("g_ffn", [L, 128, D]), ("g_fin", [128, D]),
                    ("w_in", [L, D, INW]), ("w_sw", [L, D, 896]),
                    ("s_are", [L, 128, 16]), ("s_aim", [L, 128, 16]), ("s_ldt", [L, 128, 16]),
                    ("s_bre", [L, 128, 16, 128]), ("s_bim", [L, 128, 16, 128]),
                    ("s_cre", [L, 128, 16, 128]), ("s_cim", [L, 128, 16, 128]), ("s_d", [L, 128, 4]),
                    ("w_glu", [L, 512, 1024]), ("w_bssm", [L, 512, 1024]), ("w_bnsa", [L, 512, 1024]),
                    ("w_out", [L, D, D]),
                    ("c_pek", [L, 64, 32]), ("c_w1k", [L, 2048, 256]), ("c_b1k", [L, 128, 2]), ("c_w2k", [L, 256, 64]),
                    ("c_pev", [L, 64, 32]), ("c_w1v", [L, 2048, 256]), ("c_b1v", [L, 128, 2]), ("c_w2v", [L, 256, 64]),
                    ("w_fin", [L, D, 2 * DFF]), ("w_fout", [L, DFF, D]), ("f_cw", [L, 128, NFC, 3]), ("f_cb", [L, 128, NFC]),
                    ("c_cos", [128, S]), ("c_sin", [128, S]), ("c_ident", [128, 128]), ("c_tau", [128, TS]),
                    ("c_caus", [128, 128]), ("c_low", [128, 128]), ("c_mgen", [128, 1024]), ("c_mt", [128, 16, 128]),
                    ("c_g", [128, 256]), ("c_ind", [64, S])):
        din(nm, shp)
    out = nc.dram_tensor("out", [S, D], F32, kind="ExternalOutput").ap()
    skind = "ExternalOutput" if dbg else "Internal"
    sc = {}

    def scr(name, shape, dt):
        sc[name] = nc.dram_tensor(name, list(shape), dt, kind=skind).ap()
    scr("qT", [512, S], BF16)
    for nm in ("kcT", "vcT", "ksT", "kwT"):
        scr(nm, [128, S], BF16)
    scr("vs", [S, 2, 65], BF16)
    scr("vw", [S, 2, 65], BF16)
    scr("sgTok", [S, 24], F32)
    scr("sgnT", [1024, S], BF16)
    scr("gssT", [1024, S], BF16)
    scr("xmid", [S, D], F32)
    if dbg:
        scr("dbg_o", [S, 512], F32)
        scr("dbg_kc", [2, 64, S // 16], BF16)
        scr("dbg_vc", [128, S // 2048, 2, 65], BF16)
    scr("x1", [S, D], F32)
    with ExitStack() as st:
        P = Prog(nc)
        k = K(nc, P)
        cst = {}
        ident, bid = k.sb(st, [128, 128], BF16, "ident")
        k.dma("pool", ident[:], di["c_ident"], bid)
        cst["ident"] = (ident, bid)
        x_src = di["x"]
        for l in range(L):
            last = l == L - 1
            if "p1" in phases:
                phase1(k, l, S, TS, x_src, di, sc, cst)
            if "p2" in phases:
                pers = ExitStack()
                cmp_t = phase2(k, pers, l, S, di, sc, cst)
            if "p3" in phases:
                phase3(k, l, S, x_src, di, sc, cst, cmp_t)
            if "p2" in phases:
                pers.close()
                P.barrier()
            if "p4" in phases:
                phase4(k, l, S, di, sc, cst, out if last else sc["x1"], last)
            x_src = sc["x1"]
        P.barrier()
        P.emit(st)
    return nc


_NC_CACHE = {}


def kernel(**inputs):
    S, L, NCORES = 8192, 2, 8
    inp = {k_: np.asarray(v) for k_, v in inputs.items()}
    hl = host_layout(inp, L)
    hc = host_consts(S, 128)
    common = {}
    common.update(hl)
    common.update(hc)
    common = {k_: np.ascontiguousarray(v, dtype=np.float32) for k_, v in common.items()}
    if "nc" not in _NC_CACHE:
        _NC_CACHE["nc"] = build(S, L)
    nc = _NC_CACHE["nc"]
    x = np.asarray(inp["x"], dtype=np.float32)
    in_maps = []
    for b in range(NCORES):
        m = dict(common)
        m["x"] = np.ascontiguousarray(x[b])
        in_maps.append(m)
    res = run_bass_kernel_spmd(nc, in_maps, core_ids=list(range(NCORES)))
    return np.stack([np.asarray(r["out"], dtype=np.float32) for r in res.results], axis=0)
```

```python
from contextlib import ExitStack
import numpy as np
import ml_dtypes
import concourse.bass as bass
import concourse.mybir as mybir
from concourse.bass_utils import run_bass_kernel_spmd

F32 = mybir.dt.float32
BF16 = mybir.dt.bfloat16
I32 = mybir.dt.int32
ALU = mybir.AluOpType
AF = mybir.ActivationFunctionType
AX = mybir.AxisListType

D = 1024
DFF = 2816
NFC = DFF // 128
INW = 3864
EPS = 1e-6
NEG = -30000.0
TWO_PI = float(2 * np.pi)
SIN_SCALE = TWO_PI * 0.999999


class Buf:
    __slots__ = ("name", "w", "rs", "sem", "cnt", "slot", "base", "uid")

    def __init__(self, name="b"):
        self.name = name
        self.w = None
        self.rs = {}
        self.sem = None
        self.cnt = 0
        self.slot = None
        self.base = 0
        self.uid = None


class Op:
    __slots__ = ("eng", "fn", "deps", "key", "val", "signal", "sigval", "dma", "slot", "semval")


ENGS = ("pe", "act", "dve", "pool", "sp")


class Prog:
    def __init__(self, nc):
        self.nc = nc
        self.ops = {e: [] for e in ENGS}
        self.seen = {e: {} for e in ENGS}
        self.dma_bufs = []
        self.last = {}
        self.slot_base = []
        self.free_slots = []
        self.live = []
        self.uid = 0

    def _get_slot(self, buf):
        if self.free_slots:
            sl = self.free_slots.pop()
        else:
            sl = len(self.slot_base)
            self.slot_base.append(0)
        self.uid += 1
        buf.sem = True
        buf.slot = sl
        buf.base = self.slot_base[sl]
        buf.cnt = 0
        buf.uid = self.uid
        self.live.append(buf)

    def barrier(self):
        lasts = list(self.last.values())
        self._barrier_ops(lasts)
        for b in self.live:
            self.slot_base[b.slot] = b.base + b.cnt
            self.free_slots.append(b.slot)
            b.sem = None
        self.live = []
        self.last = {kk: v for kk, v in self.last.items() if not isinstance(kk, tuple)}

    def _barrier_ops(self, lasts):
        for e in ENGS:
            o = Op()
            o.eng = e
            o.fn = None
            o.deps = []
            o.signal = False
            o.sigval = None
            o.dma = None
            o.key = e
            o.val = len(self.ops[e])
            for d in lasts:
                if d.key == e:
                    continue
                if self.seen[e].get(d.key, -1) >= d.val:
                    continue
                self.seen[e][d.key] = d.val
                o.deps.append(d)
            self.ops[e].append(o)

    def _dep(self, eng, d, deps, same_ok):
        if d is None:
            return
        key = d.key
        if key == eng:
            if eng == "pe" or same_ok:
                return
        if self.seen[eng].get(key, -1) >= d.val:
            return
        self.seen[eng][key] = d.val
        deps.append(d)

    def op(self, eng, fn, reads=(), writes=(), dma=None):
        o = Op()
        o.eng = eng
        o.fn = fn
        o.deps = []
        o.signal = False
        o.sigval = None
        o.dma = dma
        writes = [b for b in writes if b is not None]
        reads = [b for b in reads if b is not None]
        o.slot = None
        o.semval = None
        if dma is not None:
            if dma.sem is None:
                self._get_slot(dma)
            dma.cnt += 1
            o.key = ("dma", dma.uid)
            o.val = dma.cnt
            o.slot = dma.slot
            o.semval = 16 * (dma.base + dma.cnt)
            if dma not in writes:
                writes.append(dma)
            reads = [b for b in reads if b is not dma]
        else:
            o.key = eng
            o.val = len(self.ops[eng])
        for b in reads:
            self._dep(eng, b.w, o.deps, False)
        for b in writes:
            self._dep(eng, b.w, o.deps, True)
            for r in b.rs.values():
                self._dep(eng, r, o.deps, True)
        for b in reads:
            b.rs[o.key] = o
        for b in writes:
            b.w = o
            b.rs = {}
        self.ops[eng].append(o)
        self.last[o.key] = o
        return o

    def emit(self, stack):
        nc = self.nc
        for e in ENGS:
            for o in self.ops[e]:
                for d in o.deps:
                    if d.dma is None:
                        d.signal = True
        for e in ENGS:
            c = 0
            for o in self.ops[e]:
                if o.dma is None and o.signal:
                    c += 1
                    o.sigval = c
        esem = {}
        for e in ("pe", "act", "dve", "pool"):
            esem[e] = stack.enter_context(nc.semaphore("s_" + e))
        dsem = [stack.enter_context(nc.semaphore("d%d" % i)) for i in range(len(self.slot_base))]
        block = stack.enter_context(nc.Block())
        prog = self

        def run(name, eng):
            for o in prog.ops[name]:
                for d in o.deps:
                    if d.dma is not None:
                        eng.wait_ge(dsem[d.slot], d.semval)
                    else:
                        eng.wait_ge(esem[d.key], d.sigval)
                if o.fn is None:
                    continue
                ins = o.fn(eng)
                if o.dma is not None:
                    ins.then_inc(dsem[o.slot], 16)
                elif o.signal:
                    ins.then_inc(esem[name], 1)

        @block.sync
        def _(eng):
            run("sp", eng)

        @block.scalar
        def _(eng):
            run("act", eng)

        @block.vector
        def _(eng):
            run("dve", eng)

        @block.gpsimd
        def _(eng):
            run("pool", eng)

        @block.tensor
        def _(eng):
            run("pe", eng)


class K:
    def __init__(self, nc, P):
        self.nc = nc
        self.P = P
        self.n = 0

    def name(self, s):
        self.n += 1
        return "%s_%d" % (s, self.n)

    def sb(self, st, shape, dt=F32, name="t"):
        t = st.enter_context(self.nc.sbuf_tensor(self.name(name), list(shape), dt))
        return t, Buf(name)

    def dma(self, eng, out, in_, buf, reads=(), writes=()):
        self.P.op(eng, lambda e: e.dma_start(out=out, in_=in_), reads=reads, writes=writes, dma=buf)

    def mm(self, out, lhsT, rhs, start, stop, r, w):
        self.P.op("pe", lambda e: e.matmul(out, lhsT=lhsT, rhs=rhs, start=start, stop=stop), reads=r, writes=w)

    def tr(self, out, in_, ident, r, w):
        self.P.op("pe", lambda e: e.transpose(out=out, in_=in_, identity=ident), reads=r, writes=w)

    def act(self, out, in_, func, r, w, bias=None, scale=None, accum=None):
        kw = {}
        if bias is not None:
            kw["bias"] = bias
        if scale is not None:
            kw["scale"] = scale
        if accum is not None:
            kw["accum_out"] = accum
        self.P.op("act", lambda e: e.activation(out=out, in_=in_, func=func, **kw), reads=r, writes=w)

    def tt(self, eng, out, in0, in1, op, r, w):
        self.P.op(eng, lambda e: e.tensor_tensor(out=out, in0=in0, in1=in1, op=op), reads=r, writes=w)

    def ts(self, eng, out, in0, s1, op0, r, w, s2=None, op1=None):
        if op1 is None:
            self.P.op(eng, lambda e: e.tensor_scalar(out=out, in0=in0, scalar1=s1, scalar2=None, op0=op0), reads=r, writes=w)
        else:
            self.P.op(eng, lambda e: e.tensor_scalar(out=out, in0=in0, scalar1=s1, scalar2=s2, op0=op0, op1=op1), reads=r, writes=w)

    def stt(self, out, in0, scalar, in1, op0, op1, r, w):
        self.P.op("dve", lambda e: e.scalar_tensor_tensor(out=out, in0=in0, scalar=scalar, in1=in1, op0=op0, op1=op1), reads=r, writes=w)

    def copy(self, eng, out, in_, r, w):
        if eng == "act":
            self.P.op("act", lambda e: e.activation(out=out, in_=in_, func=AF.Copy), reads=r, writes=w)
        else:
            self.P.op(eng, lambda e: e.tensor_copy(out=out, in_=in_), reads=r, writes=w)

    def memset(self, eng, ap, val, w):
        self.P.op(eng, lambda e: e.memset(ap, val), writes=w)

    def scan(self, out, d0, d1, init, r, w):
        self.P.op("dve", lambda e: e.tensor_tensor_scan(out=out, data0=d0, data1=d1, initial=init, op0=ALU.mult, op1=ALU.add), reads=r, writes=w)

    def recip(self, out, in_, r, w):
        self.P.op("dve", lambda e: e.reciprocal(out=out, in_=in_), reads=r, writes=w)


class PsumRing:
    def __init__(self, k, st, n=8):
        self.banks = []
        for i in range(n):
            t = st.enter_context(k.nc.psum_tensor(k.name("ps"), [128, 512], F32))
            self.banks.append((t, Buf("ps%d" % i)))
        self.i = 0

    def get(self):
        t, b = self.banks[self.i % len(self.banks)]
        self.i += 1
        return t, b


def _swap_halves(w):
    sh = w.shape
    w4 = w.reshape(sh[:-1] + (sh[-1] // 64, 2, 32))
    return np.ascontiguousarray(w4[..., ::-1, :]).reshape(sh)


def host_consts(S, TS):
    c = {}
    inv = (10000.0 ** (-np.arange(0, 64, 2, dtype=np.float32) / np.float32(64))).astype(np.float32)
    ang = (np.arange(S, dtype=np.float32)[:, None] * inv[None, :]).astype(np.float32)
    cs, sn = np.cos(ang).astype(np.float32), np.sin(ang).astype(np.float32)
    cosT = np.concatenate([cs.T, cs.T], 0)
    sinT = np.concatenate([-sn.T, sn.T], 0)
    c["c_cos"] = np.ascontiguousarray(np.concatenate([cosT, cosT], 0))
    c["c_sin"] = np.ascontiguousarray(np.concatenate([sinT, sinT], 0))
    c["c_ident"] = np.eye(128, dtype=np.float32)
    c["c_tau"] = np.ascontiguousarray(np.broadcast_to(np.arange(TS, dtype=np.float32)[None, :], (128, TS)))
    k = np.arange(128)[:, None]
    q = np.arange(128)[None, :]
    c["c_caus"] = np.where(k <= q, 0.0, NEG).astype(np.float32)
    c["c_low"] = np.where(k > q, 0.0, NEG).astype(np.float32)
    cc = np.arange(1024)[None, :]
    qi = np.arange(128)[:, None]
    c["c_mgen"] = np.where(16 * (cc - 512) + 31 <= qi, 0.0, NEG).astype(np.float32)
    m = np.arange(16)[None, :, None]
    ni = np.arange(128)[:, None, None]
    qq = np.arange(128)[None, None, :]
    c["c_mt"] = np.where(16 * ni + 31 <= 128 * m + qq, 0.0, NEG).astype(np.float32)
    rel = np.arange(256)[None, :] - 126
    cur = (np.arange(128)[:, None] >= 64).astype(np.int64)
    g = np.where(rel > cur, -1e30, 0.0) + np.where((rel == cur) | (rel == cur - 1), 1e4, 0.0)
    c["c_g"] = g.astype(np.float32)
    NT = S // 128
    j = np.arange(128)[:, None, None]
    i = np.arange(NT)[None, :, None]
    kk = np.arange(128)[None, None, :]
    c["c_e"] = (j == 2 * i + (kk >= 64)).astype(np.float32)
    r_ = np.arange(64)[:, None]
    cidx = np.arange(S)[None, :]
    c["c_ind"] = (((cidx // 64) % 64) == r_).astype(np.float32)
    return c


def host_layout(inp, L):
    o = {}
    f = np.float32
    o["g_mix"] = np.ascontiguousarray(np.broadcast_to(inp["norm_mix"][:, None, :], (L, 128, D))).astype(f)
    o["g_ffn"] = np.ascontiguousarray(np.broadcast_to(inp["norm_ffn"][:, None, :], (L, 128, D))).astype(f)
    o["g_fin"] = np.ascontiguousarray(np.broadcast_to(inp["norm_final"][None, :], (128, D))).astype(f)
    w_in = inp["w_in"]
    o["w_in"] = w_in
    sw = np.concatenate([_swap_halves(w_in[:, :, 512:1024]), _swap_halves(w_in[:, :, 1024:1152]),
                         _swap_halves(w_in[:, :, 1280:1408]), _swap_halves(w_in[:, :, 1536:1664])], axis=-1)
    o["w_sw"] = np.ascontiguousarray(sw)

    def pair(a):
        return np.ascontiguousarray(a.reshape(L, 16, 2, 64).transpose(0, 2, 3, 1).reshape(L, 128, 16))
    o["s_are"] = pair(inp["ssm_a_re"])
    o["s_aim"] = pair(inp["ssm_a_im"])
    o["s_ldt"] = pair(np.broadcast_to(inp["ssm_log_dt"][:, :, None], (L, 32, 64)))
    for nm, src in (("s_bre", "ssm_b_re"), ("s_bim", "ssm_b_im")):
        b = inp[src].reshape(L, 16, 2, 64, 16)
        pad = np.zeros((L, 8, 16, 16, 2, 64), f)
        for j in range(16):
            for gl in range(2):
                pad[:, 2 * (j % 4) + gl, :, j, gl, :] = b[:, j, gl].transpose(0, 2, 1)
        o[nm] = pad.reshape(L, 128, 16, 128)
    for nm, src in (("s_cre", "ssm_c_re"), ("s_cim", "ssm_c_im")):
        cmat = inp[src].reshape(L, 16, 2, 16, 64)
        pad = np.zeros((L, 2, 64, 16, 8, 16), f)
        for j in range(16):
            for gl in range(2):
                pad[:, gl, :, j, 2 * (j % 4) + gl, :] = cmat[:, j, gl].transpose(0, 2, 1)
        o[nm] = pad.reshape(L, 128, 16, 128)
    o["s_d"] = np.ascontiguousarray(inp["ssm_d"].reshape(L, 4, 128).transpose(0, 2, 1))
    o["w_glu"] = inp["ssm_w_glu"]
    o["w_bssm"] = inp["w_branch_ssm"]
    o["w_bnsa"] = inp["w_branch_nsa"]
    o["w_out"] = inp["w_out"]
    for t in ("k", "v"):
        o["c_pe" + t] = np.ascontiguousarray(inp["cmp_pe_" + t].transpose(0, 2, 1))
        o["c_w1" + t] = inp["cmp_w1_" + t]
        o["c_b1" + t] = np.ascontiguousarray(inp["cmp_b1_" + t].reshape(L, 2, 128).transpose(0, 2, 1))
        o["c_w2" + t] = inp["cmp_w2_" + t]
    o["w_fin"] = inp["w_ffn_in"]
    o["w_fout"] = inp["w_ffn_out"]
    o["f_cw"] = np.ascontiguousarray(inp["ffn_conv_w"].reshape(L, 3, NFC, 128).transpose(0, 3, 2, 1))
    o["f_cb"] = np.ascontiguousarray(inp["ffn_conv_b"].reshape(L, NFC, 128).transpose(0, 2, 1))
    return o


def load_w(k, st, src2d, nch, ncols, prow=128, eng="pool", name="w"):
    t, _ = k.sb(st, [prow, nch, ncols], BF16, name)
    bufs = []
    for c in range(nch):
        b = Buf(name)
        k.dma(eng, t[:, c, :], src2d[c * prow:(c + 1) * prow, :], b)
        bufs.append(b)
    return t, bufs


def sincos(k, st, arg, n, out_sin=None, out_cos=None, rb=(), wsin=None, wcos=None):
    for (dst, off, wb) in ((out_sin, 0.0, wsin), (out_cos, 0.25, wcos)):
        if dst is None:
            continue
        a2, ba2 = k.sb(st, [128, n], F32, "sc_a")
        ti, bti = k.sb(st, [128, n], I32, "sc_i")
        tf, btf = k.sb(st, [128, n], F32, "sc_f")
        k.ts("dve", a2[:], arg, off, ALU.add, r=list(rb), w=[ba2])
        k.copy("dve", ti[:], a2[:], r=[ba2], w=[bti])
        k.copy("dve", tf[:], ti[:], r=[bti], w=[btf])
        k.tt("dve", a2[:], a2[:], tf[:], ALU.subtract, r=[ba2, btf], w=[ba2])
        k.act(dst, a2[:], AF.Sin, r=[ba2], w=[wb], scale=SIN_SCALE)


def phase1(k, l, S, TS, x_src, di, sc, cst):
    P = k.P
    TT = TS
    NTT = S // TT
    with ExitStack() as st:
        ring = PsumRing(k, st)
        ident, bid = cst["ident"]
        gam, bgam = k.sb(st, [128, D], F32, "gam")
        k.dma("sp", gam[:], di["g_mix"][l], bgam)
        dvec, bdvec = k.sb(st, [128, 4], F32, "dvec")
        k.dma("sp", dvec[:], di["s_d"][l], bdvec)
        RFre, bRFre = k.sb(st, [128, 16, TS], F32, "RFre")
        RFim, bRFim = k.sb(st, [128, 16, TS], F32, "RFim")
        COSb, bCOSb = k.sb(st, [128, 16, TS], BF16, "COSb")
        SINb, bSINb = k.sb(st, [128, 16, TS], BF16, "SINb")
        NSINb, bNSINb = k.sb(st, [128, 16, TS], BF16, "NSINb")
        dec, bdec = k.sb(st, [128, 16], F32, "dec")
        cT, bcT = k.sb(st, [128, 16], F32, "cT")
        sT, bsT = k.sb(st, [128, 16], F32, "sT")
        nsT, bnsT = k.sb(st, [128, 16], F32, "nsT")
        with ExitStack() as s2:
            are, bare = k.sb(s2, [128, 16], F32, "are")
            aim, baim = k.sb(s2, [128, 16], F32, "aim")
            ldt, bldt = k.sb(s2, [128, 16], F32, "ldt")
            tau, btau = k.sb(s2, [128, TS], F32, "tau")
            k.dma("sp", are[:], di["s_are"][l], bare)
            k.dma("sp", aim[:], di["s_aim"][l], baim)
            k.dma("sp", ldt[:], di["s_ldt"][l], bldt)
            k.dma("sp", tau[:], di["c_tau"], btau)
            dt_, bdt = k.sb(s2, [128, 16], F32, "dt")
            k.act(dt_[:], ldt[:], AF.Exp, r=[bldt], w=[bdt])
            rho, brho = k.sb(s2, [128, 16], F32, "rho")
            thn, bthn = k.sb(s2, [128, 16], F32, "thn")
            k.tt("dve", rho[:], are[:], dt_[:], ALU.mult, r=[bare, bdt], w=[brho])
            k.tt("dve", thn[:], aim[:], dt_[:], ALU.mult, r=[baim, bdt], w=[bthn])
            k.ts("dve", thn[:], thn[:], 1.0 / TWO_PI, ALU.mult, r=[bthn], w=[bthn])
            k.act(dec[:], rho[:], AF.Exp, r=[brho], w=[bdec])
            s1, bs1 = k.sb(s2, [128, 16], F32, "s1")
            c1, bc1 = k.sb(s2, [128, 16], F32, "c1")
            sincos(k, s2, thn[:], 16, s1[:], c1[:], rb=[bthn], wsin=bs1, wcos=bc1)
            abre, babre = k.sb(s2, [128, 16], F32, "abre")
            abim, babim = k.sb(s2, [128, 16], F32, "abim")
            k.tt("dve", abre[:], dec[:], c1[:], ALU.mult, r=[bdec, bc1], w=[babre])
            k.ts("dve", abre[:], abre[:], -1.0, ALU.add, r=[babre], w=[babre])
            k.tt("dve", abim[:], dec[:], s1[:], ALU.mult, r=[bdec, bs1], w=[babim])
            den, bden = k.sb(s2, [128, 16], F32, "den")
            t0, bt0 = k.sb(s2, [128, 16], F32, "t0")
            k.tt("dve", den[:], are[:], are[:], ALU.mult, r=[bare], w=[bden])
            k.tt("dve", t0[:], aim[:], aim[:], ALU.mult, r=[baim], w=[bt0])
            k.tt("dve", den[:], den[:], t0[:], ALU.add, r=[bden, bt0], w=[bden])
            k.recip(den[:], den[:], r=[bden], w=[bden])
            fre, bfre = k.sb(s2, [128, 16], F32, "fre")
            fim, bfim = k.sb(s2, [128, 16], F32, "fim")
            t1, bt1 = k.sb(s2, [128, 16], F32, "t1")
            k.tt("dve", fre[:], abre[:], are[:], ALU.mult, r=[babre, bare], w=[bfre])
            k.tt("dve", t1[:], abim[:], aim[:], ALU.mult, r=[babim, baim], w=[bt1])
            k.tt("dve", fre[:], fre[:], t1[:], ALU.add, r=[bfre, bt1], w=[bfre])
            k.tt("dve", fre[:], fre[:], den[:], ALU.mult, r=[bfre, bden], w=[bfre])
            k.tt("dve", fim[:], abim[:], are[:], ALU.mult, r=[babim, bare], w=[bfim])
            k.tt("dve", t1[:], abre[:], aim[:], ALU.mult, r=[babre, baim], w=[bt1])
            k.tt("dve", fim[:], fim[:], t1[:], ALU.subtract, r=[bfim, bt1], w=[bfim])
            k.tt("dve", fim[:], fim[:], den[:], ALU.mult, r=[bfim, bden], w=[bfim])
            aT, baT = k.sb(s2, [128, 16], F32, "aT")
            k.ts("dve", aT[:], thn[:], float(TS), ALU.mult, r=[bthn], w=[baT])
            sincos(k, s2, aT[:], 16, sT[:], cT[:], rb=[baT], wsin=bsT, wcos=bcT)
            k.ts("dve", nsT[:], sT[:], -1.0, ALU.mult, r=[bsT], w=[bnsT])
            ANG, bANG = k.sb(s2, [128, 16, TS], F32, "ANG")
            SINf, bSINf = k.sb(s2, [128, 16 * TS], F32, "SINf")
            COSf, bCOSf = k.sb(s2, [128, 16 * TS], F32, "COSf")
            for j in range(16):
                k.ts("dve", ANG[:, j, :], tau[:], thn[:, j:j + 1], ALU.mult, r=[btau, bthn], w=[bANG])
            sincos(k, s2, ANG[:].rearrange("p j t -> p (j t)"), 16 * TS, SINf[:], COSf[:], rb=[bANG], wsin=bSINf, wcos=bCOSf)
            SIN3 = SINf[:].rearrange("p (j t) -> p j t", j=16)
            COS3 = COSf[:].rearrange("p (j t) -> p j t", j=16)
            tmp, btmp = k.sb(s2, [128, TS], F32, "tmp")
            for j in range(16):
                k.ts("dve", tmp[:], SIN3[:, j, :], fim[:, j:j + 1], ALU.mult, r=[bSINf, bfim], w=[btmp])
                k.stt(RFre[:, j, :], COS3[:, j, :], fre[:, j:j + 1], tmp[:], ALU.mult, ALU.add, r=[bCOSf, bfre, btmp], w=[bRFre])
                k.ts("dve", tmp[:], SIN3[:, j, :], fre[:, j:j + 1], ALU.mult, r=[bSINf, bfre], w=[btmp])
                k.stt(RFim[:, j, :], COS3[:, j, :], fim[:, j:j + 1], tmp[:], ALU.mult, ALU.subtract, r=[bCOSf, bfim, btmp], w=[bRFim])
            k.copy("dve", COSb[:].rearrange("p j t -> p (j t)"), COSf[:], r=[bCOSf], w=[bCOSb])
            k.copy("dve", SINb[:].rearrange("p j t -> p (j t)"), SINf[:], r=[bSINf], w=[bSINb])
            k.ts("dve", NSINb[:].rearrange("p j t -> p (j t)"), SINf[:], -1.0, ALU.mult, r=[bSINf], w=[bNSINb])
        P.barrier()
        w_in, bw_in = load_w(k, st, di["w_in"][l], 8, INW, name="w_in")
        w_sw, bw_sw = load_w(k, st, di["w_sw"][l], 8, 896, name="w_sw")
        w_glu, bw_glu = load_w(k, st, di["w_glu"][l], 4, 1024, name="w_glu")
        w_bs, bw_bs = load_w(k, st, di["w_bssm"][l], 4, 1024, name="w_bs")
        bre, bbre = load_w(k, st, di["s_bre"][l].rearrange("p j m -> p (j m)"), 1, 2048, name="bre")
        bim, bbim = load_w(k, st, di["s_bim"][l].rearrange("p j m -> p (j m)"), 1, 2048, name="bim")
        cre, bcre = load_w(k, st, di["s_cre"][l].rearrange("p j m -> p (j m)"), 1, 2048, name="cre")
        cim, bcim = load_w(k, st, di["s_cim"][l].rearrange("p j m -> p (j m)"), 1, 2048, name="cim")
        xt = [k.sb(st, [128, TT // 128, D], F32, "xt") for _ in range(2)]
        cosr = [k.sb(st, [128, TT], F32, "cosr") for _ in range(2)]
        sinr = [k.sb(st, [128, TT], F32, "sinr") for _ in range(2)]
        NSB = TT // 128
        junk, bjunk = k.sb(st, [128, D], BF16, "junk")
        ss, bss = k.sb(st, [128, NSB], F32, "ss")
        ms, bms = k.sb(st, [128, NSB], F32, "ms")
        sd, bsd = k.sb(st, [128, NSB], F32, "sd")
        rstd, brstd = k.sb(st, [128, NSB], F32, "rstd")
        hh = [k.sb(st, [128, D], BF16, "h") for _ in range(2)]
        hT, bhT = k.sb(st, [128, 8, TT], BF16, "hT")
        uT2 = [k.sb(st, [128, 4, TT], BF16, "uT") for _ in range(2)]
        qTs, bqTs = k.sb(st, [128, 4, TT], BF16, "qTs")
        kvs = {nm: k.sb(st, [128, TT], BF16, nm) for nm in ("kcT", "vcT", "ksT", "kwT")}
        sgs2 = [k.sb(st, [128, 8, TT], BF16, "sgs") for _ in range(2)]
        sgn, bsgn = k.sb(st, [128, 8, TT], BF16, "sgn")
        sg, bsg = k.sb(st, [128, TT // 128, 24], F32, "sg")
        vsel, bvsel = k.sb(st, [128, NSB, 2, 65], BF16, "vsel")
        vwin, bvwin = k.sb(st, [128, NSB, 2, 65], BF16, "vwin")
        k.memset("pool", vsel[:], 1.0, [bvsel])
        k.memset("pool", vwin[:], 1.0, [bvwin])
        tmps = [k.sb(st, [128, TT], F32, "tmp") for _ in range(12)]
        tmpi = [0]

        def gettmp():
            t = tmps[tmpi[0] % len(tmps)]
            tmpi[0] += 1
            return t
        bsc = [k.sb(st, [128, TT], F32, "bsc") for _ in range(4)]
        wall, _ = k.sb(st, [128, 16, 2, TT], F32, "wall")
        bwall = [Buf("wall") for _ in range(16)]
        cwt = [k.sb(st, [128, 16], F32, "cwt") for _ in range(3)]
        xre, bxre = k.sb(st, [128, 16, TT], BF16, "xre")
        nxim, bnxim = k.sb(st, [128, 16, TT], BF16, "nxim")
        car, bcar = k.sb(st, [128, 2, 16], F32, "car")
        k.memset("dve", car[:], 0.0, [bcar])
        ypre, bypre = k.sb(st, [128, TT], F32, "ypre")
        yT, byT = k.sb(st, [128, 4, TT], BF16, "yT")
        sgz, bsgz = k.sb(st, [128, TT], F32, "sgz")
        zzT, bzzT = k.sb(st, [128, 4, TT], BF16, "zzT")
        gss, bgss = k.sb(st, [128, 8, TT], BF16, "gss")

        def load_tile(i):
            t, b = xt[i % 2]
            k.dma("sp", t[:], x_src[i * TT:(i + 1) * TT, :].rearrange("(s p) d -> p s d", p=128), b)
            k.dma("sp", cosr[i % 2][0][:], di["c_cos"][:, i * TT:(i + 1) * TT], cosr[i % 2][1])
            k.dma("sp", sinr[i % 2][0][:], di["c_sin"][:, i * TT:(i + 1) * TT], sinr[i % 2][1])

        def proj(wt, wb, col0, M=128):
            ps, bp = ring.get()
            for c in range(8):
                k.mm(ps[0:M, 0:TT], wt[:, c, col0:col0 + M], hT[:, c, :], c == 0, c == 7, r=[wb[c], bhT], w=[bp])
            return ps, bp

        def inproj_gen(i):
            uT, buT = uT2[i % 2]
            sgs, bsgs = sgs2[i % 2]
            x_t, bx = xt[i % 2]
            cos_t, bcos = cosr[i % 2]
            sin_t, bsin = sinr[i % 2]
            tok = slice(i * TT, (i + 1) * TT)
            for s_ in range(NSB):
                h_t, bh = hh[s_ % 2]
                k.act(junk[:], x_t[:, s_, :], AF.Square, r=[bx], w=[bjunk, bss], accum=ss[:, s_:s_ + 1])
                k.ts("dve", ms[:, s_:s_ + 1], ss[:, s_:s_ + 1], 1.0 / D, ALU.mult, r=[bss], w=[bms], s2=EPS, op1=ALU.add)
                k.act(sd[:, s_:s_ + 1], ms[:, s_:s_ + 1], AF.Sqrt, r=[bms], w=[bsd])
                k.recip(rstd[:, s_:s_ + 1], sd[:, s_:s_ + 1], r=[bsd], w=[brstd])
                k.stt(h_t[:], x_t[:, s_, :], rstd[:, s_:s_ + 1], gam[:], ALU.mult, ALU.mult, r=[bx, brstd, bgam], w=[bh])
            yield
            yield
            yield
            for s_ in range(NSB):
                h_t, bh = hh[s_ % 2]
                ps, bp = ring.get()
                pbf = ps[:].bitcast(BF16)
                for c in range(8):
                    k.tr(pbf[:, c * 128:(c + 1) * 128], h_t[:, c * 128:(c + 1) * 128], ident[:], r=[bh, bid], w=[bp])
                k.copy("act", hT[:, :, s_ * 128:(s_ + 1) * 128], pbf.rearrange("p (c t) -> p c t", c=8), r=[bp], w=[bhT])
            yield
            yield
            for c4 in range(4):
                ps, bp = proj(w_in, bw_in, c4 * 128)
                k.copy("act", uT[:, c4, :], ps[:, 0:TT], r=[bp], w=[buT])
                yield
            def rope(col, swcol, dst, bdst):
                psA, bA = proj(w_in, bw_in, col)
                psB, bB = proj(w_sw, bw_sw, swcol)
                t1_, bt1_ = gettmp()
                t2_, bt2_ = gettmp()
                k.tt("dve", t1_[:], psA[:, 0:TT], cos_t[:], ALU.mult, r=[bA, bcos], w=[bt1_])
                k.tt("dve", t2_[:], psB[:, 0:TT], sin_t[:], ALU.mult, r=[bB, bsin], w=[bt2_])
                k.tt("pool", dst, t1_[:], t2_[:], ALU.add, r=[bt1_, bt2_], w=[bdst])
            for c in range(4):
                rope(512 + c * 128, c * 128, qTs[:, c, :], bqTs)
                yield
            rope(1024, 512, kvs["kcT"][0][:], kvs["kcT"][1])
            yield
            rope(1280, 640, kvs["ksT"][0][:], kvs["ksT"][1])
            yield
            rope(1536, 768, kvs["kwT"][0][:], kvs["kwT"][1])
            yield
            ps, bp = proj(w_in, bw_in, 1152)
            k.copy("act", kvs["vcT"][0][:], ps[:, 0:TT], r=[bp], w=[kvs["vcT"][1]])
            for c in range(8):
                ps, bp = proj(w_in, bw_in, 1816 + c * 128)
                k.act(sgs[:, c, :], ps[:, 0:TT], AF.Sigmoid, r=[bp], w=[bsgs])
                yield
            for c in range(8):
                ps, bp = proj(w_in, bw_in, 2840 + c * 128)
                k.act(sgn[:, c, :], ps[:, 0:TT], AF.Sigmoid, r=[bp], w=[bsgn])
                yield
            for s_ in range(NSB):
                ps, bp = ring.get()
                for c in range(8):
                    k.mm(ps[:, 0:24], hT[:, c, s_ * 128:(s_ + 1) * 128], w_in[:, c, 1792:1816], c == 0, c == 7, r=[bw_in[c], bhT], w=[bp])
                k.act(sg[:, s_, :], ps[:, 0:24], AF.Sigmoid, r=[bp], w=[bsg])
            for s_ in range(NSB):
                for (col, vt, bv) in ((1408, vsel, bvsel), (1664, vwin, bvwin)):
                    ps, bp = ring.get()
                    for c in range(8):
                        k.mm(ps[:, 0:128], hT[:, c, s_ * 128:(s_ + 1) * 128], w_in[:, c, col:col + 128], c == 0, c == 7, r=[bw_in[c], bhT], w=[bp])
                    k.copy("act", vt[:, s_, :, 0:64], ps[:, 0:128].rearrange("p (h d) -> p h d", h=2), r=[bp], w=[bv])
                    yield
            k.dma("sp", sc["qT"].rearrange("(c p) s -> p c s", p=128)[:, :, tok], qTs[:], bqTs)
            for nm in ("kcT", "vcT", "ksT", "kwT"):
                k.dma("sp", sc[nm][:, tok], kvs[nm][0][:], kvs[nm][1])
            k.dma("sp", sc["sgnT"].rearrange("(c p) s -> p c s", p=128)[:, :, tok], sgn[:], bsgn)
            k.dma("sp", sc["sgTok"][tok].rearrange("(s p) c -> p s c", p=128), sg[:], bsg)
            k.dma("sp", sc["vs"][tok].rearrange("(s p) h c -> p s h c", p=128), vsel[:], bvsel)
            k.dma("sp", sc["vw"][tok].rearrange("(s p) h c -> p s h c", p=128), vwin[:], bvwin)
        def s5_gen(i):
            uT, buT = uT2[i % 2]
            sgs, bsgs = sgs2[i % 2]
            tok = slice(i * TT, (i + 1) * TT)
            def stageA(j):
                c4 = j // 4
                psr, bpr = ring.get()
                psi, bpi = ring.get()
                k.mm(psr[:, 0:TT], bre[:, 0, j * 128:(j + 1) * 128], uT[:, c4, :], True, True, r=[bbre[0], buT], w=[bpr])
                k.mm(psi[:, 0:TT], bim[:, 0, j * 128:(j + 1) * 128], uT[:, c4, :], True, True, r=[bbim[0], buT], w=[bpi])
                b_re, bb_re = bsc[(2 * j) % 4]
                b_im, bb_im = bsc[(2 * j + 1) % 4]
                t1_, bt1_ = gettmp()
                t2_, bt2_ = gettmp()
                k.tt("dve", t1_[:], psr[:, 0:TT], RFre[:, j, :], ALU.mult, r=[bpr, bRFre], w=[bt1_])
                k.tt("dve", t2_[:], psi[:, 0:TT], RFim[:, j, :], ALU.mult, r=[bpi, bRFim], w=[bt2_])
                k.tt("pool", b_re[:], t1_[:], t2_[:], ALU.subtract, r=[bt1_, bt2_], w=[bb_re])
                t3_, bt3_ = gettmp()
                t4_, bt4_ = gettmp()
                k.tt("dve", t3_[:], psi[:, 0:TT], RFre[:, j, :], ALU.mult, r=[bpi, bRFre], w=[bt3_])
                k.tt("dve", t4_[:], psr[:, 0:TT], RFim[:, j, :], ALU.mult, r=[bpr, bRFim], w=[bt4_])
                k.tt("pool", b_im[:], t3_[:], t4_[:], ALU.add, r=[bt3_, bt4_], w=[bb_im])

            def stageB(j):
                b_re, bb_re = bsc[(2 * j) % 4]
                b_im, bb_im = bsc[(2 * j + 1) % 4]
                w_re, bw_re = wall[:, j, 0, :], bwall[j]
                w_im, bw_im = wall[:, j, 1, :], bwall[j]
                dj = dec[:, j:j + 1].to_broadcast([128, TT])
                k.scan(w_re, dj, b_re[:], car[:, 0, j:j + 1], r=[bdec, bb_re, bcar], w=[bw_re])
                k.scan(w_im, dj, b_im[:], car[:, 1, j:j + 1], r=[bdec, bb_im, bcar], w=[bw_im])
                t5_, bt5_ = gettmp()
                t6_, bt6_ = gettmp()
                k.tt("dve", t5_[:], w_re, COSb[:, j, :], ALU.mult, r=[bw_re, bCOSb], w=[bt5_])
                k.tt("dve", t6_[:], w_im, SINb[:, j, :], ALU.mult, r=[bw_im, bSINb], w=[bt6_])
                k.tt("pool", xre[:, j, :], t5_[:], t6_[:], ALU.subtract, r=[bt5_, bt6_], w=[bxre])
                t7_, bt7_ = gettmp()
                t8_, bt8_ = gettmp()
                k.tt("pool", t7_[:], w_re, NSINb[:, j, :], ALU.mult, r=[bw_re, bNSINb], w=[bt7_])
                k.tt("pool", t8_[:], w_im, COSb[:, j, :], ALU.mult, r=[bw_im, bCOSb], w=[bt8_])
                k.tt("pool", nxim[:, j, :], t7_[:], t8_[:], ALU.subtract, r=[bt7_, bt8_], w=[bnxim])

            stageA(0)
            for j in range(16):
                if j + 1 < 16:
                    stageA(j + 1)
                stageB(j)
                yield
            wl_re = wall[:, :, 0, TT - 1]
            wl_im = wall[:, :, 1, TT - 1]
            (c0, bc0), (c1, bc1), (c2, bc2) = cwt
            k.tt("dve", c0[:], wl_re, cT[:], ALU.mult, r=bwall + [bcT], w=[bc0])
            k.tt("dve", c1[:], wl_im, nsT[:], ALU.mult, r=bwall + [bnsT], w=[bc1])
            k.tt("dve", car[:, 0, :], c0[:], c1[:], ALU.add, r=[bc0, bc1], w=[bcar])
            k.tt("dve", c2[:], wl_im, cT[:], ALU.mult, r=bwall + [bcT], w=[bc2])
            k.tt("dve", c0[:], wl_re, sT[:], ALU.mult, r=bwall + [bsT], w=[bc0])
            k.tt("dve", car[:, 1, :], c2[:], c0[:], ALU.add, r=[bc2, bc0], w=[bcar])
            for c4 in range(4):
                ps, bp = ring.get()
                for jj in range(4):
                    j = 4 * c4 + jj
                    k.mm(ps[:, 0:TT], cre[:, 0, j * 128:(j + 1) * 128], xre[:, j, :], jj == 0, False, r=[bcre[0], bxre], w=[bp])
                    k.mm(ps[:, 0:TT], cim[:, 0, j * 128:(j + 1) * 128], nxim[:, j, :], False, jj == 3, r=[bcim[0], bnxim], w=[bp])
                k.stt(ypre[:], uT[:, c4, :], dvec[:, c4:c4 + 1], ps[:, 0:TT], ALU.mult, ALU.add, r=[buT, bdvec, bp], w=[bypre])
                k.act(yT[:, c4, :], ypre[:], AF.Gelu_apprx_tanh, r=[bypre], w=[byT])
                yield
            for kk in range(4):
                psg, bpg = ring.get()
                for c4 in range(4):
                    k.mm(psg[:, 0:TT], w_glu[:, c4, (4 + kk) * 128:(5 + kk) * 128], yT[:, c4, :], c4 == 0, c4 == 3, r=[bw_glu[c4], byT], w=[bpg])
                k.act(sgz[:], psg[:, 0:TT], AF.Sigmoid, r=[bpg], w=[bsgz])
                psv, bpv = ring.get()
                for c4 in range(4):
                    k.mm(psv[:, 0:TT], w_glu[:, c4, kk * 128:(kk + 1) * 128], yT[:, c4, :], c4 == 0, c4 == 3, r=[bw_glu[c4], byT], w=[bpv])
                k.tt("dve", zzT[:, kk, :], psv[:, 0:TT], sgz[:], ALU.mult, r=[bpv, bsgz], w=[bzzT])
                yield
            for fc in range(8):
                ps, bp = ring.get()
                for kk in range(4):
                    k.mm(ps[:, 0:TT], w_bs[:, kk, fc * 128:(fc + 1) * 128], zzT[:, kk, :], kk == 0, kk == 3, r=[bw_bs[kk], bzzT], w=[bp])
                k.tt("dve", gss[:, fc, :], ps[:, 0:TT], sgs[:, fc, :], ALU.mult, r=[bp, bsgs], w=[bgss])
                yield
            k.dma("sp", sc["gssT"].rearrange("(c p) s -> p c s", p=128)[:, :, tok], gss[:], bgss)
        def step(g):
            try:
                next(g)
                return True
            except StopIteration:
                return False

        load_tile(0)
        if NTT > 1:
            load_tile(1)
        for _ in inproj_gen(0):
            pass
        for i in range(NTT):
            if i + 2 < NTT:
                load_tile(i + 2)
            gs = s5_gen(i)
            gi = inproj_gen(i + 1) if i + 1 < NTT else iter(())
            alive_s, alive_i = True, True
            n_ = 0
            while alive_s or alive_i:
                if alive_s:
                    alive_s = step(gs)
                for _ in range(1 + (n_ % 2)):
                    if alive_i:
                        alive_i = step(gi)
                n_ += 1
    P.barrier()


def phase2(k, pers, l, S, di, sc, cst):
    P = k.P
    NC = S // 16 - 1
    NCP = S // 16
    NCT = NCP // 128
    KcT, bKcT = k.sb(pers, [128, NCP], BF16, "KcT")
    Vc, bVc = k.sb(pers, [128, NCT, 2, 65], BF16, "Vc")
    k.memset("pool", KcT[:], 0.0, [bKcT])
    k.memset("pool", Vc[:], 1.0, [bVc])
    with ExitStack() as st:
        ring = PsumRing(k, st)
        for typ in ("k", "v"):
            with ExitStack() as s2:
                xT, bxT = k.sb(s2, [128, S], BF16, "cxT")
                k.dma("sp", xT[:], sc["kcT" if typ == "k" else "vcT"], bxT)
                w1, bw1 = k.sb(s2, [128, 32, 256], BF16, "w1")
                bw1b = Buf("w1b")
                src = di["c_w1" + typ][l].rearrange("(l d) c -> d l c", d=64)
                k.dma("pool", w1[0:64], src, bw1)
                k.dma("pool", w1[64:128], src, bw1b)
                pe2, bpe2 = k.sb(s2, [64, 32, 2], BF16, "pe2")
                pe_f, bpe_f = k.sb(s2, [64, 32], F32, "pe_f")
                k.dma("sp", pe_f[:], di["c_pe" + typ][l], bpe_f)
                k.copy("dve", pe2[:, :, 0], pe_f[:], r=[bpe_f], w=[bpe2])
                k.copy("dve", pe2[:, :, 1], pe_f[:], r=[bpe_f], w=[bpe2])
                b1, bb1 = k.sb(s2, [128, 2], F32, "b1")
                k.dma("sp", b1[:], di["c_b1" + typ][l], bb1)
                w2, bw2 = k.sb(s2, [128, 2, 64], BF16, "w2")
                k.dma("pool", w2[:], di["c_w2" + typ][l].rearrange("(c p) d -> p c d", p=128), bw2)
                bias, bbias = k.sb(s2, [128, 2], F32, "bias")
                for cc in range(2):
                    ps, bp = ring.get()
                    for li in range(32):
                        k.mm(ps[:, 0:2], w1[0:64, li, cc * 128:(cc + 1) * 128], pe2[:, li, :], li == 0, li == 31, r=[bw1, bpe2], w=[bp])
                    k.tt("dve", bias[:, cc:cc + 1], ps[:, 0:1], b1[:, cc:cc + 1], ALU.add, r=[bp, bb1], w=[bbias])
                for hk in range(2):
                    hid, bhid = k.sb(s2, [128, 2, NCP], BF16, "hid")
                    k.memset("pool", hid[:], 0.0, [bhid])
                    bw = bw1 if hk == 0 else bw1b
                    for cc in range(2):
                        ps, bp = ring.get()
                        for li in range(32):
                            k.mm(ps[:, 0:NC], w1[hk * 64:(hk + 1) * 64, li, cc * 128:(cc + 1) * 128],
                                 xT[hk * 64:(hk + 1) * 64, li:li + 16 * (NC - 1) + 1:16], li == 0, li == 31, r=[bw, bxT], w=[bp])
                        k.act(hid[:, cc, 0:NC], ps[:, 0:NC], AF.Gelu_apprx_tanh, r=[bp, bbias], w=[bhid], bias=bias[:, cc:cc + 1])
                    if typ == "k":
                        ps, bp = ring.get()
                        for cc in range(2):
                            k.mm(ps[hk * 64:(hk + 1) * 64, 0:NC], w2[:, cc, :], hid[:, cc, 0:NC], cc == 0, cc == 1, r=[bw2, bhid], w=[bp])
                        k.copy("act", KcT[hk * 64:(hk + 1) * 64, 0:NC], ps[hk * 64:(hk + 1) * 64, 0:NC], r=[bp], w=[bKcT])
                    else:
                        for nt in range(NCT):
                            ps, bp = ring.get()
                            for cc in range(2):
                                k.mm(ps[:, 0:64], hid[:, cc, nt * 128:(nt + 1) * 128], w2[:, cc, :], cc == 0, cc == 1, r=[bhid, bw2], w=[bp])
                            k.copy("act", Vc[:, nt, hk, 0:64], ps[:, 0:64], r=[bp], w=[bVc])
            P.barrier()
    if "dbg_kc" in sc:
        k.dma("sp", sc["dbg_kc"].rearrange("h d n -> (h d) n"), KcT[:], bKcT)
        k.dma("sp", sc["dbg_vc"], Vc[:], bVc)
    return (KcT, bKcT), (Vc, bVc)


def phase3(k, l, S, x_src, di, sc, cst, cmp_t):
    P = k.P
    NT = S // 128
    NCP = S // 16
    NCT = NCP // 128
    NB = S // 64
    (KcT, bKcT), (Vc, bVc) = cmp_t
    ident, bid = cst["ident"]
    with ExitStack() as st:
        ring = PsumRing(k, st, 3)
        ringO = PsumRing(k, st, 3)
        ringM = PsumRing(k, st, 2)
        KsM = []
        for hk in range(2):
            t, b = k.sb(st, [128, S], BF16, "KsM")
            b2 = Buf("KsMi")
            k.dma("sp", t[hk * 64:(hk + 1) * 64], sc["ksT"][hk * 64:(hk + 1) * 64, :], b)
            k.dma("pool", t[(1 - hk) * 64:(2 - hk) * 64], di["c_ind"], b2)
            KsM.append((t, b, b2))
        Vs, bVs = k.sb(st, [128, NT, 2, 65], BF16, "Vs")
        k.dma("sp", Vs[:], sc["vs"].rearrange("(n p) h c -> p n h c", p=128), bVs)

        def cload(name, shape, src, dt=BF16):
            t, b = k.sb(st, shape, dt, name)
            k.dma("pool" if dt == BF16 else "sp", t[:], src, b)
            return t, b
        caus, bcaus = cload("caus", [128, 128], di["c_caus"])
        low, blow = cload("low", [128, 128], di["c_low"])
        mgen, bmgen = cload("mgen", [128, 1024], di["c_mgen"])
        mt, bmt = cload("mt", [128, 16, 128], di["c_mt"])
        G, bG = cload("G", [128, 256], di["c_g"], F32)
        wbn, bwbn = load_w(k, st, di["w_bnsa"][l], 4, 1024, name="wbn")
        ident32, bid32 = cload("ident32", [128, 128], di["c_ident"], F32)
        w_out, bw_out = load_w(k, st, di["w_out"][l], 8, 1024, name="w_out")
        QT = [[k.sb(st, [128, 4, 128], BF16, "QT") for _ in range(2)] for _ in range(2)]
        for pb_ in range(2):
            for hk_ in range(2):
                k.memset("pool", QT[pb_][hk_][0][:], 0.0, [QT[pb_][hk_][1]])
        NHALF = max(1, NB // 64)
        Qsel = [[[k.sb(st, [128, 4, 128], BF16, "Qsel") + (Buf("Qselm"),) for _ in range(NHALF)] for _ in range(2)] for _ in range(2)]
        negm_sw, bnegm_sw = k.sb(st, [128, 128], BF16, "negm_sw")
        k.memset("pool", negm_sw[:], 0.0, [bnegm_sw])
        KwT = [k.sb(st, [128, 640], BF16, "KwT") for _ in range(2)]
        Vw = [k.sb(st, [128, 5, 2, 65], BF16, "Vw") for _ in range(2)]
        gtok = [k.sb(st, [128, 24], F32, "gtok") for _ in range(2)]
        sgn = [k.sb(st, [128, 8, 128], BF16, "sgn") for _ in range(2)]
        gss = [k.sb(st, [128, 8, 128], BF16, "gss") for _ in range(2)]
        xin = [k.sb(st, [128, D], F32, "xin") for _ in range(2)]
        eg = [k.sb(st, [128, NCP], F32, "eg") for _ in range(4)]
        den4, bden4 = k.sb(st, [128, 4], F32, "den4")
        rden4, brden4 = k.sb(st, [128, 4], F32, "rden4")
        pg, bpg = k.sb(st, [128, NCP + 8], F32, "pg")
        k.memset("pool", pg[:], 0.0, [bpg])
        blk, bblk = k.sb(st, [128, NB], F32, "blk")
        blk2, bblk2 = k.sb(st, [128, NB], F32, "blk2")
        m8, bm8 = k.sb(st, [128, 16], F32, "m8")
        negm, bnegm = k.sb(st, [128, 128], BF16, "negm")
        k.memset("pool", negm[:], 0.0, [bnegm])
        pTs = [k.sb(st, [128, 512], BF16, "pT") for _ in range(4)]
        pti = [0]
        osbs = [k.sb(st, [65, 512], F32, "osb") for _ in range(4)]
        s4s = [k.sb(st, [128, 4], F32, "s4") for _ in range(4)]
        oq = [k.sb(st, [128, 8, 64], F32, "oq") for _ in range(2)]
        oqb, boqb = k.sb(st, [128, 512], BF16, "oqb")
        oT2, boT2 = k.sb(st, [128, 4, 128], BF16, "oT2")
        mrg, bmrg = k.sb(st, [128, 8, 128], BF16, "mrg")
        xm = [k.sb(st, [128, D], F32, "xm") for _ in range(1)]

        def loads(qb):
            s0 = qb * 128
            pb = qb % 2
            qv = sc["qT"].rearrange("(h d) s -> d h s", d=64)
            for hk in range(2):
                t, b = QT[pb][hk]
                k.dma("sp", t[hk * 64:(hk + 1) * 64], qv[:, hk * 4:(hk + 1) * 4, s0:s0 + 128], b)
                for hf in range(NHALF):
                    if hf * 32 <= qb:
                        t, b, _ = Qsel[pb][hk][hf]
                        k.dma("sp", t[hk * 64:(hk + 1) * 64], qv[:, hk * 4:(hk + 1) * 4, s0:s0 + 128], b)
            lo = max(0, s0 - 512)
            t, b = KwT[pb]
            k.dma("sp", t[:, 640 - (s0 + 128 - lo):640], sc["kwT"][:, lo:s0 + 128], b)
            nw = (s0 + 128 - lo) // 128
            t, b = Vw[pb]
            k.dma("sp", t[:, 5 - nw:5], sc["vw"][lo:s0 + 128].rearrange("(n p) h c -> p n h c", p=128), b)
            t, b = gtok[pb]
            k.dma("sp", t[:], sc["sgTok"][s0:s0 + 128, :], b)
            t, b = sgn[pb]
            k.dma("sp", t[:], sc["sgnT"].rearrange("(c p) s -> p c s", p=128)[:, :, s0:s0 + 128], b)
            t, b = gss[pb]
            k.dma("sp", t[:], sc["gssT"].rearrange("(c p) s -> p c s", p=128)[:, :, s0:s0 + 128], b)
            t, b = xin[pb]
            k.dma("sp", t[:], x_src[s0:s0 + 128, :], b)

        DEPTH = 2
        pipe = []
        delayed = []

        def tick():
            for d in delayed:
                d[0] -= 1
            while delayed and delayed[0][0] <= 0:
                delayed.pop(0)[1]()

        cur_tag = [0]

        def push(score_fn, pv_fn, after=None):
            tok_ = score_fn()
            pipe.append((pv_fn, tok_, after, cur_tag[0]))
            if len(pipe) > DEPTH:
                pv, tk, af, _ = pipe.pop(0)
                pv(tk)
                if af is not None:
                    af()
            tick()

        def flush():
            while pipe:
                pv, tk, af, _ = pipe.pop(0)
                pv(tk)
                if af is not None:
                    af()
            while delayed:
                delayed.pop(0)[1]()

        def attn_tile(Ops, bO, first, last_, KT_ap, bKT, V_ap, bV, Q2, bQ, smask=None, emask=None, after=None):
            def score():
                psS, bS = ring.get()
                nmask = (4 if smask is not None else 0) + (1 if emask is not None else 0)
                rl = (bKT if isinstance(bKT, list) else [bKT]) + (bQ if isinstance(bQ, list) else [bQ])
                k.mm(psS[:, 0:512], KT_ap, Q2, True, nmask == 0, r=rl, w=[bS])
                done = 0
                assert emask is None
                if smask is not None:
                    m_ap, bm = smask
                    for g in range(4):
                        done += 1
                        k.mm(psS[:, g * 128:(g + 1) * 128], ident[:], m_ap, False, done == nmask, r=[bid, bm], w=[bS])
                pT, bpT = pTs[pti[0] % len(pTs)]
                pti[0] += 1
                k.act(pT[:], psS[:, 0:512], AF.Exp, r=[bS], w=[bpT], scale=0.125)
                return (pT, bpT)

            def pv(tk):
                pT, bpT = tk
                k.mm(Ops[0:65, 0:512], V_ap, pT[:], first, last_, r=[bV, bpT], w=[bO])
            push(score, pv, after)

        fin_i = [0]

        def finalize(Ops, bO, qb_, hk_, br_, first_branch):
            tag_ = cur_tag[0]

            def stage_a():
                while sum(1 for d_ in delayed if len(d_) > 3) >= 3:
                    delayed.pop(0)[1]()
                osb, bosb = osbs[fin_i[0] % 4]
                s4, bs4 = s4s[fin_i[0] % 4]
                fin_i[0] += 1
                k.copy("act", osb[:], Ops[0:65, 0:512], r=[bO], w=[bosb])

                def stage_b():
                    pst, bpst = ringM.get()
                    for g in range(4):
                        k.tr(pst[:, g * 65:(g + 1) * 65], osb[:, g * 128:(g + 1) * 128], ident32[0:65, 0:65], r=[bosb, bid32], w=[bpst])
                    p3 = pst[:, 0:260].rearrange("p (g c) -> p g c", c=65)
                    gt_, bgt_ = gtok[qb_ % 2]
                    oq_t, boq = oq[qb_ % 2]
                    k.ts("dve", s4[:], p3[:, :, 64], 1e-20, ALU.max, r=[bpst], w=[bs4])
                    k.recip(s4[:], s4[:], r=[bs4], w=[bs4])
                    c0_ = hk_ * 12 + br_
                    k.tt("dve", s4[:], s4[:], gt_[:, c0_:c0_ + 10:3], ALU.mult, r=[bs4, bgt_], w=[bs4])
                    for g in range(4):
                        h_ = hk_ * 4 + g
                        if first_branch:
                            k.ts("dve", oq_t[:, h_, :], p3[:, g, 0:64], s4[:, g:g + 1], ALU.mult, r=[bpst, bs4], w=[boq])
                        else:
                            k.stt(oq_t[:, h_, :], p3[:, g, 0:64], s4[:, g:g + 1], oq_t[:, h_, :], ALU.mult, ALU.add, r=[bpst, bs4, boq], w=[boq])
                delayed.append([3, stage_b, tag_, "fin"])
            return stage_a

        mtmps = [k.sb(st, [128, 128], F32, "mtmp") for _ in range(2)]

        def epilogue1(qb):
            pb = qb % 2
            oq_t, boq = oq[pb]
            if "dbg_o" in sc:
                k.dma("sp", sc["dbg_o"][qb * 128:(qb + 1) * 128, :], oq_t[:].rearrange("p h d -> p (h d)"), boq)
            k.copy("dve", oqb[:], oq_t[:].rearrange("p h d -> p (h d)"), r=[boq], w=[boqb])
            pst, bpst = ringM.get()
            pbf = pst[:].bitcast(BF16)
            for c in range(4):
                k.tr(pbf[:, c * 128:(c + 1) * 128], oqb[:, c * 128:(c + 1) * 128], ident[:], r=[boqb, bid], w=[bpst])
            k.copy("dve", oT2[:].rearrange("p c q -> p (c q)"), pbf[:, 0:512], r=[bpst], w=[boT2])
            sg_t, bsgn_ = sgn[pb]
            gs_t, bgs_ = gss[pb]
            for half in range(2):
                ps, bp = ringM.get()
                for f4 in range(4):
                    fc = half * 4 + f4
                    for c in range(4):
                        k.mm(ps[:, f4 * 128:(f4 + 1) * 128], wbn[:, c, fc * 128:(fc + 1) * 128], oT2[:, c, :],
                             c == 0, c == 3, r=[bwbn[c], boT2], w=[bp])
                for f4 in range(4):
                    fc = half * 4 + f4
                    mt_, bmt_ = mtmps[fc % 2]
                    k.tt("dve", mt_[:], ps[:, f4 * 128:(f4 + 1) * 128], sg_t[:, fc, :], ALU.mult, r=[bp, bsgn_], w=[bmt_])
                    k.tt("pool", mrg[:, fc, :], mt_[:], gs_t[:, fc, :], ALU.add, r=[bmt_, bgs_], w=[bmrg])
            delayed.append([8, lambda: epilogue2(qb), qb])

        def epilogue2(qb):
            s0 = qb * 128
            pb = qb % 2
            x_t, bx = xin[pb]
            xm_t, bxm = xm[0]
            for half in range(2):
                ps, bp = ringM.get()
                for fc in range(8):
                    k.mm(ps[:, 0:512], mrg[:, fc, :], w_out[:, fc, half * 512:(half + 1) * 512], fc == 0, fc == 7, r=[bmrg, bw_out[fc]], w=[bp])
                k.tt("dve", xm_t[:, half * 512:(half + 1) * 512], ps[:, 0:512], x_t[:, half * 512:(half + 1) * 512], ALU.add, r=[bp, bx], w=[bxm])
            k.dma("sp", sc["xmid"][s0:s0 + 128, :], xm_t[:], bxm)

        def force(tag_max):
            while pipe and pipe[0][3] <= tag_max:
                pv, tk, af, _ = pipe.pop(0)
                pv(tk)
                if af is not None:
                    af()
            progressed = True
            while progressed:
                progressed = False
                for idx_, d_ in enumerate(delayed):
                    if d_[2] <= tag_max:
                        delayed.pop(idx_)
                        d_[1]()
                        progressed = True
                        break

        def chainA(qb, hk):
            pb = qb % 2
            Qt, bQ = QT[pb][hk]
            for g in range(4):
                ps, bp = ringM.get()
                k.mm(ps[:, 0:NCP], Qt[:, g, :], KcT[:, 0:NCP], True, False, r=[bQ, bKcT], w=[bp])
                k.mm(ps[:, 0:NCP], ident[:], mgen[:, 512 - 8 * qb:512 - 8 * qb + NCP], False, True, r=[bid, bmgen], w=[bp])
                k.act(eg[g][0][:], ps[:, 0:NCP], AF.Exp, r=[bp], w=[eg[g][1], bden4], scale=0.125, accum=den4[:, g:g + 1])
            k.ts("dve", rden4[:], den4[:], 1e-20, ALU.max, r=[bden4], w=[brden4])
            k.recip(rden4[:], rden4[:], r=[brden4], w=[brden4])
            k.ts("dve", pg[:, 1:1 + NCP], eg[0][0][:], rden4[:, 0:1], ALU.mult, r=[eg[0][1], brden4], w=[bpg])
            for g in range(1, 4):
                k.stt(pg[:, 1:1 + NCP], eg[g][0][:], rden4[:, g:g + 1], pg[:, 1:1 + NCP], ALU.mult, ALU.add, r=[eg[g][1], brden4, bpg], w=[bpg])
            P.op("dve", lambda e: e.tensor_reduce(out=blk[:], in_=pg[:, 0:NCP].rearrange("p (j o) -> p j o", o=4), axis=AX.X, op=ALU.add), reads=[bpg], writes=[bblk])
            k.tt("dve", blk[:], blk[:], pg[:, 4:4 + 4 * NB:4], ALU.add, r=[bblk, bpg], w=[bblk])
            k.tt("dve", blk[:], blk[:], G[:, 126 - 2 * qb:126 - 2 * qb + NB], ALU.add, r=[bblk, bG], w=[bblk])
            if qb >= 1:
                k.ts("dve", blk[:, 0:1], blk[:, 0:1], 1e4, ALU.add, r=[bblk], w=[bblk])
            P.op("dve", lambda e: e.max(out=m8[:, 0:8], in_=blk[:]), reads=[bblk], writes=[bm8])
            P.op("dve", lambda e: e.match_replace(out=blk2[:], in_to_replace=m8[:, 0:8], in_values=blk[:], imm_value=-3e38), reads=[bblk, bm8], writes=[bblk2])
            P.op("dve", lambda e: e.max(out=m8[:, 8:16], in_=blk2[:]), reads=[bblk2], writes=[bm8])
            nhalf_used = 1 if qb < 32 else NHALF
            need_nat = (hk == 1) or nhalf_used > 1
            need_sw = (hk == 0) or nhalf_used > 1
            if need_nat:
                k.ts("dve", negm[:, 0:NB], blk[:], m8[:, 15:16], ALU.is_lt, r=[bblk, bm8], w=[bnegm], s2=NEG, op1=ALU.mult)
            if need_sw:
                n0 = min(NB, 64)
                k.ts("dve", negm_sw[:, 64:64 + n0], blk[:, 0:n0], m8[:, 15:16], ALU.is_lt, r=[bblk, bm8], w=[bnegm_sw], s2=NEG, op1=ALU.mult)
                if NB > 64:
                    k.ts("dve", negm_sw[:, 0:NB - 64], blk[:, 64:NB], m8[:, 15:16], ALU.is_lt, r=[bblk, bm8], w=[bnegm_sw], s2=NEG, op1=ALU.mult)

        def chainB(qb, hk):
            pb = qb % 2
            os_ = slice((1 - hk) * 64, (2 - hk) * 64)
            nhalf_used = 1 if qb < 32 else NHALF
            for hf in range(nhalf_used):
                use_sw = (hk == 0 and hf == 0) or (hk == 1 and hf == 1)
                src_t, bsrc = (negm_sw, bnegm_sw) if use_sw else (negm, bnegm)
                ps, bp = ringM.get()
                pbf = ps[:].bitcast(BF16)
                k.tr(pbf[:, 0:128], src_t[:], ident[:], r=[bsrc, bid], w=[bp])
                qs_t, _, bqm = Qsel[pb][hk][hf]
                for g in range(4):
                    k.copy("dve", qs_t[os_, g, :], pbf[os_, 0:128], r=[bp], w=[bqm])

        loads(0)
        chainA(0, 0)
        for qb in range(NT):
            s0 = qb * 128
            pb = qb % 2
            for hk in range(2):
                cur_tag[0] = qb
                Qt, bQ = QT[pb][hk]
                Q2 = Qt[:].rearrange("d g q -> d (g q)")
                chainB(qb, hk)
                Ops, bO = ringO.get()
                fa = finalize(Ops, bO, qb, hk, 0, True)
                tiles = [nt for nt in range(NCT) if qb - 16 * nt >= 0]
                for idx, nt in enumerate(tiles):
                    m = qb - 16 * nt
                    sm = (mt[:, m, :], bmt) if m < 16 else None
                    attn_tile(Ops, bO, idx == 0, idx == len(tiles) - 1, KcT[:, nt * 128:(nt + 1) * 128], bKcT,
                              Vc[:, nt, hk, :], bVc, Q2, bQ, smask=sm, after=fa if idx == len(tiles) - 1 else None)
                Ops, bO = ringO.get()
                fa = finalize(Ops, bO, qb, hk, 2, False)
                tiles = [wt for wt in range(5) if s0 - 512 + 128 * wt >= 0]
                kw_full, bkw = KwT[pb]
                kw_t = kw_full
                vw_t, bvw = Vw[pb]
                for idx, wt in enumerate(tiles):
                    sm = (low[:], blow) if wt == 0 else ((caus[:], bcaus) if wt == 4 else None)
                    attn_tile(Ops, bO, idx == 0, idx == len(tiles) - 1, kw_t[:, wt * 128:(wt + 1) * 128], bkw,
                              vw_t[:, wt, hk, :], bvw, Q2, bQ, smask=sm, after=fa if idx == len(tiles) - 1 else None)
                if hk == 1 and qb + 1 < NT:
                    force(qb - 1)
                    loads(qb + 1)
                if hk == 0:
                    chainA(qb, 1)
                elif qb + 1 < NT:
                    chainA(qb + 1, 0)
                Ops, bO = ringO.get()
                fa = finalize(Ops, bO, qb, hk, 1, False)
                for i in range(qb + 1):
                    sm = (caus[:], bcaus) if i == qb else None
                    qs_t, bqs, bqm = Qsel[pb][hk][i // 32]
                    attn_tile(Ops, bO, i == 0, i == qb, KsM[hk][0][:, i * 128:(i + 1) * 128], [KsM[hk][1], KsM[hk][2]],
                              Vs[:, i, hk, :], bVs, qs_t[:].rearrange("d g q -> d (g q)"), [bqs, bqm], smask=sm, after=fa if i == qb else None)
                if hk == 1:
                    delayed_ep = (lambda q_=qb: (lambda: delayed.append([6, lambda: epilogue1(q_), q_])))(qb)
                    pipe[-1] = (pipe[-1][0], pipe[-1][1], (lambda f1=pipe[-1][2], f2=delayed_ep: (f1(), f2())), pipe[-1][3])
        flush()
    P.barrier()


def phase4(k, l, S, di, sc, cst, dst, last):
    P = k.P
    TT = 256
    NTT = S // TT
    NSB = TT // 128
    ident, bid = cst["ident"]
    with ExitStack() as st:
        ring = PsumRing(k, st)
        w_fin, bw_fin = load_w(k, st, di["w_fin"][l], 8, 2 * DFF, name="w_fin")
        w_fo, bw_fo = load_w(k, st, di["w_fout"][l], NFC, D, name="w_fo")
        gam, bgam = k.sb(st, [128, D], F32, "gam")
        k.dma("sp", gam[:], di["g_ffn"][l], bgam)
        cw, bcw = k.sb(st, [128, NFC, 3], F32, "cw")
        k.dma("sp", cw[:], di["f_cw"][l], bcw)
        cb, bcb = k.sb(st, [128, NFC], F32, "cb")
        k.dma("sp", cb[:], di["f_cb"][l], bcb)
        if last:
            gfin, bgfin = k.sb(st, [128, D], F32, "gfin")
            k.dma("sp", gfin[:], di["g_fin"], bgfin)
        halo, bhalo = k.sb(st, [128, NFC, 2], F32, "halo")
        k.memset("pool", halo[:], 0.0, [bhalo])
        xt = [k.sb(st, [128, NSB, D], F32, "xt") for _ in range(2)]
        junk, bjunk = k.sb(st, [128, D], BF16, "junk")
        ss, bss = k.sb(st, [128, 2 * NSB], F32, "ss")
        ms, bms = k.sb(st, [128, 2 * NSB], F32, "ms")
        sd, bsd = k.sb(st, [128, 2 * NSB], F32, "sd")
        rstd, brstd = k.sb(st, [128, 2 * NSB], F32, "rstd")
        hh = [k.sb(st, [128, D], BF16, "h") for _ in range(2)]
        hTs = [k.sb(st, [128, 8, TT], BF16, "hT") for _ in range(2)]
        a_sb = [k.sb(st, [128, TT + 2], F32, "a_sb") for _ in range(2)]
        cv = [k.sb(st, [128, TT], F32, "cv") for _ in range(2)]
        gl = [k.sb(st, [128, TT], F32, "gl") for _ in range(2)]
        actTs = [k.sb(st, [128, NFC, TT], BF16, "actT") for _ in range(2)]
        xo = [k.sb(st, [128, NSB, D], F32, "xo") for _ in range(1)]

        def load_tile(i):
            t, b = xt[i % 2]
            k.dma("sp", t[:], sc["xmid"][i * TT:(i + 1) * TT, :].rearrange("(s p) d -> p s d", p=128), b)

        def rms(x_ap, bx, col, g_t, bg, out_ap, bout):
            k.act(junk[:], x_ap, AF.Square, r=[bx], w=[bjunk, bss], accum=ss[:, col:col + 1])
            k.ts("dve", ms[:, col:col + 1], ss[:, col:col + 1], 1.0 / D, ALU.mult, r=[bss], w=[bms], s2=EPS, op1=ALU.add)
            k.act(sd[:, col:col + 1], ms[:, col:col + 1], AF.Sqrt, r=[bms], w=[bsd])
            k.recip(rstd[:, col:col + 1], sd[:, col:col + 1], r=[bsd], w=[brstd])
            k.stt(out_ap, x_ap, rstd[:, col:col + 1], g_t[:], ALU.mult, ALU.mult, r=[bx, brstd, bg], w=[bout])

        def prep(i):
            x_t, bx = xt[i % 2]
            hT, bhT = hTs[i % 2]
            for s_ in range(NSB):
                h_t, bh = hh[s_ % 2]
                rms(x_t[:, s_, :], bx, s_, gam, bgam, h_t[:], bh)
                ps, bp = ring.get()
                pbf = ps[:].bitcast(BF16)
                for c in range(8):
                    k.tr(pbf[:, c * 128:(c + 1) * 128], h_t[:, c * 128:(c + 1) * 128], ident[:], r=[bh, bid], w=[bp])
                k.copy("act", hT[:, :, s_ * 128:(s_ + 1) * 128], pbf.rearrange("p (c t) -> p c t", c=8), r=[bp], w=[bhT])

        def inproj(i):
            hT, bhT = hTs[i % 2]
            actT, bactT = actTs[i % 2]
            pend = []
            for fc in range(NFC + 1):
                if fc == NFC:
                    pend.pop(0)()
                    break
                psa, bpa = ring.get()
                for c in range(8):
                    k.mm(psa[:, 0:TT], w_fin[:, c, fc * 128:(fc + 1) * 128], hT[:, c, :], c == 0, c == 7, r=[bw_fin[c], bhT], w=[bpa])
                psb, bpb = ring.get()
                for c in range(8):
                    k.mm(psb[:, 0:TT], w_fin[:, c, DFF + fc * 128:DFF + (fc + 1) * 128], hT[:, c, :], c == 0, c == 7, r=[bw_fin[c], bhT], w=[bpb])
                a_t, ba = a_sb[fc % 2]
                c_t, bc = cv[fc % 2]
                g_t, bg = gl[fc % 2]
                k.copy("pool", a_t[:, 0:2], halo[:, fc, :], r=[bhalo], w=[ba])
                k.copy("act", a_t[:, 2:2 + TT], psa[:, 0:TT], r=[bpa], w=[ba])
                k.copy("pool", halo[:, fc, :], a_t[:, TT:TT + 2], r=[ba], w=[bhalo])
                k.act(c_t[:], psa[:, 0:TT], AF.Identity, r=[bpa, bcw, bcb], w=[bc], scale=cw[:, fc, 2:3], bias=cb[:, fc:fc + 1])
                k.stt(c_t[:], a_t[:, 1:1 + TT], cw[:, fc, 1:2], c_t[:], ALU.mult, ALU.add, r=[ba, bcw, bc], w=[bc])
                k.stt(c_t[:], a_t[:, 0:TT], cw[:, fc, 0:1], c_t[:], ALU.mult, ALU.add, r=[ba, bcw, bc], w=[bc])

                def stage2(fc=fc, c_t=c_t, bc=bc, g_t=g_t, bg=bg, psb=psb, bpb=bpb):
                    k.act(g_t[:], c_t[:], AF.Gelu_apprx_tanh, r=[bc], w=[bg])
                    k.tt("dve", actT[:, fc, :], psb[:, 0:TT], g_t[:], ALU.mult, r=[bpb, bg], w=[bactT])
                pend.append(stage2)
                if len(pend) > 1:
                    pend.pop(0)()

        def outproj(i):
            x_t, bx = xt[i % 2]
            actT, bactT = actTs[i % 2]
            xo_t, bxo = xo[0]
            for s_ in range(NSB):
                for half in range(2):
                    ps, bp = ring.get()
                    for fc in range(NFC):
                        k.mm(ps[:, 0:512], actT[:, fc, s_ * 128:(s_ + 1) * 128], w_fo[:, fc, half * 512:(half + 1) * 512], fc == 0, fc == NFC - 1,
                             r=[bactT, bw_fo[fc]], w=[bp])
                    k.tt("dve", xo_t[:, s_, half * 512:(half + 1) * 512], ps[:, 0:512], x_t[:, s_, half * 512:(half + 1) * 512], ALU.add, r=[bp, bx], w=[bxo])
            if last:
                for s_ in range(NSB):
                    rms(xo_t[:, s_, :], bxo, NSB + s_, gfin, bgfin, xo_t[:, s_, :], bxo)
            k.dma("sp", dst[i * TT:(i + 1) * TT, :].rearrange("(s p) d -> p s d", p=128), xo_t[:], bxo)

        load_tile(0)
        if NTT > 1:
            load_tile(1)
        prep(0)
        for i in range(NTT):
            inproj(i)
            if i + 1 < NTT:
                prep(i + 1)
            outproj(i)
            if i + 2 < NTT:
                load_tile(i + 2)
    P.barrier()


INPUT_SHAPES = None


def build(S, L, TS=128, dbg=False, phases=("p1", "p2", "p3", "p4")):
    nc = bass.Bass("TRN2", target_bir_lowering=False)
    NT = S // 128
    di = {}

    def din(name, shape):
        di[name] = nc.dram_tensor(name, list(shape), F32, kind="ExternalInput").ap()
    din("x", [S, D])
    for nm, shp in (("g_mix", [L, 128, D]), ("g_ffn", [L, 128, D]), ("g_fin", [128, D]),
                    ("w_in", [L, D, INW]), ("w_sw", [L, D, 896]),
                    ("s_are", [L, 128, 16]), ("s_aim", [L, 128, 16]), ("s_ldt", [L, 128, 16]),
                    ("s_bre", [L, 128, 16, 128]), ("s_bim", [L, 128, 16, 128]),
                    ("s_cre", [L, 128, 16, 128]), ("s_cim", [L, 128, 16, 128]), ("s_d", [L, 128, 4]),
                    ("w_glu", [L, 512, 1024]), ("w_bssm", [L, 512, 1024]), ("w_bnsa", [L, 512, 1024]),
                    ("w_out", [L, D, D]),
                    ("c_pek", [L, 64, 32]), ("c_w1k", [L, 2048, 256]), ("c_b1k", [L, 128, 2]), ("c_w2k", [L, 256, 64]),
                    ("c_pev", [L, 64, 32]), ("c_w1v", [L, 2048, 256]), ("c_b1v", [L, 128, 2]), ("c_w2v", [L, 256, 64]),
                    ("w_fin", [L, D, 2 * DFF]), ("w_fout", [L, DFF, D]), ("f_cw", [L, 128, NFC, 3]), ("f_cb", [L, 128, NFC]),
                    ("c_cos", [128, S]), ("c_sin", [128, S]), ("c_ident", [128, 128]), ("c_tau", [128, TS]),
                    ("c_caus", [128, 128]), ("c_low", [128, 128]), ("c_mgen", [128, 1024]), ("c_mt", [128, 16, 128]),
                    ("c_g", [128, 256]), ("c_ind", [64, S])):
        din(nm, shp)
    out = nc.dram_tensor("out", [S, D], F32, kind="ExternalOutput").ap()
    skind = "ExternalOutput" if dbg else "Internal"
    sc = {}

    def scr(name, shape, dt):
        sc[name] = nc.dram_tensor(name, list(shape), dt, kind=skind).ap()
    scr("qT", [512, S], BF16)
    for nm in ("kcT", "vcT", "ksT", "kwT"):
        scr(nm, [128, S], BF16)
    scr("vs", [S, 2, 65], BF16)
    scr("vw", [S, 2, 65], BF16)
    scr("sgTok", [S, 24], F32)
    scr("sgnT", [1024, S], BF16)
    scr("gssT", [1024, S], BF16)
    scr("xmid", [S, D], F32)
    if dbg:
        scr("dbg_o", [S, 512], F32)
        scr("dbg_kc", [2, 64, S // 16], BF16)
        scr("dbg_vc", [128, S // 2048, 2, 65], BF16)
    scr("x1", [S, D], F32)
    with ExitStack() as st:
        P = Prog(nc)
        k = K(nc, P)
        cst = {}
        ident, bid = k.sb(st, [128, 128], BF16, "ident")
        k.dma("pool", ident[:], di["c_ident"], bid)
        cst["ident"] = (ident, bid)
        x_src = di["x"]
        for l in range(L):
            last = l == L - 1
            if "p1" in phases:
                phase1(k, l, S, TS, x_src, di, sc, cst)
            if "p2" in phases:
                pers = ExitStack()
                cmp_t = phase2(k, pers, l, S, di, sc, cst)
            if "p3" in phases:
                phase3(k, l, S, x_src, di, sc, cst, cmp_t)
            if "p2" in phases:
                pers.close()
                P.barrier()
            if "p4" in phases:
                phase4(k, l, S, di, sc, cst, out if last else sc["x1"], last)
            x_src = sc["x1"]
        P.barrier()
        P.emit(st)
    return nc


_NC_CACHE = {}


def kernel(**inputs):
    S, L, NCORES = 8192, 2, 8
    inp = {k_: np.asarray(v) for k_, v in inputs.items()}
    hl = host_layout(inp, L)
    hc = host_consts(S, 128)
    common = {}
    common.update(hl)
    common.update(hc)
    common = {k_: np.ascontiguousarray(v, dtype=np.float32) for k_, v in common.items()}
    if "nc" not in _NC_CACHE:
        _NC_CACHE["nc"] = build(S, L)
    nc = _NC_CACHE["nc"]
    x = np.asarray(inp["x"], dtype=np.float32)
    in_maps = []
    for b in range(NCORES):
        m = dict(common)
        m["x"] = np.ascontiguousarray(x[b])
        in_maps.append(m)
    res = run_bass_kernel_spmd(nc, in_maps, core_ids=list(range(NCORES)))
    return np.stack([np.asarray(r["out"], dtype=np.float32) for r in res.results], axis=0)
```

```python
from contextlib import ExitStack
import numpy as np
import ml_dtypes
import concourse.bass as bass
import concourse.mybir as mybir
from concourse.bass_utils import run_bass_kernel_spmd

F32 = mybir.dt.float32
BF16 = mybir.dt.bfloat16
I32 = mybir.dt.int32
ALU = mybir.AluOpType
AF = mybir.ActivationFunctionType
AX = mybir.AxisListType

D = 1024
DFF = 2816
NFC = DFF // 128
INW = 3864
EPS = 1e-6
NEG = -30000.0
TWO_PI = float(2 * np.pi)
SIN_SCALE = TWO_PI * 0.999999


class Buf:
    __slots__ = ("name", "w", "rs", "sem", "cnt", "slot", "base", "uid")

    def __init__(self, name="b"):
        self.name = name
        self.w = None
        self.rs = {}
        self.sem = None
        self.cnt = 0
        self.slot = None
        self.base = 0
        self.uid = None


class Op:
    __slots__ = ("eng", "fn", "deps", "key", "val", "signal", "sigval", "dma", "slot", "semval")


ENGS = ("pe", "act", "dve", "pool", "sp")


class Prog:
    def __init__(self, nc):
        self.nc = nc
        self.ops = {e: [] for e in ENGS}
        self.seen = {e: {} for e in ENGS}
        self.dma_bufs = []
        self.last = {}
        self.slot_base = []
        self.free_slots = []
        self.live = []
        self.uid = 0

    def _get_slot(self, buf):
        if self.free_slots:
            sl = self.free_slots.pop()
        else:
            sl = len(self.slot_base)
            self.slot_base.append(0)
        self.uid += 1
        buf.sem = True
        buf.slot = sl
        buf.base = self.slot_base[sl]
        buf.cnt = 0
        buf.uid = self.uid
        self.live.append(buf)

    def barrier(self):
        lasts = list(self.last.values())
        self._barrier_ops(lasts)
        for b in self.live:
            self.slot_base[b.slot] = b.base + b.cnt
            self.free_slots.append(b.slot)
            b.sem = None
        self.live = []
        self.last = {kk: v for kk, v in self.last.items() if not isinstance(kk, tuple)}

    def _barrier_ops(self, lasts):
        for e in ENGS:
            o = Op()
            o.eng = e
            o.fn = None
            o.deps = []
            o.signal = False
            o.sigval = None
            o.dma = None
            o.key = e
            o.val = len(self.ops[e])
            for d in lasts:
                if d.key == e:
                    continue
                if self.seen[e].get(d.key, -1) >= d.val:
                    continue
                self.seen[e][d.key] = d.val
                o.deps.append(d)
            self.ops[e].append(o)

    def _dep(self, eng, d, deps, same_ok):
        if d is None:
            return
        key = d.key
        if key == eng:
            if eng == "pe" or same_ok:
                return
        if self.seen[eng].get(key, -1) >= d.val:
            return
        self.seen[eng][key] = d.val
        deps.append(d)

    def op(self, eng, fn, reads=(), writes=(), dma=None):
        o = Op()
        o.eng = eng
        o.fn = fn
        o.deps = []
        o.signal = False
        o.sigval = None
        o.dma = dma
        writes = [b for b in writes if b is not None]
        reads = [b for b in reads if b is not None]
        o.slot = None
        o.semval = None
        if dma is not None:
            if dma.sem is None:
                self._get_slot(dma)
            dma.cnt += 1
            o.key = ("dma", dma.uid)
            o.val = dma.cnt
            o.slot = dma.slot
            o.semval = 16 * (dma.base + dma.cnt)
            if dma not in writes:
                writes.append(dma)
            reads = [b for b in reads if b is not dma]
        else:
            o.key = eng
            o.val = len(self.ops[eng])
        for b in reads:
            self._dep(eng, b.w, o.deps, False)
        for b in writes:
            self._dep(eng, b.w, o.deps, True)
            for r in b.rs.values():
                self._dep(eng, r, o.deps, True)
        for b in reads:
            b.rs[o.key] = o
        for b in writes:
            b.w = o
            b.rs = {}
        self.ops[eng].append(o)
        self.last[o.key] = o
        return o

    def emit(self, stack):
        nc = self.nc
        for e in ENGS:
            for o in self.ops[e]:
                for d in o.deps:
                    if d.dma is None:
                        d.signal = True
        for e in ENGS:
            c = 0
            for o in self.ops[e]:
                if o.dma is None and o.signal:
                    c += 1
                    o.sigval = c
        esem = {}
        for e in ("pe", "act", "dve", "pool"):
            esem[e] = stack.enter_context(nc.semaphore("s_" + e))
        dsem = [stack.enter_context(nc.semaphore("d%d" % i)) for i in range(len(self.slot_base))]
        block = stack.enter_context(nc.Block())
        prog = self

        def run(name, eng):
            for o in prog.ops[name]:
                for d in o.deps:
                    if d.dma is not None:
                        eng.wait_ge(dsem[d.slot], d.semval)
                    else:
                        eng.wait_ge(esem[d.key], d.sigval)
                if o.fn is None:
                    continue
                ins = o.fn(eng)
                if o.dma is not None:
                    ins.then_inc(dsem[o.slot], 16)
                elif o.signal:
                    ins.then_inc(esem[name], 1)

        @block.sync
        def _(eng):
            run("sp", eng)

        @block.scalar
        def _(eng):
            run("act", eng)

        @block.vector
        def _(eng):
            run("dve", eng)

        @block.gpsimd
        def _(eng):
            run("pool", eng)

        @block.tensor
        def _(eng):
            run("pe", eng)


class K:
    def __init__(self, nc, P):
        self.nc = nc
        self.P = P
        self.n = 0

    def name(self, s):
        self.n += 1
        return "%s_%d" % (s, self.n)

    def sb(self, st, shape, dt=F32, name="t"):
        t = st.enter_context(self.nc.sbuf_tensor(self.name(name), list(shape), dt))
        return t, Buf(name)

    def dma(self, eng, out, in_, buf, reads=(), writes=()):
        self.P.op(eng, lambda e: e.dma_start(out=out, in_=in_), reads=reads, writes=writes, dma=buf)

    def mm(self, out, lhsT, rhs, start, stop, r, w):
        self.P.op("pe", lambda e: e.matmul(out, lhsT=lhsT, rhs=rhs, start=start, stop=stop), reads=r, writes=w)

    def tr(self, out, in_, ident, r, w):
        self.P.op("pe", lambda e: e.transpose(out=out, in_=in_, identity=ident), reads=r, writes=w)

    def act(self, out, in_, func, r, w, bias=None, scale=None, accum=None):
        kw = {}
        if bias is not None:
            kw["bias"] = bias
        if scale is not None:
            kw["scale"] = scale
        if accum is not None:
            kw["accum_out"] = accum
        self.P.op("act", lambda e: e.activation(out=out, in_=in_, func=func, **kw), reads=r, writes=w)

    def tt(self, eng, out, in0, in1, op, r, w):
        self.P.op(eng, lambda e: e.tensor_tensor(out=out, in0=in0, in1=in1, op=op), reads=r, writes=w)

    def ts(self, eng, out, in0, s1, op0, r, w, s2=None, op1=None):
        if op1 is None:
            self.P.op(eng, lambda e: e.tensor_scalar(out=out, in0=in0, scalar1=s1, scalar2=None, op0=op0), reads=r, writes=w)
        else:
            self.P.op(eng, lambda e: e.tensor_scalar(out=out, in0=in0, scalar1=s1, scalar2=s2, op0=op0, op1=op1), reads=r, writes=w)

    def stt(self, out, in0, scalar, in1, op0, op1, r, w):
        self.P.op("dve", lambda e: e.scalar_tensor_tensor(out=out, in0=in0, scalar=scalar, in1=in1, op0=op0, op1=op1), reads=r, writes=w)

    def copy(self, eng, out, in_, r, w):
        if eng == "act":
            self.P.op("act", lambda e: e.activation(out=out, in_=in_, func=AF.Copy), reads=r, writes=w)
        else:
            self.P.op(eng, lambda e: e.tensor_copy(out=out, in_=in_), reads=r, writes=w)

    def memset(self, eng, ap, val, w):
        self.P.op(eng, lambda e: e.memset(ap, val), writes=w)

    def scan(self, out, d0, d1, init, r, w):
        self.P.op("dve", lambda e: e.tensor_tensor_scan(out=out, data0=d0, data1=d1, initial=init, op0=ALU.mult, op1=ALU.add), reads=r, writes=w)

    def recip(self, out, in_, r, w):
        self.P.op("dve", lambda e: e.reciprocal(out=out, in_=in_), reads=r, writes=w)


class PsumRing:
    def __init__(self, k, st, n=8):
        self.banks = []
        for i in range(n):
            t = st.enter_context(k.nc.psum_tensor(k.name("ps"), [128, 512], F32))
            self.banks.append((t, Buf("ps%d" % i)))
        self.i = 0

    def get(self):
        t, b = self.banks[self.i % len(self.banks)]
        self.i += 1
        return t, b


def _swap_halves(w):
    sh = w.shape
    w4 = w.reshape(sh[:-1] + (sh[-1] // 64, 2, 32))
    return np.ascontiguousarray(w4[..., ::-1, :]).reshape(sh)


def host_consts(S, TS):
    c = {}
    inv = (10000.0 ** (-np.arange(0, 64, 2, dtype=np.float32) / np.float32(64))).astype(np.float32)
    ang = (np.arange(S, dtype=np.float32)[:, None] * inv[None, :]).astype(np.float32)
    cs, sn = np.cos(ang).astype(np.float32), np.sin(ang).astype(np.float32)
    cosT = np.concatenate([cs.T, cs.T], 0)
    sinT = np.concatenate([-sn.T, sn.T], 0)
    c["c_cos"] = np.ascontiguousarray(np.concatenate([cosT, cosT], 0))
    c["c_sin"] = np.ascontiguousarray(np.concatenate([sinT, sinT], 0))
    c["c_ident"] = np.eye(128, dtype=np.float32)
    c["c_tau"] = np.ascontiguousarray(np.broadcast_to(np.arange(TS, dtype=np.float32)[None, :], (128, TS)))
    k = np.arange(128)[:, None]
    q = np.arange(128)[None, :]
    c["c_caus"] = np.where(k <= q, 0.0, NEG).astype(np.float32)
    c["c_low"] = np.where(k > q, 0.0, NEG).astype(np.float32)
    cc = np.arange(1024)[None, :]
    qi = np.arange(128)[:, None]
    c["c_mgen"] = np.where(16 * (cc - 512) + 31 <= qi, 0.0, NEG).astype(np.float32)
    m = np.arange(16)[None, :, None]
    ni = np.arange(128)[:, None, None]
    qq = np.arange(128)[None, None, :]
    c["c_mt"] = np.where(16 * ni + 31 <= 128 * m + qq, 0.0, NEG).astype(np.float32)
    rel = np.arange(256)[None, :] - 126
    cur = (np.arange(128)[:, None] >= 64).astype(np.int64)
    g = np.where(rel > cur, -1e30, 0.0) + np.where((rel == cur) | (rel == cur - 1), 1e4, 0.0)
    c["c_g"] = g.astype(np.float32)
    NT = S // 128
    j = np.arange(128)[:, None, None]
    i = np.arange(NT)[None, :, None]
    kk = np.arange(128)[None, None, :]
    c["c_e"] = (j == 2 * i + (kk >= 64)).astype(np.float32)
    r_ = np.arange(64)[:, None]
    cidx = np.arange(S)[None, :]
    c["c_ind"] = (((cidx // 64) % 64) == r_).astype(np.float32)
    return c


def host_layout(inp, L):
    o = {}
    f = np.float32
    o["g_mix"] = np.ascontiguousarray(np.broadcast_to(inp["norm_mix"][:, None, :], (L, 128, D))).astype(f)
    o["g_ffn"] = np.ascontiguousarray(np.broadcast_to(inp["norm_ffn"][:, None, :], (L, 128, D))).astype(f)
    o["g_fin"] = np.ascontiguousarray(np.broadcast_to(inp["norm_final"][None, :], (128, D))).astype(f)
    w_in = inp["w_in"]
    o["w_in"] = w_in
    sw = np.concatenate([_swap_halves(w_in[:, :, 512:1024]), _swap_halves(w_in[:, :, 1024:1152]),
                         _swap_halves(w_in[:, :, 1280:1408]), _swap_halves(w_in[:, :, 1536:1664])], axis=-1)
    o["w_sw"] = np.ascontiguousarray(sw)

    def pair(a):
        return np.ascontiguousarray(a.reshape(L, 16, 2, 64).transpose(0, 2, 3, 1).reshape(L, 128, 16))
    o["s_are"] = pair(inp["ssm_a_re"])
    o["s_aim"] = pair(inp["ssm_a_im"])
    o["s_ldt"] = pair(np.broadcast_to(inp["ssm_log_dt"][:, :, None], (L, 32, 64)))
    for nm, src in (("s_bre", "ssm_b_re"), ("s_bim", "ssm_b_im")):
        b = inp[src].reshape(L, 16, 2, 64, 16)
        pad = np.zeros((L, 8, 16, 16, 2, 64), f)
        for j in range(16):
            for gl in range(2):
                pad[:, 2 * (j % 4) + gl, :, j, gl, :] = b[:, j, gl].transpose(0, 2, 1)
        o[nm] = pad.reshape(L, 128, 16, 128)
    for nm, src in (("s_cre", "ssm_c_re"), ("s_cim", "ssm_c_im")):
        cmat = inp[src].reshape(L, 16, 2, 16, 64)
        pad = np.zeros((L, 2, 64, 16, 8, 16), f)
        for j in range(16):
            for gl in range(2):
                pad[:, gl, :, j, 2 * (j % 4) + gl, :] = cmat[:, j, gl].transpose(0, 2, 1)
        o[nm] = pad.reshape(L, 128, 16, 128)
    o["s_d"] = np.ascontiguousarray(inp["ssm_d"].reshape(L, 4, 128).transpose(0, 2, 1))
    o["w_glu"] = inp["ssm_w_glu"]
    o["w_bssm"] = inp["w_branch_ssm"]
    o["w_bnsa"] = inp["w_branch_nsa"]
    o["w_out"] = inp["w_out"]
    for t in ("k", "v"):
        o["c_pe" + t] = np.ascontiguousarray(inp["cmp_pe_" + t].transpose(0, 2, 1))
        o["c_w1" + t] = inp["cmp_w1_" + t]
        o["c_b1" + t] = np.ascontiguousarray(inp["cmp_b1_" + t].reshape(L, 2, 128).transpose(0, 2, 1))
        o["c_w2" + t] = inp["cmp_w2_" + t]
    o["w_fin"] = inp["w_ffn_in"]
    o["w_fout"] = inp["w_ffn_out"]
    o["f_cw"] = np.ascontiguousarray(inp["ffn_conv_w"].reshape(L, 3, NFC, 128).transpose(0, 3, 2, 1))
    o["f_cb"] = np.ascontiguousarray(inp["ffn_conv_b"].reshape(L, NFC, 128).transpose(0, 2, 1))
    return o


def load_w(k, st, src2d, nch, ncols, prow=128, eng="pool", name="w"):
    t, _ = k.sb(st, [prow, nch, ncols], BF16, name)
    bufs = []
    for c in range(nch):
        b = Buf(name)
        k.dma(eng, t[:, c, :], src2d[c * prow:(c + 1) * prow, :], b)
        bufs.append(b)
    return t, bufs


def sincos(k, st, arg, n, out_sin=None, out_cos=None, rb=(), wsin=None, wcos=None):
    for (dst, off, wb) in ((out_sin, 0.0, wsin), (out_cos, 0.25, wcos)):
        if dst is None:
            continue
        a2, ba2 = k.sb(st, [128, n], F32, "sc_a")
        ti, bti = k.sb(st, [128, n], I32, "sc_i")
        tf, btf = k.sb(st, [128, n], F32, "sc_f")
        k.ts("dve", a2[:], arg, off, ALU.add, r=list(rb), w=[ba2])
        k.copy("dve", ti[:], a2[:], r=[ba2], w=[bti])
        k.copy("dve", tf[:], ti[:], r=[bti], w=[btf])
        k.tt("dve", a2[:], a2[:], tf[:], ALU.subtract, r=[ba2, btf], w=[ba2])
        k.act(dst, a2[:], AF.Sin, r=[ba2], w=[wb], scale=SIN_SCALE)


def phase1(k, l, S, TS, x_src, di, sc, cst):
    P = k.P
    TT = TS
    NTT = S // TT
    with ExitStack() as st:
        ring = PsumRing(k, st)
        ident, bid = cst["ident"]
        gam, bgam = k.sb(st, [128, D], F32, "gam")
        k.dma("sp", gam[:], di["g_mix"][l], bgam)
        dvec, bdvec = k.sb(st, [128, 4], F32, "dvec")
        k.dma("sp", dvec[:], di["s_d"][l], bdvec)
        RFre, bRFre = k.sb(st, [128, 16, TS], F32, "RFre")
        RFim, bRFim = k.sb(st, [128, 16, TS], F32, "RFim")
        COSb, bCOSb = k.sb(st, [128, 16, TS], BF16, "COSb")
        SINb, bSINb = k.sb(st, [128, 16, TS], BF16, "SINb")
        dec, bdec = k.sb(st, [128, 16], F32, "dec")
        cT, bcT = k.sb(st, [128, 16], F32, "cT")
        sT, bsT = k.sb(st, [128, 16], F32, "sT")
        nsT, bnsT = k.sb(st, [128, 16], F32, "nsT")
        with ExitStack() as s2:
            are, bare = k.sb(s2, [128, 16], F32, "are")
            aim, baim = k.sb(s2, [128, 16], F32, "aim")
            ldt, bldt = k.sb(s2, [128, 16], F32, "ldt")
            tau, btau = k.sb(s2, [128, TS], F32, "tau")
            k.dma("sp", are[:], di["s_are"][l], bare)
            k.dma("sp", aim[:], di["s_aim"][l], baim)
            k.dma("sp", ldt[:], di["s_ldt"][l], bldt)
            k.dma("sp", tau[:], di["c_tau"], btau)
            dt_, bdt = k.sb(s2, [128, 16], F32, "dt")
            k.act(dt_[:], ldt[:], AF.Exp, r=[bldt], w=[bdt])
            rho, brho = k.sb(s2, [128, 16], F32, "rho")
            thn, bthn = k.sb(s2, [128, 16], F32, "thn")
            k.tt("dve", rho[:], are[:], dt_[:], ALU.mult, r=[bare, bdt], w=[brho])
            k.tt("dve", thn[:], aim[:], dt_[:], ALU.mult, r=[baim, bdt], w=[bthn])
            k.ts("dve", thn[:], thn[:], 1.0 / TWO_PI, ALU.mult, r=[bthn], w=[bthn])
            k.act(dec[:], rho[:], AF.Exp, r=[brho], w=[bdec])
            s1, bs1 = k.sb(s2, [128, 16], F32, "s1")
            c1, bc1 = k.sb(s2, [128, 16], F32, "c1")
            sincos(k, s2, thn[:], 16, s1[:], c1[:], rb=[bthn], wsin=bs1, wcos=bc1)
            abre, babre = k.sb(s2, [128, 16], F32, "abre")
            abim, babim = k.sb(s2, [128, 16], F32, "abim")
            k.tt("dve", abre[:], dec[:], c1[:], ALU.mult, r=[bdec, bc1], w=[babre])
            k.ts("dve", abre[:], abre[:], -1.0, ALU.add, r=[babre], w=[babre])
            k.tt("dve", abim[:], dec[:], s1[:], ALU.mult, r=[bdec, bs1], w=[babim])
            den, bden = k.sb(s2, [128, 16], F32, "den")
            t0, bt0 = k.sb(s2, [128, 16], F32, "t0")
            k.tt("dve", den[:], are[:], are[:], ALU.mult, r=[bare], w=[bden])
            k.tt("dve", t0[:], aim[:], aim[:], ALU.mult, r=[baim], w=[bt0])
            k.tt("dve", den[:], den[:], t0[:], ALU.add, r=[bden, bt0], w=[bden])
            k.recip(den[:], den[:], r=[bden], w=[bden])
            fre, bfre = k.sb(s2, [128, 16], F32, "fre")
            fim, bfim = k.sb(s2, [128, 16], F32, "fim")
            t1, bt1 = k.sb(s2, [128, 16], F32, "t1")
            k.tt("dve", fre[:], abre[:], are[:], ALU.mult, r=[babre, bare], w=[bfre])
            k.tt("dve", t1[:], abim[:], aim[:], ALU.mult, r=[babim, baim], w=[bt1])
            k.tt("dve", fre[:], fre[:], t1[:], ALU.add, r=[bfre, bt1], w=[bfre])
            k.tt("dve", fre[:], fre[:], den[:], ALU.mult, r=[bfre, bden], w=[bfre])
            k.tt("dve", fim[:], abim[:], are[:], ALU.mult, r=[babim, bare], w=[bfim])
            k.tt("dve", t1[:], abre[:], aim[:], ALU.mult, r=[babre, baim], w=[bt1])
            k.tt("dve", fim[:], fim[:], t1[:], ALU.subtract, r=[bfim, bt1], w=[bfim])
            k.tt("dve", fim[:], fim[:], den[:], ALU.mult, r=[bfim, bden], w=[bfim])
            aT, baT = k.sb(s2, [128, 16], F32, "aT")
            k.ts("dve", aT[:], thn[:], float(TS), ALU.mult, r=[bthn], w=[baT])
            sincos(k, s2, aT[:], 16, sT[:], cT[:], rb=[baT], wsin=bsT, wcos=bcT)
            k.ts("dve", nsT[:], sT[:], -1.0, ALU.mult, r=[bsT], w=[bnsT])
            ANG, bANG = k.sb(s2, [128, 16, TS], F32, "ANG")
            SINf, bSINf = k.sb(s2, [128, 16 * TS], F32, "SINf")
            COSf, bCOSf = k.sb(s2, [128, 16 * TS], F32, "COSf")
            for j in range(16):
                k.ts("dve", ANG[:, j, :], tau[:], thn[:, j:j + 1], ALU.mult, r=[btau, bthn], w=[bANG])
            sincos(k, s2, ANG[:].rearrange("p j t -> p (j t)"), 16 * TS, SINf[:], COSf[:], rb=[bANG], wsin=bSINf, wcos=bCOSf)
            SIN3 = SINf[:].rearrange("p (j t) -> p j t", j=16)
            COS3 = COSf[:].rearrange("p (j t) -> p j t", j=16)
            tmp, btmp = k.sb(s2, [128, TS], F32, "tmp")
            for j in range(16):
                k.ts("dve", tmp[:], SIN3[:, j, :], fim[:, j:j + 1], ALU.mult, r=[bSINf, bfim], w=[btmp])
                k.stt(RFre[:, j, :], COS3[:, j, :], fre[:, j:j + 1], tmp[:], ALU.mult, ALU.add, r=[bCOSf, bfre, btmp], w=[bRFre])
                k.ts("dve", tmp[:], SIN3[:, j, :], fre[:, j:j + 1], ALU.mult, r=[bSINf, bfre], w=[btmp])
                k.stt(RFim[:, j, :], COS3[:, j, :], fim[:, j:j + 1], tmp[:], ALU.mult, ALU.subtract, r=[bCOSf, bfim, btmp], w=[bRFim])
            k.copy("dve", COSb[:].rearrange("p j t -> p (j t)"), COSf[:], r=[bCOSf], w=[bCOSb])
            k.copy("dve", SINb[:].rearrange("p j t -> p (j t)"), SINf[:], r=[bSINf], w=[bSINb])
        P.barrier()
        w_in, bw_in = load_w(k, st, di["w_in"][l], 8, INW, name="w_in")
        w_sw, bw_sw = load_w(k, st, di["w_sw"][l], 8, 896, name="w_sw")
        w_glu, bw_glu = load_w(k, st, di["w_glu"][l], 4, 1024, name="w_glu")
        w_bs, bw_bs = load_w(k, st, di["w_bssm"][l], 4, 1024, name="w_bs")
        bre, bbre = load_w(k, st, di["s_bre"][l].rearrange("p j m -> p (j m)"), 1, 2048, name="bre")
        bim, bbim = load_w(k, st, di["s_bim"][l].rearrange("p j m -> p (j m)"), 1, 2048, name="bim")
        cre, bcre = load_w(k, st, di["s_cre"][l].rearrange("p j m -> p (j m)"), 1, 2048, name="cre")
        cim, bcim = load_w(k, st, di["s_cim"][l].rearrange("p j m -> p (j m)"), 1, 2048, name="cim")
        k.ts("pool", cim[:, 0, :], cim[:, 0, :], -1.0, ALU.mult, r=[bcim[0]], w=[bcim[0]])
        xt = [k.sb(st, [128, TT // 128, D], F32, "xt") for _ in range(2)]
        cosr = [k.sb(st, [128, TT], F32, "cosr") for _ in range(2)]
        sinr = [k.sb(st, [128, TT], F32, "sinr") for _ in range(2)]
        NSB = TT // 128
        ss, bss = k.sb(st, [128, NSB], F32, "ss")
        ms, bms = k.sb(st, [128, NSB], F32, "ms")
        sd, bsd = k.sb(st, [128, NSB], F32, "sd")
        rstd, brstd = k.sb(st, [128, NSB], F32, "rstd")
        hh = [k.sb(st, [128, D], BF16, "h") for _ in range(1)]
        hT, bhT = k.sb(st, [128, 8, TT], BF16, "hT")
        uT2 = [k.sb(st, [128, 4, TT], BF16, "uT") for _ in range(3)]
        qTs, bqTs = k.sb(st, [128, 4, TT], BF16, "qTs")
        kvs = {nm: k.sb(st, [128, TT], BF16, nm) for nm in ("kcT", "vcT", "ksT", "kwT")}
        sgs2 = [k.sb(st, [128, 8, TT], BF16, "sgs") for _ in range(3)]
        sgn, bsgn = k.sb(st, [128, 8, TT], BF16, "sgn")
        sg, bsg = k.sb(st, [128, TT // 128, 24], F32, "sg")
        vsel, bvsel = k.sb(st, [128, NSB, 2, 65], BF16, "vsel")
        vwin, bvwin = k.sb(st, [128, NSB, 2, 65], BF16, "vwin")
        k.memset("pool", vsel[:], 1.0, [bvsel])
        k.memset("pool", vwin[:], 1.0, [bvwin])
        tmps = [k.sb(st, [128, TT], F32, "tmp") for _ in range(10)]
        tmpi = [0]

        def gettmp():
            t = tmps[tmpi[0] % len(tmps)]
            tmpi[0] += 1
            return t
        bsc = [k.sb(st, [128, TT], F32, "bsc") for _ in range(4)]
        wall, _ = k.sb(st, [128, 16, 2, TT], F32, "wall")
        bwall = [Buf("wall") for _ in range(16)]
        cwt = [k.sb(st, [128, 16], F32, "cwt") for _ in range(3)]
        xre2 = [k.sb(st, [128, 16, TT], BF16, "xre") for _ in range(2)]
        nxim2 = [k.sb(st, [128, 16, TT], BF16, "nxim") for _ in range(2)]
        car, bcar = k.sb(st, [128, 2, 16], F32, "car")
        k.memset("dve", car[:], 0.0, [bcar])
        ypre, bypre = k.sb(st, [128, TT], F32, "ypre")
        yT, byT = k.sb(st, [128, 4, TT], BF16, "yT")
        sgz, bsgz = k.sb(st, [128, TT], F32, "sgz")
        zzT, bzzT = k.sb(st, [128, 4, TT], BF16, "zzT")
        gss, bgss = k.sb(st, [128, 8, TT], BF16, "gss")

        def load_tile(i):
            t, b = xt[i % 2]
            k.dma("sp", t[:], x_src[i * TT:(i + 1) * TT, :].rearrange("(s p) d -> p s d", p=128), b)
            k.dma("sp", cosr[i % 2][0][:], di["c_cos"][:, i * TT:(i + 1) * TT], cosr[i % 2][1])
            k.dma("sp", sinr[i % 2][0][:], di["c_sin"][:, i * TT:(i + 1) * TT], sinr[i % 2][1])

        def proj(wt, wb, col0, M=128):
            ps, bp = ring.get()
            for c in range(8):
                k.mm(ps[0:M, 0:TT], wt[:, c, col0:col0 + M], hT[:, c, :], c == 0, c == 7, r=[wb[c], bhT], w=[bp])
            return ps, bp

        def inproj_gen(i):
            uT, buT = uT2[i % 3]
            sgs, bsgs = sgs2[i % 3]
            x_t, bx = xt[i % 2]
            cos_t, bcos = cosr[i % 2]
            sin_t, bsin = sinr[i % 2]
            tok = slice(i * TT, (i + 1) * TT)
            for s_ in range(NSB):
                h_t, bh = hh[0]
                k.act(h_t[:], x_t[:, s_, :], AF.Square, r=[bx], w=[bh, bss], accum=ss[:, s_:s_ + 1])
                k.ts("dve", ms[:, s_:s_ + 1], ss[:, s_:s_ + 1], 1.0 / D, ALU.mult, r=[bss], w=[bms], s2=EPS, op1=ALU.add)
                k.act(sd[:, s_:s_ + 1], ms[:, s_:s_ + 1], AF.Sqrt, r=[bms], w=[bsd])
                k.recip(rstd[:, s_:s_ + 1], sd[:, s_:s_ + 1], r=[bsd], w=[brstd])
                k.stt(h_t[:], x_t[:, s_, :], rstd[:, s_:s_ + 1], gam[:], ALU.mult, ALU.mult, r=[bx, brstd, bgam], w=[bh])
            yield
            yield
            yield
            for s_ in range(NSB):
                h_t, bh = hh[0]
                ps, bp = ring.get()
                pbf = ps[:].bitcast(BF16)
                for c in range(8):
                    k.tr(pbf[:, c * 128:(c + 1) * 128], h_t[:, c * 128:(c + 1) * 128], ident[:], r=[bh, bid], w=[bp])
                k.copy("act", hT[:, :, s_ * 128:(s_ + 1) * 128], pbf.rearrange("p (c t) -> p c t", c=8), r=[bp], w=[bhT])
            yield
            yield
            for c4 in range(4):
                ps, bp = proj(w_in, bw_in, c4 * 128)
                k.copy("act", uT[:, c4, :], ps[:, 0:TT], r=[bp], w=[buT])
                yield
            def rope(col, swcol, dst, bdst):
                psA, bA = proj(w_in, bw_in, col)
                psB, bB = proj(w_sw, bw_sw, swcol)
                t1_, bt1_ = gettmp()
                t2_, bt2_ = gettmp()
                k.tt("dve", t1_[:], psA[:, 0:TT], cos_t[:], ALU.mult, r=[bA, bcos], w=[bt1_])
                k.tt("dve", t2_[:], psB[:, 0:TT], sin_t[:], ALU.mult, r=[bB, bsin], w=[bt2_])
                k.tt("pool", dst, t1_[:], t2_[:], ALU.add, r=[bt1_, bt2_], w=[bdst])
            for c in range(4):
                rope(512 + c * 128, c * 128, qTs[:, c, :], bqTs)
                yield
            rope(1024, 512, kvs["kcT"][0][:], kvs["kcT"][1])
            yield
            rope(1280, 640, kvs["ksT"][0][:], kvs["ksT"][1])
            yield
            rope(1536, 768, kvs["kwT"][0][:], kvs["kwT"][1])
            yield
            ps, bp = proj(w_in, bw_in, 1152)
            k.copy("act", kvs["vcT"][0][:], ps[:, 0:TT], r=[bp], w=[kvs["vcT"][1]])
            for c in range(8):
                ps, bp = proj(w_in, bw_in, 1816 + c * 128)
                k.act(sgs[:, c, :], ps[:, 0:TT], AF.Sigmoid, r=[bp], w=[bsgs])
                yield
            for c in range(8):
                ps, bp = proj(w_in, bw_in, 2840 + c * 128)
                k.act(sgn[:, c, :], ps[:, 0:TT], AF.Sigmoid, r=[bp], w=[bsgn])
                yield
            for s_ in range(NSB):
                ps, bp = ring.get()
                for c in range(8):
                    k.mm(ps[:, 0:24], hT[:, c, s_ * 128:(s_ + 1) * 128], w_in[:, c, 1792:1816], c == 0, c == 7, r=[bw_in[c], bhT], w=[bp])
                k.act(sg[:, s_, :], ps[:, 0:24], AF.Sigmoid, r=[bp], w=[bsg])
            for s_ in range(NSB):
                for (col, vt, bv) in ((1408, vsel, bvsel), (1664, vwin, bvwin)):
                    ps, bp = ring.get()
                    for c in range(8):
                        k.mm(ps[:, 0:128], hT[:, c, s_ * 128:(s_ + 1) * 128], w_in[:, c, col:col + 128], c == 0, c == 7, r=[bw_in[c], bhT], w=[bp])
                    k.copy("act", vt[:, s_, :, 0:64], ps[:, 0:128].rearrange("p (h d) -> p h d", h=2), r=[bp], w=[bv])
                    yield
            k.dma("sp", sc["qT"].rearrange("(c p) s -> p c s", p=128)[:, :, tok], qTs[:], bqTs)
            for nm in ("kcT", "vcT", "ksT", "kwT"):
                k.dma("sp", sc[nm][:, tok], kvs[nm][0][:], kvs[nm][1])
            k.dma("sp", sc["sgnT"].rearrange("(c p) s -> p c s", p=128)[:, :, tok], sgn[:], bsgn)
            k.dma("sp", sc["sgTok"][tok].rearrange("(s p) c -> p s c", p=128), sg[:], bsg)
            k.dma("sp", sc["vs"][tok].rearrange("(s p) h c -> p s h c", p=128), vsel[:], bvsel)
            k.dma("sp", sc["vw"][tok].rearrange("(s p) h c -> p s h c", p=128), vwin[:], bvwin)
        def s5_gen(i):
            uT, buT = uT2[i % 3]
            xre, bxre = xre2[i % 2]
            nxim, bnxim = nxim2[i % 2]
            def stageA(j):
                c4 = j // 4
                psr, bpr = ring.get()
                psi, bpi = ring.get()
                k.mm(psr[:, 0:TT], bre[:, 0, j * 128:(j + 1) * 128], uT[:, c4, :], True, True, r=[bbre[0], buT], w=[bpr])
                k.mm(psi[:, 0:TT], bim[:, 0, j * 128:(j + 1) * 128], uT[:, c4, :], True, True, r=[bbim[0], buT], w=[bpi])
                b_re, bb_re = bsc[(2 * j) % 4]
                b_im, bb_im = bsc[(2 * j + 1) % 4]
                t1_, bt1_ = gettmp()
                t2_, bt2_ = gettmp()
                k.tt("dve", t1_[:], psr[:, 0:TT], RFre[:, j, :], ALU.mult, r=[bpr, bRFre], w=[bt1_])
                k.tt("dve", t2_[:], psi[:, 0:TT], RFim[:, j, :], ALU.mult, r=[bpi, bRFim], w=[bt2_])
                k.tt("pool", b_re[:], t1_[:], t2_[:], ALU.subtract, r=[bt1_, bt2_], w=[bb_re])
                t3_, bt3_ = gettmp()
                t4_, bt4_ = gettmp()
                k.tt("dve", t3_[:], psi[:, 0:TT], RFre[:, j, :], ALU.mult, r=[bpi, bRFre], w=[bt3_])
                k.tt("dve", t4_[:], psr[:, 0:TT], RFim[:, j, :], ALU.mult, r=[bpr, bRFim], w=[bt4_])
                k.tt("pool", b_im[:], t3_[:], t4_[:], ALU.add, r=[bt3_, bt4_], w=[bb_im])

            def stageB(j):
                b_re, bb_re = bsc[(2 * j) % 4]
                b_im, bb_im = bsc[(2 * j + 1) % 4]
                w_re, bw_re = wall[:, j, 0, :], bwall[j]
                w_im, bw_im = wall[:, j, 1, :], bwall[j]
                dj = dec[:, j:j + 1].to_broadcast([128, TT])
                k.scan(w_re, dj, b_re[:], car[:, 0, j:j + 1], r=[bdec, bb_re, bcar], w=[bw_re])
                k.scan(w_im, dj, b_im[:], car[:, 1, j:j + 1], r=[bdec, bb_im, bcar], w=[bw_im])
                t5_, bt5_ = gettmp()
                t6_, bt6_ = gettmp()
                k.tt("dve", t5_[:], w_re, COSb[:, j, :], ALU.mult, r=[bw_re, bCOSb], w=[bt5_])
                k.tt("dve", t6_[:], w_im, SINb[:, j, :], ALU.mult, r=[bw_im, bSINb], w=[bt6_])
                k.tt("pool", xre[:, j, :], t5_[:], t6_[:], ALU.subtract, r=[bt5_, bt6_], w=[bxre])
                t7_, bt7_ = gettmp()
                t8_, bt8_ = gettmp()
                k.tt("pool", t7_[:], w_re, SINb[:, j, :], ALU.mult, r=[bw_re, bSINb], w=[bt7_])
                k.tt("pool", t8_[:], w_im, COSb[:, j, :], ALU.mult, r=[bw_im, bCOSb], w=[bt8_])
                k.tt("pool", nxim[:, j, :], t7_[:], t8_[:], ALU.add, r=[bt7_, bt8_], w=[bnxim])

            stageA(0)
            for j in range(16):
                if j + 1 < 16:
                    stageA(j + 1)
                stageB(j)
                yield
            wl_re = wall[:, :, 0, TT - 1]
            wl_im = wall[:, :, 1, TT - 1]
            (c0, bc0), (c1, bc1), (c2, bc2) = cwt
            k.tt("dve", c0[:], wl_re, cT[:], ALU.mult, r=bwall + [bcT], w=[bc0])
            k.tt("dve", c1[:], wl_im, nsT[:], ALU.mult, r=bwall + [bnsT], w=[bc1])
            k.tt("dve", car[:, 0, :], c0[:], c1[:], ALU.add, r=[bc0, bc1], w=[bcar])
            k.tt("dve", c2[:], wl_im, cT[:], ALU.mult, r=bwall + [bcT], w=[bc2])
            k.tt("dve", c0[:], wl_re, sT[:], ALU.mult, r=bwall + [bsT], w=[bc0])
            k.tt("dve", car[:, 1, :], c2[:], c0[:], ALU.add, r=[bc2, bc0], w=[bcar])
        def tail_gen(i):
            uT, buT = uT2[i % 3]
            sgs, bsgs = sgs2[i % 3]
            xre, bxre = xre2[i % 2]
            nxim, bnxim = nxim2[i % 2]
            tok = slice(i * TT, (i + 1) * TT)
            yield
            for c4 in range(4):
                ps, bp = ring.get()
                for jj in range(4):
                    j = 4 * c4 + jj
                    k.mm(ps[:, 0:TT], cre[:, 0, j * 128:(j + 1) * 128], xre[:, j, :], jj == 0, False, r=[bcre[0], bxre], w=[bp])
                    k.mm(ps[:, 0:TT], cim[:, 0, j * 128:(j + 1) * 128], nxim[:, j, :], False, jj == 3, r=[bcim[0], bnxim], w=[bp])
                k.stt(ypre[:], uT[:, c4, :], dvec[:, c4:c4 + 1], ps[:, 0:TT], ALU.mult, ALU.add, r=[buT, bdvec, bp], w=[bypre])
                k.act(yT[:, c4, :], ypre[:], AF.Gelu_apprx_tanh, r=[bypre], w=[byT])
                yield
            yield
            for kk in range(4):
                psg, bpg = ring.get()
                for c4 in range(4):
                    k.mm(psg[:, 0:TT], w_glu[:, c4, (4 + kk) * 128:(5 + kk) * 128], yT[:, c4, :], c4 == 0, c4 == 3, r=[bw_glu[c4], byT], w=[bpg])
                k.act(sgz[:], psg[:, 0:TT], AF.Sigmoid, r=[bpg], w=[bsgz])
                psv, bpv = ring.get()
                for c4 in range(4):
                    k.mm(psv[:, 0:TT], w_glu[:, c4, kk * 128:(kk + 1) * 128], yT[:, c4, :], c4 == 0, c4 == 3, r=[bw_glu[c4], byT], w=[bpv])
                k.tt("dve", zzT[:, kk, :], psv[:, 0:TT], sgz[:], ALU.mult, r=[bpv, bsgz], w=[bzzT])
                yield
            yield
            for fc in range(8):
                ps, bp = ring.get()
                for kk in range(4):
                    k.mm(ps[:, 0:TT], w_bs[:, kk, fc * 128:(fc + 1) * 128], zzT[:, kk, :], kk == 0, kk == 3, r=[bw_bs[kk], bzzT], w=[bp])
                k.tt("dve", gss[:, fc, :], ps[:, 0:TT], sgs[:, fc, :], ALU.mult, r=[bp, bsgs], w=[bgss])
                yield
            k.dma("sp", sc["gssT"].rearrange("(c p) s -> p c s", p=128)[:, :, tok], gss[:], bgss)
        def step(g):
            try:
                next(g)
                return True
            except StopIteration:
                return False

        load_tile(0)
        if NTT > 1:
            load_tile(1)
        for _ in inproj_gen(0):
            pass
        for i in range(NTT + 1):
            if i + 2 < NTT:
                load_tile(i + 2)
            gens = []
            if i < NTT:
                gens.append(s5_gen(i))
            if i >= 1:
                gens.append(tail_gen(i - 1))
            gi = inproj_gen(i + 1) if i + 1 < NTT else iter(())
            alive = [True] * len(gens)
            alive_i = True
            n_ = 0
            while any(alive) or alive_i:
                for gi_, g_ in enumerate(gens):
                    if alive[gi_]:
                        alive[gi_] = step(g_)
                for _ in range(1 + (n_ % 2)):
                    if alive_i:
                        alive_i = step(gi)
                n_ += 1
    P.barrier()


def phase2(k, pers, l, S, di, sc, cst):
    P = k.P
    NC = S // 16 - 1
    NCP = S // 16
    NCT = NCP // 128
    KcT, bKcT = k.sb(pers, [128, NCP], BF16, "KcT")
    Vc, bVc = k.sb(pers, [128, NCT, 2, 65], BF16, "Vc")
    k.memset("pool", KcT[:], 0.0, [bKcT])
    k.memset("pool", Vc[:], 1.0, [bVc])
    with ExitStack() as st:
        ring = PsumRing(k, st)
        for typ in ("k", "v"):
            with ExitStack() as s2:
                xT, bxT = k.sb(s2, [128, S], BF16, "cxT")
                k.dma("sp", xT[:], sc["kcT" if typ == "k" else "vcT"], bxT)
                w1, bw1 = k.sb(s2, [128, 32, 256], BF16, "w1")
                bw1b = Buf("w1b")
                src = di["c_w1" + typ][l].rearrange("(l d) c -> d l c", d=64)
                k.dma("pool", w1[0:64], src, bw1)
                k.dma("pool", w1[64:128], src, bw1b)
                pe2, bpe2 = k.sb(s2, [64, 32, 2], BF16, "pe2")
                pe_f, bpe_f = k.sb(s2, [64, 32], F32, "pe_f")
                k.dma("sp", pe_f[:], di["c_pe" + typ][l], bpe_f)
                k.copy("dve", pe2[:, :, 0], pe_f[:], r=[bpe_f], w=[bpe2])
                k.copy("dve", pe2[:, :, 1], pe_f[:], r=[bpe_f], w=[bpe2])
                b1, bb1 = k.sb(s2, [128, 2], F32, "b1")
                k.dma("sp", b1[:], di["c_b1" + typ][l], bb1)
                w2, bw2 = k.sb(s2, [128, 2, 64], BF16, "w2")
                k.dma("pool", w2[:], di["c_w2" + typ][l].rearrange("(c p) d -> p c d", p=128), bw2)
                bias, bbias = k.sb(s2, [128, 2], F32, "bias")
                for cc in range(2):
                    ps, bp = ring.get()
                    for li in range(32):
                        k.mm(ps[:, 0:2], w1[0:64, li, cc * 128:(cc + 1) * 128], pe2[:, li, :], li == 0, li == 31, r=[bw1, bpe2], w=[bp])
                    k.tt("dve", bias[:, cc:cc + 1], ps[:, 0:1], b1[:, cc:cc + 1], ALU.add, r=[bp, bb1], w=[bbias])
                for hk in range(2):
                    hid, bhid = k.sb(s2, [128, 2, NCP], BF16, "hid")
                    k.memset("pool", hid[:], 0.0, [bhid])
                    bw = bw1 if hk == 0 else bw1b
                    for cc in range(2):
                        ps, bp = ring.get()
                        for li in range(32):
                            k.mm(ps[:, 0:NC], w1[hk * 64:(hk + 1) * 64, li, cc * 128:(cc + 1) * 128],
                                 xT[hk * 64:(hk + 1) * 64, li:li + 16 * (NC - 1) + 1:16], li == 0, li == 31, r=[bw, bxT], w=[bp])
                        k.act(hid[:, cc, 0:NC], ps[:, 0:NC], AF.Gelu_apprx_tanh, r=[bp, bbias], w=[bhid], bias=bias[:, cc:cc + 1])
                    if typ == "k":
                        ps, bp = ring.get()
                        for cc in range(2):
                            k.mm(ps[hk * 64:(hk + 1) * 64, 0:NC], w2[:, cc, :], hid[:, cc, 0:NC], cc == 0, cc == 1, r=[bw2, bhid], w=[bp])
                        k.copy("act", KcT[hk * 64:(hk + 1) * 64, 0:NC], ps[hk * 64:(hk + 1) * 64, 0:NC], r=[bp], w=[bKcT])
                    else:
                        for nt in range(NCT):
                            ps, bp = ring.get()
                            for cc in range(2):
                                k.mm(ps[:, 0:64], hid[:, cc, nt * 128:(nt + 1) * 128], w2[:, cc, :], cc == 0, cc == 1, r=[bhid, bw2], w=[bp])
                            k.copy("act", Vc[:, nt, hk, 0:64], ps[:, 0:64], r=[bp], w=[bVc])
            P.barrier()
    if "dbg_kc" in sc:
        k.dma("sp", sc["dbg_kc"].rearrange("h d n -> (h d) n"), KcT[:], bKcT)
        k.dma("sp", sc["dbg_vc"], Vc[:], bVc)
    return (KcT, bKcT), (Vc, bVc)


def phase3(k, l, S, x_src, di, sc, cst, cmp_t):
    P = k.P
    NT = S // 128
    NCP = S // 16
    NCT = NCP // 128
    NB = S // 64
    (KcT, bKcT), (Vc, bVc) = cmp_t
    ident, bid = cst["ident"]
    with ExitStack() as st:
        ring = PsumRing(k, st, 3)
        ringO = PsumRing(k, st, 3)
        ringM = PsumRing(k, st, 2)
        KsM = []
        for hk in range(2):
            t, b = k.sb(st, [128, S], BF16, "KsM")
            b2 = Buf("KsMi")
            k.dma("sp", t[hk * 64:(hk + 1) * 64], sc["ksT"][hk * 64:(hk + 1) * 64, :], b)
            k.dma("pool", t[(1 - hk) * 64:(2 - hk) * 64], di["c_ind"], b2)
            KsM.append((t, b, b2))
        Vs, bVs = k.sb(st, [128, NT, 2, 65], BF16, "Vs")
        k.dma("sp", Vs[:], sc["vs"].rearrange("(n p) h c -> p n h c", p=128), bVs)

        def cload(name, shape, src, dt=BF16):
            t, b = k.sb(st, shape, dt, name)
            k.dma("pool" if dt == BF16 else "sp", t[:], src, b)
            return t, b
        caus, bcaus = cload("caus", [128, 128], di["c_caus"])
        low, blow = cload("low", [128, 128], di["c_low"])
        mgen, bmgen = cload("mgen", [128, 1024], di["c_mgen"])
        mt, bmt = cload("mt", [128, 16, 128], di["c_mt"])
        G, bG = cload("G", [128, 256], di["c_g"], F32)
        wbn, bwbn = load_w(k, st, di["w_bnsa"][l], 4, 1024, name="wbn")
        ident32, bid32 = cload("ident32", [128, 128], di["c_ident"], F32)
        w_out, bw_out = load_w(k, st, di["w_out"][l], 8, 1024, name="w_out")
        QT = [[k.sb(st, [128, 4, 128], BF16, "QT") for _ in range(2)] for _ in range(2)]
        for pb_ in range(2):
            for hk_ in range(2):
                k.memset("pool", QT[pb_][hk_][0][:], 0.0, [QT[pb_][hk_][1]])
        NHALF = max(1, NB // 64)
        Qsel = [[[k.sb(st, [128, 4, 128], BF16, "Qsel") + (Buf("Qselm"),) for _ in range(NHALF)] for _ in range(2)] for _ in range(2)]
        negm_sw, bnegm_sw = k.sb(st, [128, 128], BF16, "negm_sw")
        k.memset("pool", negm_sw[:], 0.0, [bnegm_sw])
        KwT = [k.sb(st, [128, 640], BF16, "KwT") for _ in range(2)]
        Vw = [k.sb(st, [128, 5, 2, 65], BF16, "Vw") for _ in range(2)]
        gtok = [k.sb(st, [128, 24], F32, "gtok") for _ in range(2)]
        sgn = [k.sb(st, [128, 8, 128], BF16, "sgn") for _ in range(2)]
        gss = [k.sb(st, [128, 8, 128], BF16, "gss") for _ in range(2)]
        xin = [k.sb(st, [128, D], F32, "xin") for _ in range(2)]
        eg = [k.sb(st, [128, NCP], F32, "eg") for _ in range(4)]
        den4, bden4 = k.sb(st, [128, 4], F32, "den4")
        rden4, brden4 = k.sb(st, [128, 4], F32, "rden4")
        pg, bpg = k.sb(st, [128, NCP + 8], F32, "pg")
        k.memset("pool", pg[:], 0.0, [bpg])
        blk, bblk = k.sb(st, [128, NB], F32, "blk")
        blk2, bblk2 = k.sb(st, [128, NB], F32, "blk2")
        m8, bm8 = k.sb(st, [128, 16], F32, "m8")
        negm, bnegm = k.sb(st, [128, 128], BF16, "negm")
        k.memset("pool", negm[:], 0.0, [bnegm])
        pTs = [k.sb(st, [128, 512], BF16, "pT") for _ in range(4)]
        pti = [0]
        osbs = [k.sb(st, [65, 512], F32, "osb") for _ in range(4)]
        s4s = [k.sb(st, [128, 4], F32, "s4") for _ in range(4)]
        oq = [k.sb(st, [128, 8, 64], F32, "oq") for _ in range(2)]
        oqb, boqb = k.sb(st, [128, 512], BF16, "oqb")
        oT2, boT2 = k.sb(st, [128, 4, 128], BF16, "oT2")
        mrg, bmrg = k.sb(st, [128, 8, 128], BF16, "mrg")
        xm = [k.sb(st, [128, D], F32, "xm") for _ in range(1)]

        def loads(qb):
            s0 = qb * 128
            pb = qb % 2
            qv = sc["qT"].rearrange("(h d) s -> d h s", d=64)
            for hk in range(2):
                t, b = QT[pb][hk]
                k.dma("sp", t[hk * 64:(hk + 1) * 64], qv[:, hk * 4:(hk + 1) * 4, s0:s0 + 128], b)
                for hf in range(NHALF):
                    if hf * 32 <= qb:
                        t, b, _ = Qsel[pb][hk][hf]
                        k.dma("sp", t[hk * 64:(hk + 1) * 64], qv[:, hk * 4:(hk + 1) * 4, s0:s0 + 128], b)
            lo = max(0, s0 - 512)
            t, b = KwT[pb]
            k.dma("sp", t[:, 640 - (s0 + 128 - lo):640], sc["kwT"][:, lo:s0 + 128], b)
            nw = (s0 + 128 - lo) // 128
            t, b = Vw[pb]
            k.dma("sp", t[:, 5 - nw:5], sc["vw"][lo:s0 + 128].rearrange("(n p) h c -> p n h c", p=128), b)
            t, b = gtok[pb]
            k.dma("sp", t[:], sc["sgTok"][s0:s0 + 128, :], b)
            t, b = sgn[pb]
            k.dma("sp", t[:], sc["sgnT"].rearrange("(c p) s -> p c s", p=128)[:, :, s0:s0 + 128], b)
            t, b = gss[pb]
            k.dma("sp", t[:], sc["gssT"].rearrange("(c p) s -> p c s", p=128)[:, :, s0:s0 + 128], b)
            t, b = xin[pb]
            k.dma("sp", t[:], x_src[s0:s0 + 128, :], b)

        DEPTH = 2
        pipe = []
        delayed = []

        def tick():
            for d in delayed:
                d[0] -= 1
            while delayed and delayed[0][0] <= 0:
                delayed.pop(0)[1]()

        cur_tag = [0]

        def push(score_fn, pv_fn, after=None):
            tok_ = score_fn()
            pipe.append((pv_fn, tok_, after, cur_tag[0]))
            if len(pipe) > DEPTH:
                pv, tk, af, _ = pipe.pop(0)
                pv(tk)
                if af is not None:
                    af()
            tick()

        def flush():
            while pipe:
                pv, tk, af, _ = pipe.pop(0)
                pv(tk)
                if af is not None:
                    af()
            while delayed:
                delayed.pop(0)[1]()

        def attn_tile(Ops, bO, first, last_, KT_ap, bKT, V_ap, bV, Q2, bQ, smask=None, emask=None, after=None):
            def score():
                psS, bS = ring.get()
                nmask = (4 if smask is not None else 0) + (1 if emask is not None else 0)
                rl = (bKT if isinstance(bKT, list) else [bKT]) + (bQ if isinstance(bQ, list) else [bQ])
                k.mm(psS[:, 0:512], KT_ap, Q2, True, nmask == 0, r=rl, w=[bS])
                done = 0
                assert emask is None
                if smask is not None:
                    m_ap, bm = smask
                    for g in range(4):
                        done += 1
                        k.mm(psS[:, g * 128:(g + 1) * 128], ident[:], m_ap, False, done == nmask, r=[bid, bm], w=[bS])
                pT, bpT = pTs[pti[0] % len(pTs)]
                pti[0] += 1
                k.act(pT[:], psS[:, 0:512], AF.Exp, r=[bS], w=[bpT], scale=0.125)
                return (pT, bpT)

            def pv(tk):
                pT, bpT = tk
                k.mm(Ops[0:65, 0:512], V_ap, pT[:], first, last_, r=[bV, bpT], w=[bO])
            push(score, pv, after)

        fin_i = [0]

        def finalize(Ops, bO, qb_, hk_, br_, first_branch):
            tag_ = cur_tag[0]

            def stage_a():
                while sum(1 for d_ in delayed if len(d_) > 3) >= 3:
                    delayed.pop(0)[1]()
                osb, bosb = osbs[fin_i[0] % 4]
                s4, bs4 = s4s[fin_i[0] % 4]
                fin_i[0] += 1
                k.copy("act", osb[:], Ops[0:65, 0:512], r=[bO], w=[bosb])

                def stage_b():
                    pst, bpst = ringM.get()
                    for g in range(4):
                        k.tr(pst[:, g * 65:(g + 1) * 65], osb[:, g * 128:(g + 1) * 128], ident32[0:65, 0:65], r=[bosb, bid32], w=[bpst])
                    p3 = pst[:, 0:260].rearrange("p (g c) -> p g c", c=65)
                    gt_, bgt_ = gtok[qb_ % 2]
                    oq_t, boq = oq[qb_ % 2]
                    k.ts("dve", s4[:], p3[:, :, 64], 1e-20, ALU.max, r=[bpst], w=[bs4])
                    k.recip(s4[:], s4[:], r=[bs4], w=[bs4])
                    c0_ = hk_ * 12 + br_
                    k.tt("dve", s4[:], s4[:], gt_[:, c0_:c0_ + 10:3], ALU.mult, r=[bs4, bgt_], w=[bs4])
                    for g in range(4):
                        h_ = hk_ * 4 + g
                        if first_branch:
                            k.ts("dve", oq_t[:, h_, :], p3[:, g, 0:64], s4[:, g:g + 1], ALU.mult, r=[bpst, bs4], w=[boq])
                        else:
                            k.stt(oq_t[:, h_, :], p3[:, g, 0:64], s4[:, g:g + 1], oq_t[:, h_, :], ALU.mult, ALU.add, r=[bpst, bs4, boq], w=[boq])
                delayed.append([3, stage_b, tag_, "fin"])
            return stage_a

        mtmps = [k.sb(st, [128, 128], F32, "mtmp") for _ in range(2)]

        def epilogue1(qb):
            pb = qb % 2
            oq_t, boq = oq[pb]
            if "dbg_o" in sc:
                k.dma("sp", sc["dbg_o"][qb * 128:(qb + 1) * 128, :], oq_t[:].rearrange("p h d -> p (h d)"), boq)
            k.copy("dve", oqb[:], oq_t[:].rearrange("p h d -> p (h d)"), r=[boq], w=[boqb])
            pst, bpst = ringM.get()
            pbf = pst[:].bitcast(BF16)
            for c in range(4):
                k.tr(pbf[:, c * 128:(c + 1) * 128], oqb[:, c * 128:(c + 1) * 128], ident[:], r=[boqb, bid], w=[bpst])
            k.copy("dve", oT2[:].rearrange("p c q -> p (c q)"), pbf[:, 0:512], r=[bpst], w=[boT2])
            sg_t, bsgn_ = sgn[pb]
            gs_t, bgs_ = gss[pb]
            for half in range(2):
                ps, bp = ringM.get()
                for f4 in range(4):
                    fc = half * 4 + f4
                    for c in range(4):
                        k.mm(ps[:, f4 * 128:(f4 + 1) * 128], wbn[:, c, fc * 128:(fc + 1) * 128], oT2[:, c, :],
                             c == 0, c == 3, r=[bwbn[c], boT2], w=[bp])
                for f4 in range(4):
                    fc = half * 4 + f4
                    mt_, bmt_ = mtmps[fc % 2]
                    k.tt("dve", mt_[:], ps[:, f4 * 128:(f4 + 1) * 128], sg_t[:, fc, :], ALU.mult, r=[bp, bsgn_], w=[bmt_])
                    k.tt("pool", mrg[:, fc, :], mt_[:], gs_t[:, fc, :], ALU.add, r=[bmt_, bgs_], w=[bmrg])
            delayed.append([8, lambda: epilogue2(qb), qb])

        def epilogue2(qb):
            s0 = qb * 128
            pb = qb % 2
            x_t, bx = xin[pb]
            xm_t, bxm = xm[0]
            for half in range(2):
                ps, bp = ringM.get()
                for fc in range(8):
                    k.mm(ps[:, 0:512], mrg[:, fc, :], w_out[:, fc, half * 512:(half + 1) * 512], fc == 0, fc == 7, r=[bmrg, bw_out[fc]], w=[bp])
                k.tt("dve", xm_t[:, half * 512:(half + 1) * 512], ps[:, 0:512], x_t[:, half * 512:(half + 1) * 512], ALU.add, r=[bp, bx], w=[bxm])
            k.dma("sp", sc["xmid"][s0:s0 + 128, :], xm_t[:], bxm)

        def force(tag_max):
            while pipe and pipe[0][3] <= tag_max:
                pv, tk, af, _ = pipe.pop(0)
                pv(tk)
                if af is not None:
                    af()
            progressed = True
            while progressed:
                progressed = False
                for idx_, d_ in enumerate(delayed):
                    if d_[2] <= tag_max:
                        delayed.pop(idx_)
                        d_[1]()
                        progressed = True
                        break

        def chainA(qb, hk):
            pb = qb % 2
            Qt, bQ = QT[pb][hk]
            for g in range(4):
                ps, bp = ringM.get()
                k.mm(ps[:, 0:NCP], Qt[:, g, :], KcT[:, 0:NCP], True, False, r=[bQ, bKcT], w=[bp])
                k.mm(ps[:, 0:NCP], ident[:], mgen[:, 512 - 8 * qb:512 - 8 * qb + NCP], False, True, r=[bid, bmgen], w=[bp])
                k.act(eg[g][0][:], ps[:, 0:NCP], AF.Exp, r=[bp], w=[eg[g][1], bden4], scale=0.125, accum=den4[:, g:g + 1])
            k.ts("dve", rden4[:], den4[:], 1e-20, ALU.max, r=[bden4], w=[brden4])
            k.recip(rden4[:], rden4[:], r=[brden4], w=[brden4])
            k.ts("dve", pg[:, 1:1 + NCP], eg[0][0][:], rden4[:, 0:1], ALU.mult, r=[eg[0][1], brden4], w=[bpg])
            for g in range(1, 4):
                k.stt(pg[:, 1:1 + NCP], eg[g][0][:], rden4[:, g:g + 1], pg[:, 1:1 + NCP], ALU.mult, ALU.add, r=[eg[g][1], brden4, bpg], w=[bpg])
            P.op("dve", lambda e: e.tensor_reduce(out=blk[:], in_=pg[:, 0:NCP].rearrange("p (j o) -> p j o", o=4), axis=AX.X, op=ALU.add), reads=[bpg], writes=[bblk])
            k.tt("dve", blk[:], blk[:], pg[:, 4:4 + 4 * NB:4], ALU.add, r=[bblk, bpg], w=[bblk])
            k.tt("dve", blk[:], blk[:], G[:, 126 - 2 * qb:126 - 2 * qb + NB], ALU.add, r=[bblk, bG], w=[bblk])
            if qb >= 1:
                k.ts("dve", blk[:, 0:1], blk[:, 0:1], 1e4, ALU.add, r=[bblk], w=[bblk])
            P.op("dve", lambda e: e.max(out=m8[:, 0:8], in_=blk[:]), reads=[bblk], writes=[bm8])
            P.op("dve", lambda e: e.match_replace(out=blk2[:], in_to_replace=m8[:, 0:8], in_values=blk[:], imm_value=-3e38), reads=[bblk, bm8], writes=[bblk2])
            P.op("dve", lambda e: e.max(out=m8[:, 8:16], in_=blk2[:]), reads=[bblk2], writes=[bm8])
            nhalf_used = 1 if qb < 32 else NHALF
            need_nat = (hk == 1) or nhalf_used > 1
            need_sw = (hk == 0) or nhalf_used > 1
            if need_nat:
                k.ts("dve", negm[:, 0:NB], blk[:], m8[:, 15:16], ALU.is_lt, r=[bblk, bm8], w=[bnegm], s2=NEG, op1=ALU.mult)
            if need_sw:
                n0 = min(NB, 64)
                k.ts("dve", negm_sw[:, 64:64 + n0], blk[:, 0:n0], m8[:, 15:16], ALU.is_lt, r=[bblk, bm8], w=[bnegm_sw], s2=NEG, op1=ALU.mult)
                if NB > 64:
                    k.ts("dve", negm_sw[:, 0:NB - 64], blk[:, 64:NB], m8[:, 15:16], ALU.is_lt, r=[bblk, bm8], w=[bnegm_sw], s2=NEG, op1=ALU.mult)

        def chainB(qb, hk):
            pb = qb % 2
            os_ = slice((1 - hk) * 64, (2 - hk) * 64)
            nhalf_used = 1 if qb < 32 else NHALF
            for hf in range(nhalf_used):
                use_sw = (hk == 0 and hf == 0) or (hk == 1 and hf == 1)
                src_t, bsrc = (negm_sw, bnegm_sw) if use_sw else (negm, bnegm)
                ps, bp = ringM.get()
                pbf = ps[:].bitcast(BF16)
                k.tr(pbf[:, 0:128], src_t[:], ident[:], r=[bsrc, bid], w=[bp])
                qs_t, _, bqm = Qsel[pb][hk][hf]
                for g in range(4):
                    k.copy("dve", qs_t[os_, g, :], pbf[os_, 0:128], r=[bp], w=[bqm])

        loads(0)
        chainA(0, 0)
        for qb in range(NT):
            s0 = qb * 128
            pb = qb % 2
            for hk in range(2):
                cur_tag[0] = qb
                Qt, bQ = QT[pb][hk]
                Q2 = Qt[:].rearrange("d g q -> d (g q)")
                chainB(qb, hk)
                Ops, bO = ringO.get()
                fa = finalize(Ops, bO, qb, hk, 0, True)
                tiles = [nt for nt in range(NCT) if qb - 16 * nt >= 0]
                for idx, nt in enumerate(tiles):
                    m = qb - 16 * nt
                    sm = (mt[:, m, :], bmt) if m < 16 else None
                    attn_tile(Ops, bO, idx == 0, idx == len(tiles) - 1, KcT[:, nt * 128:(nt + 1) * 128], bKcT,
                              Vc[:, nt, hk, :], bVc, Q2, bQ, smask=sm, after=fa if idx == len(tiles) - 1 else None)
                Ops, bO = ringO.get()
                fa = finalize(Ops, bO, qb, hk, 2, False)
                tiles = [wt for wt in range(5) if s0 - 512 + 128 * wt >= 0]
                kw_full, bkw = KwT[pb]
                kw_t = kw_full
                vw_t, bvw = Vw[pb]
                for idx, wt in enumerate(tiles):
                    sm = (low[:], blow) if wt == 0 else ((caus[:], bcaus) if wt == 4 else None)
                    attn_tile(Ops, bO, idx == 0, idx == len(tiles) - 1, kw_t[:, wt * 128:(wt + 1) * 128], bkw,
                              vw_t[:, wt, hk, :], bvw, Q2, bQ, smask=sm, after=fa if idx == len(tiles) - 1 else None)
                if hk == 1 and qb + 1 < NT:
                    force(qb - 1)
                    loads(qb + 1)
                if hk == 0:
                    chainA(qb, 1)
                elif qb + 1 < NT:
                    chainA(qb + 1, 0)
                Ops, bO = ringO.get()
                fa = finalize(Ops, bO, qb, hk, 1, False)
                for i in range(qb + 1):
                    sm = (caus[:], bcaus) if i == qb else None
                    qs_t, bqs, bqm = Qsel[pb][hk][i // 32]
                    attn_tile(Ops, bO, i == 0, i == qb, KsM[hk][0][:, i * 128:(i + 1) * 128], [KsM[hk][1], KsM[hk][2]],
                              Vs[:, i, hk, :], bVs, qs_t[:].rearrange("d g q -> d (g q)"), [bqs, bqm], smask=sm, after=fa if i == qb else None)
                if hk == 1:
                    delayed_ep = (lambda q_=qb: (lambda: delayed.append([6, lambda: epilogue1(q_), q_])))(qb)
                    pipe[-1] = (pipe[-1][0], pipe[-1][1], (lambda f1=pipe[-1][2], f2=delayed_ep: (f1(), f2())), pipe[-1][3])
        flush()
    P.barrier()


def phase4(k, l, S, di, sc, cst, dst, last):
    P = k.P
    TT = 256
    NTT = S // TT
    NSB = TT // 128
    ident, bid = cst["ident"]
    with ExitStack() as st:
        ring = PsumRing(k, st)
        w_fin, bw_fin = load_w(k, st, di["w_fin"][l], 8, 2 * DFF, name="w_fin")
        w_fo, bw_fo = load_w(k, st, di["w_fout"][l], NFC, D, name="w_fo")
        gam, bgam = k.sb(st, [128, D], F32, "gam")
        k.dma("sp", gam[:], di["g_ffn"][l], bgam)
        cw, bcw = k.sb(st, [128, NFC, 3], F32, "cw")
        k.dma("sp", cw[:], di["f_cw"][l], bcw)
        cb, bcb = k.sb(st, [128, NFC], F32, "cb")
        k.dma("sp", cb[:], di["f_cb"][l], bcb)
        if last:
            gfin, bgfin = k.sb(st, [128, D], F32, "gfin")
            k.dma("sp", gfin[:], di["g_fin"], bgfin)
        halo, bhalo = k.sb(st, [128, NFC, 2], F32, "halo")
        k.memset("pool", halo[:], 0.0, [bhalo])
        xt = [k.sb(st, [128, NSB, D], F32, "xt") for _ in range(2)]
        junk, bjunk = k.sb(st, [128, D], BF16, "junk")
        ss, bss = k.sb(st, [128, 2 * NSB], F32, "ss")
        ms, bms = k.sb(st, [128, 2 * NSB], F32, "ms")
        sd, bsd = k.sb(st, [128, 2 * NSB], F32, "sd")
        rstd, brstd = k.sb(st, [128, 2 * NSB], F32, "rstd")
        hh = [k.sb(st, [128, D], BF16, "h") for _ in range(2)]
        hTs = [k.sb(st, [128, 8, TT], BF16, "hT") for _ in range(2)]
        a_sb = [k.sb(st, [128, TT + 2], F32, "a_sb") for _ in range(2)]
        cv = [k.sb(st, [128, TT], F32, "cv") for _ in range(2)]
        gl = [k.sb(st, [128, TT], F32, "gl") for _ in range(2)]
        actTs = [k.sb(st, [128, NFC, TT], BF16, "actT") for _ in range(2)]
        xo = [k.sb(st, [128, NSB, D], F32, "xo") for _ in range(1)]

        def load_tile(i):
            t, b = xt[i % 2]
            k.dma("sp", t[:], sc["xmid"][i * TT:(i + 1) * TT, :].rearrange("(s p) d -> p s d", p=128), b)

        def rms(x_ap, bx, col, g_t, bg, out_ap, bout):
            k.act(junk[:], x_ap, AF.Square, r=[bx], w=[bjunk, bss], accum=ss[:, col:col + 1])
            k.ts("dve", ms[:, col:col + 1], ss[:, col:col + 1], 1.0 / D, ALU.mult, r=[bss], w=[bms], s2=EPS, op1=ALU.add)
            k.act(sd[:, col:col + 1], ms[:, col:col + 1], AF.Sqrt, r=[bms], w=[bsd])
            k.recip(rstd[:, col:col + 1], sd[:, col:col + 1], r=[bsd], w=[brstd])
            k.stt(out_ap, x_ap, rstd[:, col:col + 1], g_t[:], ALU.mult, ALU.mult, r=[bx, brstd, bg], w=[bout])

        def prepA(i):
            x_t, bx = xt[i % 2]
            for s_ in range(NSB):
                h_t, bh = hh[s_ % 2]
                rms(x_t[:, s_, :], bx, s_, gam, bgam, h_t[:], bh)

        def prep(i):
            hT, bhT = hTs[i % 2]
            for s_ in range(NSB):
                h_t, bh = hh[s_ % 2]
                ps, bp = ring.get()
                pbf = ps[:].bitcast(BF16)
                for c in range(8):
                    k.tr(pbf[:, c * 128:(c + 1) * 128], h_t[:, c * 128:(c + 1) * 128], ident[:], r=[bh, bid], w=[bp])
                k.copy("act", hT[:, :, s_ * 128:(s_ + 1) * 128], pbf.rearrange("p (c t) -> p c t", c=8), r=[bp], w=[bhT])

        def inproj(i):
            hT, bhT = hTs[i % 2]
            actT, bactT = actTs[i % 2]
            pend = []
            for fc in range(NFC + 1):
                if fc == NFC:
                    pend.pop(0)()
                    break
                if fc == NFC // 2 and i + 1 < NTT:
                    prepA(i + 1)
                psa, bpa = ring.get()
                for c in range(8):
                    k.mm(psa[:, 0:TT], w_fin[:, c, fc * 128:(fc + 1) * 128], hT[:, c, :], c == 0, c == 7, r=[bw_fin[c], bhT], w=[bpa])
                psb, bpb = ring.get()
                for c in range(8):
                    k.mm(psb[:, 0:TT], w_fin[:, c, DFF + fc * 128:DFF + (fc + 1) * 128], hT[:, c, :], c == 0, c == 7, r=[bw_fin[c], bhT], w=[bpb])
                a_t, ba = a_sb[fc % 2]
                c_t, bc = cv[fc % 2]
                g_t, bg = gl[fc % 2]
                k.copy("pool", a_t[:, 0:2], halo[:, fc, :], r=[bhalo], w=[ba])
                k.copy("act", a_t[:, 2:2 + TT], psa[:, 0:TT], r=[bpa], w=[ba])
                k.copy("pool", halo[:, fc, :], a_t[:, TT:TT + 2], r=[ba], w=[bhalo])
                k.act(c_t[:], psa[:, 0:TT], AF.Identity, r=[bpa, bcw, bcb], w=[bc], scale=cw[:, fc, 2:3], bias=cb[:, fc:fc + 1])
                k.stt(c_t[:], a_t[:, 1:1 + TT], cw[:, fc, 1:2], c_t[:], ALU.mult, ALU.add, r=[ba, bcw, bc], w=[bc])
                k.stt(c_t[:], a_t[:, 0:TT], cw[:, fc, 0:1], c_t[:], ALU.mult, ALU.add, r=[ba, bcw, bc], w=[bc])

                def stage2(fc=fc, c_t=c_t, bc=bc, g_t=g_t, bg=bg, psb=psb, bpb=bpb):
                    k.act(g_t[:], c_t[:], AF.Gelu_apprx_tanh, r=[bc], w=[bg])
                    k.tt("dve", actT[:, fc, :], psb[:, 0:TT], g_t[:], ALU.mult, r=[bpb, bg], w=[bactT])
                pend.append(stage2)
                if len(pend) > 1:
                    pend.pop(0)()

        def outproj(i):
            x_t, bx = xt[i % 2]
            actT, bactT = actTs[i % 2]
            xo_t, bxo = xo[0]
            for s_ in range(NSB):
                for half in range(2):
                    ps, bp = ring.get()
                    for fc in range(NFC):
                        k.mm(ps[:, 0:512], actT[:, fc, s_ * 128:(s_ + 1) * 128], w_fo[:, fc, half * 512:(half + 1) * 512], fc == 0, fc == NFC - 1,
                             r=[bactT, bw_fo[fc]], w=[bp])
                    k.tt("dve", xo_t[:, s_, half * 512:(half + 1) * 512], ps[:, 0:512], x_t[:, s_, half * 512:(half + 1) * 512], ALU.add, r=[bp, bx], w=[bxo])
            if last:
                for s_ in range(NSB):
                    rms(xo_t[:, s_, :], bxo, NSB + s_, gfin, bgfin, xo_t[:, s_, :], bxo)
            k.dma("sp", dst[i * TT:(i + 1) * TT, :].rearrange("(s p) d -> p s d", p=128), xo_t[:], bxo)

        load_tile(0)
        if NTT > 1:
            load_tile(1)
        prepA(0)
        prep(0)
        for i in range(NTT):
            inproj(i)
            if i + 1 < NTT:
                prep(i + 1)
            outproj(i)
            if i + 2 < NTT:
                load_tile(i + 2)
    P.barrier()


INPUT_SHAPES = None


def build(S, L, TS=128, dbg=False, phases=("p1", "p2", "p3", "p4")):
    nc = bass.Bass("TRN2", target_bir_lowering=False)
    NT = S // 128
    di = {}

    def din(name, shape):
        di[name] = nc.dram_tensor(name, list(shape), F32, kind="ExternalInput").ap()
    din("x", [S, D])
    for nm, shp in (("g_mix", [L, 128, D]), ("g_ffn", [L, 128, D]), ("g_fin", [128, D]),
                    ("w_in", [L, D, INW]), ("w_sw", [L, D, 896]),
                    ("s_are", [L, 128, 16]), ("s_aim", [L, 128, 16]), ("s_ldt", [L, 128, 16]),
                    ("s_bre", [L, 128, 16, 128]), ("s_bim", [L, 128, 16, 128]),
                    ("s_cre", [L, 128, 16, 128]), ("s_cim", [L, 128, 16, 128]), ("s_d", [L, 128, 4]),
                    ("w_glu", [L, 512, 1024]), ("w_bssm", [L, 512, 1024]), ("w_bnsa", [L, 512, 1024]),
                    ("w_out", [L, D, D]),
                    ("c_pek", [L, 64, 32]), ("c_w1k", [L, 2048, 256]), ("c_b1k", [L, 128, 2]), ("c_w2k", [L, 256, 64]),
                    ("c_pev", [L, 64, 32]), ("c_w1v", [L, 2048, 256]), ("c_b1v", [L, 128, 2]), ("c_w2v", [L, 256, 64]),
                    ("w_fin", [L, D, 2 * DFF]), ("w_fout", [L, DFF, D]), ("f_cw", [L, 128, NFC, 3]), ("f_cb", [L, 128, NFC]),
                    ("c_cos", [128, S]), ("c_sin", [128, S]), ("c_ident", [128, 128]), ("c_tau", [128, TS]),
                    ("c_caus", [128, 128]), ("c_low", [128, 128]), ("c_mgen", [128, 1024]), ("c_mt", [128, 16, 128]),
                    ("c_g", [128, 256]), ("c_ind", [64, S])):
        din(nm, shp)
    out = nc.dram_tensor("out", [S, D], F32, kind="ExternalOutput").ap()
    skind = "ExternalOutput" if dbg else "Internal"
    sc = {}

    def scr(name, shape, dt):
        sc[name] = nc.dram_tensor(name, list(shape), dt, kind=skind).ap()
    scr("qT", [512, S], BF16)
    for nm in ("kcT", "vcT", "ksT", "kwT"):
        scr(nm, [128, S], BF16)
    scr("vs", [S, 2, 65], BF16)
    scr("vw", [S, 2, 65], BF16)
    scr("sgTok", [S, 24], F32)
    scr("sgnT", [1024, S], BF16)
    scr("gssT", [1024, S], BF16)
    scr("xmid", [S, D], F32)
    if dbg:
        scr("dbg_o", [S, 512], F32)
        scr("dbg_kc", [2, 64, S // 16], BF16)
        scr("dbg_vc", [128, S // 2048, 2, 65], BF16)
    scr("x1", [S, D], F32)
    with ExitStack() as st:
        P = Prog(nc)
        k = K(nc, P)
        cst = {}
        ident, bid = k.sb(st, [128, 128], BF16, "ident")
        k.dma("pool", ident[:], di["c_ident"], bid)
        cst["ident"] = (ident, bid)
        x_src = di["x"]
        for l in range(L):
            last = l == L - 1
            if "p1" in phases:
                phase1(k, l, S, TS, x_src, di, sc, cst)
            if "p2" in phases:
                pers = ExitStack()
                cmp_t = phase2(k, pers, l, S, di, sc, cst)
            if "p3" in phases:
                phase3(k, l, S, x_src, di, sc, cst, cmp_t)
            if "p2" in phases:
                pers.close()
                P.barrier()
            if "p4" in phases:
                phase4(k, l, S, di, sc, cst, out if last else sc["x1"], last)
            x_src = sc["x1"]
        P.barrier()
        P.emit(st)
    return nc


_NC_CACHE = {}


def kernel(**inputs):
    S, L, NCORES = 8192, 2, 8
    inp = {k_: np.asarray(v) for k_, v in inputs.items()}
    hl = host_layout(inp, L)
    hc = host_consts(S, 128)
    common = {}
    common.update(hl)
    common.update(hc)
    common = {k_: np.ascontiguousarray(v, dtype=np.float32) for k_, v in common.items()}
    if "nc" not in _NC_CACHE:
        _NC_CACHE["nc"] = build(S, L)
    nc = _NC_CACHE["nc"]
    x = np.asarray(inp["x"], dtype=np.float32)
    in_maps = []
    for b in range(NCORES):
        m = dict(common)
        m["x"] = np.ascontiguousarray(x[b])
        in_maps.append(m)
    res = run_bass_kernel_spmd(nc, in_maps, core_ids=list(range(NCORES)))
    return np.stack([np.asarray(r["out"], dtype=np.float32) for r in res.results], axis=0)
```

```python
from contextlib import ExitStack
import numpy as np
import ml_dtypes
import concourse.bass as bass
import concourse.mybir as mybir
from concourse.bass_utils import run_bass_kernel_spmd

F32 = mybir.dt.float32
BF16 = mybir.dt.bfloat16
I32 = mybir.dt.int32
ALU = mybir.AluOpType
AF = mybir.ActivationFunctionType
AX = mybir.AxisListType

D = 1024
DFF = 2816
NFC = DFF // 128
INW = 3864
EPS = 1e-6
NEG = -30000.0
TWO_PI = float(2 * np.pi)
SIN_SCALE = TWO_PI * 0.999999


class Buf:
    __slots__ = ("name", "w", "rs", "sem", "cnt", "slot", "base", "uid")

    def __init__(self, name="b"):
        self.name = name
        self.w = None
        self.rs = {}
        self.sem = None
        self.cnt = 0
        self.slot = None
        self.base = 0
        self.uid = None


class Op:
    __slots__ = ("eng", "fn", "deps", "key", "val", "signal", "sigval", "dma", "slot", "semval")


ENGS = ("pe", "act", "dve", "pool", "sp")


class Prog:
    def __init__(self, nc):
        self.nc = nc
        self.ops = {e: [] for e in ENGS}
        self.seen = {e: {} for e in ENGS}
        self.dma_bufs = []
        self.last = {}
        self.slot_base = []
        self.free_slots = []
        self.live = []
        self.uid = 0

    def _get_slot(self, buf):
        if self.free_slots:
            sl = self.free_slots.pop()
        else:
            sl = len(self.slot_base)
            self.slot_base.append(0)
        self.uid += 1
        buf.sem = True
        buf.slot = sl
        buf.base = self.slot_base[sl]
        buf.cnt = 0
        buf.uid = self.uid
        self.live.append(buf)

    def barrier(self):
        lasts = list(self.last.values())
        self._barrier_ops(lasts)
        for b in self.live:
            self.slot_base[b.slot] = b.base + b.cnt
            self.free_slots.append(b.slot)
            b.sem = None
        self.live = []
        self.last = {kk: v for kk, v in self.last.items() if not isinstance(kk, tuple)}

    def _barrier_ops(self, lasts):
        for e in ENGS:
            o = Op()
            o.eng = e
            o.fn = None
            o.deps = []
            o.signal = False
            o.sigval = None
            o.dma = None
            o.key = e
            o.val = len(self.ops[e])
            for d in lasts:
                if d.key == e:
                    continue
                if self.seen[e].get(d.key, -1) >= d.val:
                    continue
                self.seen[e][d.key] = d.val
                o.deps.append(d)
            self.ops[e].append(o)

    def _dep(self, eng, d, deps, same_ok):
        if d is None:
            return
        key = d.key
        if key == eng:
            if eng == "pe" or same_ok:
                return
        if self.seen[eng].get(key, -1) >= d.val:
            return
        self.seen[eng][key] = d.val
        deps.append(d)

    def op(self, eng, fn, reads=(), writes=(), dma=None):
        o = Op()
        o.eng = eng
        o.fn = fn
        o.deps = []
        o.signal = False
        o.sigval = None
        o.dma = dma
        writes = [b for b in writes if b is not None]
        reads = [b for b in reads if b is not None]
        o.slot = None
        o.semval = None
        if dma is not None:
            if dma.sem is None:
                self._get_slot(dma)
            dma.cnt += 1
            o.key = ("dma", dma.uid)
            o.val = dma.cnt
            o.slot = dma.slot
            o.semval = 16 * (dma.base + dma.cnt)
            if dma not in writes:
                writes.append(dma)
            reads = [b for b in reads if b is not dma]
        else:
            o.key = eng
            o.val = len(self.ops[eng])
        for b in reads:
            self._dep(eng, b.w, o.deps, False)
        for b in writes:
            self._dep(eng, b.w, o.deps, True)
            for r in b.rs.values():
                self._dep(eng, r, o.deps, True)
        for b in reads:
            b.rs[o.key] = o
        for b in writes:
            b.w = o
            b.rs = {}
        self.ops[eng].append(o)
        self.last[o.key] = o
        return o

    def emit(self, stack):
        nc = self.nc
        for e in ENGS:
            for o in self.ops[e]:
                for d in o.deps:
                    if d.dma is None:
                        d.signal = True
        for e in ENGS:
            c = 0
            for o in self.ops[e]:
                if o.dma is None and o.signal:
                    c += 1
                    o.sigval = c
        esem = {}
        for e in ("pe", "act", "dve", "pool"):
            esem[e] = stack.enter_context(nc.semaphore("s_" + e))
        dsem = [stack.enter_context(nc.semaphore("d%d" % i)) for i in range(len(self.slot_base))]
        block = stack.enter_context(nc.Block())
        prog = self

        def run(name, eng):
            for o in prog.ops[name]:
                for d in o.deps:
                    if d.dma is not None:
                        eng.wait_ge(dsem[d.slot], d.semval)
                    else:
                        eng.wait_ge(esem[d.key], d.sigval)
                if o.fn is None:
                    continue
                ins = o.fn(eng)
                if o.dma is not None:
                    ins.then_inc(dsem[o.slot], 16)
                elif o.signal:
                    ins.then_inc(esem[name], 1)

        @block.sync
        def _(eng):
            run("sp", eng)

        @block.scalar
        def _(eng):
            run("act", eng)

        @block.vector
        def _(eng):
            run("dve", eng)

        @block.gpsimd
        def _(eng):
            run("pool", eng)

        @block.tensor
        def _(eng):
            run("pe", eng)


class K:
    def __init__(self, nc, P):
        self.nc = nc
        self.P = P
        self.n = 0

    def name(self, s):
        self.n += 1
        return "%s_%d" % (s, self.n)

    def sb(self, st, shape, dt=F32, name="t"):
        t = st.enter_context(self.nc.sbuf_tensor(self.name(name), list(shape), dt))
        return t, Buf(name)

    def dma(self, eng, out, in_, buf, reads=(), writes=()):
        self.P.op(eng, lambda e: e.dma_start(out=out, in_=in_), reads=reads, writes=writes, dma=buf)

    def mm(self, out, lhsT, rhs, start, stop, r, w):
        self.P.op("pe", lambda e: e.matmul(out, lhsT=lhsT, rhs=rhs, start=start, stop=stop), reads=r, writes=w)

    def tr(self, out, in_, ident, r, w):
        self.P.op("pe", lambda e: e.transpose(out=out, in_=in_, identity=ident), reads=r, writes=w)

    def act(self, out, in_, func, r, w, bias=None, scale=None, accum=None):
        kw = {}
        if bias is not None:
            kw["bias"] = bias
        if scale is not None:
            kw["scale"] = scale
        if accum is not None:
            kw["accum_out"] = accum
        self.P.op("act", lambda e: e.activation(out=out, in_=in_, func=func, **kw), reads=r, writes=w)

    def tt(self, eng, out, in0, in1, op, r, w):
        self.P.op(eng, lambda e: e.tensor_tensor(out=out, in0=in0, in1=in1, op=op), reads=r, writes=w)

    def ts(self, eng, out, in0, s1, op0, r, w, s2=None, op1=None):
        if op1 is None:
            self.P.op(eng, lambda e: e.tensor_scalar(out=out, in0=in0, scalar1=s1, scalar2=None, op0=op0), reads=r, writes=w)
        else:
            self.P.op(eng, lambda e: e.tensor_scalar(out=out, in0=in0, scalar1=s1, scalar2=s2, op0=op0, op1=op1), reads=r, writes=w)

    def stt(self, out, in0, scalar, in1, op0, op1, r, w):
        self.P.op("dve", lambda e: e.scalar_tensor_tensor(out=out, in0=in0, scalar=scalar, in1=in1, op0=op0, op1=op1), reads=r, writes=w)

    def copy(self, eng, out, in_, r, w):
        if eng == "act":
            self.P.op("act", lambda e: e.activation(out=out, in_=in_, func=AF.Copy), reads=r, writes=w)
        else:
            self.P.op(eng, lambda e: e.tensor_copy(out=out, in_=in_), reads=r, writes=w)

    def memset(self, eng, ap, val, w):
        self.P.op(eng, lambda e: e.memset(ap, val), writes=w)

    def scan(self, out, d0, d1, init, r, w):
        self.P.op("dve", lambda e: e.tensor_tensor_scan(out=out, data0=d0, data1=d1, initial=init, op0=ALU.mult, op1=ALU.add), reads=r, writes=w)

    def recip(self, out, in_, r, w):
        self.P.op("dve", lambda e: e.reciprocal(out=out, in_=in_), reads=r, writes=w)


class PsumRing:
    def __init__(self, k, st, n=8):
        self.banks = []
        for i in range(n):
            t = st.enter_context(k.nc.psum_tensor(k.name("ps"), [128, 512], F32))
            self.banks.append((t, Buf("ps%d" % i)))
        self.i = 0

    def get(self):
        t, b = self.banks[self.i % len(self.banks)]
        self.i += 1
        return t, b


def _swap_halves(w):
    sh = w.shape
    w4 = w.reshape(sh[:-1] + (sh[-1] // 64, 2, 32))
    return np.ascontiguousarray(w4[..., ::-1, :]).reshape(sh)


def host_consts(S, TS):
    c = {}
    inv = (10000.0 ** (-np.arange(0, 64, 2, dtype=np.float32) / np.float32(64))).astype(np.float32)
    ang = (np.arange(S, dtype=np.float32)[:, None] * inv[None, :]).astype(np.float32)
    cs, sn = np.cos(ang).astype(np.float32), np.sin(ang).astype(np.float32)
    cosT = np.concatenate([cs.T, cs.T], 0)
    sinT = np.concatenate([-sn.T, sn.T], 0)
    c["c_cos"] = np.ascontiguousarray(np.concatenate([cosT, cosT], 0))
    c["c_sin"] = np.ascontiguousarray(np.concatenate([sinT, sinT], 0))
    c["c_ident"] = np.eye(128, dtype=np.float32)
    c["c_tau"] = np.ascontiguousarray(np.broadcast_to(np.arange(TS, dtype=np.float32)[None, :], (128, TS)))
    k = np.arange(128)[:, None]
    q = np.arange(128)[None, :]
    c["c_caus"] = np.where(k <= q, 0.0, NEG).astype(np.float32)
    c["c_low"] = np.where(k > q, 0.0, NEG).astype(np.float32)
    cc = np.arange(1024)[None, :]
    qi = np.arange(128)[:, None]
    c["c_mgen"] = np.where(16 * (cc - 512) + 31 <= qi, 0.0, NEG).astype(np.float32)
    m = np.arange(16)[None, :, None]
    ni = np.arange(128)[:, None, None]
    qq = np.arange(128)[None, None, :]
    c["c_mt"] = np.where(16 * ni + 31 <= 128 * m + qq, 0.0, NEG).astype(np.float32)
    rel = np.arange(256)[None, :] - 126
    cur = (np.arange(128)[:, None] >= 64).astype(np.int64)
    g = np.where(rel > cur, -1e30, 0.0) + np.where((rel == cur) | (rel == cur - 1), 1e4, 0.0)
    c["c_g"] = g.astype(np.float32)
    NT = S // 128
    j = np.arange(128)[:, None, None]
    i = np.arange(NT)[None, :, None]
    kk = np.arange(128)[None, None, :]
    c["c_e"] = (j == 2 * i + (kk >= 64)).astype(np.float32)
    r_ = np.arange(64)[:, None]
    cidx = np.arange(S)[None, :]
    c["c_ind"] = (((cidx // 64) % 64) == r_).astype(np.float32)
    return c


def host_layout(inp, L):
    o = {}
    f = np.float32
    o["g_mix"] = np.ascontiguousarray(np.broadcast_to(inp["norm_mix"][:, None, :], (L, 128, D))).astype(f)
    o["g_ffn"] = np.ascontiguousarray(np.broadcast_to(inp["norm_ffn"][:, None, :], (L, 128, D))).astype(f)
    o["g_fin"] = np.ascontiguousarray(np.broadcast_to(inp["norm_final"][None, :], (128, D))).astype(f)
    w_in = inp["w_in"]
    o["w_in"] = w_in
    sw = np.concatenate([_swap_halves(w_in[:, :, 512:1024]), _swap_halves(w_in[:, :, 1024:1152]),
                         _swap_halves(w_in[:, :, 1280:1408]), _swap_halves(w_in[:, :, 1536:1664])], axis=-1)
    o["w_sw"] = np.ascontiguousarray(sw)

    def pair(a):
        return np.ascontiguousarray(a.reshape(L, 16, 2, 64).transpose(0, 2, 3, 1).reshape(L, 128, 16))
    o["s_are"] = pair(inp["ssm_a_re"])
    o["s_aim"] = pair(inp["ssm_a_im"])
    o["s_ldt"] = pair(np.broadcast_to(inp["ssm_log_dt"][:, :, None], (L, 32, 64)))
    for nm, src in (("s_bre", "ssm_b_re"), ("s_bim", "ssm_b_im")):
        b = inp[src].reshape(L, 16, 2, 64, 16)
        pad = np.zeros((L, 8, 16, 16, 2, 64), f)
        for j in range(16):
            for gl in range(2):
                pad[:, 2 * (j % 4) + gl, :, j, gl, :] = b[:, j, gl].transpose(0, 2, 1)
        o[nm] = pad.reshape(L, 128, 16, 128)
    for nm, src in (("s_cre", "ssm_c_re"), ("s_cim", "ssm_c_im")):
        cmat = inp[src].reshape(L, 16, 2, 16, 64)
        pad = np.zeros((L, 2, 64, 16, 8, 16), f)
        for j in range(16):
            for gl in range(2):
                pad[:, gl, :, j, 2 * (j % 4) + gl, :] = cmat[:, j, gl].transpose(0, 2, 1)
        o[nm] = pad.reshape(L, 128, 16, 128)
    o["s_d"] = np.ascontiguousarray(inp["ssm_d"].reshape(L, 4, 128).transpose(0, 2, 1))
    o["w_glu"] = inp["ssm_w_glu"]
    o["w_bssm"] = inp["w_branch_ssm"]
    o["w_bnsa"] = inp["w_branch_nsa"]
    o["w_out"] = inp["w_out"]
    for t in ("k", "v"):
        o["c_pe" + t] = np.ascontiguousarray(inp["cmp_pe_" + t].transpose(0, 2, 1))
        o["c_w1" + t] = inp["cmp_w1_" + t]
        o["c_b1" + t] = np.ascontiguousarray(inp["cmp_b1_" + t].reshape(L, 2, 128).transpose(0, 2, 1))
        o["c_w2" + t] = inp["cmp_w2_" + t]
    o["w_fin"] = inp["w_ffn_in"]
    o["w_fout"] = inp["w_ffn_out"]
    o["f_cw"] = np.ascontiguousarray(inp["ffn_conv_w"].reshape(L, 3, NFC, 128).transpose(0, 3, 2, 1))
    o["f_cb"] = np.ascontiguousarray(inp["ffn_conv_b"].reshape(L, NFC, 128).transpose(0, 2, 1))
    return o


def load_w(k, st, src2d, nch, ncols, prow=128, eng="pool", name="w"):
    t, _ = k.sb(st, [prow, nch, ncols], BF16, name)
    bufs = []
    for c in range(nch):
        b = Buf(name)
        k.dma(eng, t[:, c, :], src2d[c * prow:(c + 1) * prow, :], b)
        bufs.append(b)
    return t, bufs


def sincos(k, st, arg, n, out_sin=None, out_cos=None, rb=(), wsin=None, wcos=None):
    for (dst, off, wb) in ((out_sin, 0.0, wsin), (out_cos, 0.25, wcos)):
        if dst is None:
            continue
        a2, ba2 = k.sb(st, [128, n], F32, "sc_a")
        ti, bti = k.sb(st, [128, n], I32, "sc_i")
        tf, btf = k.sb(st, [128, n], F32, "sc_f")
        k.ts("dve", a2[:], arg, off, ALU.add, r=list(rb), w=[ba2])
        k.copy("dve", ti[:], a2[:], r=[ba2], w=[bti])
        k.copy("dve", tf[:], ti[:], r=[bti], w=[btf])
        k.tt("dve", a2[:], a2[:], tf[:], ALU.subtract, r=[ba2, btf], w=[ba2])
        k.act(dst, a2[:], AF.Sin, r=[ba2], w=[wb], scale=SIN_SCALE)


def phase1(k, l, S, TS, x_src, di, sc, cst):
    P = k.P
    TT = TS
    NTT = S // TT
    with ExitStack() as st:
        ring = PsumRing(k, st)
        ident, bid = cst["ident"]
        gam, bgam = k.sb(st, [128, D], F32, "gam")
        k.dma("sp", gam[:], di["g_mix"][l], bgam)
        dvec, bdvec = k.sb(st, [128, 4], F32, "dvec")
        k.dma("sp", dvec[:], di["s_d"][l], bdvec)
        w_in, bw_in = load_w(k, st, di["w_in"][l], 8, INW, name="w_in")
        w_glu, bw_glu = load_w(k, st, di["w_glu"][l], 4, 1024, name="w_glu")
        w_bs, bw_bs = load_w(k, st, di["w_bssm"][l], 4, 1024, name="w_bs")
        bre, bbre = load_w(k, st, di["s_bre"][l].rearrange("p j m -> p (j m)"), 1, 2048, name="bre")
        bim, bbim = load_w(k, st, di["s_bim"][l].rearrange("p j m -> p (j m)"), 1, 2048, name="bim")
        cre, bcre = load_w(k, st, di["s_cre"][l].rearrange("p j m -> p (j m)"), 1, 2048, name="cre")
        cim, bcim = load_w(k, st, di["s_cim"][l].rearrange("p j m -> p (j m)"), 1, 2048, name="cim")
        k.ts("pool", cim[:, 0, :], cim[:, 0, :], -1.0, ALU.mult, r=[bcim[0]], w=[bcim[0]])
        RFre, bRFre = k.sb(st, [128, 16, TS], F32, "RFre")
        RFim, bRFim = k.sb(st, [128, 16, TS], F32, "RFim")
        COSb, bCOSb = k.sb(st, [128, 16, TS], BF16, "COSb")
        SINb, bSINb = k.sb(st, [128, 16, TS], BF16, "SINb")
        dec, bdec = k.sb(st, [128, 16], F32, "dec")
        cT, bcT = k.sb(st, [128, 16], F32, "cT")
        sT, bsT = k.sb(st, [128, 16], F32, "sT")
        nsT, bnsT = k.sb(st, [128, 16], F32, "nsT")
        with ExitStack() as s2:
            are, bare = k.sb(s2, [128, 16], F32, "are")
            aim, baim = k.sb(s2, [128, 16], F32, "aim")
            ldt, bldt = k.sb(s2, [128, 16], F32, "ldt")
            tau, btau = k.sb(s2, [128, TS], F32, "tau")
            k.dma("sp", are[:], di["s_are"][l], bare)
            k.dma("sp", aim[:], di["s_aim"][l], baim)
            k.dma("sp", ldt[:], di["s_ldt"][l], bldt)
            k.dma("sp", tau[:], di["c_tau"], btau)
            dt_, bdt = k.sb(s2, [128, 16], F32, "dt")
            k.act(dt_[:], ldt[:], AF.Exp, r=[bldt], w=[bdt])
            rho, brho = k.sb(s2, [128, 16], F32, "rho")
            thn, bthn = k.sb(s2, [128, 16], F32, "thn")
            k.tt("dve", rho[:], are[:], dt_[:], ALU.mult, r=[bare, bdt], w=[brho])
            k.tt("dve", thn[:], aim[:], dt_[:], ALU.mult, r=[baim, bdt], w=[bthn])
            k.ts("dve", thn[:], thn[:], 1.0 / TWO_PI, ALU.mult, r=[bthn], w=[bthn])
            k.act(dec[:], rho[:], AF.Exp, r=[brho], w=[bdec])
            s1, bs1 = k.sb(s2, [128, 16], F32, "s1")
            c1, bc1 = k.sb(s2, [128, 16], F32, "c1")
            sincos(k, s2, thn[:], 16, s1[:], c1[:], rb=[bthn], wsin=bs1, wcos=bc1)
            abre, babre = k.sb(s2, [128, 16], F32, "abre")
            abim, babim = k.sb(s2, [128, 16], F32, "abim")
            k.tt("dve", abre[:], dec[:], c1[:], ALU.mult, r=[bdec, bc1], w=[babre])
            k.ts("dve", abre[:], abre[:], -1.0, ALU.add, r=[babre], w=[babre])
            k.tt("dve", abim[:], dec[:], s1[:], ALU.mult, r=[bdec, bs1], w=[babim])
            den, bden = k.sb(s2, [128, 16], F32, "den")
            t0, bt0 = k.sb(s2, [128, 16], F32, "t0")
            k.tt("dve", den[:], are[:], are[:], ALU.mult, r=[bare], w=[bden])
            k.tt("dve", t0[:], aim[:], aim[:], ALU.mult, r=[baim], w=[bt0])
            k.tt("dve", den[:], den[:], t0[:], ALU.add, r=[bden, bt0], w=[bden])
            k.recip(den[:], den[:], r=[bden], w=[bden])
            fre, bfre = k.sb(s2, [128, 16], F32, "fre")
            fim, bfim = k.sb(s2, [128, 16], F32, "fim")
            t1, bt1 = k.sb(s2, [128, 16], F32, "t1")
            k.tt("dve", fre[:], abre[:], are[:], ALU.mult, r=[babre, bare], w=[bfre])
            k.tt("dve", t1[:], abim[:], aim[:], ALU.mult, r=[babim, baim], w=[bt1])
            k.tt("dve", fre[:], fre[:], t1[:], ALU.add, r=[bfre, bt1], w=[bfre])
            k.tt("dve", fre[:], fre[:], den[:], ALU.mult, r=[bfre, bden], w=[bfre])
            k.tt("dve", fim[:], abim[:], are[:], ALU.mult, r=[babim, bare], w=[bfim])
            k.tt("dve", t1[:], abre[:], aim[:], ALU.mult, r=[babre, baim], w=[bt1])
            k.tt("dve", fim[:], fim[:], t1[:], ALU.subtract, r=[bfim, bt1], w=[bfim])
            k.tt("dve", fim[:], fim[:], den[:], ALU.mult, r=[bfim, bden], w=[bfim])
            aT, baT = k.sb(s2, [128, 16], F32, "aT")
            k.ts("dve", aT[:], thn[:], float(TS), ALU.mult, r=[bthn], w=[baT])
            sincos(k, s2, aT[:], 16, sT[:], cT[:], rb=[baT], wsin=bsT, wcos=bcT)
            k.ts("dve", nsT[:], sT[:], -1.0, ALU.mult, r=[bsT], w=[bnsT])
            ANG, bANG = k.sb(s2, [128, 16, TS], F32, "ANG")
            SINf, bSINf = k.sb(s2, [128, 16 * TS], F32, "SINf")
            COSf, bCOSf = k.sb(s2, [128, 16 * TS], F32, "COSf")
            for j in range(16):
                k.ts("dve", ANG[:, j, :], tau[:], thn[:, j:j + 1], ALU.mult, r=[btau, bthn], w=[bANG])
            sincos(k, s2, ANG[:].rearrange("p j t -> p (j t)"), 16 * TS, SINf[:], COSf[:], rb=[bANG], wsin=bSINf, wcos=bCOSf)
            SIN3 = SINf[:].rearrange("p (j t) -> p j t", j=16)
            COS3 = COSf[:].rearrange("p (j t) -> p j t", j=16)
            tmp, btmp = k.sb(s2, [128, TS], F32, "tmp")
            for j in range(16):
                k.ts("dve", tmp[:], SIN3[:, j, :], fim[:, j:j + 1], ALU.mult, r=[bSINf, bfim], w=[btmp])
                k.stt(RFre[:, j, :], COS3[:, j, :], fre[:, j:j + 1], tmp[:], ALU.mult, ALU.add, r=[bCOSf, bfre, btmp], w=[bRFre])
                k.ts("dve", tmp[:], SIN3[:, j, :], fre[:, j:j + 1], ALU.mult, r=[bSINf, bfre], w=[btmp])
                k.stt(RFim[:, j, :], COS3[:, j, :], fim[:, j:j + 1], tmp[:], ALU.mult, ALU.subtract, r=[bCOSf, bfim, btmp], w=[bRFim])
            k.copy("dve", COSb[:].rearrange("p j t -> p (j t)"), COSf[:], r=[bCOSf], w=[bCOSb])
            k.copy("dve", SINb[:].rearrange("p j t -> p (j t)"), SINf[:], r=[bSINf], w=[bSINb])
        P.barrier()
        w_sw, bw_sw = load_w(k, st, di["w_sw"][l], 8, 896, name="w_sw")
        xt = [k.sb(st, [128, TT // 128, D], F32, "xt") for _ in range(2)]
        cosr = [k.sb(st, [128, TT], F32, "cosr") for _ in range(2)]
        sinr = [k.sb(st, [128, TT], F32, "sinr") for _ in range(2)]
        NSB = TT // 128
        ss, bss = k.sb(st, [128, NSB], F32, "ss")
        ms, bms = k.sb(st, [128, NSB], F32, "ms")
        sd, bsd = k.sb(st, [128, NSB], F32, "sd")
        rstd, brstd = k.sb(st, [128, NSB], F32, "rstd")
        hh = [k.sb(st, [128, D], BF16, "h") for _ in range(1)]
        hT, bhT = k.sb(st, [128, 8, TT], BF16, "hT")
        uT2 = [k.sb(st, [128, 4, TT], BF16, "uT") for _ in range(3)]
        qTs, bqTs = k.sb(st, [128, 4, TT], BF16, "qTs")
        kvs = {nm: k.sb(st, [128, TT], BF16, nm) for nm in ("kcT", "vcT", "ksT", "kwT")}
        sgs2 = [k.sb(st, [128, 8, TT], BF16, "sgs") for _ in range(3)]
        sgn, bsgn = k.sb(st, [128, 8, TT], BF16, "sgn")
        sg, bsg = k.sb(st, [128, TT // 128, 24], F32, "sg")
        vsel, bvsel = k.sb(st, [128, NSB, 2, 65], BF16, "vsel")
        vwin, bvwin = k.sb(st, [128, NSB, 2, 65], BF16, "vwin")
        k.memset("pool", vsel[:], 1.0, [bvsel])
        k.memset("pool", vwin[:], 1.0, [bvwin])
        tmps = [k.sb(st, [128, TT], F32, "tmp") for _ in range(10)]
        tmpi = [0]

        def gettmp():
            t = tmps[tmpi[0] % len(tmps)]
            tmpi[0] += 1
            return t
        bsc = [k.sb(st, [128, TT], F32, "bsc") for _ in range(4)]
        wall, _ = k.sb(st, [128, 16, 2, TT], F32, "wall")
        bwall = [Buf("wall") for _ in range(16)]
        cwt = [k.sb(st, [128, 16], F32, "cwt") for _ in range(3)]
        xre2 = [k.sb(st, [128, 16, TT], BF16, "xre") for _ in range(2)]
        nxim2 = [k.sb(st, [128, 16, TT], BF16, "nxim") for _ in range(2)]
        car, bcar = k.sb(st, [128, 2, 16], F32, "car")
        k.memset("dve", car[:], 0.0, [bcar])
        ypre, bypre = k.sb(st, [128, TT], F32, "ypre")
        yT, byT = k.sb(st, [128, 4, TT], BF16, "yT")
        sgz, bsgz = k.sb(st, [128, TT], F32, "sgz")
        zzT, bzzT = k.sb(st, [128, 4, TT], BF16, "zzT")
        gss, bgss = k.sb(st, [128, 8, TT], BF16, "gss")

        def load_tile(i):
            t, b = xt[i % 2]
            k.dma("sp", t[:], x_src[i * TT:(i + 1) * TT, :].rearrange("(s p) d -> p s d", p=128), b)
            k.dma("sp", cosr[i % 2][0][:], di["c_cos"][:, i * TT:(i + 1) * TT], cosr[i % 2][1])
            k.dma("sp", sinr[i % 2][0][:], di["c_sin"][:, i * TT:(i + 1) * TT], sinr[i % 2][1])

        def proj(wt, wb, col0, M=128):
            ps, bp = ring.get()
            for c in range(8):
                k.mm(ps[0:M, 0:TT], wt[:, c, col0:col0 + M], hT[:, c, :], c == 0, c == 7, r=[wb[c], bhT], w=[bp])
            return ps, bp

        def inproj_gen(i):
            uT, buT = uT2[i % 3]
            sgs, bsgs = sgs2[i % 3]
            x_t, bx = xt[i % 2]
            cos_t, bcos = cosr[i % 2]
            sin_t, bsin = sinr[i % 2]
            tok = slice(i * TT, (i + 1) * TT)
            for s_ in range(NSB):
                h_t, bh = hh[0]
                k.act(h_t[:], x_t[:, s_, :], AF.Square, r=[bx], w=[bh, bss], accum=ss[:, s_:s_ + 1])
                k.ts("dve", ms[:, s_:s_ + 1], ss[:, s_:s_ + 1], 1.0 / D, ALU.mult, r=[bss], w=[bms], s2=EPS, op1=ALU.add)
                k.act(sd[:, s_:s_ + 1], ms[:, s_:s_ + 1], AF.Sqrt, r=[bms], w=[bsd])
                k.recip(rstd[:, s_:s_ + 1], sd[:, s_:s_ + 1], r=[bsd], w=[brstd])
                k.stt(h_t[:], x_t[:, s_, :], rstd[:, s_:s_ + 1], gam[:], ALU.mult, ALU.mult, r=[bx, brstd, bgam], w=[bh])
            yield
            yield
            yield
            for s_ in range(NSB):
                h_t, bh = hh[0]
                ps, bp = ring.get()
                pbf = ps[:].bitcast(BF16)
                for c in range(8):
                    k.tr(pbf[:, c * 128:(c + 1) * 128], h_t[:, c * 128:(c + 1) * 128], ident[:], r=[bh, bid], w=[bp])
                k.copy("act", hT[:, :, s_ * 128:(s_ + 1) * 128], pbf.rearrange("p (c t) -> p c t", c=8), r=[bp], w=[bhT])
            yield
            yield
            for c4 in range(4):
                ps, bp = proj(w_in, bw_in, c4 * 128)
                k.copy("act", uT[:, c4, :], ps[:, 0:TT], r=[bp], w=[buT])
                yield
            def rope(col, swcol, dst, bdst):
                psA, bA = proj(w_in, bw_in, col)
                psB, bB = proj(w_sw, bw_sw, swcol)
                t1_, bt1_ = gettmp()
                t2_, bt2_ = gettmp()
                k.tt("dve", t1_[:], psA[:, 0:TT], cos_t[:], ALU.mult, r=[bA, bcos], w=[bt1_])
                k.tt("dve", t2_[:], psB[:, 0:TT], sin_t[:], ALU.mult, r=[bB, bsin], w=[bt2_])
                k.tt("pool", dst, t1_[:], t2_[:], ALU.add, r=[bt1_, bt2_], w=[bdst])
            for c in range(4):
                rope(512 + c * 128, c * 128, qTs[:, c, :], bqTs)
                yield
            rope(1024, 512, kvs["kcT"][0][:], kvs["kcT"][1])
            yield
            rope(1280, 640, kvs["ksT"][0][:], kvs["ksT"][1])
            yield
            rope(1536, 768, kvs["kwT"][0][:], kvs["kwT"][1])
            yield
            ps, bp = proj(w_in, bw_in, 1152)
            k.copy("act", kvs["vcT"][0][:], ps[:, 0:TT], r=[bp], w=[kvs["vcT"][1]])
            for c in range(8):
                ps, bp = proj(w_in, bw_in, 1816 + c * 128)
                k.act(sgs[:, c, :], ps[:, 0:TT], AF.Sigmoid, r=[bp], w=[bsgs])
                yield
            for c in range(8):
                ps, bp = proj(w_in, bw_in, 2840 + c * 128)
                k.act(sgn[:, c, :], ps[:, 0:TT], AF.Sigmoid, r=[bp], w=[bsgn])
                yield
            for s_ in range(NSB):
                ps, bp = ring.get()
                for c in range(8):
                    k.mm(ps[:, 0:24], hT[:, c, s_ * 128:(s_ + 1) * 128], w_in[:, c, 1792:1816], c == 0, c == 7, r=[bw_in[c], bhT], w=[bp])
                k.act(sg[:, s_, :], ps[:, 0:24], AF.Sigmoid, r=[bp], w=[bsg])
            for s_ in range(NSB):
                for (col, vt, bv) in ((1408, vsel, bvsel), (1664, vwin, bvwin)):
                    ps, bp = ring.get()
                    for c in range(8):
                        k.mm(ps[:, 0:128], hT[:, c, s_ * 128:(s_ + 1) * 128], w_in[:, c, col:col + 128], c == 0, c == 7, r=[bw_in[c], bhT], w=[bp])
                    k.copy("act", vt[:, s_, :, 0:64], ps[:, 0:128].rearrange("p (h d) -> p h d", h=2), r=[bp], w=[bv])
                    yield
            k.dma("sp", sc["qT"].rearrange("(c p) s -> p c s", p=128)[:, :, tok], qTs[:], bqTs)
            for nm in ("kcT", "vcT", "ksT", "kwT"):
                k.dma("sp", sc[nm][:, tok], kvs[nm][0][:], kvs[nm][1])
            k.dma("sp", sc["sgnT"].rearrange("(c p) s -> p c s", p=128)[:, :, tok], sgn[:], bsgn)
            k.dma("sp", sc["sgTok"][tok].rearrange("(s p) c -> p s c", p=128), sg[:], bsg)
            k.dma("sp", sc["vs"][tok].rearrange("(s p) h c -> p s h c", p=128), vsel[:], bvsel)
            k.dma("sp", sc["vw"][tok].rearrange("(s p) h c -> p s h c", p=128), vwin[:], bvwin)
        def s5_gen(i):
            uT, buT = uT2[i % 3]
            xre, bxre = xre2[i % 2]
            nxim, bnxim = nxim2[i % 2]
            def stageA(j):
                c4 = j // 4
                psr, bpr = ring.get()
                psi, bpi = ring.get()
                k.mm(psr[:, 0:TT], bre[:, 0, j * 128:(j + 1) * 128], uT[:, c4, :], True, True, r=[bbre[0], buT], w=[bpr])
                k.mm(psi[:, 0:TT], bim[:, 0, j * 128:(j + 1) * 128], uT[:, c4, :], True, True, r=[bbim[0], buT], w=[bpi])
                b_re, bb_re = bsc[(2 * j) % 4]
                b_im, bb_im = bsc[(2 * j + 1) % 4]
                t1_, bt1_ = gettmp()
                t2_, bt2_ = gettmp()
                k.tt("dve", t1_[:], psr[:, 0:TT], RFre[:, j, :], ALU.mult, r=[bpr, bRFre], w=[bt1_])
                k.tt("dve", t2_[:], psi[:, 0:TT], RFim[:, j, :], ALU.mult, r=[bpi, bRFim], w=[bt2_])
                k.tt("pool", b_re[:], t1_[:], t2_[:], ALU.subtract, r=[bt1_, bt2_], w=[bb_re])
                t3_, bt3_ = gettmp()
                t4_, bt4_ = gettmp()
                k.tt("dve", t3_[:], psi[:, 0:TT], RFre[:, j, :], ALU.mult, r=[bpi, bRFre], w=[bt3_])
                k.tt("dve", t4_[:], psr[:, 0:TT], RFim[:, j, :], ALU.mult, r=[bpr, bRFim], w=[bt4_])
                k.tt("pool", b_im[:], t3_[:], t4_[:], ALU.add, r=[bt3_, bt4_], w=[bb_im])

            def stageB(j):
                b_re, bb_re = bsc[(2 * j) % 4]
                b_im, bb_im = bsc[(2 * j + 1) % 4]
                w_re, bw_re = wall[:, j, 0, :], bwall[j]
                w_im, bw_im = wall[:, j, 1, :], bwall[j]
                dj = dec[:, j:j + 1].to_broadcast([128, TT])
                k.scan(w_re, dj, b_re[:], car[:, 0, j:j + 1], r=[bdec, bb_re, bcar], w=[bw_re])
                k.scan(w_im, dj, b_im[:], car[:, 1, j:j + 1], r=[bdec, bb_im, bcar], w=[bw_im])
                t5_, bt5_ = gettmp()
                t6_, bt6_ = gettmp()
                k.tt("dve", t5_[:], w_re, COSb[:, j, :], ALU.mult, r=[bw_re, bCOSb], w=[bt5_])
                k.tt("dve", t6_[:], w_im, SINb[:, j, :], ALU.mult, r=[bw_im, bSINb], w=[bt6_])
                k.tt("pool", xre[:, j, :], t5_[:], t6_[:], ALU.subtract, r=[bt5_, bt6_], w=[bxre])
                t7_, bt7_ = gettmp()
                t8_, bt8_ = gettmp()
                k.tt("pool", t7_[:], w_re, SINb[:, j, :], ALU.mult, r=[bw_re, bSINb], w=[bt7_])
                k.tt("pool", t8_[:], w_im, COSb[:, j, :], ALU.mult, r=[bw_im, bCOSb], w=[bt8_])
                k.tt("pool", nxim[:, j, :], t7_[:], t8_[:], ALU.add, r=[bt7_, bt8_], w=[bnxim])

            stageA(0)
            for j in range(16):
                if j + 1 < 16:
                    stageA(j + 1)
                stageB(j)
                yield
            wl_re = wall[:, :, 0, TT - 1]
            wl_im = wall[:, :, 1, TT - 1]
            (c0, bc0), (c1, bc1), (c2, bc2) = cwt
            k.tt("dve", c0[:], wl_re, cT[:], ALU.mult, r=bwall + [bcT], w=[bc0])
            k.tt("dve", c1[:], wl_im, nsT[:], ALU.mult, r=bwall + [bnsT], w=[bc1])
            k.tt("dve", car[:, 0, :], c0[:], c1[:], ALU.add, r=[bc0, bc1], w=[bcar])
            k.tt("dve", c2[:], wl_im, cT[:], ALU.mult, r=bwall + [bcT], w=[bc2])
            k.tt("dve", c0[:], wl_re, sT[:], ALU.mult, r=bwall + [bsT], w=[bc0])
            k.tt("dve", car[:, 1, :], c2[:], c0[:], ALU.add, r=[bc2, bc0], w=[bcar])
        def tail_gen(i):
            uT, buT = uT2[i % 3]
            sgs, bsgs = sgs2[i % 3]
            xre, bxre = xre2[i % 2]
            nxim, bnxim = nxim2[i % 2]
            tok = slice(i * TT, (i + 1) * TT)
            yield
            for c4 in range(4):
                ps, bp = ring.get()
                for jj in range(4):
                    j = 4 * c4 + jj
                    k.mm(ps[:, 0:TT], cre[:, 0, j * 128:(j + 1) * 128], xre[:, j, :], jj == 0, False, r=[bcre[0], bxre], w=[bp])
                    k.mm(ps[:, 0:TT], cim[:, 0, j * 128:(j + 1) * 128], nxim[:, j, :], False, jj == 3, r=[bcim[0], bnxim], w=[bp])
                k.stt(ypre[:], uT[:, c4, :], dvec[:, c4:c4 + 1], ps[:, 0:TT], ALU.mult, ALU.add, r=[buT, bdvec, bp], w=[bypre])
                k.act(yT[:, c4, :], ypre[:], AF.Gelu_apprx_tanh, r=[bypre], w=[byT])
                yield
            yield
            for kk in range(4):
                psg, bpg = ring.get()
                for c4 in range(4):
                    k.mm(psg[:, 0:TT], w_glu[:, c4, (4 + kk) * 128:(5 + kk) * 128], yT[:, c4, :], c4 == 0, c4 == 3, r=[bw_glu[c4], byT], w=[bpg])
                k.act(sgz[:], psg[:, 0:TT], AF.Sigmoid, r=[bpg], w=[bsgz])
                psv, bpv = ring.get()
                for c4 in range(4):
                    k.mm(psv[:, 0:TT], w_glu[:, c4, kk * 128:(kk + 1) * 128], yT[:, c4, :], c4 == 0, c4 == 3, r=[bw_glu[c4], byT], w=[bpv])
                k.tt("dve", zzT[:, kk, :], psv[:, 0:TT], sgz[:], ALU.mult, r=[bpv, bsgz], w=[bzzT])
                yield
            yield
            for fc in range(8):
                ps, bp = ring.get()
                for kk in range(4):
                    k.mm(ps[:, 0:TT], w_bs[:, kk, fc * 128:(fc + 1) * 128], zzT[:, kk, :], kk == 0, kk == 3, r=[bw_bs[kk], bzzT], w=[bp])
                k.tt("dve", gss[:, fc, :], ps[:, 0:TT], sgs[:, fc, :], ALU.mult, r=[bp, bsgs], w=[bgss])
                yield
            k.dma("sp", sc["gssT"].rearrange("(c p) s -> p c s", p=128)[:, :, tok], gss[:], bgss)
        def step(g):
            try:
                next(g)
                return True
            except StopIteration:
                return False

        load_tile(0)
        if NTT > 1:
            load_tile(1)
        for _ in inproj_gen(0):
            pass
        for i in range(NTT + 1):
            if i + 2 < NTT:
                load_tile(i + 2)
            gens = []
            if i < NTT:
                gens.append(s5_gen(i))
            if i >= 1:
                gens.append(tail_gen(i - 1))
            gi = inproj_gen(i + 1) if i + 1 < NTT else iter(())
            alive = [True] * len(gens)
            alive_i = True
            n_ = 0
            while any(alive) or alive_i:
                for gi_, g_ in enumerate(gens):
                    if alive[gi_]:
                        alive[gi_] = step(g_)
                for _ in range(1 + (n_ % 2)):
                    if alive_i:
                        alive_i = step(gi)
                n_ += 1
    P.barrier()


def phase2(k, pers, l, S, di, sc, cst):
    P = k.P
    NC = S // 16 - 1
    NCP = S // 16
    NCT = NCP // 128
    KcT, bKcT = k.sb(pers, [128, NCP], BF16, "KcT")
    Vc, bVc = k.sb(pers, [128, NCT, 2, 65], BF16, "Vc")
    k.memset("pool", KcT[:], 0.0, [bKcT])
    k.memset("pool", Vc[:], 1.0, [bVc])
    with ExitStack() as st:
        ring = PsumRing(k, st)
        for typ in ("k", "v"):
            with ExitStack() as s2:
                xT, bxT = k.sb(s2, [128, S], BF16, "cxT")
                k.dma("sp", xT[:], sc["kcT" if typ == "k" else "vcT"], bxT)
                w1, bw1 = k.sb(s2, [128, 32, 256], BF16, "w1")
                bw1b = Buf("w1b")
                src = di["c_w1" + typ][l].rearrange("(l d) c -> d l c", d=64)
                k.dma("pool", w1[0:64], src, bw1)
                k.dma("pool", w1[64:128], src, bw1b)
                pe2, bpe2 = k.sb(s2, [64, 32, 2], BF16, "pe2")
                pe_f, bpe_f = k.sb(s2, [64, 32], F32, "pe_f")
                k.dma("sp", pe_f[:], di["c_pe" + typ][l], bpe_f)
                k.copy("dve", pe2[:, :, 0], pe_f[:], r=[bpe_f], w=[bpe2])
                k.copy("dve", pe2[:, :, 1], pe_f[:], r=[bpe_f], w=[bpe2])
                b1, bb1 = k.sb(s2, [128, 2], F32, "b1")
                k.dma("sp", b1[:], di["c_b1" + typ][l], bb1)
                w2, bw2 = k.sb(s2, [128, 2, 64], BF16, "w2")
                k.dma("pool", w2[:], di["c_w2" + typ][l].rearrange("(c p) d -> p c d", p=128), bw2)
                bias, bbias = k.sb(s2, [128, 2], F32, "bias")
                for cc in range(2):
                    ps, bp = ring.get()
                    for li in range(32):
                        k.mm(ps[:, 0:2], w1[0:64, li, cc * 128:(cc + 1) * 128], pe2[:, li, :], li == 0, li == 31, r=[bw1, bpe2], w=[bp])
                    k.tt("dve", bias[:, cc:cc + 1], ps[:, 0:1], b1[:, cc:cc + 1], ALU.add, r=[bp, bb1], w=[bbias])
                for hk in range(2):
                    hid, bhid = k.sb(s2, [128, 2, NCP], BF16, "hid")
                    k.memset("pool", hid[:], 0.0, [bhid])
                    bw = bw1 if hk == 0 else bw1b
                    for cc in range(2):
                        ps, bp = ring.get()
                        for li in range(32):
                            k.mm(ps[:, 0:NC], w1[hk * 64:(hk + 1) * 64, li, cc * 128:(cc + 1) * 128],
                                 xT[hk * 64:(hk + 1) * 64, li:li + 16 * (NC - 1) + 1:16], li == 0, li == 31, r=[bw, bxT], w=[bp])
                        k.act(hid[:, cc, 0:NC], ps[:, 0:NC], AF.Gelu_apprx_tanh, r=[bp, bbias], w=[bhid], bias=bias[:, cc:cc + 1])
                    if typ == "k":
                        ps, bp = ring.get()
                        for cc in range(2):
                            k.mm(ps[hk * 64:(hk + 1) * 64, 0:NC], w2[:, cc, :], hid[:, cc, 0:NC], cc == 0, cc == 1, r=[bw2, bhid], w=[bp])
                        k.copy("act", KcT[hk * 64:(hk + 1) * 64, 0:NC], ps[hk * 64:(hk + 1) * 64, 0:NC], r=[bp], w=[bKcT])
                    else:
                        for nt in range(NCT):
                            ps, bp = ring.get()
                            for cc in range(2):
                                k.mm(ps[:, 0:64], hid[:, cc, nt * 128:(nt + 1) * 128], w2[:, cc, :], cc == 0, cc == 1, r=[bhid, bw2], w=[bp])
                            k.copy("act", Vc[:, nt, hk, 0:64], ps[:, 0:64], r=[bp], w=[bVc])
            P.barrier()
    if "dbg_kc" in sc:
        k.dma("sp", sc["dbg_kc"].rearrange("h d n -> (h d) n"), KcT[:], bKcT)
        k.dma("sp", sc["dbg_vc"], Vc[:], bVc)
    return (KcT, bKcT), (Vc, bVc)


def phase3(k, l, S, x_src, di, sc, cst, cmp_t):
    P = k.P
    NT = S // 128
    NCP = S // 16
    NCT = NCP // 128
    NB = S // 64
    (KcT, bKcT), (Vc, bVc) = cmp_t
    ident, bid = cst["ident"]
    with ExitStack() as st:
        ring = PsumRing(k, st, 3)
        ringO = PsumRing(k, st, 3)
        ringM = PsumRing(k, st, 2)
        KsM = []
        for hk in range(2):
            t, b = k.sb(st, [128, S], BF16, "KsM")
            b2 = Buf("KsMi")
            k.dma("sp", t[hk * 64:(hk + 1) * 64], sc["ksT"][hk * 64:(hk + 1) * 64, :], b)
            k.dma("pool", t[(1 - hk) * 64:(2 - hk) * 64], di["c_ind"], b2)
            KsM.append((t, b, b2))
        Vs, bVs = k.sb(st, [128, NT, 2, 65], BF16, "Vs")
        k.dma("sp", Vs[:], sc["vs"].rearrange("(n p) h c -> p n h c", p=128), bVs)

        def cload(name, shape, src, dt=BF16):
            t, b = k.sb(st, shape, dt, name)
            k.dma("pool" if dt == BF16 else "sp", t[:], src, b)
            return t, b
        caus, bcaus = cload("caus", [128, 128], di["c_caus"])
        low, blow = cload("low", [128, 128], di["c_low"])
        mgen, bmgen = cload("mgen", [128, 1024], di["c_mgen"])
        mt, bmt = cload("mt", [128, 16, 128], di["c_mt"])
        G, bG = cload("G", [128, 256], di["c_g"], F32)
        wbn, bwbn = load_w(k, st, di["w_bnsa"][l], 4, 1024, name="wbn")
        ident32, bid32 = cload("ident32", [128, 128], di["c_ident"], F32)
        w_out, bw_out = load_w(k, st, di["w_out"][l], 8, 1024, name="w_out")
        QT = [[k.sb(st, [128, 4, 128], BF16, "QT") for _ in range(2)] for _ in range(2)]
        for pb_ in range(2):
            for hk_ in range(2):
                k.memset("pool", QT[pb_][hk_][0][:], 0.0, [QT[pb_][hk_][1]])
        NHALF = max(1, NB // 64)
        Qsel = [[[k.sb(st, [128, 4, 128], BF16, "Qsel") + (Buf("Qselm"),) for _ in range(NHALF)] for _ in range(2)] for _ in range(2)]
        negm_sw, bnegm_sw = k.sb(st, [128, 128], BF16, "negm_sw")
        k.memset("pool", negm_sw[:], 0.0, [bnegm_sw])
        KwT = [k.sb(st, [128, 640], BF16, "KwT") for _ in range(2)]
        Vw = [k.sb(st, [128, 5, 2, 65], BF16, "Vw") for _ in range(2)]
        gtok = [k.sb(st, [128, 24], F32, "gtok") for _ in range(2)]
        sgn = [k.sb(st, [128, 8, 128], BF16, "sgn") for _ in range(2)]
        gss = [k.sb(st, [128, 8, 128], BF16, "gss") for _ in range(2)]
        xin = [k.sb(st, [128, D], F32, "xin") for _ in range(2)]
        eg = [k.sb(st, [128, NCP], F32, "eg") for _ in range(4)]
        den4, bden4 = k.sb(st, [128, 4], F32, "den4")
        rden4, brden4 = k.sb(st, [128, 4], F32, "rden4")
        pg, bpg = k.sb(st, [128, NCP + 8], F32, "pg")
        k.memset("pool", pg[:], 0.0, [bpg])
        blk, bblk = k.sb(st, [128, NB], F32, "blk")
        blk2, bblk2 = k.sb(st, [128, NB], F32, "blk2")
        m8, bm8 = k.sb(st, [128, 16], F32, "m8")
        negm, bnegm = k.sb(st, [128, 128], BF16, "negm")
        k.memset("pool", negm[:], 0.0, [bnegm])
        pTs = [k.sb(st, [128, 512], BF16, "pT") for _ in range(4)]
        pti = [0]
        osbs = [k.sb(st, [65, 512], BF16, "osb") for _ in range(4)]
        s4s = [k.sb(st, [128, 4], F32, "s4") for _ in range(4)]
        oq = [k.sb(st, [128, 8, 64], F32, "oq") for _ in range(2)]
        oqb, boqb = k.sb(st, [128, 512], BF16, "oqb")
        oT2, boT2 = k.sb(st, [128, 4, 128], BF16, "oT2")
        mrg, bmrg = k.sb(st, [128, 8, 128], BF16, "mrg")
        xm = [k.sb(st, [128, D], F32, "xm") for _ in range(1)]

        def loads(qb):
            s0 = qb * 128
            pb = qb % 2
            qv = sc["qT"].rearrange("(h d) s -> d h s", d=64)
            for hk in range(2):
                t, b = QT[pb][hk]
                k.dma("sp", t[hk * 64:(hk + 1) * 64], qv[:, hk * 4:(hk + 1) * 4, s0:s0 + 128], b)
                for hf in range(NHALF):
                    if hf * 32 <= qb:
                        t, b, _ = Qsel[pb][hk][hf]
                        k.dma("sp", t[hk * 64:(hk + 1) * 64], qv[:, hk * 4:(hk + 1) * 4, s0:s0 + 128], b)
            lo = max(0, s0 - 512)
            t, b = KwT[pb]
            k.dma("sp", t[:, 640 - (s0 + 128 - lo):640], sc["kwT"][:, lo:s0 + 128], b)
            nw = (s0 + 128 - lo) // 128
            t, b = Vw[pb]
            k.dma("sp", t[:, 5 - nw:5], sc["vw"][lo:s0 + 128].rearrange("(n p) h c -> p n h c", p=128), b)
            t, b = gtok[pb]
            k.dma("sp", t[:], sc["sgTok"][s0:s0 + 128, :], b)
            t, b = sgn[pb]
            k.dma("sp", t[:], sc["sgnT"].rearrange("(c p) s -> p c s", p=128)[:, :, s0:s0 + 128], b)
            t, b = gss[pb]
            k.dma("sp", t[:], sc["gssT"].rearrange("(c p) s -> p c s", p=128)[:, :, s0:s0 + 128], b)
            t, b = xin[pb]
            k.dma("sp", t[:], x_src[s0:s0 + 128, :], b)

        DEPTH = 2
        pipe = []
        delayed = []

        def tick():
            for d in delayed:
                d[0] -= 1
            while delayed and delayed[0][0] <= 0:
                delayed.pop(0)[1]()

        cur_tag = [0]

        def push(score_fn, pv_fn, after=None):
            tok_ = score_fn()
            pipe.append((pv_fn, tok_, after, cur_tag[0]))
            if len(pipe) > DEPTH:
                pv, tk, af, _ = pipe.pop(0)
                pv(tk)
                if af is not None:
                    af()
            tick()

        def flush():
            while pipe:
                pv, tk, af, _ = pipe.pop(0)
                pv(tk)
                if af is not None:
                    af()
            while delayed:
                delayed.pop(0)[1]()

        def attn_tile(Ops, bO, first, last_, KT_ap, bKT, V_ap, bV, Q2, bQ, smask=None, emask=None, after=None):
            def score():
                psS, bS = ring.get()
                nmask = (4 if smask is not None else 0) + (1 if emask is not None else 0)
                rl = (bKT if isinstance(bKT, list) else [bKT]) + (bQ if isinstance(bQ, list) else [bQ])
                k.mm(psS[:, 0:512], KT_ap, Q2, True, nmask == 0, r=rl, w=[bS])
                done = 0
                assert emask is None
                if smask is not None:
                    m_ap, bm = smask
                    for g in range(4):
                        done += 1
                        k.mm(psS[:, g * 128:(g + 1) * 128], ident[:], m_ap, False, done == nmask, r=[bid, bm], w=[bS])
                pT, bpT = pTs[pti[0] % len(pTs)]
                pti[0] += 1
                k.act(pT[:], psS[:, 0:512], AF.Exp, r=[bS], w=[bpT], scale=0.125)
                return (pT, bpT)

            def pv(tk):
                pT, bpT = tk
                k.mm(Ops[0:65, 0:512], V_ap, pT[:], first, last_, r=[bV, bpT], w=[bO])
            push(score, pv, after)

        fin_i = [0]

        def finalize(Ops, bO, qb_, hk_, br_, first_branch):
            tag_ = cur_tag[0]

            def stage_a():
                while sum(1 for d_ in delayed if len(d_) > 3) >= 3:
                    delayed.pop(0)[1]()
                osb, bosb = osbs[fin_i[0] % 4]
                s4, bs4 = s4s[fin_i[0] % 4]
                fin_i[0] += 1
                k.copy("act", osb[:], Ops[0:65, 0:512], r=[bO], w=[bosb])

                def stage_b():
                    pst32, bpst = ringM.get()
                    pst = pst32[:].bitcast(BF16)
                    for g in range(4):
                        k.tr(pst[:, g * 66:g * 66 + 65], osb[:, g * 128:(g + 1) * 128], ident[0:65, 0:65], r=[bosb, bid], w=[bpst])
                    p3 = pst[:, 0:264].rearrange("p (g c) -> p g c", c=66)
                    gt_, bgt_ = gtok[qb_ % 2]
                    oq_t, boq = oq[qb_ % 2]
                    k.ts("dve", s4[:], p3[:, :, 64], 1e-20, ALU.max, r=[bpst], w=[bs4])
                    k.recip(s4[:], s4[:], r=[bs4], w=[bs4])
                    c0_ = hk_ * 12 + br_
                    k.tt("dve", s4[:], s4[:], gt_[:, c0_:c0_ + 10:3], ALU.mult, r=[bs4, bgt_], w=[bs4])
                    for g in range(4):
                        h_ = hk_ * 4 + g
                        if first_branch:
                            k.ts("dve", oq_t[:, h_, :], p3[:, g, 0:64], s4[:, g:g + 1], ALU.mult, r=[bpst, bs4], w=[boq])
                        else:
                            k.stt(oq_t[:, h_, :], p3[:, g, 0:64], s4[:, g:g + 1], oq_t[:, h_, :], ALU.mult, ALU.add, r=[bpst, bs4, boq], w=[boq])
                delayed.append([3, stage_b, tag_, "fin"])
            return stage_a

        mtmps = [k.sb(st, [128, 128], F32, "mtmp") for _ in range(2)]

        def epilogue1(qb):
            pb = qb % 2
            oq_t, boq = oq[pb]
            if "dbg_o" in sc:
                k.dma("sp", sc["dbg_o"][qb * 128:(qb + 1) * 128, :], oq_t[:].rearrange("p h d -> p (h d)"), boq)
            k.copy("dve", oqb[:], oq_t[:].rearrange("p h d -> p (h d)"), r=[boq], w=[boqb])
            pst, bpst = ringM.get()
            pbf = pst[:].bitcast(BF16)
            for c in range(4):
                k.tr(pbf[:, c * 128:(c + 1) * 128], oqb[:, c * 128:(c + 1) * 128], ident[:], r=[boqb, bid], w=[bpst])
            k.copy("dve", oT2[:].rearrange("p c q -> p (c q)"), pbf[:, 0:512], r=[bpst], w=[boT2])
            sg_t, bsgn_ = sgn[pb]
            gs_t, bgs_ = gss[pb]
            for half in range(2):
                ps, bp = ringM.get()
                for f4 in range(4):
                    fc = half * 4 + f4
                    for c in range(4):
                        k.mm(ps[:, f4 * 128:(f4 + 1) * 128], wbn[:, c, fc * 128:(fc + 1) * 128], oT2[:, c, :],
                             c == 0, c == 3, r=[bwbn[c], boT2], w=[bp])
                for f4 in range(4):
                    fc = half * 4 + f4
                    mt_, bmt_ = mtmps[fc % 2]
                    k.tt("dve", mt_[:], ps[:, f4 * 128:(f4 + 1) * 128], sg_t[:, fc, :], ALU.mult, r=[bp, bsgn_], w=[bmt_])
                    k.tt("pool", mrg[:, fc, :], mt_[:], gs_t[:, fc, :], ALU.add, r=[bmt_, bgs_], w=[bmrg])
            delayed.append([8, lambda: epilogue2(qb), qb])

        def epilogue2(qb):
            s0 = qb * 128
            pb = qb % 2
            x_t, bx = xin[pb]
            xm_t, bxm = xm[0]
            for half in range(2):
                ps, bp = ringM.get()
                for fc in range(8):
                    k.mm(ps[:, 0:512], mrg[:, fc, :], w_out[:, fc, half * 512:(half + 1) * 512], fc == 0, fc == 7, r=[bmrg, bw_out[fc]], w=[bp])
                k.tt("dve", xm_t[:, half * 512:(half + 1) * 512], ps[:, 0:512], x_t[:, half * 512:(half + 1) * 512], ALU.add, r=[bp, bx], w=[bxm])
            k.dma("sp", sc["xmid"][s0:s0 + 128, :], xm_t[:], bxm)

        def force(tag_max):
            while pipe and pipe[0][3] <= tag_max:
                pv, tk, af, _ = pipe.pop(0)
                pv(tk)
                if af is not None:
                    af()
            progressed = True
            while progressed:
                progressed = False
                for idx_, d_ in enumerate(delayed):
                    if d_[2] <= tag_max:
                        delayed.pop(idx_)
                        d_[1]()
                        progressed = True
                        break

        def chainA(qb, hk):
            pb = qb % 2
            Qt, bQ = QT[pb][hk]
            for g in range(4):
                ps, bp = ringM.get()
                k.mm(ps[:, 0:NCP], Qt[:, g, :], KcT[:, 0:NCP], True, False, r=[bQ, bKcT], w=[bp])
                k.mm(ps[:, 0:NCP], ident[:], mgen[:, 512 - 8 * qb:512 - 8 * qb + NCP], False, True, r=[bid, bmgen], w=[bp])
                k.act(eg[g][0][:], ps[:, 0:NCP], AF.Exp, r=[bp], w=[eg[g][1], bden4], scale=0.125, accum=den4[:, g:g + 1])
            k.ts("dve", rden4[:], den4[:], 1e-20, ALU.max, r=[bden4], w=[brden4])
            k.recip(rden4[:], rden4[:], r=[brden4], w=[brden4])
            k.ts("dve", pg[:, 1:1 + NCP], eg[0][0][:], rden4[:, 0:1], ALU.mult, r=[eg[0][1], brden4], w=[bpg])
            for g in range(1, 4):
                k.stt(pg[:, 1:1 + NCP], eg[g][0][:], rden4[:, g:g + 1], pg[:, 1:1 + NCP], ALU.mult, ALU.add, r=[eg[g][1], brden4, bpg], w=[bpg])
            P.op("dve", lambda e: e.tensor_reduce(out=blk[:], in_=pg[:, 0:NCP].rearrange("p (j o) -> p j o", o=4), axis=AX.X, op=ALU.add), reads=[bpg], writes=[bblk])
            k.tt("dve", blk[:], blk[:], pg[:, 4:4 + 4 * NB:4], ALU.add, r=[bblk, bpg], w=[bblk])
            k.tt("dve", blk[:], blk[:], G[:, 126 - 2 * qb:126 - 2 * qb + NB], ALU.add, r=[bblk, bG], w=[bblk])
            if qb >= 1:
                k.ts("dve", blk[:, 0:1], blk[:, 0:1], 1e4, ALU.add, r=[bblk], w=[bblk])
            P.op("dve", lambda e: e.max(out=m8[:, 0:8], in_=blk[:]), reads=[bblk], writes=[bm8])
            P.op("dve", lambda e: e.match_replace(out=blk2[:], in_to_replace=m8[:, 0:8], in_values=blk[:], imm_value=-3e38), reads=[bblk, bm8], writes=[bblk2])
            P.op("dve", lambda e: e.max(out=m8[:, 8:16], in_=blk2[:]), reads=[bblk2], writes=[bm8])
            nhalf_used = 1 if qb < 32 else NHALF
            need_nat = (hk == 1) or nhalf_used > 1
            need_sw = (hk == 0) or nhalf_used > 1
            if need_nat:
                k.ts("dve", negm[:, 0:NB], blk[:], m8[:, 15:16], ALU.is_lt, r=[bblk, bm8], w=[bnegm], s2=NEG, op1=ALU.mult)
            if need_sw:
                n0 = min(NB, 64)
                k.ts("dve", negm_sw[:, 64:64 + n0], blk[:, 0:n0], m8[:, 15:16], ALU.is_lt, r=[bblk, bm8], w=[bnegm_sw], s2=NEG, op1=ALU.mult)
                if NB > 64:
                    k.ts("dve", negm_sw[:, 0:NB - 64], blk[:, 64:NB], m8[:, 15:16], ALU.is_lt, r=[bblk, bm8], w=[bnegm_sw], s2=NEG, op1=ALU.mult)

        def chainB(qb, hk):
            pb = qb % 2
            os_ = slice((1 - hk) * 64, (2 - hk) * 64)
            nhalf_used = 1 if qb < 32 else NHALF
            for hf in range(nhalf_used):
                use_sw = (hk == 0 and hf == 0) or (hk == 1 and hf == 1)
                src_t, bsrc = (negm_sw, bnegm_sw) if use_sw else (negm, bnegm)
                ps, bp = ringM.get()
                pbf = ps[:].bitcast(BF16)
                k.tr(pbf[:, 0:128], src_t[:], ident[:], r=[bsrc, bid], w=[bp])
                qs_t, _, bqm = Qsel[pb][hk][hf]
                for g in range(4):
                    k.copy("dve", qs_t[os_, g, :], pbf[os_, 0:128], r=[bp], w=[bqm])

        loads(0)
        chainA(0, 0)
        for qb in range(NT):
            s0 = qb * 128
            pb = qb % 2
            for hk in range(2):
                cur_tag[0] = qb
                Qt, bQ = QT[pb][hk]
                Q2 = Qt[:].rearrange("d g q -> d (g q)")
                chainB(qb, hk)
                Ops, bO = ringO.get()
                fa = finalize(Ops, bO, qb, hk, 0, True)
                tiles = [nt for nt in range(NCT) if qb - 16 * nt >= 0]
                for idx, nt in enumerate(tiles):
                    m = qb - 16 * nt
                    sm = (mt[:, m, :], bmt) if m < 16 else None
                    attn_tile(Ops, bO, idx == 0, idx == len(tiles) - 1, KcT[:, nt * 128:(nt + 1) * 128], bKcT,
                              Vc[:, nt, hk, :], bVc, Q2, bQ, smask=sm, after=fa if idx == len(tiles) - 1 else None)
                Ops, bO = ringO.get()
                fa = finalize(Ops, bO, qb, hk, 2, False)
                tiles = [wt for wt in range(5) if s0 - 512 + 128 * wt >= 0]
                kw_full, bkw = KwT[pb]
                kw_t = kw_full
                vw_t, bvw = Vw[pb]
                for idx, wt in enumerate(tiles):
                    sm = (low[:], blow) if wt == 0 else ((caus[:], bcaus) if wt == 4 else None)
                    attn_tile(Ops, bO, idx == 0, idx == len(tiles) - 1, kw_t[:, wt * 128:(wt + 1) * 128], bkw,
                              vw_t[:, wt, hk, :], bvw, Q2, bQ, smask=sm, after=fa if idx == len(tiles) - 1 else None)
                if hk == 1 and qb + 1 < NT:
                    force(qb - 1)
                    loads(qb + 1)
                if hk == 0:
                    chainA(qb, 1)
                elif qb + 1 < NT:
                    chainA(qb + 1, 0)
                Ops, bO = ringO.get()
                fa = finalize(Ops, bO, qb, hk, 1, False)
                for i in range(qb + 1):
                    sm = (caus[:], bcaus) if i == qb else None
                    qs_t, bqs, bqm = Qsel[pb][hk][i // 32]
                    attn_tile(Ops, bO, i == 0, i == qb, KsM[hk][0][:, i * 128:(i + 1) * 128], [KsM[hk][1], KsM[hk][2]],
                              Vs[:, i, hk, :], bVs, qs_t[:].rearrange("d g q -> d (g q)"), [bqs, bqm], smask=sm, after=fa if i == qb else None)
                if hk == 1:
                    delayed_ep = (lambda q_=qb: (lambda: delayed.append([6, lambda: epilogue1(q_), q_])))(qb)
                    pipe[-1] = (pipe[-1][0], pipe[-1][1], (lambda f1=pipe[-1][2], f2=delayed_ep: (f1(), f2())), pipe[-1][3])
        flush()
    P.barrier()


def phase4(k, l, S, di, sc, cst, dst, last):
    P = k.P
    TT = 256
    NTT = S // TT
    NSB = TT // 128
    ident, bid = cst["ident"]
    with ExitStack() as st:
        ring = PsumRing(k, st)
        w_fin, bw_fin = load_w(k, st, di["w_fin"][l], 8, 2 * DFF, name="w_fin")
        w_fo, bw_fo = load_w(k, st, di["w_fout"][l], NFC, D, name="w_fo")
        gam, bgam = k.sb(st, [128, D], F32, "gam")
        k.dma("sp", gam[:], di["g_ffn"][l], bgam)
        cw, bcw = k.sb(st, [128, NFC, 3], F32, "cw")
        k.dma("sp", cw[:], di["f_cw"][l], bcw)
        cb, bcb = k.sb(st, [128, NFC], F32, "cb")
        k.dma("sp", cb[:], di["f_cb"][l], bcb)
        if last:
            gfin, bgfin = k.sb(st, [128, D], F32, "gfin")
            k.dma("sp", gfin[:], di["g_fin"], bgfin)
        halo, bhalo = k.sb(st, [128, NFC, 2], F32, "halo")
        k.memset("pool", halo[:], 0.0, [bhalo])
        xt = [k.sb(st, [128, NSB, D], F32, "xt") for _ in range(2)]
        junk, bjunk = k.sb(st, [128, D], BF16, "junk")
        ss, bss = k.sb(st, [128, 2 * NSB], F32, "ss")
        ms, bms = k.sb(st, [128, 2 * NSB], F32, "ms")
        sd, bsd = k.sb(st, [128, 2 * NSB], F32, "sd")
        rstd, brstd = k.sb(st, [128, 2 * NSB], F32, "rstd")
        hh = [k.sb(st, [128, D], BF16, "h") for _ in range(2)]
        hTs = [k.sb(st, [128, 8, TT], BF16, "hT") for _ in range(2)]
        a_sb = [k.sb(st, [128, TT + 2], F32, "a_sb") for _ in range(2)]
        cv = [k.sb(st, [128, TT], F32, "cv") for _ in range(2)]
        gl = [k.sb(st, [128, TT], F32, "gl") for _ in range(2)]
        actTs = [k.sb(st, [128, NFC, TT], BF16, "actT") for _ in range(2)]
        xo = [k.sb(st, [128, NSB, D], F32, "xo") for _ in range(1)]

        def load_tile(i):
            t, b = xt[i % 2]
            k.dma("sp", t[:], sc["xmid"][i * TT:(i + 1) * TT, :].rearrange("(s p) d -> p s d", p=128), b)

        def rms(x_ap, bx, col, g_t, bg, out_ap, bout):
            k.act(junk[:], x_ap, AF.Square, r=[bx], w=[bjunk, bss], accum=ss[:, col:col + 1])
            k.ts("dve", ms[:, col:col + 1], ss[:, col:col + 1], 1.0 / D, ALU.mult, r=[bss], w=[bms], s2=EPS, op1=ALU.add)
            k.act(sd[:, col:col + 1], ms[:, col:col + 1], AF.Sqrt, r=[bms], w=[bsd])
            k.recip(rstd[:, col:col + 1], sd[:, col:col + 1], r=[bsd], w=[brstd])
            k.stt(out_ap, x_ap, rstd[:, col:col + 1], g_t[:], ALU.mult, ALU.mult, r=[bx, brstd, bg], w=[bout])

        def prepA(i):
            x_t, bx = xt[i % 2]
            for s_ in range(NSB):
                h_t, bh = hh[s_ % 2]
                rms(x_t[:, s_, :], bx, s_, gam, bgam, h_t[:], bh)

        def prep(i):
            hT, bhT = hTs[i % 2]
            for s_ in range(NSB):
                h_t, bh = hh[s_ % 2]
                ps, bp = ring.get()
                pbf = ps[:].bitcast(BF16)
                for c in range(8):
                    k.tr(pbf[:, c * 128:(c + 1) * 128], h_t[:, c * 128:(c + 1) * 128], ident[:], r=[bh, bid], w=[bp])
                k.copy("act", hT[:, :, s_ * 128:(s_ + 1) * 128], pbf.rearrange("p (c t) -> p c t", c=8), r=[bp], w=[bhT])

        def inproj(i):
            hT, bhT = hTs[i % 2]
            actT, bactT = actTs[i % 2]
            pend = []
            for fc in range(NFC + 1):
                if fc == NFC:
                    pend.pop(0)()
                    break
                if fc == NFC // 2 and i + 1 < NTT:
                    prepA(i + 1)
                psa, bpa = ring.get()
                for c in range(8):
                    k.mm(psa[:, 0:TT], w_fin[:, c, fc * 128:(fc + 1) * 128], hT[:, c, :], c == 0, c == 7, r=[bw_fin[c], bhT], w=[bpa])
                psb, bpb = ring.get()
                for c in range(8):
                    k.mm(psb[:, 0:TT], w_fin[:, c, DFF + fc * 128:DFF + (fc + 1) * 128], hT[:, c, :], c == 0, c == 7, r=[bw_fin[c], bhT], w=[bpb])
                a_t, ba = a_sb[fc % 2]
                c_t, bc = cv[fc % 2]
                g_t, bg = gl[fc % 2]
                k.copy("pool", a_t[:, 0:2], halo[:, fc, :], r=[bhalo], w=[ba])
                k.copy("act", a_t[:, 2:2 + TT], psa[:, 0:TT], r=[bpa], w=[ba])
                k.copy("pool", halo[:, fc, :], a_t[:, TT:TT + 2], r=[ba], w=[bhalo])
                k.act(c_t[:], psa[:, 0:TT], AF.Identity, r=[bpa, bcw, bcb], w=[bc], scale=cw[:, fc, 2:3], bias=cb[:, fc:fc + 1])
                k.stt(c_t[:], a_t[:, 1:1 + TT], cw[:, fc, 1:2], c_t[:], ALU.mult, ALU.add, r=[ba, bcw, bc], w=[bc])
                k.stt(c_t[:], a_t[:, 0:TT], cw[:, fc, 0:1], c_t[:], ALU.mult, ALU.add, r=[ba, bcw, bc], w=[bc])

                def stage2(fc=fc, c_t=c_t, bc=bc, g_t=g_t, bg=bg, psb=psb, bpb=bpb):
                    k.act(g_t[:], c_t[:], AF.Gelu_apprx_tanh, r=[bc], w=[bg])
                    k.tt("dve", actT[:, fc, :], psb[:, 0:TT], g_t[:], ALU.mult, r=[bpb, bg], w=[bactT])
                pend.append(stage2)
                if len(pend) > 1:
                    pend.pop(0)()

        def outproj(i):
            x_t, bx = xt[i % 2]
            actT, bactT = actTs[i % 2]
            xo_t, bxo = xo[0]
            for s_ in range(NSB):
                for half in range(2):
                    ps, bp = ring.get()
                    for fc in range(NFC):
                        k.mm(ps[:, 0:512], actT[:, fc, s_ * 128:(s_ + 1) * 128], w_fo[:, fc, half * 512:(half + 1) * 512], fc == 0, fc == NFC - 1,
                             r=[bactT, bw_fo[fc]], w=[bp])
                    k.tt("dve", xo_t[:, s_, half * 512:(half + 1) * 512], ps[:, 0:512], x_t[:, s_, half * 512:(half + 1) * 512], ALU.add, r=[bp, bx], w=[bxo])
            if last:
                for s_ in range(NSB):
                    rms(xo_t[:, s_, :], bxo, NSB + s_, gfin, bgfin, xo_t[:, s_, :], bxo)
            k.dma("sp", dst[i * TT:(i + 1) * TT, :].rearrange("(s p) d -> p s d", p=128), xo_t[:], bxo)

        load_tile(0)
        if NTT > 1:
            load_tile(1)
        prepA(0)
        prep(0)
        for i in range(NTT):
            inproj(i)
            if i + 1 < NTT:
                prep(i + 1)
            outproj(i)
            if i + 2 < NTT:
                load_tile(i + 2)
    P.barrier()


INPUT_SHAPES = None


def build(S, L, TS=128, dbg=False, phases=("p1", "p2", "p3", "p4")):
    nc = bass.Bass("TRN2", target_bir_lowering=False)
    NT = S // 128
    di = {}

    def din(name, shape):
        di[name] = nc.dram_tensor(name, list(shape), F32, kind="ExternalInput").ap()
    din("x", [S, D])
    for nm, shp in (("g_mix", [L, 128, D]), ("g_ffn", [L, 128, D]), ("g_fin", [128, D]),
                    ("w_in", [L, D, INW]), ("w_sw", [L, D, 896]),
                    ("s_are", [L, 128, 16]), ("s_aim", [L, 128, 16]), ("s_ldt", [L, 128, 16]),
                    ("s_bre", [L, 128, 16, 128]), ("s_bim", [L, 128, 16, 128]),
                    ("s_cre", [L, 128, 16, 128]), ("s_cim", [L, 128, 16, 128]), ("s_d", [L, 128, 4]),
                    ("w_glu", [L, 512, 1024]), ("w_bssm", [L, 512, 1024]), ("w_bnsa", [L, 512, 1024]),
                    ("w_out", [L, D, D]),
                    ("c_pek", [L, 64, 32]), ("c_w1k", [L, 2048, 256]), ("c_b1k", [L, 128, 2]), ("c_w2k", [L, 256, 64]),
                    ("c_pev", [L, 64, 32]), ("c_w1v", [L, 2048, 256]), ("c_b1v", [L, 128, 2]), ("c_w2v", [L, 256, 64]),
                    ("w_fin", [L, D, 2 * DFF]), ("w_fout", [L, DFF, D]), ("f_cw", [L, 128, NFC, 3]), ("f_cb", [L, 128, NFC]),
                    ("c_cos", [128, S]), ("c_sin", [128, S]), ("c_ident", [128, 128]), ("c_tau", [128, TS]),
                    ("c_caus", [128, 128]), ("c_low", [128, 128]), ("c_mgen", [128, 1024]), ("c_mt", [128, 16, 128]),
                    ("c_g", [128, 256]), ("c_ind", [64, S])):
        din(nm, shp)
    out = nc.dram_tensor("out", [S, D], F32, kind="ExternalOutput").ap()
    skind = "ExternalOutput" if dbg else "Internal"
    sc = {}

    def scr(name, shape, dt):
        sc[name] = nc.dram_tensor(name, list(shape), dt, kind=skind).ap()
    scr("qT", [512, S], BF16)
    for nm in ("kcT", "vcT", "ksT", "kwT"):
        scr(nm, [128, S], BF16)
    scr("vs", [S, 2, 65], BF16)
    scr("vw", [S, 2, 65], BF16)
    scr("sgTok", [S, 24], F32)
    scr("sgnT", [1024, S], BF16)
    scr("gssT", [1024, S], BF16)
    scr("xmid", [S, D], F32)
    if dbg:
        scr("dbg_o", [S, 512], F32)
        scr("dbg_kc", [2, 64, S // 16], BF16)
        scr("dbg_vc", [128, S // 2048, 2, 65], BF16)
    scr("x1", [S, D], F32)
    with ExitStack() as st:
        P = Prog(nc)
        k = K(nc, P)
        cst = {}
        ident, bid = k.sb(st, [128, 128], BF16, "ident")
        k.dma("pool", ident[:], di["c_ident"], bid)
        cst["ident"] = (ident, bid)
        x_src = di["x"]
        for l in range(L):
            last = l == L - 1
            if "p1" in phases:
                phase1(k, l, S, TS, x_src, di, sc, cst)
            if "p2" in phases:
                pers = ExitStack()
                cmp_t = phase2(k, pers, l, S, di, sc, cst)
            if "p3" in phases:
                phase3(k, l, S, x_src, di, sc, cst, cmp_t)
            if "p2" in phases:
                pers.close()
                P.barrier()
            if "p4" in phases:
                phase4(k, l, S, di, sc, cst, out if last else sc["x1"], last)
            x_src = sc["x1"]
        P.barrier()
        P.emit(st)
    return nc


_NC_CACHE = {}


def kernel(**inputs):
    S, L, NCORES = 8192, 2, 8
    inp = {k_: np.asarray(v) for k_, v in inputs.items()}
    hl = host_layout(inp, L)
    hc = host_consts(S, 128)
    common = {}
    common.update(hl)
    common.update(hc)
    common = {k_: np.ascontiguousarray(v, dtype=np.float32) for k_, v in common.items()}
    if "nc" not in _NC_CACHE:
        _NC_CACHE["nc"] = build(S, L)
    nc = _NC_CACHE["nc"]
    x = np.asarray(inp["x"], dtype=np.float32)
    in_maps = []
    for b in range(NCORES):
        m = dict(common)
        m["x"] = np.ascontiguousarray(x[b])
        in_maps.append(m)
    res = run_bass_kernel_spmd(nc, in_maps, core_ids=list(range(NCORES)))
    return np.stack([np.asarray(r["out"], dtype=np.float32) for r in res.results], axis=0)
```
